# Optimizing a Trainium2 kernel written in Bass

```python
import math
import jax, jax.numpy as jnp
from jax import lax
import numpy as np

D_MODEL = 1024
BATCH = 2
SEQ = 8192
DEPTH = 1
DEC_BATCH = 128
DEC_SEQ = 4
PAST_LEN = 2048
PAGE_SIZE = 128

SB_HEADS = 8
SB_HEAD_DIM = D_MODEL // 16
SB_DIM = SB_HEADS * SB_HEAD_DIM
SB_Q_BLOCK = 128
SB_BIAS_LO = -7.0
SB_BIAS_HI = -5.0
GDN_HEADS = 4
GDN_HEAD_DIM = D_MODEL // 8
GDN_DIM = GDN_HEADS * GDN_HEAD_DIM
CONV_WIDTH = 4
CONV_DIM = 3 * GDN_DIM
GDN_CHUNK = 64
D_FF = 4 * D_MODEL
NORM_EPS = 1e-6
L2_EPS = 1e-6
IN_SPLITS = (SB_DIM, SB_DIM, SB_DIM, CONV_DIM, GDN_DIM, GDN_HEADS, GDN_HEADS, D_MODEL, D_MODEL)
IN_DIM = sum(IN_SPLITS)

kernel_name = "stickbreak_gdn_parallel_hybrid_step"

F32 = jnp.float32


def rmsnorm(x, w):
    xf = x.astype(F32)
    return xf * lax.rsqrt(jnp.mean(xf * xf, axis=-1, keepdims=True) + NORM_EPS) * w.astype(F32)


def l2norm(x):
    return x * lax.rsqrt(jnp.sum(x * x, axis=-1, keepdims=True) + L2_EPS)


def stick_breaking(q, k, v, q_pos, k_pos, sb_bias):
    z = (jnp.einsum('bqhd,bkhd->bhqk', q, k) * (q.shape[-1] ** -0.5)
         + sb_bias.astype(F32)[None, :, None, None])
    mask = k_pos[None, :] < q_pos[:, None]
    log_1m = jnp.where(mask, jax.nn.log_sigmoid(-z), 0.0)
    suffix = lax.cumsum(log_1m, axis=3, reverse=True) - log_1m
    w = jnp.where(mask, jnp.exp(jax.nn.log_sigmoid(z) + suffix), 0.0)
    return jnp.einsum('bhqk,bkhd->bqhd', w, v)


def sb_prompt(q, k, v, sb_bias):
    bsz, t_len, n_h, d = q.shape
    n_blk = t_len // SB_Q_BLOCK
    qb = jnp.moveaxis(q.reshape(bsz, n_blk, SB_Q_BLOCK, n_h, d), 1, 0)
    pb = jnp.arange(t_len).reshape(n_blk, SB_Q_BLOCK)
    k_pos = jnp.arange(t_len)
    out = lax.map(lambda a: stick_breaking(a[0], k, v, a[1], k_pos, sb_bias), (qb, pb))
    return jnp.moveaxis(out, 0, 1).reshape(bsz, t_len, n_h, d)


def sb_extend(q, k, v, k_past, v_past, sb_bias):
    past = k_past.shape[1]
    t_len = q.shape[1]
    k_all = jnp.concatenate([k_past, k], axis=1)
    v_all = jnp.concatenate([v_past, v], axis=1)
    q_pos = past + jnp.arange(t_len)
    k_pos = jnp.arange(past + t_len)
    return stick_breaking(q, k_all, v_all, q_pos, k_pos, sb_bias)


def short_conv(u_ext, w_conv):
    out = lax.conv_general_dilated(u_ext, w_conv[:, None, :], window_strides=(1,), padding='VALID',
                                   dimension_numbers=('NWC', 'WIO', 'NWC'),
                                   feature_group_count=u_ext.shape[-1])
    return jax.nn.silu(out)


def gated_delta_chunked(q, k, v, g, beta, s0, chunk):
    bsz, t_len, n_h, _ = q.shape
    d_v = v.shape[-1]
    n_chunks = t_len // chunk

    def to_chunks(a):
        a = a.reshape((bsz, n_chunks, chunk) + a.shape[2:])
        return jnp.moveaxis(jnp.swapaxes(a, 2, 3), 1, 0)

    incl = jnp.tril(jnp.ones((chunk, chunk), dtype=bool))
    strict = jnp.tril(jnp.ones((chunk, chunk), dtype=bool), -1)
    eye = jnp.eye(chunk, dtype=F32)

    def step(s, inp):
        qc, kc, vc, gc, bc = inp
        cg = jnp.cumsum(gc, axis=-1)
        decay = jnp.exp(jnp.where(incl, cg[..., :, None] - cg[..., None, :], -jnp.inf))
        m = jnp.where(strict, bc[..., :, None] * jnp.einsum('bhik,bhjk->bhij', kc, kc) * decay, 0.0)
        rhs = jnp.concatenate([vc * bc[..., None], kc * (bc * jnp.exp(cg))[..., None]], axis=-1)
        sol = lax.linalg.triangular_solve(eye + m, rhs, left_side=True, lower=True, unit_diagonal=True)
        u, w = sol[..., :d_v], sol[..., d_v:]
        v_new = u - jnp.einsum('bhck,bhkv->bhcv', w, s)
        o = (jnp.einsum('bhck,bhkv->bhcv', qc * jnp.exp(cg)[..., None], s)
             + jnp.einsum('bhij,bhjv->bhiv', jnp.einsum('bhik,bhjk->bhij', qc, kc) * decay, v_new))
        g_last = cg[..., -1:]
        s = s * jnp.exp(g_last)[..., None] + jnp.einsum('bhck,bhcv->bhkv', kc * jnp.exp(g_last - cg)[..., None], v_new)
        return s, o

    s_fin, o = lax.scan(step, s0, (to_chunks(q), to_chunks(k), to_chunks(v), to_chunks(g), to_chunks(beta)))
    o = jnp.swapaxes(jnp.moveaxis(o, 0, 1), 2, 3).reshape(bsz, t_len, n_h, d_v)
    return o, s_fin


def gdn_branch(conv_out, z, a, b, s0, a_log, dt_bias, gdn_norm_w):
    bsz, t_len, _ = conv_out.shape
    q, k, v = jnp.split(conv_out, 3, axis=-1)
    shp = (bsz, t_len, GDN_HEADS, GDN_HEAD_DIM)
    q = l2norm(q.reshape(shp)) * (GDN_HEAD_DIM ** -0.5)
    k = l2norm(k.reshape(shp))
    v = v.reshape(shp)
    beta = jax.nn.sigmoid(b)
    g = -jnp.exp(a_log.astype(F32)) * jax.nn.softplus(a + dt_bias.astype(F32))
    chunk = math.gcd(GDN_CHUNK, t_len)
    o, s_new = gated_delta_chunked(q, k, v, g, beta, s0.astype(F32), chunk)
    o = o * lax.rsqrt(jnp.mean(o * o, axis=-1, keepdims=True) + NORM_EPS) * gdn_norm_w.astype(F32)
    o = o * jax.nn.silu(z.reshape(shp))
    return o.reshape(bsz, t_len, GDN_DIM), s_new


def decoder_layer(x, k_past, v_past, conv_prev, ssm_prev, norm_mix_w, w_in, sb_bias, conv_w, a_log, dt_bias,
                  gdn_norm_w, w_pa, w_pb, w_o, norm_mlp_w, w_up, w_down):
    bsz, t_len, _ = x.shape
    h = rmsnorm(x, norm_mix_w)
    p = h @ w_in.astype(F32)
    q_a, k_a, v_a, u, z, a, b, g_a, g_b = jnp.split(p, [int(i) for i in np.cumsum(IN_SPLITS)[:-1]], axis=-1)
    hs = (bsz, t_len, SB_HEADS, SB_HEAD_DIM)
    q_a, k_a, v_a = q_a.reshape(hs), k_a.reshape(hs), v_a.reshape(hs)
    if k_past is None:
        o_a = sb_prompt(q_a, k_a, v_a, sb_bias)
    else:
        o_a = sb_extend(q_a, k_a, v_a, k_past.astype(F32), v_past.astype(F32), sb_bias)
    u_ext = jnp.concatenate([conv_prev.astype(F32), u], axis=1)
    conv_new = u_ext[:, -(CONV_WIDTH - 1):]
    o_b, ssm_new = gdn_branch(short_conv(u_ext, conv_w.astype(F32)), z, a, b, ssm_prev, a_log, dt_bias, gdn_norm_w)
    y_a = o_a.reshape(bsz, t_len, SB_DIM) @ w_pa.astype(F32)
    y_b = o_b @ w_pb.astype(F32)
    mix = (jax.nn.sigmoid(g_a) * y_a + jax.nn.sigmoid(g_b) * y_b) @ w_o.astype(F32)
    x = x + mix
    hm = rmsnorm(x, norm_mlp_w)
    x = x + jnp.square(jax.nn.relu(hm @ w_up.astype(F32))) @ w_down.astype(F32)
    return x, k_a, v_a, conv_new, ssm_new


def setup_inputs(seed: int = 0) -> dict:
    key = jax.random.key(seed)
    ks = jax.random.split(key, 24)
    n_pages = PAST_LEN // PAGE_SIZE
    n_used = DEC_BATCH * n_pages
    n_pool = n_used + n_used // 4
    nrm = jax.random.normal
    x_prompt = nrm(ks[0], (BATCH, SEQ, D_MODEL), F32)
    x_sample = nrm(ks[1], (DEC_BATCH, DEC_SEQ, D_MODEL), F32)
    cache_k = nrm(ks[2], (DEPTH, n_pool, PAGE_SIZE, SB_HEADS, SB_HEAD_DIM), F32)
    cache_v = nrm(ks[3], (DEPTH, n_pool, PAGE_SIZE, SB_HEADS, SB_HEAD_DIM), F32)
    page_table = jax.random.permutation(ks[4], n_pool)[:n_used].reshape(DEC_BATCH, n_pages).astype(jnp.int32)
    state_conv = nrm(ks[5], (DEPTH, DEC_BATCH, CONV_WIDTH - 1, CONV_DIM), F32)
    state_ssm = 0.1 * nrm(ks[6], (DEPTH, DEC_BATCH, GDN_HEADS, GDN_HEAD_DIM, GDN_HEAD_DIM), F32)
    norm_mix_w = 1.0 + 0.02 * nrm(ks[7], (DEPTH, D_MODEL), F32)
    w_in = nrm(ks[8], (DEPTH, D_MODEL, IN_DIM), F32) * D_MODEL ** -0.5
    sb_bias = jax.random.uniform(ks[20], (DEPTH, SB_HEADS), F32, SB_BIAS_LO, SB_BIAS_HI)
    conv_w = nrm(ks[9], (DEPTH, CONV_WIDTH, CONV_DIM), F32) * CONV_WIDTH ** -0.5
    a_log = jnp.log(jax.random.uniform(ks[10], (DEPTH, GDN_HEADS), F32, 1.0, 16.0))
    dt = jnp.exp(jax.random.uniform(ks[11], (DEPTH, GDN_HEADS), F32, math.log(1e-3), math.log(1e-1)))
    dt_bias = dt + jnp.log(-jnp.expm1(-dt))
    gdn_norm_w = 1.0 + 0.02 * nrm(ks[12], (DEPTH, GDN_HEAD_DIM), F32)
    w_pa = nrm(ks[13], (DEPTH, SB_DIM, D_MODEL), F32) * SB_DIM ** -0.5
    w_pb = nrm(ks[14], (DEPTH, GDN_DIM, D_MODEL), F32) * GDN_DIM ** -0.5
    w_o = nrm(ks[15], (DEPTH, D_MODEL, D_MODEL), F32) * D_MODEL ** -0.5
    norm_mlp_w = 1.0 + 0.02 * nrm(ks[16], (DEPTH, D_MODEL), F32)
    w_up = nrm(ks[17], (DEPTH, D_MODEL, D_FF), F32) * D_MODEL ** -0.5
    w_down = nrm(ks[18], (DEPTH, D_FF, D_MODEL), F32) * D_FF ** -0.5
    norm_final_w = 1.0 + 0.02 * nrm(ks[19], (D_MODEL,), F32)
    return {"x_prompt": x_prompt, "x_sample": x_sample, "cache_k": cache_k, "cache_v": cache_v,
            "page_table": page_table, "state_conv": state_conv, "state_ssm": state_ssm,
            "norm_mix_w": norm_mix_w, "w_in": w_in, "sb_bias": sb_bias, "conv_w": conv_w, "a_log": a_log,
            "dt_bias": dt_bias, "gdn_norm_w": gdn_norm_w, "w_pa": w_pa, "w_pb": w_pb, "w_o": w_o,
            "norm_mlp_w": norm_mlp_w, "w_up": w_up, "w_down": w_down, "norm_final_w": norm_final_w}


def reference(x_prompt, x_sample, cache_k, cache_v, page_table, state_conv, state_ssm,
              norm_mix_w, w_in, sb_bias, conv_w, a_log, dt_bias, gdn_norm_w, w_pa, w_pb, w_o,
              norm_mlp_w, w_up, w_down, norm_final_w):
    b_p = x_prompt.shape[0]
    b_s = x_sample.shape[0]
    h_p = x_prompt.astype(F32)
    h_s = x_sample.astype(F32)
    conv0 = jnp.zeros((b_p, CONV_WIDTH - 1, CONV_DIM), F32)
    ssm0 = jnp.zeros((b_p, GDN_HEADS, GDN_HEAD_DIM, GDN_HEAD_DIM), F32)
    kp_l, vp_l, ks_l, vs_l, cp_l, cs_l, sp_l, ss_l = [], [], [], [], [], [], [], []
    for l in range(DEPTH):
        lw = (norm_mix_w[l], w_in[l], sb_bias[l], conv_w[l], a_log[l], dt_bias[l], gdn_norm_w[l], w_pa[l],
              w_pb[l], w_o[l], norm_mlp_w[l], w_up[l], w_down[l])
        h_p, k_p, v_p, c_p, s_p = decoder_layer(h_p, None, None, conv0, ssm0, *lw)
        k_past = cache_k[l][page_table].reshape(b_s, -1, SB_HEADS, SB_HEAD_DIM)
        v_past = cache_v[l][page_table].reshape(b_s, -1, SB_HEADS, SB_HEAD_DIM)
        h_s, k_s, v_s, c_s, s_s = decoder_layer(h_s, k_past, v_past, state_conv[l], state_ssm[l], *lw)
        kp_l.append(k_p); vp_l.append(v_p); ks_l.append(k_s); vs_l.append(v_s)
        cp_l.append(c_p); cs_l.append(c_s); sp_l.append(s_p); ss_l.append(s_s)
    y_prompt = rmsnorm(h_p, norm_final_w).astype(x_prompt.dtype)
    y_sample = rmsnorm(h_s, norm_final_w).astype(x_sample.dtype)
    page_shape = (DEPTH, b_p, -1, PAGE_SIZE, SB_HEADS, SB_HEAD_DIM)
    new_k_prompt = jnp.stack(kp_l).reshape(page_shape).astype(cache_k.dtype)
    new_v_prompt = jnp.stack(vp_l).reshape(page_shape).astype(cache_v.dtype)
    new_k_sample = jnp.stack(ks_l).astype(cache_k.dtype)
    new_v_sample = jnp.stack(vs_l).astype(cache_v.dtype)
    new_conv_prompt = jnp.stack(cp_l).astype(state_conv.dtype)
    new_conv_sample = jnp.stack(cs_l).astype(state_conv.dtype)
    new_ssm_prompt = jnp.stack(sp_l).astype(state_ssm.dtype)
    new_ssm_sample = jnp.stack(ss_l).astype(state_ssm.dtype)
    return (y_prompt, y_sample, new_k_prompt, new_v_prompt, new_k_sample, new_v_sample,
            new_conv_prompt, new_conv_sample, new_ssm_prompt, new_ssm_sample)
```

```python
import os
import numpy as np
import concourse.bass as bass
import concourse.mybir as mybir
from concourse.bass_utils import run_bass_kernel_spmd

F32 = mybir.dt.float32
BF16 = mybir.dt.bfloat16
I32 = mybir.dt.int32
AF = mybir.ActivationFunctionType
ALU = mybir.AluOpType
AX = mybir.AxisListType

EPOCH = 20000


class Res:
    _n = 0

    def __init__(self, name):
        Res._n += 1
        self.name = f"{name}#{Res._n}"
        self.writers = []
        self.readers = []
        self.dma_sem = None
        self.dma_cnt = 0
        self.pe_acc = False
        self.store_res = None


class T:
    def __init__(self, ap, res):
        self.ap = ap
        self.res = res

    def __getitem__(self, idx):
        return T(self.ap[idx], self.res)


class Op:
    __slots__ = ("eng", "idx", "fn", "waits", "needed", "dma_res", "clock", "semval", "is_dma")


class Prog:
    ENGS = ("pe", "act", "dve", "pool", "sp")

    def __init__(self, nc):
        self.nc = nc
        self.ops = {e: [] for e in self.ENGS}
        self.clock = {e: {} for e in self.ENGS}
        self.dma_res = []
        self.dma_clock = {}
        self._dom2res = {}
        self.all_res = []

    def res(self, name):
        r = Res(name)
        self.all_res.append(r)
        return r

    def _add(self, eng, fn, reads, writes, dma=False, pe_acc=False, evres=None, nowaw=False):
        reads = [r for r in reads if r is not None and r.res is not None]
        writes = [w for w in writes if w is not None and w.res is not None]
        op = Op()
        op.eng = eng
        op.idx = len(self.ops[eng]) + 1
        op.fn = fn
        op.needed = False
        op.is_dma = dma
        op.dma_res = None
        deps = []
        for r in reads:
            deps += r.res.writers
        for w in writes:
            if pe_acc and eng == "pe" and w.res.pe_acc and all(ev[0] == "pe" for ev in w.res.writers):
                deps += w.res.readers
            else:
                if not nowaw:
                    deps += w.res.writers
                deps += w.res.readers
        clk = self.clock[eng]
        waits = []
        best = {}
        for ev in deps:
            dom, val = ev[0], ev[1]
            if dom == eng and eng in ("pe", "sp"):
                continue
            if clk.get(dom, 0) >= val:
                continue
            if best.get(dom, 0) < val:
                best[dom] = val
        for dom, val in best.items():
            waits.append((dom, val))
            if isinstance(dom, str):
                src = self.ops[dom][val - 1]
                src.needed = True
                for d2, v2 in src.clock.items():
                    if clk.get(d2, 0) < v2:
                        clk[d2] = v2
            else:
                snap = self.dma_clock.get((dom, val), {})
                for d2, v2 in snap.items():
                    if clk.get(d2, 0) < v2:
                        clk[d2] = v2
            if clk.get(dom, 0) < val:
                clk[dom] = val
        op.waits = waits
        if dma:
            tgt = evres
            if tgt.dma_sem is None:
                tgt.dma_sem = True
                self.dma_res.append(tgt)
            tgt.dma_cnt += 1
            if eng == "pool":
                tgt.sw = True
            op.dma_res = tgt
            ev = (("dma", tgt.name), tgt.dma_cnt)
            self.dma_clock[ev] = dict(clk)
            self._dom2res[("dma", tgt.name)] = tgt
        else:
            ev = (eng, op.idx)
        op.clock = dict(clk)
        if not dma:
            op.clock[eng] = op.idx
        for r in reads:
            r.res.readers.append(ev)
        for w in writes:
            if pe_acc and eng == "pe" and w.res.pe_acc and all(e2[0] == "pe" for e2 in w.res.writers):
                w.res.writers = [ev]
            else:
                w.res.writers = [ev]
            w.res.readers = []
            w.res.pe_acc = bool(pe_acc and eng == "pe")
        self.ops[eng].append(op)
        return op

    def op(self, eng, fn, reads=(), writes=(), pe_acc=False):
        return self._add(eng, fn, list(reads), list(writes), pe_acc=pe_acc)

    def dma(self, eng, out, in_, store=False, nowaw=False, fn=None, **kw):
        if store:
            if in_.res.store_res is None:
                in_.res.store_res = Res("st_" + in_.res.name)
            evres = in_.res.store_res
        else:
            evres = out.res
        if fn is None:
            def fn(e, o=out.ap, i=in_.ap, kw=kw):
                return e.dma_start(out=o, in_=i, **kw)
        return self._add(eng, fn, [in_], [out], dma=True, evres=evres, nowaw=nowaw)

    def barrier(self):
        evs = []
        for e in self.ENGS:
            for op in reversed(self.ops[e]):
                if not op.is_dma and op.fn is not None:
                    evs.append((e, op.idx))
                    break
        for r in self.dma_res:
            if r.dma_cnt:
                evs.append((("dma", r.name), r.dma_cnt))
        bres = T(None, Res("barrier"))
        bres.res.writers = evs
        for e in self.ENGS:
            self._add(e, None, [bres], [])

    def setup_sems(self, stack, n_dma=70):
        nc = self.nc
        self.dma_pool_sw = [[stack.enter_context(nc.semaphore(f"dsw_{k}")), 0] for k in range(8)]
        self.sems = {e: [stack.enter_context(nc.semaphore(f"s_{e}_{k}")) for k in range(3)] for e in self.ENGS}
        self.count = {e: 0 for e in self.ENGS}
        self.dma_pool = [[stack.enter_context(nc.semaphore(f"d_{k}")), 0] for k in range(n_dma)]
        self.all_res = []

    def emit(self):
        nc = self.nc
        self.barrier()
        nsem = {}
        for e in self.ENGS:
            c = self.count[e]
            for op in self.ops[e]:
                if op.needed and not op.is_dma:
                    c += 1
                    op.semval = c
                else:
                    op.semval = None
            nsem[e] = c - self.count[e]
            self.count[e] = c
        ihw = isw = 0
        for r in self.dma_res:
            if getattr(r, "sw", False):
                slot = self.dma_pool_sw[isw]
                isw += 1
            else:
                slot = self.dma_pool[ihw]
                ihw += 1
            r.dma_sem = slot
            r.dma_base = slot[1]
            slot[1] += 16 * r.dma_cnt
        print("pass ops:", {e: len(self.ops[e]) for e in self.ENGS}, "flagged:", nsem, "dma sems:", len(self.dma_res), flush=True)
        if os.environ.get("DUMP_WAITS"):
            for e in self.ENGS:
                for op in self.ops[e][:60]:
                    ws = []
                    for dom, val in op.waits:
                        if isinstance(dom, str):
                            ws.append((dom, val, self.ops[dom][val - 1].semval))
                        else:
                            r = self._dom2res[dom]
                            ws.append((dom[1], val, r.dma_base + 16 * val))
                    print("  ", e, op.idx, "dma" if op.is_dma else "", "sem=%s" % op.semval, ws)
        sems = self.sems
        engobj = {"pe": "tensor", "act": "scalar", "dve": "vector", "pool": "gpsimd", "sp": "sync"}
        with nc.Block() as block:
            def make(e):
                def body(eng):
                    for op in self.ops[e]:
                        for dom, val in op.waits:
                            if isinstance(dom, str):
                                sv = self.ops[dom][val - 1].semval
                                k = (sv - 1) // EPOCH
                                eng.wait_ge(sems[dom][k], sv - k * EPOCH)
                            else:
                                r = self._dom2res[dom]
                                eng.wait_ge(r.dma_sem[0], r.dma_base + 16 * val)
                        if op.fn is None:
                            continue
                        ins = op.fn(eng)
                        if op.is_dma:
                            ins.then_inc(op.dma_res.dma_sem[0], 16)
                        elif op.semval is not None:
                            k = (op.semval - 1) // EPOCH
                            ins.then_inc(sems[e][k], 1)
                return body
            for e in self.ENGS:
                getattr(block, engobj[e])(make(e))
        self.ops = {e: [] for e in self.ENGS}
        self.clock = {e: {} for e in self.ENGS}
        for r in self.all_res:
            r.writers = []
            r.readers = []
            r.dma_sem = None
            r.dma_cnt = 0
            r.pe_acc = False
            r.store_res = None
            r.sw = False
        self.dma_res = []
        self.dma_clock = {}
        self._dom2res = {}


from contextlib import ExitStack
import os
import ml_dtypes

NCORE = 8
D = 1024
IN_DIM = 5640
SEQ = 8192
NT_FULL = SEQ // 128
OWN = 2048
NT_OWN = OWN // 128
NS_TOK = 64
NCST = 13
BIG = 300.0
EPS = 1e-6


class Ring:
    def __init__(self, tiles):
        self.tiles = tiles
        self.i = 0

    def next(self):
        t = self.tiles[self.i % len(self.tiles)]
        self.i += 1
        return t


def build():
    nc = bass.Bass("TRN2", target_bir_lowering=False)

    def din(name, shape, dt=F32):
        return nc.dram_tensor(name, list(shape), dt, kind="ExternalInput").ap()

    def dout(name, shape, dt=F32):
        return nc.dram_tensor(name, list(shape), dt, kind="ExternalOutput").ap()

    xfull = din("xfull", [SEQ, D])
    xown = din("xown", [OWN, D])
    xs = din("xs", [NS_TOK, D])
    w_in = din("w_in", [D, IN_DIM])
    nmw = din("nmw", [128, 8])
    ident_d = din("ident", [128, 128])
    cst_d = din("cst", [128, NCST, 128])
    cw_d = din("cw", [128, 12, 4])
    alog_d = din("alog_bc", [128, 4])
    dtb_d = din("dtb_bc", [128, 4])
    gnw_d = din("gnw_bc", [128, 128])
    sel_d = din("sel", [128, 4])
    sbb_d = din("sbb", [128, 8])
    mask_d = din("maskd", [128, 16, 512], BF16)
    sc_d = din("sc", [48, 1536])
    NPOOL = int(os.environ.get('NPOOL_DBG', '2560'))
    ck_d = din("cache_k", [NPOOL * 128, 512])
    cv_d = din("cache_v", [NPOOL * 128, 512])
    pt_d = din("pt", [1, 256], I32)
    iota_d = din("iota", [128, 256], I32)
    sbrow_d = din("sbrow", [128, 512])
    mnew_d = din("mnew", [128, 32])
    selE_d = din("selE", [64, 256])
    dm_d = din("dm", [32, 512])
    ssm_d = din("ssm", [16, 4, 128, 128])
    nss = dout("nss", [16, 4, 128, 128])
    nsp = dout("nsp", [4, 128, 128])
    hT_d = nc.dram_tensor("hT_d", [NT_FULL, 128, 1024], BF16, kind="Internal").ap()
    hTown_d = nc.dram_tensor("hTown_d", [NT_OWN, 128, 1024], BF16, kind="Internal").ap()
    ps_d = nc.dram_tensor("ps_d", [NS_TOK, IN_DIM], F32, kind="Internal").ap()
    x1_d = nc.dram_tensor("x1_d", [OWN + NS_TOK, D], F32, kind="Internal").ap()
    hmT_d = nc.dram_tensor("hmT_d", [NT_OWN + 1, 128, 1024], BF16, kind="Internal").ap()
    w_pa_d = din("w_pa", [512, D])
    w_pb_d = din("w_pb", [512, D])
    w_o_d = din("w_o", [D, D])
    w_up_d = din("w_up", [D, 4 * D])
    w_down_d = din("w_down", [4 * D, D])
    nmlw_d = din("nmlw", [128, 8])
    nfw_d = din("nfw_bc", [128, D])
    y_own = dout("y_own", [OWN, D])
    y_s = dout("y_s", [NS_TOK, D])

    nk_own = dout("nk_own", [OWN, 512])
    nv_own = dout("nv_own", [OWN, 512])
    nks = dout("nks", [NS_TOK, 512])
    nvs = dout("nvs", [NS_TOK, 512])
    ncp = dout("ncp", [3, 1536])
    ncs = dout("ncs", [16, 3, 1536])

    P = Prog(nc)
    with ExitStack() as gst:
        P.setup_sems(gst)

        uid = [0]

        def sbt(st, name, shape, dt):
            uid[0] += 1
            name = f"{name}_{uid[0]}"
            return T(st.enter_context(nc.sbuf_tensor(name, list(shape), dt))[:], P.res(name))

        def pst(st, name, shape, dt):
            uid[0] += 1
            name = f"{name}_{uid[0]}"
            return T(st.enter_context(nc.psum_tensor(name, list(shape), dt))[:], P.res(name))

        def ring(st, fn, name, shape, dt, n):
            return Ring([fn(st, f"{name}{i}", shape, dt) for i in range(n)])

        identf = sbt(gst, "identf", [128, 128], F32)
        identb = sbt(gst, "identb", [128, 128], BF16)
        epsc = sbt(gst, "epsc", [128, 1], F32)
        nmw_t = sbt(gst, "nmw_t", [128, 8], F32)
        hsT = sbt(gst, "hsT", [128, 8, NS_TOK], BF16)
        hTlast = sbt(gst, "hTlast", [128, 8, 128], BF16)
        onec = sbt(gst, "onec", [128, 1], F32)
        sel_t = sbt(gst, "sel_t", [128, 4], F32)


        def apx(v):
            return v.ap if isinstance(v, T) else v

        def mm(out, lhsT, rhs, start=True, stop=True):
            P.op("pe", lambda e: e.matmul(out=out.ap, lhsT=lhsT.ap, rhs=rhs.ap, start=start, stop=stop), [lhsT, rhs], [out], pe_acc=True)

        def tr(out, in_, idn):
            P.op("pe", lambda e: e.transpose(out=out.ap, in_=in_.ap, identity=idn.ap), [in_, idn], [out], pe_acc=True)

        def act(out, in_, func, bias=None, scale=None):
            kw = {}
            rd = [in_]
            if bias is not None:
                kw["bias"] = apx(bias)
                if isinstance(bias, T):
                    rd.append(bias)
            if scale is not None:
                kw["scale"] = apx(scale)
                if isinstance(scale, T):
                    rd.append(scale)
            P.op("act", lambda e: e.activation(out=out.ap, in_=in_.ap, func=func, **kw), rd, [out])

        def ts(out, in0, s1, op0, s2=None, op1=None, eng="dve"):
            rd = [in0] + [v for v in (s1, s2) if isinstance(v, T)]
            if op1 is None:
                P.op(eng, lambda e: e.tensor_scalar(out=out.ap, in0=in0.ap, scalar1=apx(s1), scalar2=None, op0=op0), rd, [out])
            else:
                P.op(eng, lambda e: e.tensor_scalar(out=out.ap, in0=in0.ap, scalar1=apx(s1), scalar2=apx(s2), op0=op0, op1=op1), rd, [out])

        def tt(out, in0, in1, op, eng="dve"):
            P.op(eng, lambda e: e.tensor_tensor(out=out.ap, in0=in0.ap, in1=in1.ap, op=op), [in0, in1], [out])

        def stt(out, in0, scalar, in1, op0, op1):
            rd = [in0, in1] + ([scalar] if isinstance(scalar, T) else [])
            P.op("dve", lambda e: e.scalar_tensor_tensor(out=out.ap, in0=in0.ap, scalar=apx(scalar), in1=in1.ap, op0=op0, op1=op1), rd, [out])

        def cp(out, in_, eng="dve"):
            if eng == "act":
                P.op("act", lambda e: e.copy(out=out.ap, in_=in_.ap), [in_], [out])
            else:
                P.op(eng, lambda e: e.tensor_copy(out=out.ap, in_=in_.ap), [in_], [out])

        def recip(out, in_):
            P.op("dve", lambda e: e.reciprocal(out=out.ap, in_=in_.ap), [in_], [out])

        def rsum(out, in_):
            P.op("dve", lambda e: e.reduce_sum(out=out.ap, in_=in_.ap, axis=AX.X), [in_], [out])

        def memset(t, v, eng="pool"):
            P.op(eng, lambda e: e.memset(t.ap, v), [], [t])

        def sub(t, idx, name):
            return T(t.ap[idx], P.res(name))

        def norm_tile(tl, x_rows_ap, hT_dst, nt=128):
            xt = tl["x"].next()
            P.dma("sp", xt[:nt], T(x_rows_ap, None))
            norm_core(tl, xt, hT_dst, nt)

        def norm_core(tl, xt, hT_dst, nt=128):
            sq = tl["sq"].next()
            P.op("act", lambda e: e.activation(out=sq.ap[:nt], in_=xt.ap[:nt], func=AF.Square), [xt], [sq])
            ss = tl["ss"].next()
            P.op("dve", lambda e: e.reduce_sum(out=ss.ap[:nt], in_=sq.ap[:nt], axis=AX.X), [sq], [ss])
            P.op("act", lambda e: e.activation(out=ss.ap[:nt], in_=ss.ap[:nt], func=AF.Ln, bias=epsc.ap[:nt], scale=1.0 / D), [ss, epsc], [ss])
            P.op("act", lambda e: e.activation(out=ss.ap[:nt], in_=ss.ap[:nt], func=AF.Exp, scale=-0.5), [ss], [ss])
            hb = tl["hb"].next()
            P.op("dve", lambda e: e.tensor_scalar(out=hb.ap[:nt], in0=xt.ap[:nt], scalar1=ss.ap[:nt, 0:1], scalar2=None, op0=ALU.mult), [xt, ss], [hb])
            pt = tl["ptr"].next()
            for k in range(8):
                P.op("pe", lambda e, k=k: e.transpose(out=pt.ap[:, k, :nt], in_=hb.ap[:nt, k * 128:(k + 1) * 128], identity=identb.ap[:nt, :nt]), [hb, identb], [pt], pe_acc=True)
            P.op("act", lambda e: e.copy(out=hT_dst.ap, in_=pt.ap[:, :, :nt]), [pt], [hT_dst])

        def load_w(tl, dst, w_dram, c0, n, scale=None, nk=8):
            for k in range(nk):
                stg = tl["wst"].next()
                P.dma("sp", stg[:, :n], T(w_dram[k * 128:(k + 1) * 128, c0:c0 + n], None))
                eng = ("pool", "dve")[k % 2]
                if scale is not None:
                    P.op(eng, lambda e, k=k, stg=stg: e.tensor_scalar(out=dst.ap[:, k, :n], in0=stg.ap[:, :n], scalar1=scale.ap[:, k:k + 1], scalar2=None, op0=ALU.mult), [stg, scale], [dst])
                else:
                    P.op(eng, lambda e, k=k, stg=stg: e.tensor_copy(out=dst.ap[:, k, :n], in_=stg.ap[:, :n]), [stg], [dst])

        with ExitStack() as st:
            tl = {
                "x": ring(st, sbt, "x", [128, D], F32, 3),
                "sq": ring(st, sbt, "sq", [128, D], F32, 2),
                "ss": ring(st, sbt, "ss", [128, 1], F32, 4),
                "hb": ring(st, sbt, "hb", [128, D], BF16, 2),
                "ptr": ring(st, pst, "ptr", [128, 8, 128], BF16, 2),
            }
            hTt = ring(st, sbt, "hTt", [128, 8, 128], BF16, 3)
            P.dma("sp", identf, T(ident_d, None))
            P.dma("sp", nmw_t, T(nmw, None))
            P.op("pool", lambda e: e.memset(epsc.ap, EPS), [], [epsc])
            P.op("pool", lambda e: e.memset(onec.ap, 1.0), [], [onec])
            P.dma("sp", sel_t, T(sel_d, None))
            P.op("dve", lambda e: e.tensor_copy(out=identb.ap, in_=identf.ap), [identf], [identb])
            for t in sorted(set(list(range(int(os.environ.get('NT0', NT_FULL)))) + [NT_FULL - 1])):
                dst = hTt.next() if t < NT_FULL - 1 else hTlast
                norm_tile(tl, xfull[t * 128:(t + 1) * 128, :], dst)
                P.dma("sp", T(hT_d[t].rearrange("p (k n) -> p k n", k=8), None), dst, store=True)
            for t in range(min(NT_OWN, int(os.environ.get('NT0', NT_OWN)))):
                dst = hTt.next()
                norm_tile(tl, xown[t * 128:(t + 1) * 128, :], dst)
                P.dma("sp", T(hTown_d[t].rearrange("p (k n) -> p k n", k=8), None), dst, store=True)
            norm_tile(tl, xs[:, :], hsT, nt=NS_TOK)
            P.emit()

        for _skip in ([] if os.environ.get('SKIP1') == '1' else [0]):
          with ExitStack() as st:
              tl = {"wst": ring(st, sbt, "wst", [128, 512], F32, 4)}
              wg = ring(st, sbt, "wg", [128, 8, 512], BF16, 2)
              wkv = sbt(st, "wkv", [128, 8, 1024], BF16)
              pmm = ring(st, pst, "pmm", [128, 512], F32, 4)
              ob = ring(st, sbt, "ob", [128, 1024], F32, 3)
              cps = sbt(st, "cps", [3, 1536], F32)
              ps_s = sbt(st, "ps_s", [NS_TOK, IN_DIM], F32)
              hTo_r = ring(st, sbt, "hTo", [128, 8, 128], BF16, 3)
              ngrp = (IN_DIM + 511) // 512
              for gi in range(ngrp):
                  c0 = gi * 512
                  n = min(512, IN_DIM - c0)
                  w = wg.next()
                  load_w(tl, w, w_in, c0, n, scale=nmw_t)
                  pm = pmm.next()
                  for k in range(8):
                      P.op("pe", lambda e, k=k, w=w, pm=pm, n=n: e.matmul(out=pm.ap[:NS_TOK, :n], lhsT=hsT.ap[:, k, :], rhs=w.ap[:, k, :n], start=(k == 0), stop=(k == 7)), [hsT, w], [pm], pe_acc=True)
                  P.op("act", lambda e, pm=pm, c0=c0, n=n: e.copy(out=ps_s.ap[:, c0:c0 + n], in_=pm.ap[:NS_TOK, :n]), [pm], [ps_s])
                  if gi in (1, 2):
                      P.op("pool", lambda e, w=w, gi=gi: e.tensor_copy(out=wkv.ap[:, :, (gi - 1) * 512:gi * 512], in_=w.ap), [w], [wkv])
                  if gi in (3, 4, 5):
                      pm2 = pmm.next()
                      for k in range(8):
                          P.op("pe", lambda e, k=k, w=w, pm2=pm2: e.matmul(out=pm2.ap[:3, :], lhsT=hTlast.ap[:, k, 125:128], rhs=w.ap[:, k, :], start=(k == 0), stop=(k == 7)), [hTlast, w], [pm2], pe_acc=True)
                      P.op("act", lambda e, pm2=pm2, gi=gi: e.copy(out=cps.ap[:, (gi - 3) * 512:(gi - 2) * 512], in_=pm2.ap[:3, :]), [pm2], [cps])
              P.dma("sp", T(ncp, None), cps, store=True)
              P.dma("sp", T(nks, None), ps_s[:, 512:1024], store=True)
              P.dma("sp", T(nvs, None), ps_s[:, 1024:1536], store=True)
              for r in range(3):
                  P.dma("sp", T(ncs[:, r, :], None), T(ps_s.ap[r + 1:NS_TOK:4, 1536:3072], ps_s.res), store=True)
              P.dma("sp", T(ps_d, None), ps_s, store=True)
              for t in range(NT_OWN):
                  o = ob.next()
                  hTo = hTo_r.next()
                  P.dma("sp", hTo, T(hTown_d[t].rearrange("p (k n) -> p k n", k=8), None))
                  for half in range(2):
                      pm = pmm.next()
                      for k in range(8):
                          P.op("pe", lambda e, k=k, pm=pm, half=half, hTo=hTo: e.matmul(out=pm.ap, lhsT=hTo.ap[:, k, :], rhs=wkv.ap[:, k, half * 512:(half + 1) * 512], start=(k == 0), stop=(k == 7)), [hTo, wkv], [pm], pe_acc=True)
                      P.op(("act", "dve")[half], lambda e, pm=pm, half=half, o=o: (e.copy if half == 0 else e.tensor_copy)(out=o.ap[:, half * 512:(half + 1) * 512], in_=pm.ap), [pm], [o])
                  P.dma("sp", T(nk_own[t * 128:(t + 1) * 128, :], None), o[:, 0:512], store=True)
                  P.dma("sp", T(nv_own[t * 128:(t + 1) * 128, :], None), o[:, 512:1024], store=True)
              P.emit()

        mid = ExitStack()
        oBown = sbt(mid, "oBown", [128, 4, OWN], BF16)
        oAown = sbt(mid, "oAown", [128, 4, OWN], BF16)
        oBs = sbt(mid, "oBs", [128, 4, NS_TOK], BF16)
        oAs = sbt(mid, "oAs", [128, 4, NS_TOK], BF16)
        with ExitStack() as st:
            memset(oBown, 0.0)
            memset(oAown, 0.0)
            memset(oAs, 0.0)
            tl = {"wst": ring(st, sbt, "wst", [128, 512], F32, 4)}
            cst = sbt(st, "cst", [128, NCST, 128], F32)
            P.dma("sp", cst, T(cst_d, None))
            C_UINCL, C_BIGU, C_NBIGL, C_SU, C_BD, C_OFFM = (cst[:, i, :] for i in range(6))
            ones = sbt(st, "ones", [128, 128], F32)
            memset(ones, 1.0)
            cw = sbt(st, "cw", [128, 12, 4], F32)
            P.dma("sp", cw, T(cw_d, None))
            negA = sbt(st, "negA", [128, 4], F32)
            dtb = sbt(st, "dtb", [128, 4], F32)
            gnw = sbt(st, "gnw", [128, 128], F32)
            P.dma("sp", negA, T(alog_d, None))
            P.dma("sp", dtb, T(dtb_d, None))
            P.dma("sp", gnw, T(gnw_d, None))
            act(negA, negA, AF.Exp)
            ts(negA, negA, -1.0, ALU.mult)
            wgd = sbt(st, "wgd", [128, 8, 2056], BF16)
            for gi in range(4):
                load_w(tl, wgd[:, :, gi * 512:(gi + 1) * 512], w_in, 1536 + gi * 512, 512, scale=nmw_t)
            load_w(tl, wgd[:, :, 2048:2056], w_in, 3584, 8, scale=nmw_t)
            S = [[sbt(st, f"S{h}_{i}", [128, 128], F32) for i in range(2)] for h in range(4)]
            for h in range(4):
                memset(S[h][0], 0.0)
            ub_r = ring(st, sbt, "ub", [128, 3, 515], F32, 2)
            hist = [sbt(st, f"hist{h}", [128, 3, 3], F32) for h in range(4)]
            for h in range(4):
                memset(hist[h], 0.0)
            hTg_r = ring(st, sbt, "hTg", [128, 8, 512], BF16, 2)
            pbank = [pst(st, f"pb{i}", [128, 4, 128], F32) for i in range(3)]
            pu = pst(st, "pu", [128, 3, 512], F32)
            pss = pst(st, "pss", [128, 512], F32)
            pq = Ring([T(pbank[i % 3].ap[:, (i // 3) % 4, :], pbank[i % 3].res) for i in range(12)])
            sq_r = Ring([sbt(st, f"sq{i}", [128, 128], F32) for i in range(32)])
            col_r = Ring([sbt(st, f"col{i}", [128, 1], F32) for i in range(32)])
            big_r = Ring([sbt(st, f"big{i}", [128, 3, 512], F32) for i in range(3)])
            nrm_r = Ring([sbt(st, f"nrm{i}", [128, 512], F32) for i in range(4)])
            abg = sbt(st, "abg", [128, 4, 8], F32)
            gb_r = Ring([sbt(st, f"gb{i}", [128, 4, 4], F32) for i in range(8)])
            ogb_r = Ring([sbt(st, f"ogb{i}", [128, 128], BF16) for i in range(2)])
            ptrb = ring(st, pst, "ptrb", [128, 128], BF16, 1)
            l2e = sbt(st, "l2e", [128, 1], F32)
            memset(l2e, 1e-6)
            dtb4 = sbt(st, "dtb4", [128, 4, 4], F32)
            negA4 = sbt(st, "negA4", [128, 4, 4], F32)
            for c in range(4):
                cp(dtb4[:, c, :], dtb, eng="pool")
                cp(negA4[:, c, :], negA, eng="pool")

            STAGE = int(os.environ.get('GDN_STAGE', '99'))

            def gdn_chunk(h, qT, kT, vT, gcol, bcol, nbcol, hTc, own_dst, selcol, Sin, Sout):
                G = sq_r.next()
                ts(G, C_UINCL, gcol, ALU.mult)
                PA, PB, pm1 = pq.next(), pq.next(), pq.next()
                mm(PA, ones, G, True, False)
                mm(PA, identf, C_BIGU, False, True)
                mm(PB, ones, G, True, False)
                mm(PB, identf, C_NBIGL, False, True)
                mm(pm1[:, 0:1], ones, G[:, 127:128])
                mm(pm1[:, 1:2], C_UINCL, gcol)
                cg, ncg, ecg, egl, edl = (col_r.next() for _ in range(5))
                cp(cg, pm1[:, 1:2], eng="act")
                ts(ncg, pm1[:, 1:2], -1.0, ALU.mult)
                decay, decayT = sq_r.next(), sq_r.next()
                act(decay, PA, AF.Exp, bias=cg, scale=-1.0)
                act(decayT, PB, AF.Exp, bias=ncg, scale=1.0)
                act(ecg, cg, AF.Exp)
                act(egl, pm1[:, 0:1], AF.Exp)
                act(edl, pm1[:, 0:1], AF.Exp, bias=ncg, scale=1.0)
                if STAGE < 4:
                    return None
                KK, QKT = pq.next(), pq.next()
                mm(KK, kT, kT)
                mm(QKT, kT, qT)
                t0, Wm, AT = sq_r.next(), sq_r.next(), sq_r.next()
                tt(t0, KK, decayT, ALU.mult)
                stt(Wm, t0, bcol, C_SU, ALU.mult, ALU.mult)
                tt(AT, QKT, decayT, ALU.mult)
                WmTp = pq.next()
                tr(WmTp, Wm, identf)
                WmT = sq_r.next()
                cp(WmT, WmTp, eng="act")
                if STAGE < 5:
                    return None
                X, XT, OffT, Pk = sq_r.next(), sq_r.next(), sq_r.next(), sq_r.next()
                tt(X, Wm, C_BD, ALU.mult, eng="pool")
                tt(XT, WmT, C_BD, ALU.mult, eng="pool")
                tt(OffT, WmT, C_OFFM, ALU.mult, eng="pool")
                tt(Pk, identf, X, ALU.subtract)
                for lvl in range(1, 6):
                    p2 = pq.next()
                    mm(p2, X, XT)
                    X2T = sq_r.next()
                    if lvl < 5:
                        p1 = pq.next()
                        mm(p1, XT, X)
                        X2 = sq_r.next()
                        cp(X2, p1, eng="act")
                    cp(X2T, p2, eng="dve")
                    p3 = pq.next()
                    mm(p3, X2T, Pk)
                    Pn = sq_r.next()
                    tt(Pn, p3, Pk, ALU.add)
                    Pk = Pn
                    XT = X2T
                    if lvl < 5:
                        X = X2
                Pd = Pk
                PdTp = pq.next()
                tr(PdTp, Pd, identf)
                PdT = sq_r.next()
                cp(PdT, PdTp, eng="act")
                t1p = pq.next()
                mm(t1p, OffT, Pd)
                t1 = sq_r.next()
                cp(t1, t1p, eng="dve")
                w2p = pq.next()
                mm(w2p, PdT, t1)
                W = sq_r.next()
                tt(W, Pd, w2p, ALU.subtract)
                if STAGE < 6:
                    return None
                ktp, vtp = pq.next(), pq.next()
                tr(ktp, kT, identf)
                tr(vtp, vT, identf)
                ke, kd, vtm = sq_r.next(), sq_r.next(), sq_r.next()
                ts(ke, ktp, ecg, ALU.mult)
                ts(kd, ktp, edl, ALU.mult)
                cp(vtm, vtp, eng="act")
                u0p, w0Tp = pq.next(), pq.next()
                mm(u0p, W, vtm)
                mm(w0Tp, ke, W)
                u0b, w0T = sq_r.next(), sq_r.next()
                ts(u0b, u0p, bcol, ALU.mult)
                cp(w0T, w0Tp, eng="act")
                if STAGE < 7:
                    return None
                wSp = pq.next()
                mm(wSp, w0T, Sin)
                vn = sq_r.next()
                stt(vn, wSp, nbcol, u0b, ALU.mult, ALU.add)
                qSp, Avp = pq.next(), pq.next()
                mm(qSp, qT, Sin)
                mm(Avp, AT, vn)
                Av, o = sq_r.next(), sq_r.next()
                cp(Av, Avp, eng="act")
                stt(o, qSp, ecg, Av, ALU.mult, ALU.add)
                KVp = pq.next()
                mm(KVp, kd, vn)
                stt(Sout, Sin, egl, KVp, ALU.mult, ALU.add)
                if STAGE < 8:
                    return None
                zp = pq.next()
                for k in range(8):
                    mm(zp, hTc[:, k, :], wgd[:, k, 1536 + h * 128:1536 + (h + 1) * 128], k == 0, k == 7)
                osq, ms = sq_r.next(), col_r.next()
                tt(osq, o, o, ALU.mult, eng="pool")
                rsum(ms, osq)
                act(ms, ms, AF.Ln, bias=epsc, scale=1.0 / 128)
                act(ms, ms, AF.Exp, scale=-0.5)
                ez, sz, og = sq_r.next(), sq_r.next(), sq_r.next()
                act(ez, zp, AF.Exp, scale=-1.0)
                ts(ez, ez, 1.0, ALU.add, eng="pool")
                recip(ez, ez)
                tt(sz, zp, ez, ALU.mult)
                stt(og, o, ms, gnw, ALU.mult, ALU.mult)
                ogb = ogb_r.next()
                tt(ogb, og, sz, ALU.mult)
                return ogb

            for g in range(int(os.environ.get('GDN_G', '16'))):
                hTg = hTg_r.next()
                for t in range(4):
                    P.dma("sp", hTg[:, :, t * 128:(t + 1) * 128], T(hT_d[4 * g + t].rearrange("p (k n) -> p k n", k=8), None), nowaw=True)
                SUB = int(os.environ.get('GDN_SUB', '9'))
                for c in range(4 if SUB >= 1 else 0):
                    abp = pq.next()
                    for k in range(8):
                        mm(abp[:, 0:8], hTg[:, k, c * 128:(c + 1) * 128], wgd[:, k, 2048:2056], k == 0, k == 7)
                    if os.environ.get('NOCP') != '1':
                        cp(abg[:, c, :], abp[:, 0:8], eng=os.environ.get("CPENG", "act"))
                xa, gg, eb, beta, nbeta = (gb_r.next() for _ in range(5))
                if SUB >= 2:
                    tt(xa, abg[:, :, 0:4], dtb4, ALU.add)
                if SUB >= 3:
                    act(xa, xa, AF.Exp)
                    act(xa, xa, AF.Ln, bias=onec, scale=1.0)
                if SUB >= 4:
                    tt(gg, xa, negA4, ALU.mult)
                    act(eb, abg[:, :, 4:8], AF.Exp, scale=-1.0)
                    ts(eb, eb, 1.0, ALU.add)
                if SUB >= 5:
                    recip(beta, eb)
                    ts(nbeta, beta, -1.0, ALU.mult)
                for h in range(int(os.environ.get('GDN_H', '4')) if STAGE >= 2 else 0):
                    for part in range(3):
                        for k in range(8):
                            mm(pu[:, part, :], wgd[:, k, part * 512 + h * 128:part * 512 + (h + 1) * 128], hTg[:, k, :], k == 0, k == 7)
                    ub = ub_r.next()
                    cp(ub[:, :, 0:3], hist[h], eng="pool")
                    cp(ub[:, :, 3:515], pu, eng="act")
                    cp(hist[h], ub[:, :, 512:515], eng="pool")
                    y = big_r.next()
                    for part in range(3):
                        ci = part * 4 + h
                        ts(y[:, part, :], ub[:, part, 0:512], cw[:, ci, 0:1], ALU.mult)
                        for w in range(1, 4):
                            stt(y[:, part, :], ub[:, part, w:w + 512], cw[:, ci, w:w + 1], y[:, part, :], ALU.mult, ALU.add)
                    e = big_r.next()
                    act(e, y, AF.Exp, scale=-1.0)
                    ts(e, e, 1.0, ALU.add, eng="pool")
                    recip(e, e)
                    sl = big_r.next()
                    tt(sl, y, e, ALU.mult, eng="pool")
                    sqq = e
                    tt(sqq[:, 0:2, :], sl[:, 0:2, :], sl[:, 0:2, :], ALU.mult, eng="pool")
                    qn, kn = nrm_r.next(), nrm_r.next()
                    for part, dst in ((0, qn), (1, kn)):
                        mm(pss, ones, sqq[:, part, :])
                        rs = nrm_r.next()
                        act(rs, pss, AF.Ln, bias=l2e, scale=1.0)
                        act(rs, rs, AF.Exp, scale=-0.5)
                        if part == 0:
                            stt(dst, sl[:, 0, :], 128.0 ** -0.5, rs, ALU.mult, ALU.mult)
                        else:
                            tt(dst, sl[:, 1, :], rs, ALU.mult)
                    for c in range(4 if STAGE >= 3 else 0):
                        cs = slice(c * 128, (c + 1) * 128)
                        nchunk = 4 * g + c
                        Sin, Sout = S[h][nchunk % 2], S[h][(nchunk + 1) % 2]
                        ogb = gdn_chunk(h, qn[:, cs], kn[:, cs], sl[:, 2, cs], gg[:, c, h:h + 1], beta[:, c, h:h + 1], nbeta[:, c, h:h + 1],
                                        hTg[:, :, cs], None, None, Sin, Sout)
                        if ogb is None:
                            continue
                        oTp = ptrb.next()
                        tr(oTp, ogb, identb)
                        dst = oBown[:, h, (g // 4) * 512 + c * 128:(g // 4) * 512 + (c + 1) * 128]
                        stt(dst, oTp, sel_t[:, (g % 4):(g % 4) + 1], dst, ALU.mult, ALU.add)
            for h in range(4):
                P.dma("sp", T(nsp[h], None), S[h][(NT_FULL) % 2], store=True)

            def flat(t, rows, c0, c1):
                return T(t.ap.rearrange("p a b -> p (a b)")[:rows, c0:c1], t.res)

            def bview(t, pat, **kw):
                return T(t.ap.rearrange(pat, **kw), t.res)

            n = NS_TOK
            CS_UINCL, CS_BIGU, CS_NBIGL, CS_SU, CS_BD = (cst[:n, 7 + i, :n] for i in range(5))
            RM = cst[:n, 12, 0:16]
            idn = identf[:n, :n]
            usb, scb, zb = big_r.next(), big_r.next(), big_r.next()
            us = flat(usb, n, 0, 1536)
            scs = flat(scb, 48, 0, 1536)
            zs = flat(zb, n, 0, 512)
            abs_ = flat(zb, n, 512, 520)
            P.dma("sp", us, T(ps_d[:, 1536:3072], None))
            P.dma("sp", scs, T(sc_d, None))
            P.dma("sp", zs, T(ps_d[:, 3072:3584], None))
            P.dma("sp", abs_, T(ps_d[:, 3584:3592], None), nowaw=True)
            uext = sbt(st, "uext", [128, 12, 16, 7], F32)
            ys = sbt(st, "ys", [128, 12, 64], F32)
            es = sbt(st, "es", [128, 12, 64], F32)
            s_all = es
            for cc in range(12):
                ptq = pq.next()
                tr(ptq[:, 0:64], us[:, cc * 128:(cc + 1) * 128], idn)
                tr(ptq[:, 64:112], scs[:, cc * 128:(cc + 1) * 128], identf[:48, :48])
                cp(uext[:, cc, :, 3:7], T(ptq.ap[:, 0:64].rearrange("p (s t) -> p s t", t=4), ptq.res), eng="act")
                cp(uext[:, cc, :, 0:3], T(ptq.ap[:, 64:112].rearrange("p (s t) -> p s t", t=3), ptq.res), eng="dve")
                yv = T(ys.ap[:, cc, :].rearrange("p (s t) -> p s t", t=4), ys.res)
                ts(yv, uext[:, cc, :, 0:4], cw[:, cc, 0:1], ALU.mult)
                for w in range(1, 4):
                    stt(yv, uext[:, cc, :, w:w + 4], cw[:, cc, w:w + 1], yv, ALU.mult, ALU.add)
            act(es, ys, AF.Exp, scale=-1.0)
            ts(es, es, 1.0, ALU.add, eng="pool")
            recip(es, es)
            tt(s_all, ys, es, ALU.mult, eng="pool")
            xas, ggs, ebs, betas = (gb_r.next() for _ in range(4))
            xa2, gg2, eb2, be2 = (T(t.ap.rearrange("p a b -> p (a b)")[:n, 0:4], t.res) for t in (xas, ggs, ebs, betas))
            tt(xa2, abs_[:, 0:4], dtb[:n], ALU.add)
            act(xa2, xa2, AF.Exp)
            act(xa2, xa2, AF.Ln, bias=onec[:n], scale=1.0)
            tt(gg2, xa2, negA[:n], ALU.mult)
            act(eb2, abs_[:, 4:8], AF.Exp, scale=-1.0)
            ts(eb2, eb2, 1.0, ALU.add)
            recip(be2, eb2)
            W3 = sbt(st, "W3", [n, 16 * n], F32)
            I3 = sbt(st, "I3", [n, 16 * n], F32)
            memset(W3, 0.0)
            memset(I3, 0.0)

            def diagv(t):
                a = t.ap
                return T(bass.AP(tensor=a.tensor, offset=a.offset, ap=[list(a.ap[0]), [n + 4, 16], [1, 4]]), t.res)
            cp(diagv(I3), T(idn.ap.rearrange("p (s t) -> p s t", t=4), idn.res), eng="pool")
            w0Tm = sbt(st, "w0Tm", [128, 16 * n], F32)
            qem = sbt(st, "qem", [128, 16 * n], F32)
            Sall = sbt(st, "Sall", [128, 16, 128], F32)
            Snew = Sall
            EGL = sbt(st, "EGL", [128, 16], F32)
            for h in range(4):
                P.dma("sp", Sall, T(ssm_d[:, h].rearrange("s p d -> p s d"), None))
                qs_, ks_, vs_ = s_all[:, h, :], s_all[:, 4 + h, :], s_all[:, 8 + h, :]
                sqv = nrm_r.next()
                tt(sqv[:, 0:n], qs_, qs_, ALU.mult, eng="pool")
                tt(sqv[:, n:2 * n], ks_, ks_, ALU.mult, eng="pool")
                mm(pss[:, 0:2 * n], ones, sqv[:, 0:2 * n])
                rsv = nrm_r.next()
                act(rsv[:, 0:2 * n], pss[:, 0:2 * n], AF.Ln, bias=l2e, scale=1.0)
                act(rsv[:, 0:2 * n], rsv[:, 0:2 * n], AF.Exp, scale=-0.5)
                qkn = nrm_r.next()
                qn, kn = qkn[:, 0:n], qkn[:, n:2 * n]
                stt(qn, qs_, 128.0 ** -0.5, rsv[:, 0:n], ALU.mult, ALU.mult)
                tt(kn, ks_, rsv[:, n:2 * n], ALU.mult)
                gcol, bcol = gg2[:, h:h + 1], be2[:, h:h + 1]
                G = sq_r.next()[:n, :n]
                ts(G, CS_UINCL, gcol, ALU.mult)
                PA, PB, Pm, pm1 = pq.next(), pq.next(), pq.next(), pq.next()
                mm(PA[:n, :n], ones[:n, :n], G, True, False)
                mm(PA[:n, :n], idn, CS_BIGU, False, True)
                mm(PB[:n, :n], ones[:n, :n], G, True, False)
                mm(PB[:n, :n], idn, CS_NBIGL, False, True)
                mm(Pm[:, :n], ones[:n, :], G)
                mm(pm1[:n, 0:1], CS_BD, gcol)
                mm(pm1[:n, 1:2], CS_UINCL, gcol)
                cg, ncg, ecg, edl = (col_r.next()[:n] for _ in range(4))
                cp(cg, pm1[:n, 1:2], eng="act")
                ts(ncg, pm1[:n, 1:2], -1.0, ALU.mult)
                decay, decayT = sq_r.next()[:n, :n], sq_r.next()[:n, :n]
                act(decay, PA[:n, :n], AF.Exp, bias=cg, scale=-1.0)
                act(decayT, PB[:n, :n], AF.Exp, bias=ncg, scale=1.0)
                act(ecg, cg, AF.Exp)
                act(edl, pm1[:n, 0:1], AF.Exp, bias=ncg, scale=1.0)
                act(EGL, T(Pm.ap[:, 3:n:4], Pm.res), AF.Exp)
                KK, QKT = pq.next(), pq.next()
                mm(KK[:n, :n], kn, kn)
                mm(QKT[:n, :n], kn, qn)
                t0, Wm, AT = sq_r.next()[:n, :n], sq_r.next()[:n, :n], sq_r.next()[:n, :n]
                tt(t0, KK[:n, :n], decayT, ALU.mult)
                stt(Wm, t0, bcol, CS_SU, ALU.mult, ALU.mult)
                tt(AT, QKT[:n, :n], decayT, ALU.mult)
                WmTp = pq.next()
                tr(WmTp[:n, :n], Wm, idn)
                WmT = sq_r.next()[:n, :n]
                cp(WmT, WmTp[:n, :n], eng="act")
                Pk = sq_r.next()[:n, :n]
                tt(Pk, idn, Wm, ALU.subtract)
                p2 = pq.next()
                mm(p2[:n, :n], Wm, WmT)
                X2T = sq_r.next()[:n, :n]
                cp(X2T, p2[:n, :n], eng="dve")
                p3 = pq.next()
                mm(p3[:n, :n], X2T, Pk)
                W = sq_r.next()[:n, :n]
                tt(W, p3[:n, :n], Pk, ALU.add)
                dgb = sq_r.next()[:n, :n]
                ts(dgb, idn, bcol, ALU.mult)
                pbr = pq.next()
                mm(pbr[:n, :n], ones[:n, :n], dgb)
                Wb = sq_r.next()[:n, :n]
                tt(Wb, pbr[:n, :n], W, ALU.mult)
                cp(diagv(W3), T(Wb.ap.rearrange("p (s t) -> p s t", t=4), Wb.res), eng="pool")
                ktp, vtp, qtp = pq.next(), pq.next(), pq.next()
                tr(ktp[:n, :], kn, identf)
                tr(vtp[:n, :], vs_, identf)
                tr(qtp[:n, :], qn, identf)
                ke, kd, vtm, qetm = sq_r.next()[:n], sq_r.next()[:n], sq_r.next()[:n], sq_r.next()[:n]
                ts(ke, ktp[:n, :], ecg, ALU.mult)
                ts(kd, ktp[:n, :], edl, ALU.mult)
                cp(vtm, vtp[:n, :], eng="act")
                ts(qetm, qtp[:n, :], ecg, ALU.mult)
                u0p = pq.next()
                mm(u0p[:n, :], Wb, vtm)
                u0b = sq_r.next()[:n]
                cp(u0b, u0p[:n, :], eng="act")
                for half in range(2):
                    mm(pu[:, half, :], ke, W3[:, half * 512:(half + 1) * 512])
                cp(w0Tm, T(pu.ap[:, 0:2, :].rearrange("p a b -> p (a b)"), pu.res), eng="act")
                for half in range(2):
                    mm(pu[:, half, :], qetm, I3[:, half * 512:(half + 1) * 512])
                cp(qem, T(pu.ap[:, 0:2, :].rearrange("p a b -> p (a b)"), pu.res), eng="dve")
                pw = pq.next()
                for sq_i in range(16):
                    mm(pw[:n, :], w0Tm[:, sq_i * n:(sq_i + 1) * n], Sall[:, sq_i, :], sq_i == 0, sq_i == 15)
                vn = sq_r.next()[:n]
                tt(vn, u0b, pw[:n, :], ALU.subtract)
                po = pq.next()
                for sq_i in range(16):
                    mm(po[:n, :], qem[:, sq_i * n:(sq_i + 1) * n], Sall[:, sq_i, :], sq_i == 0, False)
                mm(po[:n, :], AT, vn, False, True)
                o = sq_r.next()[:n]
                cp(o, po[:n, :], eng="act")
                for sq_i in range(16):
                    kdm = sq_r.next()[:n]
                    ts(kdm, kd, RM[:, sq_i:sq_i + 1], ALU.mult, eng="pool")
                    pkv = pq.next()
                    mm(pkv, kdm, vn)
                    stt(Snew[:, sq_i, :], Sall[:, sq_i, :], EGL[:, sq_i:sq_i + 1], pkv, ALU.mult, ALU.add)
                P.dma("sp", T(nss[:, h].rearrange("s p d -> p s d"), None), Snew, store=True)
                osq, ms = sq_r.next()[:n], col_r.next()[:n]
                tt(osq, o, o, ALU.mult, eng="pool")
                rsum(ms, osq)
                act(ms, ms, AF.Ln, bias=epsc[:n], scale=1.0 / 128)
                act(ms, ms, AF.Exp, scale=-0.5)
                zh = zs[:, h * 128:(h + 1) * 128]
                ez, sz, og = sq_r.next()[:n], sq_r.next()[:n], sq_r.next()[:n]
                act(ez, zh, AF.Exp, scale=-1.0)
                ts(ez, ez, 1.0, ALU.add, eng="pool")
                recip(ez, ez)
                tt(sz, zh, ez, ALU.mult)
                stt(og, o, ms, gnw[:n], ALU.mult, ALU.mult)
                ogb = ogb_r.next()[:n]
                tt(ogb, og, sz, ALU.mult)
                oTp = ptrb.next()
                tr(oTp[:, :n], ogb, identb[:n, :n])
                cp(oBs[:, h, :], oTp[:, :n], eng="act")
            P.emit()

        with ExitStack() as st:
            tl = {"wst": ring(st, sbt, "wst", [128, 512], F32, 4)}
            NPAIR = int(os.environ.get('ATT_PAIRS', '4'))
            cst6 = sbt(st, "cst6", [128, 128], F32)
            P.dma("sp", cst6, T(cst_d[:, 6, :], None))
            trilb = sbt(st, "trilb", [128, 128], BF16)
            onesb = sbt(st, "onesb", [128, 128], BF16)
            cp(trilb, cst6)
            memset(onesb, 1.0)
            sbb = sbt(st, "sbb", [128, 8], F32)
            P.dma("sp", sbb, T(sbb_d, None))
            maskt = sbt(st, "maskt", [128, 16, 512], BF16)
            P.dma("sp", maskt, T(mask_d, None))
            KT = sbt(st, "KT", [128, SEQ], BF16)
            Vt = sbt(st, "Vt", [128, NT_FULL, 128], BF16)
            qT = sbt(st, "qT", [128, OWN], BF16)
            wqkv = sbt(st, "wqkv", [128, 8, 384], BF16)
            hTg_r = ring(st, sbt, "hTg3", [128, 8, 512], BF16, 2)
            pproj = ring(st, pst, "pproj", [128, 512], F32, 2)
            pS = ring(st, pst, "pS", [128, 512], F32, 2)
            pR = pst(st, "pR", [128, 512], F32)
            pT = pst(st, "pT", [128, 512], F32)
            pO = pst(st, "pO", [128, 512], F32)
            e_r = ring(st, sbt, "e3", [128, 512], F32, 3)
            sp_r = ring(st, sbt, "sp3", [128, 512], BF16, 2)
            Rt_r = ring(st, sbt, "Rt3", [128, 512], F32, 2)
            ex_r = ring(st, sbt, "ex3", [128, 512], F32, 2)
            w_r = ring(st, sbt, "w3", [128, 512], BF16, 2)
            carry = [sbt(st, f"carry{i}", [128, 512], F32) for i in range(2)]
            for p in range(NPAIR):
                load_w(tl, wqkv[:, :, 0:128], w_in, p * 128, 128, scale=nmw_t)
                load_w(tl, wqkv[:, :, 128:256], w_in, 512 + p * 128, 128, scale=nmw_t)
                load_w(tl, wqkv[:, :, 256:384], w_in, 1024 + p * 128, 128, scale=nmw_t)
                for g in range(16):
                    hTg = hTg_r.next()
                    for t in range(4):
                        P.dma("sp", hTg[:, :, t * 128:(t + 1) * 128], T(hT_d[4 * g + t].rearrange("p (k n) -> p k n", k=8), None), nowaw=True)
                    pk = pproj.next()
                    for k in range(8):
                        mm(pk, wqkv[:, k, 128:256], hTg[:, k, :], k == 0, k == 7)
                    cp(KT[:, g * 512:(g + 1) * 512], pk, eng="act")
                    pv = pproj.next()
                    for t in range(4):
                        for k in range(8):
                            mm(pv[:, t * 128:(t + 1) * 128], hTg[:, k, t * 128:(t + 1) * 128], wqkv[:, k, 256:384], k == 0, k == 7)
                    cp(Vt[:, 4 * g:4 * g + 4, :], T(pv.ap.rearrange("p (t c) -> p t c", t=4), pv.res), eng="dve")
                for i in range(4):
                    hTg = hTg_r.next()
                    for t in range(4):
                        P.dma("sp", hTg[:, :, t * 128:(t + 1) * 128], T(hTown_d[4 * i + t].rearrange("p (k n) -> p k n", k=8), None), nowaw=True)
                    pq_ = pproj.next()
                    for k in range(8):
                        mm(pq_, wqkv[:, k, 0:128], hTg[:, k, :], k == 0, k == 7)
                    cp(qT[:, i * 512:(i + 1) * 512], pq_, eng="act")
                for hh in range(2):
                    h = 2 * p + hh
                    hs = slice(64 * hh, 64 * hh + 64)
                    for i in range(int(os.environ.get('ATT_SLOTS', '4'))):
                        nkb = 16 * i + 16
                        cur = 0
                        memset(carry[0], 0.0)
                        for idx, KB in enumerate(range(nkb - 1, -1, -1)):
                            Sp = pS.next()
                            mm(Sp, KT[hs, KB * 128:(KB + 1) * 128], qT[hs, i * 512:(i + 1) * 512])
                            e = e_r.next()
                            act(e, Sp, AF.Exp, bias=sbb[:, h:h + 1], scale=0.125)
                            if KB >= 16 * i:
                                tt(e, e, maskt[:, KB - 16 * i, :], ALU.mult, eng="pool")
                            sp = sp_r.next()
                            act(sp, e, AF.Ln, bias=onec, scale=1.0)
                            mm(pR, trilb, sp)
                            mm(pT, onesb, sp)
                            Rt = Rt_r.next()
                            tt(Rt, pR, carry[cur], ALU.add)
                            tt(carry[1 - cur], pT, carry[cur], ALU.add)
                            cur = 1 - cur
                            ex = ex_r.next()
                            act(ex, Rt, AF.Exp, scale=-1.0)
                            w = w_r.next()
                            tt(w, e, ex, ALU.mult, eng="pool")
                            mm(pO, Vt[:, KB, :], w, idx == 0, idx == nkb - 1)
                        cp(oAown[hs, p, i * 512:(i + 1) * 512], pO[hs, :], eng="act")
            P.emit()


        with ExitStack() as st:
            NSEQ = int(os.environ.get('SATT_SEQS', '16'))
            SST = int(os.environ.get('SATT_STAGE', '9'))
            n = NS_TOK
            cst6 = sbt(st, "cst6s", [128, 128], F32)
            P.dma("sp", cst6, T(cst_d[:, 6, :], None))
            trilb = sbt(st, "trilbs", [128, 128], BF16)
            onesb = sbt(st, "onesbs", [128, 128], BF16)
            cp(trilb, cst6)
            memset(onesb, 1.0)
            ptt = sbt(st, "ptt", [128, 256], I32)
            idx = sbt(st, "idx", [128, 256], I32)
            iot = sbt(st, "iot", [128, 256], I32)
            P.dma("sp", ptt, T(pt_d.partition_broadcast(128), None))
            P.dma("sp", iot, T(iota_d, None))
            P.op("dve", lambda e: e.tensor_scalar(out=idx.ap, in0=ptt.ap, scalar1=7, scalar2=None, op0=ALU.logical_shift_left), [ptt], [idx])
            P.op("dve", lambda e: e.tensor_tensor(out=idx.ap, in0=idx.ap, in1=iot.ap, op=ALU.bitwise_or), [idx, iot], [idx])
            sbrow = sbt(st, "sbrow", [128, 512], F32)
            P.dma("sp", sbrow, T(sbrow_d, None))
            mnew = sbt(st, "mnew", [128, 32], F32)
            P.dma("sp", mnew, T(mnew_d, None))
            selE = sbt(st, "selE", [64, 256], F32)
            P.dma("sp", selE, T(selE_d, None))
            dm = sbt(st, "dm", [32, 512], F32)
            P.dma("sp", dm, T(dm_d, None))
            qkv = sbt(st, "qkvs", [n, 1536], F32)
            P.dma("sp", qkv, T(ps_d[:, 0:1536], None))
            qsT = sbt(st, "qsT", [128, 4, n], BF16)
            ksT = sbt(st, "ksT", [128, 4, 256], BF16)
            memset(ksT, 0.0)
            ptk_r = ring(st, pst, "ptk", [128, 4, 128], F32, 2)
            pS_r = ring(st, pst, "pSs", [128, 512], F32, 2)
            pR = pst(st, "pRs", [128, 512], F32)
            pT = pst(st, "pTs", [128, 512], F32)
            pO = pst(st, "pOs", [128, 512], F32)
            psm = pst(st, "psm", [128, 512], F32)
            for p in range(4):
                ptk = ptk_r.next()
                tr(ptk[:, 0, :n], qkv[:, p * 128:(p + 1) * 128], identf[:n, :n])
                tr(ptk[:, 1, :n], qkv[:, 512 + p * 128:512 + (p + 1) * 128], identf[:n, :n])
                cp(qsT[:, p, :], ptk[:, 0, :n], eng="act")
                cp(ksT[:, p, 0:n], ptk[:, 1, :n], eng="dve")
            kpg_r = ring(st, sbt, "kpg", [128, 512], F32, 3)
            vpg_r = ring(st, sbt, "vpg", [128, 512], F32, 3)
            Vb_r = ring(st, sbt, "Vb", [128, 16, 512], BF16, 2)
            KTs_r = ring(st, sbt, "KTs", [128, 4, 128], BF16, 3)
            f_r = ring(st, sbt, "fs", [128, 512], F32, 5)
            b_r = ring(st, sbt, "bs", [128, 512], BF16, 3)
            C = sbt(st, "Cs", [128, 16, 32], F32)
            sm_r = ring(st, sbt, "sms", [128, 512], F32, 4)
            smb_r = ring(st, sbt, "smbs", [128, 512], BF16, 4)
            o2 = sbt(st, "o2s", [32, 2, 64], F32)
            for sq_i in range(NSEQ):
                Sp = pS_r.next()
                Vb = Vb_r.next()
                tsl = slice(4 * sq_i, 4 * sq_i + 4)
                for pg in range(16):
                    col = sq_i * 16 + pg
                    kpg, vpg = kpg_r.next(), vpg_r.next()
                    for dst, src in ((kpg, ck_d), (vpg, cv_d)):
                        def gfn(e, dst=dst, src=src, col=col):
                            return e.indirect_dma_start(out=dst.ap, out_offset=None, in_=src,
                                                        in_offset=bass.IndirectOffsetOnAxis(ap=idx.ap[:, col:col + 1], axis=0))
                        P._add("pool", gfn, [idx], [dst], dma=True, evres=dst.res)
                    cp(Vb[:, pg, :], vpg, eng=("dve", "act")[pg % 2])
                    if SST < 2:
                        tr(ptk_r.next()[:, 0, :], kpg[:, 0:128], identf)
                        continue
                    ptk = ptk_r.next()
                    for p in range(4):
                        tr(ptk[:, p, :], kpg[:, p * 128:(p + 1) * 128], identf)
                    KTs = KTs_r.next()
                    cp(KTs, ptk, eng=("act", "dve")[pg % 2])
                    if SST < 3:
                        continue
                    for p in range(4):
                        for hh in range(2):
                            hs = slice(64 * hh, 64 * hh + 64)
                            c0 = pg * 32 + (2 * p + hh) * 4
                            mm(Sp[:, c0:c0 + 4], KTs[hs, p, :], qsT[hs, p, tsl])
                if SST < 4:
                    continue
                for p in range(4):
                    for hh in range(2):
                        hs = slice(64 * hh, 64 * hh + 64)
                        c0 = (2 * p + hh) * 4
                        mm(psm[:, c0:c0 + 4], ksT[hs, p, 4 * sq_i:4 * sq_i + 128], qsT[hs, p, tsl])
                z, e = f_r.next(), f_r.next()
                stt(z, Sp, 0.125, sbrow, ALU.mult, ALU.add)
                act(e, z, AF.Exp)
                sp = b_r.next()
                act(sp, e, AF.Ln, bias=onec, scale=1.0)
                zn, en = sm_r.next(), sm_r.next()
                stt(zn[:, 0:32], psm[:, 0:32], 0.125, sbrow[:, 0:32], ALU.mult, ALU.add)
                act(en[:, 0:32], zn[:, 0:32], AF.Exp)
                tt(en[:, 0:32], en[:, 0:32], mnew, ALU.mult)
                spn = smb_r.next()
                act(spn[:, 0:32], en[:, 0:32], AF.Ln, bias=onec, scale=1.0)
                mm(pR, trilb, sp)
                mm(pT, onesb, sp)
                mm(psm[:, 64:96], onesb, spn[:, 0:32])
                mm(psm[:, 128:160], trilb, spn[:, 0:32])
                cp(C[:, 15, :], psm[:, 64:96], eng="dve")
                for pg in range(14, -1, -1):
                    tt(C[:, pg, :], pT[:, (pg + 1) * 32:(pg + 2) * 32], C[:, pg + 1, :], ALU.add)
                R, ex = f_r.next(), f_r.next()
                tt(R, pR, T(C.ap.rearrange("p a b -> p (a b)"), C.res), ALU.add)
                act(ex, R, AF.Exp, scale=-1.0)
                w = b_r.next()
                tt(w, e, ex, ALU.mult, eng="pool")
                exn = sm_r.next()
                act(exn[:, 0:32], psm[:, 128:160], AF.Exp, scale=-1.0)
                wn = smb_r.next()
                tt(wn[:, 0:32], en[:, 0:32], exn[:, 0:32], ALU.mult)
                if SST < 5:
                    continue
                mm(pO, selE[:, 4 * sq_i:4 * sq_i + 128], qkv[:, 1024:1536])
                v4 = smb_r.next()
                cp(v4, pO, eng="act")
                for pg in range(16):
                    mm(pR[0:32, :], w[:, pg * 32:(pg + 1) * 32], Vb[:, pg, :], pg == 0, False)
                mm(pR[0:32, :], wn[:, 0:32], v4, False, True)
                od = sm_r.next()[0:32]
                tt(od, pR[0:32, :], dm, ALU.mult)
                rsum(o2[:, 0, :], T(od.ap.rearrange("p (h d) -> p d h", h=8), od.res))
                cp(o2[:, 1, :], o2[:, 0, :], eng="pool")
                tr(psm[:, 256:288], T(o2.ap.rearrange("p a b -> p (a b)"), o2.res), identf[:32, :32])
                for hh in range(2):
                    hs = slice(64 * hh, 64 * hh + 64)
                    src = T(psm.ap[hs, 256:288].rearrange("p (a b c) -> p a b c", a=4, b=2)[:, :, hh, :], psm.res)
                    cp(oAs[hs, :, tsl], src, eng=("act", "dve")[hh])
            P.emit()

        with ExitStack() as st:
            tl = {"wst": ring(st, sbt, "wst", [128, 512], F32, 4),
                  "sq": ring(st, sbt, "sq4", [128, D], F32, 1),
                  "ss": ring(st, sbt, "ss4", [128, 1], F32, 4),
                  "hb": ring(st, sbt, "hb4", [128, D], BF16, 2),
                  "ptr": ring(st, pst, "ptr4", [128, 8, 128], BF16, 1)}
            w2 = sbt(st, "w2", [128, 8, 2048], BF16)
            for gi in range(4):
                load_w(tl, w2[:, :, gi * 512:(gi + 1) * 512], w_in, 3592 + gi * 512, 512, scale=nmw_t)
            wpa = sbt(st, "wpa", [128, 4, D], BF16)
            wpb = sbt(st, "wpb", [128, 4, D], BF16)
            wo = sbt(st, "wo", [128, 8, D], BF16)
            for half in range(2):
                load_w(tl, wpa[:, :, half * 512:(half + 1) * 512], w_pa_d, half * 512, 512, nk=4)
                load_w(tl, wpb[:, :, half * 512:(half + 1) * 512], w_pb_d, half * 512, 512, nk=4)
                load_w(tl, wo[:, :, half * 512:(half + 1) * 512], w_o_d, half * 512, 512, nk=8)
            hTg_r = ring(st, sbt, "hTg4", [128, 8, 512], BF16, 2)
            pg = ring(st, pst, "pg4", [128, 512], F32, 5)
            pmix = ring(st, pst, "pmix4", [128, 512], F32, 2)
            sg_r = ring(st, sbt, "sg4", [128, 512], F32, 4)
            mm_r = ring(st, sbt, "mm4", [128, 512], F32, 4)
            mT_r = ring(st, sbt, "mT4", [128, 8, 512], BF16, 2)
            x_r = ring(st, sbt, "x4", [128, D], F32, 3)
            hT_r = ring(st, sbt, "hTt4", [128, 8, 128], BF16, 2)
            groups = []
            for G in range(4):
                groups.append(dict(n=512, hT_src=[hTown_d[4 * G + t] for t in range(4)], hT_sb=None,
                                   oA=oAown[:, :, G * 512:(G + 1) * 512], oB=oBown[:, :, G * 512:(G + 1) * 512],
                                   x_rows=[xown[(4 * G + t) * 128:(4 * G + t + 1) * 128, :] for t in range(4)],
                                   row0=G * 512, tile0=4 * G))
            groups.append(dict(n=NS_TOK, hT_src=[], hT_sb=hsT, oA=oAs, oB=oBs, x_rows=[xs[:, :]], row0=OWN, tile0=NT_OWN))
            for gd in groups:
                n = gd["n"]
                if gd["hT_sb"] is not None:
                    hTg = gd["hT_sb"]
                else:
                    hTg = hTg_r.next()
                for t, src in enumerate(gd["hT_src"]):
                    P.dma("sp", hTg[:, :, t * 128:(t + 1) * 128], T(src.rearrange("p (k n) -> p k n", k=8), None), nowaw=True)
                mT = mT_r.next()
                for c in range(8):
                    cs = slice(c * 128, (c + 1) * 128)
                    pgA, pgB, pyA, pyB = pg.next(), pg.next(), pg.next(), pg.next()
                    for k in range(8):
                        mm(pgA[:, :n], w2[:, k, cs], hTg[:, k, :n], k == 0, k == 7)
                    for k in range(8):
                        mm(pgB[:, :n], w2[:, k, 1024 + c * 128:1024 + (c + 1) * 128], hTg[:, k, :n], k == 0, k == 7)
                    for k in range(4):
                        mm(pyA[:, :n], wpa[:, k, cs], gd["oA"][:, k, :], k == 0, k == 3)
                    for k in range(4):
                        mm(pyB[:, :n], wpb[:, k, cs], gd["oB"][:, k, :], k == 0, k == 3)
                    sgA, sgB = sg_r.next(), sg_r.next()
                    for sg_, pg_ in ((sgA, pgA), (sgB, pgB)):
                        act(sg_[:, :n], pg_[:, :n], AF.Exp, scale=-1.0)
                        ts(sg_[:, :n], sg_[:, :n], 1.0, ALU.add, eng="pool")
                        recip(sg_[:, :n], sg_[:, :n])
                    mA, mB = mm_r.next(), mm_r.next()
                    tt(mA[:, :n], pyA[:, :n], sgA[:, :n], ALU.mult)
                    tt(mB[:, :n], pyB[:, :n], sgB[:, :n], ALU.mult)
                    tt(mT[:, c, :n], mA[:, :n], mB[:, :n], ALU.add, eng="pool")
                for t, xr in enumerate(gd["x_rows"]):
                    nt = min(128, n - t * 128)
                    xt = x_r.next()
                    P.dma("sp", xt[:nt], T(xr, None))
                    for half in range(2):
                        pm = pmix.next()
                        for c in range(8):
                            mm(pm[:nt, :], mT[:, c, t * 128:t * 128 + nt], wo[:, c, half * 512:(half + 1) * 512], c == 0, c == 7)
                        tt(xt[:nt, half * 512:(half + 1) * 512], pm[:nt, :], xt[:nt, half * 512:(half + 1) * 512], ALU.add)
                    r0 = gd["row0"] + t * 128
                    P.dma("sp", T(x1_d[r0:r0 + nt, :], None), xt[:nt], store=True)
                    hTt = hT_r.next()
                    norm_core(tl, xt, hTt[:, :, :nt], nt)
                    P.dma("sp", T(hmT_d[gd["tile0"] + t].rearrange("p (k n) -> p k n", k=8)[:, :, :nt], None), hTt[:, :, :nt], store=True)
            P.emit()

        mid.close()
        with ExitStack() as st:
            tl = {"wst": ring(st, sbt, "wst", [128, 512], F32, 4)}
            nmlw = sbt(st, "nmlw", [128, 8], F32)
            P.dma("sp", nmlw, T(nmlw_d, None))
            nfw = sbt(st, "nfw", [128, D], F32)
            P.dma("sp", nfw, T(nfw_d, None))
            wup = sbt(st, "wup", [128, 8, 4 * D], BF16)
            wdn = sbt(st, "wdn", [128, 32, D], BF16)
            for gi in range(8):
                load_w(tl, wup[:, :, gi * 512:(gi + 1) * 512], w_up_d, gi * 512, 512, scale=nmlw)
            for half in range(2):
                load_w(tl, wdn[:, :, half * 512:(half + 1) * 512], w_down_d, half * 512, 512, nk=32)
            actT = sbt(st, "actT", [128, 32, 256], BF16)
            hm_r = ring(st, sbt, "hm5", [128, 8, 256], BF16, 2)
            pu5 = ring(st, pst, "pu5", [128, 512], F32, 3)
            pd5 = ring(st, pst, "pd5", [128, 512], F32, 2)
            tmp_r = ring(st, sbt, "tmp5", [128, 256], F32, 2)
            x_r = ring(st, sbt, "x5", [128, D], F32, 2)
            sq5 = sbt(st, "sq5", [128, D], F32)
            ss_r = ring(st, sbt, "ss5", [128, 1], F32, 4)
            y_r = ring(st, sbt, "y5", [128, D], F32, 2)
            subs = []
            for sg in range(8):
                subs.append(dict(n=256, tiles=[2 * sg, 2 * sg + 1], row0=sg * 256, out=y_own))
            subs.append(dict(n=NS_TOK, tiles=[NT_OWN], row0=OWN, out=y_s))
            for sd in subs:
                n = sd["n"]
                hm = hm_r.next()
                for t, tile in enumerate(sd["tiles"]):
                    nt = min(128, n - t * 128)
                    P.dma("sp", hm[:, :, t * 128:t * 128 + nt], T(hmT_d[tile].rearrange("p (k n) -> p k n", k=8)[:, :, :nt], None), nowaw=True)
                for f in range(32):
                    pu = pu5.next()
                    for k in range(8):
                        mm(pu[:, :n], wup[:, k, f * 128:(f + 1) * 128], hm[:, k, :n], k == 0, k == 7)
                    tmp = tmp_r.next()
                    ts(tmp[:, :n], pu[:, :n], 0.0, ALU.max)
                    tt(actT[:, f, :n], tmp[:, :n], tmp[:, :n], ALU.mult, eng="pool")
                for t in range(len(sd["tiles"])):
                    nt = min(128, n - t * 128)
                    r0 = sd["row0"] + t * 128
                    xt = x_r.next()
                    P.dma("sp", xt[:nt], T(x1_d[r0:r0 + nt, :], None))
                    for half in range(2):
                        pd = pd5.next()
                        for f in range(32):
                            mm(pd[:nt, :], actT[:, f, t * 128:t * 128 + nt], wdn[:, f, half * 512:(half + 1) * 512], f == 0, f == 31)
                        tt(xt[:nt, half * 512:(half + 1) * 512], pd[:nt, :], xt[:nt, half * 512:(half + 1) * 512], ALU.add)
                    tt(sq5[:nt], xt[:nt], xt[:nt], ALU.mult, eng="pool")
                    ss = ss_r.next()
                    rsum(ss[:nt], sq5[:nt])
                    act(ss[:nt], ss[:nt], AF.Ln, bias=epsc[:nt], scale=1.0 / D)
                    act(ss[:nt], ss[:nt], AF.Exp, scale=-0.5)
                    y = y_r.next()
                    stt(y[:nt], xt[:nt], ss[:nt, 0:1], nfw[:nt], ALU.mult, ALU.mult)
                    ro = sd["row0"] - (0 if sd["out"] is y_own else OWN) + t * 128
                    P.dma("sp", T(sd["out"][ro:ro + nt, :], None), y[:nt], store=True)
            P.emit()
    return nc


_NC = None


def kernel(x_prompt, x_sample, cache_k, cache_v, page_table, state_conv, state_ssm,
           norm_mix_w, w_in, sb_bias, conv_w, a_log, dt_bias, gdn_norm_w, w_pa, w_pb, w_o,
           norm_mlp_w, w_up, w_down, norm_final_w):
    global _NC
    f32 = np.float32
    x_prompt = np.asarray(x_prompt, f32)
    x_sample = np.asarray(x_sample, f32)
    nc = build()
    ident = np.eye(128, dtype=f32)
    nmw = np.ascontiguousarray(np.asarray(norm_mix_w, f32)[0].reshape(8, 128).T)
    w_in0 = np.ascontiguousarray(np.asarray(w_in, f32)[0])
    cst = np.zeros((128, NCST, 128), f32)
    ii = np.arange(128)
    r_, c_ = ii[:, None], ii[None, :]
    cst[:, 0] = (r_ <= c_)
    cst[:, 1] = BIG * (c_ > r_)
    cst[:, 2] = -BIG * (r_ > c_)
    cst[:, 3] = (r_ < c_)
    cst[:, 4] = ((r_ // 64) == (c_ // 64))
    cst[:, 5] = 1.0 - cst[:, 4]
    cst[:, 6] = (r_ >= c_)
    same = ((r_ // 4) == (c_ // 4))
    cst[:, 7] = (r_ <= c_) & same
    cst[:, 8] = BIG * ((c_ > r_) | ~same)
    cst[:, 9] = -BIG * ((r_ > c_) | ~same)
    cst[:, 10] = (r_ < c_) & same
    cst[:, 11] = same
    cst[:, 12] = ((r_ // 4) == c_)
    state_conv = np.asarray(state_conv, f32)
    ck = np.asarray(cache_k, f32).reshape(2560 * 128, 512)
    cv = np.asarray(cache_v, f32).reshape(2560 * 128, 512)
    page_table = np.asarray(page_table, np.int32)
    if os.environ.get('NPOOL_DBG'):
        npd = int(os.environ['NPOOL_DBG'])
        ck, cv, page_table = ck[:npd * 128], cv[:npd * 128], page_table % npd
    iota = np.ascontiguousarray(np.tile(np.arange(128, dtype=np.int32)[:, None], (1, 256)))
    sbrow = np.ascontiguousarray(np.tile(np.repeat(np.asarray(sb_bias, f32)[0], 4)[None, :], (128, 16)))
    mnew = np.zeros((128, 32), f32)
    selE = np.zeros((64, 256), f32)
    selE[np.arange(64), np.arange(64)] = 1.0
    for t1 in range(4):
        for hq in range(32):
            mnew[t1, hq] = 1.0 if t1 < (hq % 4) else 0.0
    dmm = np.zeros((32, 8, 64), f32)
    for hq in range(32):
        dmm[hq, hq // 4, :] = 1.0
    dmm = dmm.reshape(32, 512)
    state_ssm = np.asarray(state_ssm, f32)
    sbb = np.ascontiguousarray(np.tile(np.asarray(sb_bias, f32)[0][None, :], (128, 1)))
    masks = []
    for jj in range(4):
        m = np.zeros((128, 16, 512), f32)
        for r in range(4):
            for kb in range(4):
                kbrel = 4 * r + kb
                for qb in range(4):
                    qbrel = 4 * jj + qb
                    if kbrel < qbrel:
                        m[:, r * 4 + kb, qb * 128:(qb + 1) * 128] = 1.0
                    elif kbrel == qbrel:
                        m[:, r * 4 + kb, qb * 128:(qb + 1) * 128] = (r_ < c_)
        masks.append(m.astype(ml_dtypes.bfloat16))
    cwl = np.ascontiguousarray(np.asarray(conv_w, f32)[0].reshape(4, 3, 4, 128).transpose(3, 1, 2, 0).reshape(128, 12, 4))
    alog_bc = np.ascontiguousarray(np.tile(np.asarray(a_log, f32)[0][None, :], (128, 1)))
    dtb_bc = np.ascontiguousarray(np.tile(np.asarray(dt_bias, f32)[0][None, :], (128, 1)))
    gnw_bc = np.ascontiguousarray(np.tile(np.asarray(gdn_norm_w, f32)[0][None, :], (128, 1)))
    nmlw = np.ascontiguousarray(np.asarray(norm_mlp_w, f32)[0].reshape(8, 128).T)
    nfw_bc = np.ascontiguousarray(np.tile(np.asarray(norm_final_w, f32)[None, :], (128, 1)))
    w_pa0 = np.ascontiguousarray(np.asarray(w_pa, f32)[0])
    w_pb0 = np.ascontiguousarray(np.asarray(w_pb, f32)[0])
    w_o0 = np.ascontiguousarray(np.asarray(w_o, f32)[0])
    w_up0 = np.ascontiguousarray(np.asarray(w_up, f32)[0])
    w_down0 = np.ascontiguousarray(np.asarray(w_down, f32)[0])
    in_maps = []
    own_idx = []
    for c in range(NCORE):
        b, j = c // 4, c % 4
        groups = [4 * i + j for i in range(4)]
        idx = np.concatenate([np.arange(512 * g, 512 * g + 512) for g in groups])
        own_idx.append(idx)
        in_maps.append({
            "xfull": np.ascontiguousarray(x_prompt[b]),
            "xown": np.ascontiguousarray(x_prompt[b][idx]),
            "xs": np.ascontiguousarray(x_sample[16 * c:16 * c + 16].reshape(64, D)),
            "w_in": w_in0, "nmw": nmw, "ident": ident, "cst": cst, "cw": cwl, "alog_bc": alog_bc, "dtb_bc": dtb_bc,
            "gnw_bc": gnw_bc, "w_pa": w_pa0, "w_pb": w_pb0, "w_o": w_o0, "w_up": w_up0, "w_down": w_down0,
            "nmlw": nmlw, "nfw_bc": nfw_bc, "sbb": sbb,
            "cache_k": ck, "cache_v": cv, "pt": np.ascontiguousarray(page_table[16 * c:16 * c + 16].reshape(1, 256)),
            "iota": iota, "sbrow": sbrow, "mnew": mnew, "dm": dmm, "selE": selE,
            "sc": np.ascontiguousarray(state_conv[0, 16 * c:16 * c + 16].reshape(48, 1536)),
            "ssm": np.ascontiguousarray(state_ssm[0, 16 * c:16 * c + 16]), "maskd": masks[j], "sel": np.ascontiguousarray(np.tile((np.arange(4) == j).astype(f32)[None, :], (128, 1))),
        })
    if os.environ.get('RETURN_MAPS') == '1':
        return nc, in_maps
    res = run_bass_kernel_spmd(nc, in_maps, core_ids=list(range(NCORE)))
    R = res.results
    y_prompt = np.zeros((2, SEQ, D), f32)
    y_sample = np.zeros((128, 4, D), f32)
    nkp = np.zeros((2, SEQ, 512), f32)
    nvp = np.zeros((2, SEQ, 512), f32)
    nksa = np.zeros((128, 4, 512), f32)
    nvsa = np.zeros((128, 4, 512), f32)
    ncpa = np.zeros((1, 2, 3, 1536), f32)
    ncsa = np.zeros((1, 128, 3, 1536), f32)
    nsp = np.zeros((1, 2, 4, 128, 128), f32)
    nss = np.zeros((1, 128, 4, 128, 128), f32)
    for c in range(NCORE):
        b, j = c // 4, c % 4
        r = R[c]
        y_prompt[b][own_idx[c]] = r["y_own"]
        y_sample[16 * c:16 * c + 16] = r["y_s"].reshape(16, 4, D)
        nkp[b][own_idx[c]] = r["nk_own"]
        nvp[b][own_idx[c]] = r["nv_own"]
        nksa[16 * c:16 * c + 16] = r["nks"].reshape(16, 4, 512)
        nvsa[16 * c:16 * c + 16] = r["nvs"].reshape(16, 4, 512)
        ncsa[0, 16 * c:16 * c + 16] = r["ncs"]
        nss[0, 16 * c:16 * c + 16] = r["nss"]
        if j == 0:
            ncpa[0, b] = r["ncp"]
            nsp[0, b] = r["nsp"]
    return (y_prompt, y_sample,
            nkp.reshape(1, 2, 64, 128, 8, 64), nvp.reshape(1, 2, 64, 128, 8, 64),
            nksa.reshape(1, 128, 4, 8, 64), nvsa.reshape(1, 128, 4, 8, 64),
            ncpa, ncsa, nsp, nss)
```

```python
import os
import numpy as np
import concourse.bass as bass
import concourse.mybir as mybir
from concourse.bass_utils import run_bass_kernel_spmd

F32 = mybir.dt.float32
BF16 = mybir.dt.bfloat16
I32 = mybir.dt.int32
AF = mybir.ActivationFunctionType
ALU = mybir.AluOpType
AX = mybir.AxisListType

EPOCH = 20000


class Res:
    _n = 0

    def __init__(self, name):
        Res._n += 1
        self.name = f"{name}#{Res._n}"
        self.writers = []
        self.readers = []
        self.dma_sem = None
        self.dma_cnt = 0
        self.pe_acc = False
        self.store_res = None


class T:
    def __init__(self, ap, res):
        self.ap = ap
        self.res = res

    def __getitem__(self, idx):
        return T(self.ap[idx], self.res)


class Op:
    __slots__ = ("eng", "idx", "fn", "waits", "needed", "dma_res", "clock", "semval", "is_dma")


class Prog:
    ENGS = ("pe", "act", "dve", "pool", "sp")

    def __init__(self, nc):
        self.nc = nc
        self.ops = {e: [] for e in self.ENGS}
        self.clock = {e: {} for e in self.ENGS}
        self.dma_res = []
        self.dma_clock = {}
        self._dom2res = {}
        self.all_res = []

    def res(self, name):
        r = Res(name)
        self.all_res.append(r)
        return r

    def _add(self, eng, fn, reads, writes, dma=False, pe_acc=False, evres=None, nowaw=False):
        reads = [r for r in reads if r is not None and r.res is not None]
        writes = [w for w in writes if w is not None and w.res is not None]
        op = Op()
        op.eng = eng
        op.idx = len(self.ops[eng]) + 1
        op.fn = fn
        op.needed = False
        op.is_dma = dma
        op.dma_res = None
        deps = []
        for r in reads:
            deps += r.res.writers
        for w in writes:
            if pe_acc and eng == "pe" and w.res.pe_acc and all(ev[0] == "pe" for ev in w.res.writers):
                deps += w.res.readers
            else:
                if not nowaw:
                    deps += w.res.writers
                deps += w.res.readers
        clk = self.clock[eng]
        waits = []
        best = {}
        for ev in deps:
            dom, val = ev[0], ev[1]
            if dom == eng and eng in ("pe", "sp"):
                continue
            if clk.get(dom, 0) >= val:
                continue
            if best.get(dom, 0) < val:
                best[dom] = val
        for dom, val in best.items():
            waits.append((dom, val))
            if isinstance(dom, str):
                src = self.ops[dom][val - 1]
                src.needed = True
                for d2, v2 in src.clock.items():
                    if clk.get(d2, 0) < v2:
                        clk[d2] = v2
            else:
                snap = self.dma_clock.get((dom, val), {})
                for d2, v2 in snap.items():
                    if clk.get(d2, 0) < v2:
                        clk[d2] = v2
            if clk.get(dom, 0) < val:
                clk[dom] = val
        op.waits = waits
        if dma:
            tgt = evres
            if tgt.dma_sem is None:
                tgt.dma_sem = True
                self.dma_res.append(tgt)
            tgt.dma_cnt += 1
            if eng == "pool":
                tgt.sw = True
            op.dma_res = tgt
            ev = (("dma", tgt.name), tgt.dma_cnt)
            self.dma_clock[ev] = dict(clk)
            self._dom2res[("dma", tgt.name)] = tgt
        else:
            ev = (eng, op.idx)
        op.clock = dict(clk)
        if not dma:
            op.clock[eng] = op.idx
        for r in reads:
            r.res.readers.append(ev)
        for w in writes:
            if pe_acc and eng == "pe" and w.res.pe_acc and all(e2[0] == "pe" for e2 in w.res.writers):
                w.res.writers = [ev]
            else:
                w.res.writers = [ev]
            w.res.readers = []
            w.res.pe_acc = bool(pe_acc and eng == "pe")
        self.ops[eng].append(op)
        return op

    def op(self, eng, fn, reads=(), writes=(), pe_acc=False):
        return self._add(eng, fn, list(reads), list(writes), pe_acc=pe_acc)

    def dma(self, eng, out, in_, store=False, nowaw=False, fn=None, **kw):
        if store:
            if in_.res.store_res is None:
                in_.res.store_res = Res("st_" + in_.res.name)
            evres = in_.res.store_res
        else:
            evres = out.res
        if fn is None:
            def fn(e, o=out.ap, i=in_.ap, kw=kw):
                return e.dma_start(out=o, in_=i, **kw)
        return self._add(eng, fn, [in_], [out], dma=True, evres=evres, nowaw=nowaw)

    def barrier(self):
        evs = []
        for e in self.ENGS:
            for op in reversed(self.ops[e]):
                if not op.is_dma and op.fn is not None:
                    evs.append((e, op.idx))
                    break
        for r in self.dma_res:
            if r.dma_cnt:
                evs.append((("dma", r.name), r.dma_cnt))
        bres = T(None, Res("barrier"))
        bres.res.writers = evs
        for e in self.ENGS:
            self._add(e, None, [bres], [])

    def setup_sems(self, stack, n_dma=70):
        nc = self.nc
        self.dma_pool_sw = [[stack.enter_context(nc.semaphore(f"dsw_{k}")), 0] for k in range(8)]
        self.sems = {e: [stack.enter_context(nc.semaphore(f"s_{e}_{k}")) for k in range(3)] for e in self.ENGS}
        self.count = {e: 0 for e in self.ENGS}
        self.dma_pool = [[stack.enter_context(nc.semaphore(f"d_{k}")), 0] for k in range(n_dma)]
        self.all_res = []

    def emit(self):
        nc = self.nc
        self.barrier()
        nsem = {}
        for e in self.ENGS:
            c = self.count[e]
            for op in self.ops[e]:
                if op.needed and not op.is_dma:
                    c += 1
                    op.semval = c
                else:
                    op.semval = None
            nsem[e] = c - self.count[e]
            self.count[e] = c
        ihw = isw = 0
        for r in self.dma_res:
            if getattr(r, "sw", False):
                slot = self.dma_pool_sw[isw]
                isw += 1
            else:
                slot = self.dma_pool[ihw]
                ihw += 1
            r.dma_sem = slot
            r.dma_base = slot[1]
            slot[1] += 16 * r.dma_cnt
        print("pass ops:", {e: len(self.ops[e]) for e in self.ENGS}, "flagged:", nsem, "dma sems:", len(self.dma_res), flush=True)
        if os.environ.get("DUMP_WAITS"):
            for e in self.ENGS:
                for op in self.ops[e][:60]:
                    ws = []
                    for dom, val in op.waits:
                        if isinstance(dom, str):
                            ws.append((dom, val, self.ops[dom][val - 1].semval))
                        else:
                            r = self._dom2res[dom]
                            ws.append((dom[1], val, r.dma_base + 16 * val))
                    print("  ", e, op.idx, "dma" if op.is_dma else "", "sem=%s" % op.semval, ws)
        sems = self.sems
        engobj = {"pe": "tensor", "act": "scalar", "dve": "vector", "pool": "gpsimd", "sp": "sync"}
        with nc.Block() as block:
            def make(e):
                def body(eng):
                    for op in self.ops[e]:
                        for dom, val in op.waits:
                            if isinstance(dom, str):
                                sv = self.ops[dom][val - 1].semval
                                k = (sv - 1) // EPOCH
                                eng.wait_ge(sems[dom][k], sv - k * EPOCH)
                            else:
                                r = self._dom2res[dom]
                                eng.wait_ge(r.dma_sem[0], r.dma_base + 16 * val)
                        if op.fn is None:
                            continue
                        ins = op.fn(eng)
                        if op.is_dma:
                            ins.then_inc(op.dma_res.dma_sem[0], 16)
                        elif op.semval is not None:
                            k = (op.semval - 1) // EPOCH
                            ins.then_inc(sems[e][k], 1)
                return body
            for e in self.ENGS:
                getattr(block, engobj[e])(make(e))
        self.ops = {e: [] for e in self.ENGS}
        self.clock = {e: {} for e in self.ENGS}
        for r in self.all_res:
            r.writers = []
            r.readers = []
            r.dma_sem = None
            r.dma_cnt = 0
            r.pe_acc = False
            r.store_res = None
            r.sw = False
        self.dma_res = []
        self.dma_clock = {}
        self._dom2res = {}


from contextlib import ExitStack
import os
import ml_dtypes

NCORE = 8
D = 1024
IN_DIM = 5640
SEQ = 8192
NT_FULL = SEQ // 128
OWN = 2048
NT_OWN = OWN // 128
NS_TOK = 64
NCST = 13
BIG = 300.0
EPS = 1e-6


class Ring:
    def __init__(self, tiles):
        self.tiles = tiles
        self.i = 0

    def next(self):
        t = self.tiles[self.i % len(self.tiles)]
        self.i += 1
        return t


def build():
    nc = bass.Bass("TRN2", target_bir_lowering=False)

    def din(name, shape, dt=F32):
        return nc.dram_tensor(name, list(shape), dt, kind="ExternalInput").ap()

    def dout(name, shape, dt=F32):
        return nc.dram_tensor(name, list(shape), dt, kind="ExternalOutput").ap()

    xfull = din("xfull", [SEQ, D])
    xown = din("xown", [OWN, D])
    xs = din("xs", [NS_TOK, D])
    w_in = din("w_in", [D, IN_DIM])
    nmw = din("nmw", [128, 8])
    ident_d = din("ident", [128, 128])
    cst_d = din("cst", [128, NCST, 128])
    cw_d = din("cw", [128, 12, 4])
    alog_d = din("alog_bc", [128, 4])
    dtb_d = din("dtb_bc", [128, 4])
    gnw_d = din("gnw_bc", [128, 128])
    sel_d = din("sel", [128, 4])
    sbb_d = din("sbb", [128, 8])
    mask_d = din("maskd", [128, 16, 512], BF16)
    sc_d = din("sc", [48, 1536])
    NPOOL = int(os.environ.get('NPOOL_DBG', '2560'))
    ck_d = din("cache_k", [NPOOL * 128, 512])
    cv_d = din("cache_v", [NPOOL * 128, 512])
    pt_d = din("pt", [1, 256], I32)
    iota_d = din("iota", [128, 256], I32)
    sbrow_d = din("sbrow", [128, 512])
    mnew_d = din("mnew", [128, 32])
    selE_d = din("selE", [64, 256])
    dm_d = din("dm", [32, 512])
    ssm_d = din("ssm", [16, 4, 128, 128])
    nss = dout("nss", [16, 4, 128, 128])
    nsp = dout("nsp", [4, 128, 128])
    hT_d = nc.dram_tensor("hT_d", [NT_FULL, 128, 1024], BF16, kind="Internal").ap()
    hTown_d = nc.dram_tensor("hTown_d", [NT_OWN, 128, 1024], BF16, kind="Internal").ap()
    ps_d = nc.dram_tensor("ps_d", [NS_TOK, IN_DIM], F32, kind="Internal").ap()
    x1_d = nc.dram_tensor("x1_d", [OWN + NS_TOK, D], F32, kind="Internal").ap()
    hmT_d = nc.dram_tensor("hmT_d", [NT_OWN + 1, 128, 1024], BF16, kind="Internal").ap()
    w_pa_d = din("w_pa", [512, D])
    w_pb_d = din("w_pb", [512, D])
    w_o_d = din("w_o", [D, D])
    w_up_d = din("w_up", [D, 4 * D])
    w_down_d = din("w_down", [4 * D, D])
    nmlw_d = din("nmlw", [128, 8])
    nfw_d = din("nfw_bc", [128, D])
    y_own = dout("y_own", [OWN, D])
    y_s = dout("y_s", [NS_TOK, D])

    nk_own = dout("nk_own", [OWN, 512])
    nv_own = dout("nv_own", [OWN, 512])
    nks = dout("nks", [NS_TOK, 512])
    nvs = dout("nvs", [NS_TOK, 512])
    ncp = dout("ncp", [3, 1536])
    ncs = dout("ncs", [16, 3, 1536])

    P = Prog(nc)
    with ExitStack() as gst:
        P.setup_sems(gst)

        uid = [0]

        def sbt(st, name, shape, dt):
            uid[0] += 1
            name = f"{name}_{uid[0]}"
            return T(st.enter_context(nc.sbuf_tensor(name, list(shape), dt))[:], P.res(name))

        def pst(st, name, shape, dt):
            uid[0] += 1
            name = f"{name}_{uid[0]}"
            return T(st.enter_context(nc.psum_tensor(name, list(shape), dt))[:], P.res(name))

        def ring(st, fn, name, shape, dt, n):
            return Ring([fn(st, f"{name}{i}", shape, dt) for i in range(n)])

        identf = sbt(gst, "identf", [128, 128], F32)
        identb = sbt(gst, "identb", [128, 128], BF16)
        epsc = sbt(gst, "epsc", [128, 1], F32)
        nmw_t = sbt(gst, "nmw_t", [128, 8], F32)
        hsT = sbt(gst, "hsT", [128, 8, NS_TOK], BF16)
        hTlast = sbt(gst, "hTlast", [128, 8, 128], BF16)
        onec = sbt(gst, "onec", [128, 1], F32)
        sel_t = sbt(gst, "sel_t", [128, 4], F32)


        def apx(v):
            return v.ap if isinstance(v, T) else v

        def mm(out, lhsT, rhs, start=True, stop=True):
            P.op("pe", lambda e: e.matmul(out=out.ap, lhsT=lhsT.ap, rhs=rhs.ap, start=start, stop=stop), [lhsT, rhs], [out], pe_acc=True)

        def tr(out, in_, idn):
            P.op("pe", lambda e: e.transpose(out=out.ap, in_=in_.ap, identity=idn.ap), [in_, idn], [out], pe_acc=True)

        def act(out, in_, func, bias=None, scale=None):
            kw = {}
            rd = [in_]
            if bias is not None:
                kw["bias"] = apx(bias)
                if isinstance(bias, T):
                    rd.append(bias)
            if scale is not None:
                kw["scale"] = apx(scale)
                if isinstance(scale, T):
                    rd.append(scale)
            P.op("act", lambda e: e.activation(out=out.ap, in_=in_.ap, func=func, **kw), rd, [out])

        def ts(out, in0, s1, op0, s2=None, op1=None, eng="dve"):
            rd = [in0] + [v for v in (s1, s2) if isinstance(v, T)]
            if op1 is None:
                P.op(eng, lambda e: e.tensor_scalar(out=out.ap, in0=in0.ap, scalar1=apx(s1), scalar2=None, op0=op0), rd, [out])
            else:
                P.op(eng, lambda e: e.tensor_scalar(out=out.ap, in0=in0.ap, scalar1=apx(s1), scalar2=apx(s2), op0=op0, op1=op1), rd, [out])

        def tt(out, in0, in1, op, eng="dve"):
            P.op(eng, lambda e: e.tensor_tensor(out=out.ap, in0=in0.ap, in1=in1.ap, op=op), [in0, in1], [out])

        def stt(out, in0, scalar, in1, op0, op1):
            rd = [in0, in1] + ([scalar] if isinstance(scalar, T) else [])
            P.op("dve", lambda e: e.scalar_tensor_tensor(out=out.ap, in0=in0.ap, scalar=apx(scalar), in1=in1.ap, op0=op0, op1=op1), rd, [out])

        def cp(out, in_, eng="dve"):
            if eng == "act":
                P.op("act", lambda e: e.copy(out=out.ap, in_=in_.ap), [in_], [out])
            else:
                P.op(eng, lambda e: e.tensor_copy(out=out.ap, in_=in_.ap), [in_], [out])

        def recip(out, in_):
            P.op("dve", lambda e: e.reciprocal(out=out.ap, in_=in_.ap), [in_], [out])

        def rsum(out, in_):
            P.op("dve", lambda e: e.reduce_sum(out=out.ap, in_=in_.ap, axis=AX.X), [in_], [out])

        def memset(t, v, eng="pool"):
            P.op(eng, lambda e: e.memset(t.ap, v), [], [t])

        def sub(t, idx, name):
            return T(t.ap[idx], P.res(name))

        def norm_tile(tl, x_rows_ap, hT_dst, nt=128):
            xt = tl["x"].next()
            P.dma("sp", xt[:nt], T(x_rows_ap, None))
            norm_core(tl, xt, hT_dst, nt)

        def norm_core(tl, xt, hT_dst, nt=128):
            sq = tl["sq"].next()
            P.op("act", lambda e: e.activation(out=sq.ap[:nt], in_=xt.ap[:nt], func=AF.Square), [xt], [sq])
            ss = tl["ss"].next()
            P.op("dve", lambda e: e.reduce_sum(out=ss.ap[:nt], in_=sq.ap[:nt], axis=AX.X), [sq], [ss])
            P.op("act", lambda e: e.activation(out=ss.ap[:nt], in_=ss.ap[:nt], func=AF.Ln, bias=epsc.ap[:nt], scale=1.0 / D), [ss, epsc], [ss])
            P.op("act", lambda e: e.activation(out=ss.ap[:nt], in_=ss.ap[:nt], func=AF.Exp, scale=-0.5), [ss], [ss])
            hb = tl["hb"].next()
            P.op("dve", lambda e: e.tensor_scalar(out=hb.ap[:nt], in0=xt.ap[:nt], scalar1=ss.ap[:nt, 0:1], scalar2=None, op0=ALU.mult), [xt, ss], [hb])
            pt = tl["ptr"].next()
            for k in range(8):
                P.op("pe", lambda e, k=k: e.transpose(out=pt.ap[:, k, :nt], in_=hb.ap[:nt, k * 128:(k + 1) * 128], identity=identb.ap[:nt, :nt]), [hb, identb], [pt], pe_acc=True)
            P.op("act", lambda e: e.copy(out=hT_dst.ap, in_=pt.ap[:, :, :nt]), [pt], [hT_dst])

        def load_w(tl, dst, w_dram, c0, n, scale=None, nk=8):
            for k in range(nk):
                stg = tl["wst"].next()
                P.dma("sp", stg[:, :n], T(w_dram[k * 128:(k + 1) * 128, c0:c0 + n], None))
                eng = ("pool", "dve")[k % 2]
                if scale is not None:
                    P.op(eng, lambda e, k=k, stg=stg: e.tensor_scalar(out=dst.ap[:, k, :n], in0=stg.ap[:, :n], scalar1=scale.ap[:, k:k + 1], scalar2=None, op0=ALU.mult), [stg, scale], [dst])
                else:
                    P.op(eng, lambda e, k=k, stg=stg: e.tensor_copy(out=dst.ap[:, k, :n], in_=stg.ap[:, :n]), [stg], [dst])

        with ExitStack() as st:
            tl = {
                "x": ring(st, sbt, "x", [128, D], F32, 3),
                "sq": ring(st, sbt, "sq", [128, D], F32, 2),
                "ss": ring(st, sbt, "ss", [128, 1], F32, 4),
                "hb": ring(st, sbt, "hb", [128, D], BF16, 2),
                "ptr": ring(st, pst, "ptr", [128, 8, 128], BF16, 2),
            }
            hTt = ring(st, sbt, "hTt", [128, 8, 128], BF16, 3)
            P.dma("sp", identf, T(ident_d, None))
            P.dma("sp", nmw_t, T(nmw, None))
            P.op("pool", lambda e: e.memset(epsc.ap, EPS), [], [epsc])
            P.op("pool", lambda e: e.memset(onec.ap, 1.0), [], [onec])
            P.dma("sp", sel_t, T(sel_d, None))
            P.op("dve", lambda e: e.tensor_copy(out=identb.ap, in_=identf.ap), [identf], [identb])
            for t in sorted(set(list(range(int(os.environ.get('NT0', NT_FULL)))) + [NT_FULL - 1])):
                dst = hTt.next() if t < NT_FULL - 1 else hTlast
                norm_tile(tl, xfull[t * 128:(t + 1) * 128, :], dst)
                P.dma("sp", T(hT_d[t].rearrange("p (k n) -> p k n", k=8), None), dst, store=True)
            for t in range(min(NT_OWN, int(os.environ.get('NT0', NT_OWN)))):
                dst = hTt.next()
                norm_tile(tl, xown[t * 128:(t + 1) * 128, :], dst)
                P.dma("sp", T(hTown_d[t].rearrange("p (k n) -> p k n", k=8), None), dst, store=True)
            norm_tile(tl, xs[:, :], hsT, nt=NS_TOK)
            P.emit()

        for _skip in ([] if os.environ.get('SKIP1') == '1' else [0]):
          with ExitStack() as st:
              tl = {"wst": ring(st, sbt, "wst", [128, 512], F32, 4)}
              wg = ring(st, sbt, "wg", [128, 8, 512], BF16, 2)
              wkv = sbt(st, "wkv", [128, 8, 1024], BF16)
              pmm = ring(st, pst, "pmm", [128, 512], F32, 4)
              ob = ring(st, sbt, "ob", [128, 1024], F32, 3)
              cps = sbt(st, "cps", [3, 1536], F32)
              ps_s = sbt(st, "ps_s", [NS_TOK, IN_DIM], F32)
              hTo_r = ring(st, sbt, "hTo", [128, 8, 128], BF16, 3)
              ngrp = (IN_DIM + 511) // 512
              for gi in range(ngrp):
                  c0 = gi * 512
                  n = min(512, IN_DIM - c0)
                  w = wg.next()
                  load_w(tl, w, w_in, c0, n, scale=nmw_t)
                  pm = pmm.next()
                  for k in range(8):
                      P.op("pe", lambda e, k=k, w=w, pm=pm, n=n: e.matmul(out=pm.ap[:NS_TOK, :n], lhsT=hsT.ap[:, k, :], rhs=w.ap[:, k, :n], start=(k == 0), stop=(k == 7)), [hsT, w], [pm], pe_acc=True)
                  P.op("act", lambda e, pm=pm, c0=c0, n=n: e.copy(out=ps_s.ap[:, c0:c0 + n], in_=pm.ap[:NS_TOK, :n]), [pm], [ps_s])
                  if gi in (1, 2):
                      P.op("pool", lambda e, w=w, gi=gi: e.tensor_copy(out=wkv.ap[:, :, (gi - 1) * 512:gi * 512], in_=w.ap), [w], [wkv])
                  if gi in (3, 4, 5):
                      pm2 = pmm.next()
                      for k in range(8):
                          P.op("pe", lambda e, k=k, w=w, pm2=pm2: e.matmul(out=pm2.ap[:3, :], lhsT=hTlast.ap[:, k, 125:128], rhs=w.ap[:, k, :], start=(k == 0), stop=(k == 7)), [hTlast, w], [pm2], pe_acc=True)
                      P.op("act", lambda e, pm2=pm2, gi=gi: e.copy(out=cps.ap[:, (gi - 3) * 512:(gi - 2) * 512], in_=pm2.ap[:3, :]), [pm2], [cps])
              P.dma("sp", T(ncp, None), cps, store=True)
              P.dma("sp", T(nks, None), ps_s[:, 512:1024], store=True)
              P.dma("sp", T(nvs, None), ps_s[:, 1024:1536], store=True)
              for r in range(3):
                  P.dma("sp", T(ncs[:, r, :], None), T(ps_s.ap[r + 1:NS_TOK:4, 1536:3072], ps_s.res), store=True)
              P.dma("sp", T(ps_d, None), ps_s, store=True)
              for t in range(NT_OWN):
                  o = ob.next()
                  hTo = hTo_r.next()
                  P.dma("sp", hTo, T(hTown_d[t].rearrange("p (k n) -> p k n", k=8), None))
                  for half in range(2):
                      pm = pmm.next()
                      for k in range(8):
                          P.op("pe", lambda e, k=k, pm=pm, half=half, hTo=hTo: e.matmul(out=pm.ap, lhsT=hTo.ap[:, k, :], rhs=wkv.ap[:, k, half * 512:(half + 1) * 512], start=(k == 0), stop=(k == 7)), [hTo, wkv], [pm], pe_acc=True)
                      P.op(("act", "dve")[half], lambda e, pm=pm, half=half, o=o: (e.copy if half == 0 else e.tensor_copy)(out=o.ap[:, half * 512:(half + 1) * 512], in_=pm.ap), [pm], [o])
                  P.dma("sp", T(nk_own[t * 128:(t + 1) * 128, :], None), o[:, 0:512], store=True)
                  P.dma("sp", T(nv_own[t * 128:(t + 1) * 128, :], None), o[:, 512:1024], store=True)
              P.emit()

        mid = ExitStack()
        oBown = sbt(mid, "oBown", [128, 4, OWN], BF16)
        oAown = sbt(mid, "oAown", [128, 4, OWN], BF16)
        oBs = sbt(mid, "oBs", [128, 4, NS_TOK], BF16)
        oAs = sbt(mid, "oAs", [128, 4, NS_TOK], BF16)
        with ExitStack() as st:
            memset(oBown, 0.0)
            memset(oAown, 0.0)
            memset(oAs, 0.0)
            tl = {"wst": ring(st, sbt, "wst", [128, 512], F32, 4)}
            cst = sbt(st, "cst", [128, NCST, 128], F32)
            P.dma("sp", cst, T(cst_d, None))
            C_UINCL, C_BIGU, C_NBIGL, C_SU, C_BD, C_OFFM = (cst[:, i, :] for i in range(6))
            ones = sbt(st, "ones", [128, 128], F32)
            memset(ones, 1.0)
            cw = sbt(st, "cw", [128, 12, 4], F32)
            P.dma("sp", cw, T(cw_d, None))
            negA = sbt(st, "negA", [128, 4], F32)
            dtb = sbt(st, "dtb", [128, 4], F32)
            gnw = sbt(st, "gnw", [128, 128], F32)
            P.dma("sp", negA, T(alog_d, None))
            P.dma("sp", dtb, T(dtb_d, None))
            P.dma("sp", gnw, T(gnw_d, None))
            act(negA, negA, AF.Exp)
            ts(negA, negA, -1.0, ALU.mult)
            wgd = sbt(st, "wgd", [128, 8, 2056], BF16)
            for gi in range(4):
                load_w(tl, wgd[:, :, gi * 512:(gi + 1) * 512], w_in, 1536 + gi * 512, 512, scale=nmw_t)
            load_w(tl, wgd[:, :, 2048:2056], w_in, 3584, 8, scale=nmw_t)
            S = [[sbt(st, f"S{h}_{i}", [128, 128], F32) for i in range(2)] for h in range(4)]
            for h in range(4):
                memset(S[h][0], 0.0)
            ub_r = ring(st, sbt, "ub", [128, 3, 515], F32, 2)
            hist = [sbt(st, f"hist{h}", [128, 3, 3], F32) for h in range(4)]
            for h in range(4):
                memset(hist[h], 0.0)
            hTg_r = ring(st, sbt, "hTg", [128, 8, 512], BF16, 2)
            pbank = [pst(st, f"pb{i}", [128, 4, 128], F32) for i in range(3)]
            pu = pst(st, "pu", [128, 3, 512], F32)
            pss = pst(st, "pss", [128, 512], F32)
            pq = Ring([T(pbank[i % 3].ap[:, (i // 3) % 4, :], pbank[i % 3].res) for i in range(12)])
            sq_r = Ring([sbt(st, f"sq{i}", [128, 128], F32) for i in range(32)])
            col_r = Ring([sbt(st, f"col{i}", [128, 1], F32) for i in range(32)])
            big_r = Ring([sbt(st, f"big{i}", [128, 3, 512], F32) for i in range(3)])
            nrm_r = Ring([sbt(st, f"nrm{i}", [128, 512], F32) for i in range(4)])
            abg = sbt(st, "abg", [128, 4, 8], F32)
            gb_r = Ring([sbt(st, f"gb{i}", [128, 4, 4], F32) for i in range(8)])
            ogb_r = Ring([sbt(st, f"ogb{i}", [128, 128], BF16) for i in range(2)])
            ptrb = ring(st, pst, "ptrb", [128, 128], BF16, 1)
            l2e = sbt(st, "l2e", [128, 1], F32)
            memset(l2e, 1e-6)
            dtb4 = sbt(st, "dtb4", [128, 4, 4], F32)
            negA4 = sbt(st, "negA4", [128, 4, 4], F32)
            for c in range(4):
                cp(dtb4[:, c, :], dtb, eng="pool")
                cp(negA4[:, c, :], negA, eng="pool")

            STAGE = int(os.environ.get('GDN_STAGE', '99'))

            def gdn_chunk(h, qT, kT, vT, gcol, bcol, nbcol, hTc, own_dst, selcol, Sin, Sout):
                G = sq_r.next()
                ts(G, C_UINCL, gcol, ALU.mult)
                PA, PB, pm1 = pq.next(), pq.next(), pq.next()
                mm(PA, ones, G, True, False)
                mm(PA, identf, C_BIGU, False, True)
                mm(PB, ones, G, True, False)
                mm(PB, identf, C_NBIGL, False, True)
                mm(pm1[:, 0:1], ones, G[:, 127:128])
                mm(pm1[:, 1:2], C_UINCL, gcol)
                cg, ncg, ecg, egl, edl = (col_r.next() for _ in range(5))
                cp(cg, pm1[:, 1:2], eng="act")
                ts(ncg, pm1[:, 1:2], -1.0, ALU.mult)
                decay, decayT = sq_r.next(), sq_r.next()
                act(decay, PA, AF.Exp, bias=cg, scale=-1.0)
                act(decayT, PB, AF.Exp, bias=ncg, scale=1.0)
                act(ecg, cg, AF.Exp)
                act(egl, pm1[:, 0:1], AF.Exp)
                act(edl, pm1[:, 0:1], AF.Exp, bias=ncg, scale=1.0)
                if STAGE < 4:
                    return None
                KK, QKT = pq.next(), pq.next()
                mm(KK, kT, kT)
                mm(QKT, kT, qT)
                t0, Wm, AT = sq_r.next(), sq_r.next(), sq_r.next()
                tt(t0, KK, decayT, ALU.mult)
                stt(Wm, t0, bcol, C_SU, ALU.mult, ALU.mult)
                tt(AT, QKT, decayT, ALU.mult)
                WmTp = pq.next()
                tr(WmTp, Wm, identf)
                WmT = sq_r.next()
                cp(WmT, WmTp, eng="act")
                if STAGE < 5:
                    return None
                X, XT, OffT, Pk = sq_r.next(), sq_r.next(), sq_r.next(), sq_r.next()
                tt(X, Wm, C_BD, ALU.mult, eng="pool")
                tt(XT, WmT, C_BD, ALU.mult, eng="pool")
                tt(OffT, WmT, C_OFFM, ALU.mult, eng="pool")
                tt(Pk, identf, X, ALU.subtract)
                for lvl in range(1, 6):
                    p2 = pq.next()
                    mm(p2, X, XT)
                    X2T = sq_r.next()
                    if lvl < 5:
                        p1 = pq.next()
                        mm(p1, XT, X)
                        X2 = sq_r.next()
                        cp(X2, p1, eng="act")
                    cp(X2T, p2, eng="dve")
                    p3 = pq.next()
                    mm(p3, X2T, Pk)
                    Pn = sq_r.next()
                    tt(Pn, p3, Pk, ALU.add)
                    Pk = Pn
                    XT = X2T
                    if lvl < 5:
                        X = X2
                Pd = Pk
                PdTp = pq.next()
                tr(PdTp, Pd, identf)
                PdT = sq_r.next()
                cp(PdT, PdTp, eng="act")
                t1p = pq.next()
                mm(t1p, OffT, Pd)
                t1 = sq_r.next()
                cp(t1, t1p, eng="dve")
                w2p = pq.next()
                mm(w2p, PdT, t1)
                W = sq_r.next()
                tt(W, Pd, w2p, ALU.subtract)
                if STAGE < 6:
                    return None
                ktp, vtp = pq.next(), pq.next()
                tr(ktp, kT, identf)
                tr(vtp, vT, identf)
                ke, kd, vtm = sq_r.next(), sq_r.next(), sq_r.next()
                ts(ke, ktp, ecg, ALU.mult)
                ts(kd, ktp, edl, ALU.mult)
                cp(vtm, vtp, eng="act")
                u0p, w0Tp = pq.next(), pq.next()
                mm(u0p, W, vtm)
                mm(w0Tp, ke, W)
                u0b, w0T = sq_r.next(), sq_r.next()
                ts(u0b, u0p, bcol, ALU.mult)
                cp(w0T, w0Tp, eng="act")
                if STAGE < 7:
                    return None
                wSp = pq.next()
                mm(wSp, w0T, Sin)
                vn = sq_r.next()
                stt(vn, wSp, nbcol, u0b, ALU.mult, ALU.add)
                qSp, Avp = pq.next(), pq.next()
                mm(qSp, qT, Sin)
                mm(Avp, AT, vn)
                Av, o = sq_r.next(), sq_r.next()
                cp(Av, Avp, eng="act")
                stt(o, qSp, ecg, Av, ALU.mult, ALU.add)
                KVp = pq.next()
                mm(KVp, kd, vn)
                stt(Sout, Sin, egl, KVp, ALU.mult, ALU.add)
                if STAGE < 8:
                    return None
                zp = pq.next()
                for k in range(8):
                    mm(zp, hTc[:, k, :], wgd[:, k, 1536 + h * 128:1536 + (h + 1) * 128], k == 0, k == 7)
                osq, ms = sq_r.next(), col_r.next()
                tt(osq, o, o, ALU.mult, eng="pool")
                rsum(ms, osq)
                act(ms, ms, AF.Ln, bias=epsc, scale=1.0 / 128)
                act(ms, ms, AF.Exp, scale=-0.5)
                ez, sz, og = sq_r.next(), sq_r.next(), sq_r.next()
                act(ez, zp, AF.Exp, scale=-1.0)
                ts(ez, ez, 1.0, ALU.add, eng="pool")
                recip(ez, ez)
                tt(sz, zp, ez, ALU.mult)
                stt(og, o, ms, gnw, ALU.mult, ALU.mult)
                ogb = ogb_r.next()
                tt(ogb, og, sz, ALU.mult)
                return ogb

            for g in range(int(os.environ.get('GDN_G', '16'))):
                hTg = hTg_r.next()
                for t in range(4):
                    P.dma("sp", hTg[:, :, t * 128:(t + 1) * 128], T(hT_d[4 * g + t].rearrange("p (k n) -> p k n", k=8), None), nowaw=True)
                SUB = int(os.environ.get('GDN_SUB', '9'))
                for c in range(4 if SUB >= 1 else 0):
                    abp = pq.next()
                    for k in range(8):
                        mm(abp[:, 0:8], hTg[:, k, c * 128:(c + 1) * 128], wgd[:, k, 2048:2056], k == 0, k == 7)
                    if os.environ.get('NOCP') != '1':
                        cp(abg[:, c, :], abp[:, 0:8], eng=os.environ.get("CPENG", "act"))
                xa, gg, eb, beta, nbeta = (gb_r.next() for _ in range(5))
                if SUB >= 2:
                    tt(xa, abg[:, :, 0:4], dtb4, ALU.add)
                if SUB >= 3:
                    act(xa, xa, AF.Exp)
                    act(xa, xa, AF.Ln, bias=onec, scale=1.0)
                if SUB >= 4:
                    tt(gg, xa, negA4, ALU.mult)
                    act(eb, abg[:, :, 4:8], AF.Exp, scale=-1.0)
                    ts(eb, eb, 1.0, ALU.add)
                if SUB >= 5:
                    recip(beta, eb)
                    ts(nbeta, beta, -1.0, ALU.mult)
                for h in range(int(os.environ.get('GDN_H', '4')) if STAGE >= 2 else 0):
                    for part in range(3):
                        for k in range(8):
                            mm(pu[:, part, :], wgd[:, k, part * 512 + h * 128:part * 512 + (h + 1) * 128], hTg[:, k, :], k == 0, k == 7)
                    ub = ub_r.next()
                    cp(ub[:, :, 0:3], hist[h], eng="pool")
                    cp(ub[:, :, 3:515], pu, eng="act")
                    cp(hist[h], ub[:, :, 512:515], eng="pool")
                    y = big_r.next()
                    for part in range(3):
                        ci = part * 4 + h
                        ts(y[:, part, :], ub[:, part, 0:512], cw[:, ci, 0:1], ALU.mult)
                        for w in range(1, 4):
                            stt(y[:, part, :], ub[:, part, w:w + 512], cw[:, ci, w:w + 1], y[:, part, :], ALU.mult, ALU.add)
                    e = big_r.next()
                    act(e, y, AF.Exp, scale=-1.0)
                    ts(e, e, 1.0, ALU.add, eng="pool")
                    recip(e, e)
                    sl = big_r.next()
                    tt(sl, y, e, ALU.mult, eng="pool")
                    sqq = e
                    tt(sqq[:, 0:2, :], sl[:, 0:2, :], sl[:, 0:2, :], ALU.mult, eng="pool")
                    qn, kn = nrm_r.next(), nrm_r.next()
                    for part, dst in ((0, qn), (1, kn)):
                        mm(pss, ones, sqq[:, part, :])
                        rs = nrm_r.next()
                        act(rs, pss, AF.Ln, bias=l2e, scale=1.0)
                        act(rs, rs, AF.Exp, scale=-0.5)
                        if part == 0:
                            stt(dst, sl[:, 0, :], 128.0 ** -0.5, rs, ALU.mult, ALU.mult)
                        else:
                            tt(dst, sl[:, 1, :], rs, ALU.mult)
                    for c in range(4 if STAGE >= 3 else 0):
                        cs = slice(c * 128, (c + 1) * 128)
                        nchunk = 4 * g + c
                        Sin, Sout = S[h][nchunk % 2], S[h][(nchunk + 1) % 2]
                        ogb = gdn_chunk(h, qn[:, cs], kn[:, cs], sl[:, 2, cs], gg[:, c, h:h + 1], beta[:, c, h:h + 1], nbeta[:, c, h:h + 1],
                                        hTg[:, :, cs], None, None, Sin, Sout)
                        if ogb is None:
                            continue
                        oTp = ptrb.next()
                        tr(oTp, ogb, identb)
                        dst = oBown[:, h, (g // 4) * 512 + c * 128:(g // 4) * 512 + (c + 1) * 128]
                        stt(dst, oTp, sel_t[:, (g % 4):(g % 4) + 1], dst, ALU.mult, ALU.add)
            for h in range(4):
                P.dma("sp", T(nsp[h], None), S[h][(NT_FULL) % 2], store=True)

            def flat(t, rows, c0, c1):
                return T(t.ap.rearrange("p a b -> p (a b)")[:rows, c0:c1], t.res)

            def bview(t, pat, **kw):
                return T(t.ap.rearrange(pat, **kw), t.res)

            n = NS_TOK
            CS_UINCL, CS_BIGU, CS_NBIGL, CS_SU, CS_BD = (cst[:n, 7 + i, :n] for i in range(5))
            RM = cst[:n, 12, 0:16]
            idn = identf[:n, :n]
            usb, scb, zb = big_r.next(), big_r.next(), big_r.next()
            us = flat(usb, n, 0, 1536)
            scs = flat(scb, 48, 0, 1536)
            zs = flat(zb, n, 0, 512)
            abs_ = flat(zb, n, 512, 520)
            P.dma("sp", us, T(ps_d[:, 1536:3072], None))
            P.dma("sp", scs, T(sc_d, None))
            P.dma("sp", zs, T(ps_d[:, 3072:3584], None))
            P.dma("sp", abs_, T(ps_d[:, 3584:3592], None), nowaw=True)
            uext = sbt(st, "uext", [128, 12, 16, 7], F32)
            ys = sbt(st, "ys", [128, 12, 64], F32)
            es = sbt(st, "es", [128, 12, 64], F32)
            s_all = es
            for cc in range(12):
                ptq = pq.next()
                tr(ptq[:, 0:64], us[:, cc * 128:(cc + 1) * 128], idn)
                tr(ptq[:, 64:112], scs[:, cc * 128:(cc + 1) * 128], identf[:48, :48])
                cp(uext[:, cc, :, 3:7], T(ptq.ap[:, 0:64].rearrange("p (s t) -> p s t", t=4), ptq.res), eng="act")
                cp(uext[:, cc, :, 0:3], T(ptq.ap[:, 64:112].rearrange("p (s t) -> p s t", t=3), ptq.res), eng="dve")
                yv = T(ys.ap[:, cc, :].rearrange("p (s t) -> p s t", t=4), ys.res)
                ts(yv, uext[:, cc, :, 0:4], cw[:, cc, 0:1], ALU.mult)
                for w in range(1, 4):
                    stt(yv, uext[:, cc, :, w:w + 4], cw[:, cc, w:w + 1], yv, ALU.mult, ALU.add)
            act(es, ys, AF.Exp, scale=-1.0)
            ts(es, es, 1.0, ALU.add, eng="pool")
            recip(es, es)
            tt(s_all, ys, es, ALU.mult, eng="pool")
            xas, ggs, ebs, betas = (gb_r.next() for _ in range(4))
            xa2, gg2, eb2, be2 = (T(t.ap.rearrange("p a b -> p (a b)")[:n, 0:4], t.res) for t in (xas, ggs, ebs, betas))
            tt(xa2, abs_[:, 0:4], dtb[:n], ALU.add)
            act(xa2, xa2, AF.Exp)
            act(xa2, xa2, AF.Ln, bias=onec[:n], scale=1.0)
            tt(gg2, xa2, negA[:n], ALU.mult)
            act(eb2, abs_[:, 4:8], AF.Exp, scale=-1.0)
            ts(eb2, eb2, 1.0, ALU.add)
            recip(be2, eb2)
            W3 = sbt(st, "W3", [n, 16 * n], F32)
            I3 = sbt(st, "I3", [n, 16 * n], F32)
            memset(W3, 0.0)
            memset(I3, 0.0)

            def diagv(t):
                a = t.ap
                return T(bass.AP(tensor=a.tensor, offset=a.offset, ap=[list(a.ap[0]), [n + 4, 16], [1, 4]]), t.res)
            cp(diagv(I3), T(idn.ap.rearrange("p (s t) -> p s t", t=4), idn.res), eng="pool")
            w0Tm = sbt(st, "w0Tm", [128, 16 * n], F32)
            qem = sbt(st, "qem", [128, 16 * n], F32)
            Sall = sbt(st, "Sall", [128, 16, 128], F32)
            Snew = Sall
            EGL = sbt(st, "EGL", [128, 16], F32)
            for h in range(4):
                P.dma("sp", Sall, T(ssm_d[:, h].rearrange("s p d -> p s d"), None))
                qs_, ks_, vs_ = s_all[:, h, :], s_all[:, 4 + h, :], s_all[:, 8 + h, :]
                sqv = nrm_r.next()
                tt(sqv[:, 0:n], qs_, qs_, ALU.mult, eng="pool")
                tt(sqv[:, n:2 * n], ks_, ks_, ALU.mult, eng="pool")
                mm(pss[:, 0:2 * n], ones, sqv[:, 0:2 * n])
                rsv = nrm_r.next()
                act(rsv[:, 0:2 * n], pss[:, 0:2 * n], AF.Ln, bias=l2e, scale=1.0)
                act(rsv[:, 0:2 * n], rsv[:, 0:2 * n], AF.Exp, scale=-0.5)
                qkn = nrm_r.next()
                qn, kn = qkn[:, 0:n], qkn[:, n:2 * n]
                stt(qn, qs_, 128.0 ** -0.5, rsv[:, 0:n], ALU.mult, ALU.mult)
                tt(kn, ks_, rsv[:, n:2 * n], ALU.mult)
                gcol, bcol = gg2[:, h:h + 1], be2[:, h:h + 1]
                G = sq_r.next()[:n, :n]
                ts(G, CS_UINCL, gcol, ALU.mult)
                PA, PB, Pm, pm1 = pq.next(), pq.next(), pq.next(), pq.next()
                mm(PA[:n, :n], ones[:n, :n], G, True, False)
                mm(PA[:n, :n], idn, CS_BIGU, False, True)
                mm(PB[:n, :n], ones[:n, :n], G, True, False)
                mm(PB[:n, :n], idn, CS_NBIGL, False, True)
                mm(Pm[:, :n], ones[:n, :], G)
                mm(pm1[:n, 0:1], CS_BD, gcol)
                mm(pm1[:n, 1:2], CS_UINCL, gcol)
                cg, ncg, ecg, edl = (col_r.next()[:n] for _ in range(4))
                cp(cg, pm1[:n, 1:2], eng="act")
                ts(ncg, pm1[:n, 1:2], -1.0, ALU.mult)
                decay, decayT = sq_r.next()[:n, :n], sq_r.next()[:n, :n]
                act(decay, PA[:n, :n], AF.Exp, bias=cg, scale=-1.0)
                act(decayT, PB[:n, :n], AF.Exp, bias=ncg, scale=1.0)
                act(ecg, cg, AF.Exp)
                act(edl, pm1[:n, 0:1], AF.Exp, bias=ncg, scale=1.0)
                act(EGL, T(Pm.ap[:, 3:n:4], Pm.res), AF.Exp)
                KK, QKT = pq.next(), pq.next()
                mm(KK[:n, :n], kn, kn)
                mm(QKT[:n, :n], kn, qn)
                t0, Wm, AT = sq_r.next()[:n, :n], sq_r.next()[:n, :n], sq_r.next()[:n, :n]
                tt(t0, KK[:n, :n], decayT, ALU.mult)
                stt(Wm, t0, bcol, CS_SU, ALU.mult, ALU.mult)
                tt(AT, QKT[:n, :n], decayT, ALU.mult)
                WmTp = pq.next()
                tr(WmTp[:n, :n], Wm, idn)
                WmT = sq_r.next()[:n, :n]
                cp(WmT, WmTp[:n, :n], eng="act")
                Pk = sq_r.next()[:n, :n]
                tt(Pk, idn, Wm, ALU.subtract)
                p2 = pq.next()
                mm(p2[:n, :n], Wm, WmT)
                X2T = sq_r.next()[:n, :n]
                cp(X2T, p2[:n, :n], eng="dve")
                p3 = pq.next()
                mm(p3[:n, :n], X2T, Pk)
                W = sq_r.next()[:n, :n]
                tt(W, p3[:n, :n], Pk, ALU.add)
                dgb = sq_r.next()[:n, :n]
                ts(dgb, idn, bcol, ALU.mult)
                pbr = pq.next()
                mm(pbr[:n, :n], ones[:n, :n], dgb)
                Wb = sq_r.next()[:n, :n]
                tt(Wb, pbr[:n, :n], W, ALU.mult)
                cp(diagv(W3), T(Wb.ap.rearrange("p (s t) -> p s t", t=4), Wb.res), eng="pool")
                ktp, vtp, qtp = pq.next(), pq.next(), pq.next()
                tr(ktp[:n, :], kn, identf)
                tr(vtp[:n, :], vs_, identf)
                tr(qtp[:n, :], qn, identf)
                ke, kd, vtm, qetm = sq_r.next()[:n], sq_r.next()[:n], sq_r.next()[:n], sq_r.next()[:n]
                ts(ke, ktp[:n, :], ecg, ALU.mult)
                ts(kd, ktp[:n, :], edl, ALU.mult)
                cp(vtm, vtp[:n, :], eng="act")
                ts(qetm, qtp[:n, :], ecg, ALU.mult)
                u0p = pq.next()
                mm(u0p[:n, :], Wb, vtm)
                u0b = sq_r.next()[:n]
                cp(u0b, u0p[:n, :], eng="act")
                for half in range(2):
                    mm(pu[:, half, :], ke, W3[:, half * 512:(half + 1) * 512])
                cp(w0Tm, T(pu.ap[:, 0:2, :].rearrange("p a b -> p (a b)"), pu.res), eng="act")
                for half in range(2):
                    mm(pu[:, half, :], qetm, I3[:, half * 512:(half + 1) * 512])
                cp(qem, T(pu.ap[:, 0:2, :].rearrange("p a b -> p (a b)"), pu.res), eng="dve")
                pw = pq.next()
                for sq_i in range(16):
                    mm(pw[:n, :], w0Tm[:, sq_i * n:(sq_i + 1) * n], Sall[:, sq_i, :], sq_i == 0, sq_i == 15)
                vn = sq_r.next()[:n]
                tt(vn, u0b, pw[:n, :], ALU.subtract)
                po = pq.next()
                for sq_i in range(16):
                    mm(po[:n, :], qem[:, sq_i * n:(sq_i + 1) * n], Sall[:, sq_i, :], sq_i == 0, False)
                mm(po[:n, :], AT, vn, False, True)
                o = sq_r.next()[:n]
                cp(o, po[:n, :], eng="act")
                for sq_i in range(16):
                    kdm = sq_r.next()[:n]
                    ts(kdm, kd, RM[:, sq_i:sq_i + 1], ALU.mult, eng="pool")
                    pkv = pq.next()
                    mm(pkv, kdm, vn)
                    stt(Snew[:, sq_i, :], Sall[:, sq_i, :], EGL[:, sq_i:sq_i + 1], pkv, ALU.mult, ALU.add)
                P.dma("sp", T(nss[:, h].rearrange("s p d -> p s d"), None), Snew, store=True)
                osq, ms = sq_r.next()[:n], col_r.next()[:n]
                tt(osq, o, o, ALU.mult, eng="pool")
                rsum(ms, osq)
                act(ms, ms, AF.Ln, bias=epsc[:n], scale=1.0 / 128)
                act(ms, ms, AF.Exp, scale=-0.5)
                zh = zs[:, h * 128:(h + 1) * 128]
                ez, sz, og = sq_r.next()[:n], sq_r.next()[:n], sq_r.next()[:n]
                act(ez, zh, AF.Exp, scale=-1.0)
                ts(ez, ez, 1.0, ALU.add, eng="pool")
                recip(ez, ez)
                tt(sz, zh, ez, ALU.mult)
                stt(og, o, ms, gnw[:n], ALU.mult, ALU.mult)
                ogb = ogb_r.next()[:n]
                tt(ogb, og, sz, ALU.mult)
                oTp = ptrb.next()
                tr(oTp[:, :n], ogb, identb[:n, :n])
                cp(oBs[:, h, :], oTp[:, :n], eng="act")
            P.emit()

        with ExitStack() as st:
            tl = {"wst": ring(st, sbt, "wst", [128, 512], F32, 4)}
            NPAIR = int(os.environ.get('ATT_PAIRS', '4'))
            cst6 = sbt(st, "cst6", [128, 128], F32)
            P.dma("sp", cst6, T(cst_d[:, 6, :], None))
            trilb = sbt(st, "trilb", [128, 128], BF16)
            onesb = sbt(st, "onesb", [128, 128], BF16)
            cp(trilb, cst6)
            memset(onesb, 1.0)
            sbb = sbt(st, "sbb", [128, 8], F32)
            P.dma("sp", sbb, T(sbb_d, None))
            maskt = sbt(st, "maskt", [128, 16, 512], BF16)
            P.dma("sp", maskt, T(mask_d, None))
            KT = sbt(st, "KT", [128, SEQ], BF16)
            Vt = sbt(st, "Vt", [128, NT_FULL, 128], BF16)
            qT = sbt(st, "qT", [128, OWN], BF16)
            wqkv = sbt(st, "wqkv", [128, 8, 384], BF16)
            hTg_r = ring(st, sbt, "hTg3", [128, 8, 512], BF16, 2)
            pS = ring(st, pst, "pS", [128, 512], F32, 3)
            pproj = Ring([pS.tiles[0]])
            pR_r = ring(st, pst, "pR", [128, 512], F32, 2)
            pT_r = ring(st, pst, "pT", [128, 512], F32, 2)
            pO = pst(st, "pO", [128, 512], F32)
            e_r = ring(st, sbt, "e3", [128, 512], F32, 4)
            sp_r = ring(st, sbt, "sp3", [128, 512], BF16, 3)
            Rt_r = ring(st, sbt, "Rt3", [128, 512], F32, 2)
            ex_r = ring(st, sbt, "ex3", [128, 512], F32, 2)
            w_r = ring(st, sbt, "w3", [128, 512], BF16, 2)
            carry = [sbt(st, f"carry{i}", [128, 512], F32) for i in range(2)]
            for p in range(NPAIR):
                load_w(tl, wqkv[:, :, 0:128], w_in, p * 128, 128, scale=nmw_t)
                load_w(tl, wqkv[:, :, 128:256], w_in, 512 + p * 128, 128, scale=nmw_t)
                load_w(tl, wqkv[:, :, 256:384], w_in, 1024 + p * 128, 128, scale=nmw_t)
                for g in range(16):
                    hTg = hTg_r.next()
                    for t in range(4):
                        P.dma("sp", hTg[:, :, t * 128:(t + 1) * 128], T(hT_d[4 * g + t].rearrange("p (k n) -> p k n", k=8), None), nowaw=True)
                    pk = pproj.next()
                    for k in range(8):
                        mm(pk, wqkv[:, k, 128:256], hTg[:, k, :], k == 0, k == 7)
                    cp(KT[:, g * 512:(g + 1) * 512], pk, eng="act")
                    pv = pproj.next()
                    for t in range(4):
                        for k in range(8):
                            mm(pv[:, t * 128:(t + 1) * 128], hTg[:, k, t * 128:(t + 1) * 128], wqkv[:, k, 256:384], k == 0, k == 7)
                    cp(Vt[:, 4 * g:4 * g + 4, :], T(pv.ap.rearrange("p (t c) -> p t c", t=4), pv.res), eng="dve")
                for i in range(4):
                    hTg = hTg_r.next()
                    for t in range(4):
                        P.dma("sp", hTg[:, :, t * 128:(t + 1) * 128], T(hTown_d[4 * i + t].rearrange("p (k n) -> p k n", k=8), None), nowaw=True)
                    pq_ = pproj.next()
                    for k in range(8):
                        mm(pq_, wqkv[:, k, 0:128], hTg[:, k, :], k == 0, k == 7)
                    cp(qT[:, i * 512:(i + 1) * 512], pq_, eng="act")
                for hh in range(2):
                    h = 2 * p + hh
                    hs = slice(64 * hh, 64 * hh + 64)
                    for i in range(int(os.environ.get('ATT_SLOTS', '4'))):
                        nkb = 16 * i + 16
                        cur = 0
                        memset(carry[0], 0.0)
                        def front(KB):
                            Sp = pS.next()
                            masked = KB >= 16 * i
                            mm(Sp, KT[hs, KB * 128:(KB + 1) * 128], qT[hs, i * 512:(i + 1) * 512], True, not masked)
                            if masked:
                                mm(Sp, identb, maskt[:, KB - 16 * i, :], False, True)
                            e = e_r.next()
                            act(e, Sp, AF.Exp, bias=sbb[:, h:h + 1], scale=0.125)
                            sp = sp_r.next()
                            act(sp, e, AF.Ln, bias=onec, scale=1.0)
                            return (KB, e, sp)

                        def midstage(stg):
                            KB, e, sp = stg
                            pR, pT = pR_r.next(), pT_r.next()
                            mm(pR, trilb, sp)
                            mm(pT, onesb, sp)
                            return (KB, e, pR, pT)

                        def back(stg, idx, cur):
                            KB, e, pR, pT = stg
                            Rt = Rt_r.next()
                            tt(Rt, pR, carry[cur], ALU.add)
                            tt(carry[1 - cur], pT, carry[cur], ALU.add)
                            ex = ex_r.next()
                            act(ex, Rt, AF.Exp, scale=-1.0)
                            w = w_r.next()
                            tt(w, e, ex, ALU.mult)
                            mm(pO, Vt[:, KB, :], w, idx == 0, idx == nkb - 1)

                        order = list(range(nkb - 1, -1, -1))
                        fq, mq = [], []
                        done = 0
                        for KB in order:
                            fq.append(front(KB))
                            if len(fq) >= 2:
                                mq.append(midstage(fq.pop(0)))
                            if len(mq) >= 2:
                                back(mq.pop(0), done, cur)
                                cur = 1 - cur
                                done += 1
                        while fq:
                            mq.append(midstage(fq.pop(0)))
                            if len(mq) >= 2:
                                back(mq.pop(0), done, cur)
                                cur = 1 - cur
                                done += 1
                        while mq:
                            back(mq.pop(0), done, cur)
                            cur = 1 - cur
                            done += 1
                        cp(oAown[hs, p, i * 512:(i + 1) * 512], pO[hs, :], eng="act")
            P.emit()


        with ExitStack() as st:
            NSEQ = int(os.environ.get('SATT_SEQS', '16'))
            SST = int(os.environ.get('SATT_STAGE', '9'))
            n = NS_TOK
            cst6 = sbt(st, "cst6s", [128, 128], F32)
            P.dma("sp", cst6, T(cst_d[:, 6, :], None))
            trilb = sbt(st, "trilbs", [128, 128], BF16)
            onesb = sbt(st, "onesbs", [128, 128], BF16)
            cp(trilb, cst6)
            memset(onesb, 1.0)
            ptt = sbt(st, "ptt", [128, 256], I32)
            idx = sbt(st, "idx", [128, 256], I32)
            iot = sbt(st, "iot", [128, 256], I32)
            P.dma("sp", ptt, T(pt_d.partition_broadcast(128), None))
            P.dma("sp", iot, T(iota_d, None))
            P.op("dve", lambda e: e.tensor_scalar(out=idx.ap, in0=ptt.ap, scalar1=7, scalar2=None, op0=ALU.logical_shift_left), [ptt], [idx])
            P.op("dve", lambda e: e.tensor_tensor(out=idx.ap, in0=idx.ap, in1=iot.ap, op=ALU.bitwise_or), [idx, iot], [idx])
            sbrow = sbt(st, "sbrow", [128, 512], F32)
            P.dma("sp", sbrow, T(sbrow_d, None))
            mnew = sbt(st, "mnew", [128, 32], F32)
            P.dma("sp", mnew, T(mnew_d, None))
            selE = sbt(st, "selE", [64, 256], F32)
            P.dma("sp", selE, T(selE_d, None))
            dm = sbt(st, "dm", [32, 512], F32)
            P.dma("sp", dm, T(dm_d, None))
            qkv = sbt(st, "qkvs", [n, 1536], F32)
            P.dma("sp", qkv, T(ps_d[:, 0:1536], None))
            qsT = sbt(st, "qsT", [128, 4, n], BF16)
            ksT = sbt(st, "ksT", [128, 4, 256], BF16)
            memset(ksT, 0.0)
            ptk_r = ring(st, pst, "ptk", [128, 4, 128], F32, 2)
            pS_r = ring(st, pst, "pSs", [128, 512], F32, 2)
            pR = pst(st, "pRs", [128, 512], F32)
            pT = pst(st, "pTs", [128, 512], F32)
            pO = pst(st, "pOs", [128, 512], F32)
            psm = pst(st, "psm", [128, 512], F32)
            for p in range(4):
                ptk = ptk_r.next()
                tr(ptk[:, 0, :n], qkv[:, p * 128:(p + 1) * 128], identf[:n, :n])
                tr(ptk[:, 1, :n], qkv[:, 512 + p * 128:512 + (p + 1) * 128], identf[:n, :n])
                cp(qsT[:, p, :], ptk[:, 0, :n], eng="act")
                cp(ksT[:, p, 0:n], ptk[:, 1, :n], eng="dve")
            kpg_r = ring(st, sbt, "kpg", [128, 512], F32, 3)
            vpg_r = ring(st, sbt, "vpg", [128, 512], F32, 3)
            Vb_r = ring(st, sbt, "Vb", [128, 16, 512], BF16, 2)
            KTs_r = ring(st, sbt, "KTs", [128, 4, 128], BF16, 3)
            f_r = ring(st, sbt, "fs", [128, 512], F32, 5)
            b_r = ring(st, sbt, "bs", [128, 512], BF16, 3)
            C = sbt(st, "Cs", [128, 16, 32], F32)
            sm_r = ring(st, sbt, "sms", [128, 512], F32, 4)
            smb_r = ring(st, sbt, "smbs", [128, 512], BF16, 4)
            o2 = sbt(st, "o2s", [32, 2, 64], F32)
            for sq_i in range(NSEQ):
                Sp = pS_r.next()
                Vb = Vb_r.next()
                tsl = slice(4 * sq_i, 4 * sq_i + 4)
                for pg in range(16):
                    col = sq_i * 16 + pg
                    kpg, vpg = kpg_r.next(), vpg_r.next()
                    for dst, src in ((kpg, ck_d), (vpg, cv_d)):
                        def gfn(e, dst=dst, src=src, col=col):
                            return e.indirect_dma_start(out=dst.ap, out_offset=None, in_=src,
                                                        in_offset=bass.IndirectOffsetOnAxis(ap=idx.ap[:, col:col + 1], axis=0))
                        P._add("pool", gfn, [idx], [dst], dma=True, evres=dst.res)
                    cp(Vb[:, pg, :], vpg, eng=("dve", "act")[pg % 2])
                    if SST < 2:
                        tr(ptk_r.next()[:, 0, :], kpg[:, 0:128], identf)
                        continue
                    ptk = ptk_r.next()
                    for p in range(4):
                        tr(ptk[:, p, :], kpg[:, p * 128:(p + 1) * 128], identf)
                    KTs = KTs_r.next()
                    cp(KTs, ptk, eng=("act", "dve")[pg % 2])
                    if SST < 3:
                        continue
                    for p in range(4):
                        for hh in range(2):
                            hs = slice(64 * hh, 64 * hh + 64)
                            c0 = pg * 32 + (2 * p + hh) * 4
                            mm(Sp[:, c0:c0 + 4], KTs[hs, p, :], qsT[hs, p, tsl])
                if SST < 4:
                    continue
                for p in range(4):
                    for hh in range(2):
                        hs = slice(64 * hh, 64 * hh + 64)
                        c0 = (2 * p + hh) * 4
                        mm(psm[:, c0:c0 + 4], ksT[hs, p, 4 * sq_i:4 * sq_i + 128], qsT[hs, p, tsl])
                z, e = f_r.next(), f_r.next()
                stt(z, Sp, 0.125, sbrow, ALU.mult, ALU.add)
                act(e, z, AF.Exp)
                sp = b_r.next()
                act(sp, e, AF.Ln, bias=onec, scale=1.0)
                zn, en = sm_r.next(), sm_r.next()
                stt(zn[:, 0:32], psm[:, 0:32], 0.125, sbrow[:, 0:32], ALU.mult, ALU.add)
                act(en[:, 0:32], zn[:, 0:32], AF.Exp)
                tt(en[:, 0:32], en[:, 0:32], mnew, ALU.mult)
                spn = smb_r.next()
                act(spn[:, 0:32], en[:, 0:32], AF.Ln, bias=onec, scale=1.0)
                mm(pR, trilb, sp)
                mm(pT, onesb, sp)
                mm(psm[:, 64:96], onesb, spn[:, 0:32])
                mm(psm[:, 128:160], trilb, spn[:, 0:32])
                cp(C[:, 15, :], psm[:, 64:96], eng="dve")
                for pg in range(14, -1, -1):
                    tt(C[:, pg, :], pT[:, (pg + 1) * 32:(pg + 2) * 32], C[:, pg + 1, :], ALU.add)
                R, ex = f_r.next(), f_r.next()
                tt(R, pR, T(C.ap.rearrange("p a b -> p (a b)"), C.res), ALU.add)
                act(ex, R, AF.Exp, scale=-1.0)
                w = b_r.next()
                tt(w, e, ex, ALU.mult, eng="pool")
                exn = sm_r.next()
                act(exn[:, 0:32], psm[:, 128:160], AF.Exp, scale=-1.0)
                wn = smb_r.next()
                tt(wn[:, 0:32], en[:, 0:32], exn[:, 0:32], ALU.mult)
                if SST < 5:
                    continue
                mm(pO, selE[:, 4 * sq_i:4 * sq_i + 128], qkv[:, 1024:1536])
                v4 = smb_r.next()
                cp(v4, pO, eng="act")
                for pg in range(16):
                    mm(pR[0:32, :], w[:, pg * 32:(pg + 1) * 32], Vb[:, pg, :], pg == 0, False)
                mm(pR[0:32, :], wn[:, 0:32], v4, False, True)
                od = sm_r.next()[0:32]
                tt(od, pR[0:32, :], dm, ALU.mult)
                rsum(o2[:, 0, :], T(od.ap.rearrange("p (h d) -> p d h", h=8), od.res))
                cp(o2[:, 1, :], o2[:, 0, :], eng="pool")
                tr(psm[:, 256:288], T(o2.ap.rearrange("p a b -> p (a b)"), o2.res), identf[:32, :32])
                for hh in range(2):
                    hs = slice(64 * hh, 64 * hh + 64)
                    src = T(psm.ap[hs, 256:288].rearrange("p (a b c) -> p a b c", a=4, b=2)[:, :, hh, :], psm.res)
                    cp(oAs[hs, :, tsl], src, eng=("act", "dve")[hh])
            P.emit()

        with ExitStack() as st:
            tl = {"wst": ring(st, sbt, "wst", [128, 512], F32, 4),
                  "sq": ring(st, sbt, "sq4", [128, D], F32, 1),
                  "ss": ring(st, sbt, "ss4", [128, 1], F32, 4),
                  "hb": ring(st, sbt, "hb4", [128, D], BF16, 2),
                  "ptr": ring(st, pst, "ptr4", [128, 8, 128], BF16, 1)}
            w2 = sbt(st, "w2", [128, 8, 2048], BF16)
            for gi in range(4):
                load_w(tl, w2[:, :, gi * 512:(gi + 1) * 512], w_in, 3592 + gi * 512, 512, scale=nmw_t)
            wpa = sbt(st, "wpa", [128, 4, D], BF16)
            wpb = sbt(st, "wpb", [128, 4, D], BF16)
            wo = sbt(st, "wo", [128, 8, D], BF16)
            for half in range(2):
                load_w(tl, wpa[:, :, half * 512:(half + 1) * 512], w_pa_d, half * 512, 512, nk=4)
                load_w(tl, wpb[:, :, half * 512:(half + 1) * 512], w_pb_d, half * 512, 512, nk=4)
                load_w(tl, wo[:, :, half * 512:(half + 1) * 512], w_o_d, half * 512, 512, nk=8)
            hTg_r = ring(st, sbt, "hTg4", [128, 8, 512], BF16, 2)
            pg = ring(st, pst, "pg4", [128, 512], F32, 5)
            pmix = ring(st, pst, "pmix4", [128, 512], F32, 2)
            sg_r = ring(st, sbt, "sg4", [128, 512], F32, 4)
            mm_r = ring(st, sbt, "mm4", [128, 512], F32, 4)
            mT_r = ring(st, sbt, "mT4", [128, 8, 512], BF16, 2)
            x_r = ring(st, sbt, "x4", [128, D], F32, 3)
            hT_r = ring(st, sbt, "hTt4", [128, 8, 128], BF16, 2)
            groups = []
            for G in range(4):
                groups.append(dict(n=512, hT_src=[hTown_d[4 * G + t] for t in range(4)], hT_sb=None,
                                   oA=oAown[:, :, G * 512:(G + 1) * 512], oB=oBown[:, :, G * 512:(G + 1) * 512],
                                   x_rows=[xown[(4 * G + t) * 128:(4 * G + t + 1) * 128, :] for t in range(4)],
                                   row0=G * 512, tile0=4 * G))
            groups.append(dict(n=NS_TOK, hT_src=[], hT_sb=hsT, oA=oAs, oB=oBs, x_rows=[xs[:, :]], row0=OWN, tile0=NT_OWN))
            for gd in groups:
                n = gd["n"]
                if gd["hT_sb"] is not None:
                    hTg = gd["hT_sb"]
                else:
                    hTg = hTg_r.next()
                for t, src in enumerate(gd["hT_src"]):
                    P.dma("sp", hTg[:, :, t * 128:(t + 1) * 128], T(src.rearrange("p (k n) -> p k n", k=8), None), nowaw=True)
                mT = mT_r.next()
                for c in range(8):
                    cs = slice(c * 128, (c + 1) * 128)
                    pgA, pgB, pyA, pyB = pg.next(), pg.next(), pg.next(), pg.next()
                    for k in range(8):
                        mm(pgA[:, :n], w2[:, k, cs], hTg[:, k, :n], k == 0, k == 7)
                    for k in range(8):
                        mm(pgB[:, :n], w2[:, k, 1024 + c * 128:1024 + (c + 1) * 128], hTg[:, k, :n], k == 0, k == 7)
                    for k in range(4):
                        mm(pyA[:, :n], wpa[:, k, cs], gd["oA"][:, k, :], k == 0, k == 3)
                    for k in range(4):
                        mm(pyB[:, :n], wpb[:, k, cs], gd["oB"][:, k, :], k == 0, k == 3)
                    sgA, sgB = sg_r.next(), sg_r.next()
                    for sg_, pg_ in ((sgA, pgA), (sgB, pgB)):
                        act(sg_[:, :n], pg_[:, :n], AF.Exp, scale=-1.0)
                        ts(sg_[:, :n], sg_[:, :n], 1.0, ALU.add, eng="pool")
                        recip(sg_[:, :n], sg_[:, :n])
                    mA, mB = mm_r.next(), mm_r.next()
                    tt(mA[:, :n], pyA[:, :n], sgA[:, :n], ALU.mult)
                    tt(mB[:, :n], pyB[:, :n], sgB[:, :n], ALU.mult)
                    tt(mT[:, c, :n], mA[:, :n], mB[:, :n], ALU.add, eng="pool")
                for t, xr in enumerate(gd["x_rows"]):
                    nt = min(128, n - t * 128)
                    xt = x_r.next()
                    P.dma("sp", xt[:nt], T(xr, None))
                    for half in range(2):
                        pm = pmix.next()
                        for c in range(8):
                            mm(pm[:nt, :], mT[:, c, t * 128:t * 128 + nt], wo[:, c, half * 512:(half + 1) * 512], c == 0, c == 7)
                        tt(xt[:nt, half * 512:(half + 1) * 512], pm[:nt, :], xt[:nt, half * 512:(half + 1) * 512], ALU.add)
                    r0 = gd["row0"] + t * 128
                    P.dma("sp", T(x1_d[r0:r0 + nt, :], None), xt[:nt], store=True)
                    hTt = hT_r.next()
                    norm_core(tl, xt, hTt[:, :, :nt], nt)
                    P.dma("sp", T(hmT_d[gd["tile0"] + t].rearrange("p (k n) -> p k n", k=8)[:, :, :nt], None), hTt[:, :, :nt], store=True)
            P.emit()

        mid.close()
        with ExitStack() as st:
            tl = {"wst": ring(st, sbt, "wst", [128, 512], F32, 4)}
            nmlw = sbt(st, "nmlw", [128, 8], F32)
            P.dma("sp", nmlw, T(nmlw_d, None))
            nfw = sbt(st, "nfw", [128, D], F32)
            P.dma("sp", nfw, T(nfw_d, None))
            wup = sbt(st, "wup", [128, 8, 4 * D], BF16)
            wdn = sbt(st, "wdn", [128, 32, D], BF16)
            for gi in range(8):
                load_w(tl, wup[:, :, gi * 512:(gi + 1) * 512], w_up_d, gi * 512, 512, scale=nmlw)
            for half in range(2):
                load_w(tl, wdn[:, :, half * 512:(half + 1) * 512], w_down_d, half * 512, 512, nk=32)
            actT = sbt(st, "actT", [128, 32, 256], BF16)
            hm_r = ring(st, sbt, "hm5", [128, 8, 256], BF16, 2)
            pu5 = ring(st, pst, "pu5", [128, 512], F32, 3)
            pd5 = ring(st, pst, "pd5", [128, 512], F32, 2)
            tmp_r = ring(st, sbt, "tmp5", [128, 256], F32, 2)
            x_r = ring(st, sbt, "x5", [128, D], F32, 2)
            sq5 = sbt(st, "sq5", [128, D], F32)
            ss_r = ring(st, sbt, "ss5", [128, 1], F32, 4)
            y_r = ring(st, sbt, "y5", [128, D], F32, 2)
            subs = []
            for sg in range(8):
                subs.append(dict(n=256, tiles=[2 * sg, 2 * sg + 1], row0=sg * 256, out=y_own))
            subs.append(dict(n=NS_TOK, tiles=[NT_OWN], row0=OWN, out=y_s))
            for sd in subs:
                n = sd["n"]
                hm = hm_r.next()
                for t, tile in enumerate(sd["tiles"]):
                    nt = min(128, n - t * 128)
                    P.dma("sp", hm[:, :, t * 128:t * 128 + nt], T(hmT_d[tile].rearrange("p (k n) -> p k n", k=8)[:, :, :nt], None), nowaw=True)
                for f in range(32):
                    pu = pu5.next()
                    for k in range(8):
                        mm(pu[:, :n], wup[:, k, f * 128:(f + 1) * 128], hm[:, k, :n], k == 0, k == 7)
                    tmp = tmp_r.next()
                    ts(tmp[:, :n], pu[:, :n], 0.0, ALU.max)
                    tt(actT[:, f, :n], tmp[:, :n], tmp[:, :n], ALU.mult, eng="pool")
                for t in range(len(sd["tiles"])):
                    nt = min(128, n - t * 128)
                    r0 = sd["row0"] + t * 128
                    xt = x_r.next()
                    P.dma("sp", xt[:nt], T(x1_d[r0:r0 + nt, :], None))
                    for half in range(2):
                        pd = pd5.next()
                        for f in range(32):
                            mm(pd[:nt, :], actT[:, f, t * 128:t * 128 + nt], wdn[:, f, half * 512:(half + 1) * 512], f == 0, f == 31)
                        tt(xt[:nt, half * 512:(half + 1) * 512], pd[:nt, :], xt[:nt, half * 512:(half + 1) * 512], ALU.add)
                    tt(sq5[:nt], xt[:nt], xt[:nt], ALU.mult, eng="pool")
                    ss = ss_r.next()
                    rsum(ss[:nt], sq5[:nt])
                    act(ss[:nt], ss[:nt], AF.Ln, bias=epsc[:nt], scale=1.0 / D)
                    act(ss[:nt], ss[:nt], AF.Exp, scale=-0.5)
                    y = y_r.next()
                    stt(y[:nt], xt[:nt], ss[:nt, 0:1], nfw[:nt], ALU.mult, ALU.mult)
                    ro = sd["row0"] - (0 if sd["out"] is y_own else OWN) + t * 128
                    P.dma("sp", T(sd["out"][ro:ro + nt, :], None), y[:nt], store=True)
            P.emit()
    return nc


_NC = None


def kernel(x_prompt, x_sample, cache_k, cache_v, page_table, state_conv, state_ssm,
           norm_mix_w, w_in, sb_bias, conv_w, a_log, dt_bias, gdn_norm_w, w_pa, w_pb, w_o,
           norm_mlp_w, w_up, w_down, norm_final_w):
    global _NC
    f32 = np.float32
    x_prompt = np.asarray(x_prompt, f32)
    x_sample = np.asarray(x_sample, f32)
    nc = build()
    ident = np.eye(128, dtype=f32)
    nmw = np.ascontiguousarray(np.asarray(norm_mix_w, f32)[0].reshape(8, 128).T)
    w_in0 = np.ascontiguousarray(np.asarray(w_in, f32)[0])
    cst = np.zeros((128, NCST, 128), f32)
    ii = np.arange(128)
    r_, c_ = ii[:, None], ii[None, :]
    cst[:, 0] = (r_ <= c_)
    cst[:, 1] = BIG * (c_ > r_)
    cst[:, 2] = -BIG * (r_ > c_)
    cst[:, 3] = (r_ < c_)
    cst[:, 4] = ((r_ // 64) == (c_ // 64))
    cst[:, 5] = 1.0 - cst[:, 4]
    cst[:, 6] = (r_ >= c_)
    same = ((r_ // 4) == (c_ // 4))
    cst[:, 7] = (r_ <= c_) & same
    cst[:, 8] = BIG * ((c_ > r_) | ~same)
    cst[:, 9] = -BIG * ((r_ > c_) | ~same)
    cst[:, 10] = (r_ < c_) & same
    cst[:, 11] = same
    cst[:, 12] = ((r_ // 4) == c_)
    state_conv = np.asarray(state_conv, f32)
    ck = np.asarray(cache_k, f32).reshape(2560 * 128, 512)
    cv = np.asarray(cache_v, f32).reshape(2560 * 128, 512)
    page_table = np.asarray(page_table, np.int32)
    if os.environ.get('NPOOL_DBG'):
        npd = int(os.environ['NPOOL_DBG'])
        ck, cv, page_table = ck[:npd * 128], cv[:npd * 128], page_table % npd
    iota = np.ascontiguousarray(np.tile(np.arange(128, dtype=np.int32)[:, None], (1, 256)))
    sbrow = np.ascontiguousarray(np.tile(np.repeat(np.asarray(sb_bias, f32)[0], 4)[None, :], (128, 16)))
    mnew = np.zeros((128, 32), f32)
    selE = np.zeros((64, 256), f32)
    selE[np.arange(64), np.arange(64)] = 1.0
    for t1 in range(4):
        for hq in range(32):
            mnew[t1, hq] = 1.0 if t1 < (hq % 4) else 0.0
    dmm = np.zeros((32, 8, 64), f32)
    for hq in range(32):
        dmm[hq, hq // 4, :] = 1.0
    dmm = dmm.reshape(32, 512)
    state_ssm = np.asarray(state_ssm, f32)
    sbb = np.ascontiguousarray(np.tile(np.asarray(sb_bias, f32)[0][None, :], (128, 1)))
    masks = []
    for jj in range(4):
        m = np.zeros((128, 16, 512), f32)
        for r in range(4):
            for kb in range(4):
                kbrel = 4 * r + kb
                for qb in range(4):
                    qbrel = 4 * jj + qb
                    if kbrel < qbrel:
                        m[:, r * 4 + kb, qb * 128:(qb + 1) * 128] = 1.0
                    elif kbrel == qbrel:
                        m[:, r * 4 + kb, qb * 128:(qb + 1) * 128] = (r_ < c_)
        masks.append(((m - 1.0) * 30000.0).astype(ml_dtypes.bfloat16))
    cwl = np.ascontiguousarray(np.asarray(conv_w, f32)[0].reshape(4, 3, 4, 128).transpose(3, 1, 2, 0).reshape(128, 12, 4))
    alog_bc = np.ascontiguousarray(np.tile(np.asarray(a_log, f32)[0][None, :], (128, 1)))
    dtb_bc = np.ascontiguousarray(np.tile(np.asarray(dt_bias, f32)[0][None, :], (128, 1)))
    gnw_bc = np.ascontiguousarray(np.tile(np.asarray(gdn_norm_w, f32)[0][None, :], (128, 1)))
    nmlw = np.ascontiguousarray(np.asarray(norm_mlp_w, f32)[0].reshape(8, 128).T)
    nfw_bc = np.ascontiguousarray(np.tile(np.asarray(norm_final_w, f32)[None, :], (128, 1)))
    w_pa0 = np.ascontiguousarray(np.asarray(w_pa, f32)[0])
    w_pb0 = np.ascontiguousarray(np.asarray(w_pb, f32)[0])
    w_o0 = np.ascontiguousarray(np.asarray(w_o, f32)[0])
    w_up0 = np.ascontiguousarray(np.asarray(w_up, f32)[0])
    w_down0 = np.ascontiguousarray(np.asarray(w_down, f32)[0])
    in_maps = []
    own_idx = []
    for c in range(NCORE):
        b, j = c // 4, c % 4
        groups = [4 * i + j for i in range(4)]
        idx = np.concatenate([np.arange(512 * g, 512 * g + 512) for g in groups])
        own_idx.append(idx)
        in_maps.append({
            "xfull": np.ascontiguousarray(x_prompt[b]),
            "xown": np.ascontiguousarray(x_prompt[b][idx]),
            "xs": np.ascontiguousarray(x_sample[16 * c:16 * c + 16].reshape(64, D)),
            "w_in": w_in0, "nmw": nmw, "ident": ident, "cst": cst, "cw": cwl, "alog_bc": alog_bc, "dtb_bc": dtb_bc,
            "gnw_bc": gnw_bc, "w_pa": w_pa0, "w_pb": w_pb0, "w_o": w_o0, "w_up": w_up0, "w_down": w_down0,
            "nmlw": nmlw, "nfw_bc": nfw_bc, "sbb": sbb,
            "cache_k": ck, "cache_v": cv, "pt": np.ascontiguousarray(page_table[16 * c:16 * c + 16].reshape(1, 256)),
            "iota": iota, "sbrow": sbrow, "mnew": mnew, "dm": dmm, "selE": selE,
            "sc": np.ascontiguousarray(state_conv[0, 16 * c:16 * c + 16].reshape(48, 1536)),
            "ssm": np.ascontiguousarray(state_ssm[0, 16 * c:16 * c + 16]), "maskd": masks[j], "sel": np.ascontiguousarray(np.tile((np.arange(4) == j).astype(f32)[None, :], (128, 1))),
        })
    if os.environ.get('RETURN_MAPS') == '1':
        return nc, in_maps
    res = run_bass_kernel_spmd(nc, in_maps, core_ids=list(range(NCORE)))
    R = res.results
    y_prompt = np.zeros((2, SEQ, D), f32)
    y_sample = np.zeros((128, 4, D), f32)
    nkp = np.zeros((2, SEQ, 512), f32)
    nvp = np.zeros((2, SEQ, 512), f32)
    nksa = np.zeros((128, 4, 512), f32)
    nvsa = np.zeros((128, 4, 512), f32)
    ncpa = np.zeros((1, 2, 3, 1536), f32)
    ncsa = np.zeros((1, 128, 3, 1536), f32)
    nsp = np.zeros((1, 2, 4, 128, 128), f32)
    nss = np.zeros((1, 128, 4, 128, 128), f32)
    for c in range(NCORE):
        b, j = c // 4, c % 4
        r = R[c]
        y_prompt[b][own_idx[c]] = r["y_own"]
        y_sample[16 * c:16 * c + 16] = r["y_s"].reshape(16, 4, D)
        nkp[b][own_idx[c]] = r["nk_own"]
        nvp[b][own_idx[c]] = r["nv_own"]
        nksa[16 * c:16 * c + 16] = r["nks"].reshape(16, 4, 512)
        nvsa[16 * c:16 * c + 16] = r["nvs"].reshape(16, 4, 512)
        ncsa[0, 16 * c:16 * c + 16] = r["ncs"]
        nss[0, 16 * c:16 * c + 16] = r["nss"]
        if j == 0:
            ncpa[0, b] = r["ncp"]
            nsp[0, b] = r["nsp"]
    return (y_prompt, y_sample,
            nkp.reshape(1, 2, 64, 128, 8, 64), nvp.reshape(1, 2, 64, 128, 8, 64),
            nksa.reshape(1, 128, 4, 8, 64), nvsa.reshape(1, 128, 4, 8, 64),
            ncpa, ncsa, nsp, nss)
```

```python
import os
import numpy as np
import concourse.bass as bass
import concourse.mybir as mybir
from concourse.bass_utils import run_bass_kernel_spmd

F32 = mybir.dt.float32
BF16 = mybir.dt.bfloat16
I32 = mybir.dt.int32
AF = mybir.ActivationFunctionType
ALU = mybir.AluOpType
AX = mybir.AxisListType

EPOCH = 20000


class Res:
    _n = 0

    def __init__(self, name):
        Res._n += 1
        self.name = f"{name}#{Res._n}"
        self.writers = []
        self.readers = []
        self.dma_sem = None
        self.dma_cnt = 0
        self.pe_acc = False
        self.store_res = None


class T:
    def __init__(self, ap, res):
        self.ap = ap
        self.res = res

    def __getitem__(self, idx):
        return T(self.ap[idx], self.res)


class Op:
    __slots__ = ("eng", "idx", "fn", "waits", "needed", "dma_res", "clock", "semval", "is_dma")


class Prog:
    ENGS = ("pe", "act", "dve", "pool", "sp")

    def __init__(self, nc):
        self.nc = nc
        self.ops = {e: [] for e in self.ENGS}
        self.clock = {e: {} for e in self.ENGS}
        self.dma_res = []
        self.dma_clock = {}
        self._dom2res = {}
        self.all_res = []

    def res(self, name):
        r = Res(name)
        self.all_res.append(r)
        return r

    def _add(self, eng, fn, reads, writes, dma=False, pe_acc=False, evres=None, nowaw=False):
        reads = [r for r in reads if r is not None and r.res is not None]
        writes = [w for w in writes if w is not None and w.res is not None]
        op = Op()
        op.eng = eng
        op.idx = len(self.ops[eng]) + 1
        op.fn = fn
        op.needed = False
        op.is_dma = dma
        op.dma_res = None
        deps = []
        for r in reads:
            deps += r.res.writers
        for w in writes:
            if pe_acc and eng == "pe" and w.res.pe_acc and all(ev[0] == "pe" for ev in w.res.writers):
                deps += w.res.readers
            else:
                if not nowaw:
                    deps += w.res.writers
                deps += w.res.readers
        clk = self.clock[eng]
        waits = []
        best = {}
        for ev in deps:
            dom, val = ev[0], ev[1]
            if dom == eng and eng in ("pe", "sp"):
                continue
            if clk.get(dom, 0) >= val:
                continue
            if best.get(dom, 0) < val:
                best[dom] = val
        for dom, val in best.items():
            waits.append((dom, val))
            if isinstance(dom, str):
                src = self.ops[dom][val - 1]
                src.needed = True
                for d2, v2 in src.clock.items():
                    if clk.get(d2, 0) < v2:
                        clk[d2] = v2
            else:
                snap = self.dma_clock.get((dom, val), {})
                for d2, v2 in snap.items():
                    if clk.get(d2, 0) < v2:
                        clk[d2] = v2
            if clk.get(dom, 0) < val:
                clk[dom] = val
        op.waits = waits
        if dma:
            tgt = evres
            if tgt.dma_sem is None:
                tgt.dma_sem = True
                self.dma_res.append(tgt)
            tgt.dma_cnt += 1
            if eng == "pool":
                tgt.sw = True
            op.dma_res = tgt
            ev = (("dma", tgt.name), tgt.dma_cnt)
            self.dma_clock[ev] = dict(clk)
            self._dom2res[("dma", tgt.name)] = tgt
        else:
            ev = (eng, op.idx)
        op.clock = dict(clk)
        if not dma:
            op.clock[eng] = op.idx
        for r in reads:
            r.res.readers.append(ev)
        for w in writes:
            if pe_acc and eng == "pe" and w.res.pe_acc and all(e2[0] == "pe" for e2 in w.res.writers):
                w.res.writers = [ev]
            else:
                w.res.writers = [ev]
            w.res.readers = []
            w.res.pe_acc = bool(pe_acc and eng == "pe")
        self.ops[eng].append(op)
        return op

    def op(self, eng, fn, reads=(), writes=(), pe_acc=False):
        return self._add(eng, fn, list(reads), list(writes), pe_acc=pe_acc)

    def dma(self, eng, out, in_, store=False, nowaw=False, fn=None, **kw):
        if store:
            if in_.res.store_res is None:
                in_.res.store_res = Res("st_" + in_.res.name)
            evres = in_.res.store_res
        else:
            evres = out.res
        if fn is None:
            def fn(e, o=out.ap, i=in_.ap, kw=kw):
                return e.dma_start(out=o, in_=i, **kw)
        return self._add(eng, fn, [in_], [out], dma=True, evres=evres, nowaw=nowaw)

    def barrier(self):
        evs = []
        for e in self.ENGS:
            for op in reversed(self.ops[e]):
                if not op.is_dma and op.fn is not None:
                    evs.append((e, op.idx))
                    break
        for r in self.dma_res:
            if r.dma_cnt:
                evs.append((("dma", r.name), r.dma_cnt))
        bres = T(None, Res("barrier"))
        bres.res.writers = evs
        for e in self.ENGS:
            self._add(e, None, [bres], [])

    def setup_sems(self, stack, n_dma=70):
        nc = self.nc
        self.dma_pool_sw = [[stack.enter_context(nc.semaphore(f"dsw_{k}")), 0] for k in range(8)]
        self.sems = {e: [stack.enter_context(nc.semaphore(f"s_{e}_{k}")) for k in range(3)] for e in self.ENGS}
        self.count = {e: 0 for e in self.ENGS}
        self.dma_pool = [[stack.enter_context(nc.semaphore(f"d_{k}")), 0] for k in range(n_dma)]
        self.all_res = []

    def emit(self):
        nc = self.nc
        self.barrier()
        nsem = {}
        for e in self.ENGS:
            c = self.count[e]
            for op in self.ops[e]:
                if op.needed and not op.is_dma:
                    c += 1
                    op.semval = c
                else:
                    op.semval = None
            nsem[e] = c - self.count[e]
            self.count[e] = c
        ihw = isw = 0
        for r in self.dma_res:
            if getattr(r, "sw", False):
                slot = self.dma_pool_sw[isw]
                isw += 1
            else:
                slot = self.dma_pool[ihw]
                ihw += 1
            r.dma_sem = slot
            r.dma_base = slot[1]
            slot[1] += 16 * r.dma_cnt
        print("pass ops:", {e: len(self.ops[e]) for e in self.ENGS}, "flagged:", nsem, "dma sems:", len(self.dma_res), flush=True)
        if os.environ.get("DUMP_WAITS"):
            for e in self.ENGS:
                for op in self.ops[e][:60]:
                    ws = []
                    for dom, val in op.waits:
                        if isinstance(dom, str):
                            ws.append((dom, val, self.ops[dom][val - 1].semval))
                        else:
                            r = self._dom2res[dom]
                            ws.append((dom[1], val, r.dma_base + 16 * val))
                    print("  ", e, op.idx, "dma" if op.is_dma else "", "sem=%s" % op.semval, ws)
        sems = self.sems
        engobj = {"pe": "tensor", "act": "scalar", "dve": "vector", "pool": "gpsimd", "sp": "sync"}
        with nc.Block() as block:
            def make(e):
                def body(eng):
                    for op in self.ops[e]:
                        for dom, val in op.waits:
                            if isinstance(dom, str):
                                sv = self.ops[dom][val - 1].semval
                                k = (sv - 1) // EPOCH
                                eng.wait_ge(sems[dom][k], sv - k * EPOCH)
                            else:
                                r = self._dom2res[dom]
                                eng.wait_ge(r.dma_sem[0], r.dma_base + 16 * val)
                        if op.fn is None:
                            continue
                        ins = op.fn(eng)
                        if op.is_dma:
                            ins.then_inc(op.dma_res.dma_sem[0], 16)
                        elif op.semval is not None:
                            k = (op.semval - 1) // EPOCH
                            ins.then_inc(sems[e][k], 1)
                return body
            for e in self.ENGS:
                getattr(block, engobj[e])(make(e))
        self.ops = {e: [] for e in self.ENGS}
        self.clock = {e: {} for e in self.ENGS}
        for r in self.all_res:
            r.writers = []
            r.readers = []
            r.dma_sem = None
            r.dma_cnt = 0
            r.pe_acc = False
            r.store_res = None
            r.sw = False
        self.dma_res = []
        self.dma_clock = {}
        self._dom2res = {}


from contextlib import ExitStack
import os
import ml_dtypes

NCORE = 8
D = 1024
IN_DIM = 5640
SEQ = 8192
NT_FULL = SEQ // 128
OWN = 2048
NT_OWN = OWN // 128
NS_TOK = 64
NCST = 13
BIG = 300.0
EPS = 1e-6


class Ring:
    def __init__(self, tiles):
        self.tiles = tiles
        self.i = 0

    def next(self):
        t = self.tiles[self.i % len(self.tiles)]
        self.i += 1
        return t


def build():
    nc = bass.Bass("TRN2", target_bir_lowering=False)

    def din(name, shape, dt=F32):
        return nc.dram_tensor(name, list(shape), dt, kind="ExternalInput").ap()

    def dout(name, shape, dt=F32):
        return nc.dram_tensor(name, list(shape), dt, kind="ExternalOutput").ap()

    xfull = din("xfull", [SEQ, D])
    xown = din("xown", [OWN, D])
    xs = din("xs", [NS_TOK, D])
    w_in = din("w_in", [D, IN_DIM])
    nmw = din("nmw", [128, 8])
    ident_d = din("ident", [128, 128])
    cst_d = din("cst", [128, NCST, 128])
    cw_d = din("cw", [128, 12, 4])
    alog_d = din("alog_bc", [128, 4])
    dtb_d = din("dtb_bc", [128, 4])
    gnw_d = din("gnw_bc", [128, 128])
    sel_d = din("sel", [128, 4])
    sbb_d = din("sbb", [128, 8])
    mask_d = din("maskd", [128, 16, 512], BF16)
    sc_d = din("sc", [48, 1536])
    NPOOL = int(os.environ.get('NPOOL_DBG', '2560'))
    ck_d = din("cache_k", [NPOOL * 128, 512])
    cv_d = din("cache_v", [NPOOL * 128, 512])
    pt_d = din("pt", [1, 256], I32)
    iota_d = din("iota", [128, 256], I32)
    sbrow_d = din("sbrow", [128, 512])
    mnew_d = din("mnew", [128, 32])
    selE_d = din("selE", [64, 256])
    dm_d = din("dm", [32, 512])
    ssm_d = din("ssm", [16, 4, 128, 128])
    nss = dout("nss", [16, 4, 128, 128])
    nsp = dout("nsp", [4, 128, 128])
    hT_d = nc.dram_tensor("hT_d", [NT_FULL, 128, 1024], BF16, kind="Internal").ap()
    hTown_d = nc.dram_tensor("hTown_d", [NT_OWN, 128, 1024], BF16, kind="Internal").ap()
    ps_d = nc.dram_tensor("ps_d", [NS_TOK, IN_DIM], F32, kind="Internal").ap()
    x1_d = nc.dram_tensor("x1_d", [OWN + NS_TOK, D], F32, kind="Internal").ap()
    hmT_d = nc.dram_tensor("hmT_d", [NT_OWN + 1, 128, 1024], BF16, kind="Internal").ap()
    w_pa_d = din("w_pa", [512, D])
    w_pb_d = din("w_pb", [512, D])
    w_o_d = din("w_o", [D, D])
    w_up_d = din("w_up", [D, 4 * D])
    w_down_d = din("w_down", [4 * D, D])
    nmlw_d = din("nmlw", [128, 8])
    nfw_d = din("nfw_bc", [128, D])
    y_own = dout("y_own", [OWN, D])
    y_s = dout("y_s", [NS_TOK, D])

    nk_own = dout("nk_own", [OWN, 512])
    nv_own = dout("nv_own", [OWN, 512])
    nks = dout("nks", [NS_TOK, 512])
    nvs = dout("nvs", [NS_TOK, 512])
    ncp = dout("ncp", [3, 1536])
    ncs = dout("ncs", [16, 3, 1536])

    P = Prog(nc)
    with ExitStack() as gst:
        P.setup_sems(gst)

        uid = [0]

        def sbt(st, name, shape, dt):
            uid[0] += 1
            name = f"{name}_{uid[0]}"
            return T(st.enter_context(nc.sbuf_tensor(name, list(shape), dt))[:], P.res(name))

        def pst(st, name, shape, dt):
            uid[0] += 1
            name = f"{name}_{uid[0]}"
            return T(st.enter_context(nc.psum_tensor(name, list(shape), dt))[:], P.res(name))

        def ring(st, fn, name, shape, dt, n):
            return Ring([fn(st, f"{name}{i}", shape, dt) for i in range(n)])

        identf = sbt(gst, "identf", [128, 128], F32)
        identb = sbt(gst, "identb", [128, 128], BF16)
        epsc = sbt(gst, "epsc", [128, 1], F32)
        nmw_t = sbt(gst, "nmw_t", [128, 8], F32)
        hsT = sbt(gst, "hsT", [128, 8, NS_TOK], BF16)
        hTlast = sbt(gst, "hTlast", [128, 8, 128], BF16)
        onec = sbt(gst, "onec", [128, 1], F32)
        sel_t = sbt(gst, "sel_t", [128, 4], F32)


        def apx(v):
            return v.ap if isinstance(v, T) else v

        def mm(out, lhsT, rhs, start=True, stop=True):
            P.op("pe", lambda e: e.matmul(out=out.ap, lhsT=lhsT.ap, rhs=rhs.ap, start=start, stop=stop), [lhsT, rhs], [out], pe_acc=True)

        def tr(out, in_, idn):
            P.op("pe", lambda e: e.transpose(out=out.ap, in_=in_.ap, identity=idn.ap), [in_, idn], [out], pe_acc=True)

        def act(out, in_, func, bias=None, scale=None):
            kw = {}
            rd = [in_]
            if bias is not None:
                kw["bias"] = apx(bias)
                if isinstance(bias, T):
                    rd.append(bias)
            if scale is not None:
                kw["scale"] = apx(scale)
                if isinstance(scale, T):
                    rd.append(scale)
            P.op("act", lambda e: e.activation(out=out.ap, in_=in_.ap, func=func, **kw), rd, [out])

        def ts(out, in0, s1, op0, s2=None, op1=None, eng="dve"):
            rd = [in0] + [v for v in (s1, s2) if isinstance(v, T)]
            if op1 is None:
                P.op(eng, lambda e: e.tensor_scalar(out=out.ap, in0=in0.ap, scalar1=apx(s1), scalar2=None, op0=op0), rd, [out])
            else:
                P.op(eng, lambda e: e.tensor_scalar(out=out.ap, in0=in0.ap, scalar1=apx(s1), scalar2=apx(s2), op0=op0, op1=op1), rd, [out])

        def tt(out, in0, in1, op, eng="dve"):
            P.op(eng, lambda e: e.tensor_tensor(out=out.ap, in0=in0.ap, in1=in1.ap, op=op), [in0, in1], [out])

        def stt(out, in0, scalar, in1, op0, op1):
            rd = [in0, in1] + ([scalar] if isinstance(scalar, T) else [])
            P.op("dve", lambda e: e.scalar_tensor_tensor(out=out.ap, in0=in0.ap, scalar=apx(scalar), in1=in1.ap, op0=op0, op1=op1), rd, [out])

        def cp(out, in_, eng="dve"):
            if eng == "act":
                P.op("act", lambda e: e.copy(out=out.ap, in_=in_.ap), [in_], [out])
            else:
                P.op(eng, lambda e: e.tensor_copy(out=out.ap, in_=in_.ap), [in_], [out])

        def recip(out, in_):
            P.op("dve", lambda e: e.reciprocal(out=out.ap, in_=in_.ap), [in_], [out])

        def rsum(out, in_):
            P.op("dve", lambda e: e.reduce_sum(out=out.ap, in_=in_.ap, axis=AX.X), [in_], [out])

        def memset(t, v, eng="pool"):
            P.op(eng, lambda e: e.memset(t.ap, v), [], [t])

        def sub(t, idx, name):
            return T(t.ap[idx], P.res(name))

        def norm_tile(tl, x_rows_ap, hT_dst, nt=128):
            xt = tl["x"].next()
            P.dma("sp", xt[:nt], T(x_rows_ap, None))
            norm_core(tl, xt, hT_dst, nt)

        def norm_core(tl, xt, hT_dst, nt=128):
            sq = tl["sq"].next()
            P.op("act", lambda e: e.activation(out=sq.ap[:nt], in_=xt.ap[:nt], func=AF.Square), [xt], [sq])
            ss = tl["ss"].next()
            P.op("dve", lambda e: e.reduce_sum(out=ss.ap[:nt], in_=sq.ap[:nt], axis=AX.X), [sq], [ss])
            P.op("act", lambda e: e.activation(out=ss.ap[:nt], in_=ss.ap[:nt], func=AF.Ln, bias=epsc.ap[:nt], scale=1.0 / D), [ss, epsc], [ss])
            P.op("act", lambda e: e.activation(out=ss.ap[:nt], in_=ss.ap[:nt], func=AF.Exp, scale=-0.5), [ss], [ss])
            hb = tl["hb"].next()
            P.op("dve", lambda e: e.tensor_scalar(out=hb.ap[:nt], in0=xt.ap[:nt], scalar1=ss.ap[:nt, 0:1], scalar2=None, op0=ALU.mult), [xt, ss], [hb])
            pt = tl["ptr"].next()
            for k in range(8):
                P.op("pe", lambda e, k=k: e.transpose(out=pt.ap[:, k, :nt], in_=hb.ap[:nt, k * 128:(k + 1) * 128], identity=identb.ap[:nt, :nt]), [hb, identb], [pt], pe_acc=True)
            P.op("act", lambda e: e.copy(out=hT_dst.ap, in_=pt.ap[:, :, :nt]), [pt], [hT_dst])

        def load_w(tl, dst, w_dram, c0, n, scale=None, nk=8):
            for k in range(nk):
                stg = tl["wst"].next()
                P.dma("sp", stg[:, :n], T(w_dram[k * 128:(k + 1) * 128, c0:c0 + n], None))
                eng = ("pool", "dve")[k % 2]
                if scale is not None:
                    P.op(eng, lambda e, k=k, stg=stg: e.tensor_scalar(out=dst.ap[:, k, :n], in0=stg.ap[:, :n], scalar1=scale.ap[:, k:k + 1], scalar2=None, op0=ALU.mult), [stg, scale], [dst])
                else:
                    P.op(eng, lambda e, k=k, stg=stg: e.tensor_copy(out=dst.ap[:, k, :n], in_=stg.ap[:, :n]), [stg], [dst])

        with ExitStack() as st:
            tl = {
                "x": ring(st, sbt, "x", [128, D], F32, 3),
                "sq": ring(st, sbt, "sq", [128, D], F32, 2),
                "ss": ring(st, sbt, "ss", [128, 1], F32, 4),
                "hb": ring(st, sbt, "hb", [128, D], BF16, 2),
                "ptr": ring(st, pst, "ptr", [128, 8, 128], BF16, 2),
            }
            hTt = ring(st, sbt, "hTt", [128, 8, 128], BF16, 3)
            P.dma("sp", identf, T(ident_d, None))
            P.dma("sp", nmw_t, T(nmw, None))
            P.op("pool", lambda e: e.memset(epsc.ap, EPS), [], [epsc])
            P.op("pool", lambda e: e.memset(onec.ap, 1.0), [], [onec])
            P.dma("sp", sel_t, T(sel_d, None))
            P.op("dve", lambda e: e.tensor_copy(out=identb.ap, in_=identf.ap), [identf], [identb])
            for t in sorted(set(list(range(int(os.environ.get('NT0', NT_FULL)))) + [NT_FULL - 1])):
                dst = hTt.next() if t < NT_FULL - 1 else hTlast
                norm_tile(tl, xfull[t * 128:(t + 1) * 128, :], dst)
                P.dma("sp", T(hT_d[t].rearrange("p (k n) -> p k n", k=8), None), dst, store=True)
            for t in range(min(NT_OWN, int(os.environ.get('NT0', NT_OWN)))):
                dst = hTt.next()
                norm_tile(tl, xown[t * 128:(t + 1) * 128, :], dst)
                P.dma("sp", T(hTown_d[t].rearrange("p (k n) -> p k n", k=8), None), dst, store=True)
            norm_tile(tl, xs[:, :], hsT, nt=NS_TOK)
            P.emit()

        for _skip in ([] if os.environ.get('SKIP1') == '1' else [0]):
          with ExitStack() as st:
              tl = {"wst": ring(st, sbt, "wst", [128, 512], F32, 4)}
              wg = ring(st, sbt, "wg", [128, 8, 512], BF16, 2)
              wkv = sbt(st, "wkv", [128, 8, 1024], BF16)
              pmm = ring(st, pst, "pmm", [128, 512], F32, 4)
              ob = ring(st, sbt, "ob", [128, 1024], F32, 3)
              cps = sbt(st, "cps", [3, 1536], F32)
              ps_s = sbt(st, "ps_s", [NS_TOK, IN_DIM], F32)
              hTo_r = ring(st, sbt, "hTo", [128, 8, 128], BF16, 3)
              ngrp = (IN_DIM + 511) // 512
              for gi in range(ngrp):
                  c0 = gi * 512
                  n = min(512, IN_DIM - c0)
                  w = wg.next()
                  load_w(tl, w, w_in, c0, n, scale=nmw_t)
                  pm = pmm.next()
                  for k in range(8):
                      P.op("pe", lambda e, k=k, w=w, pm=pm, n=n: e.matmul(out=pm.ap[:NS_TOK, :n], lhsT=hsT.ap[:, k, :], rhs=w.ap[:, k, :n], start=(k == 0), stop=(k == 7)), [hsT, w], [pm], pe_acc=True)
                  P.op("act", lambda e, pm=pm, c0=c0, n=n: e.copy(out=ps_s.ap[:, c0:c0 + n], in_=pm.ap[:NS_TOK, :n]), [pm], [ps_s])
                  if gi in (1, 2):
                      P.op("pool", lambda e, w=w, gi=gi: e.tensor_copy(out=wkv.ap[:, :, (gi - 1) * 512:gi * 512], in_=w.ap), [w], [wkv])
                  if gi in (3, 4, 5):
                      pm2 = pmm.next()
                      for k in range(8):
                          P.op("pe", lambda e, k=k, w=w, pm2=pm2: e.matmul(out=pm2.ap[:3, :], lhsT=hTlast.ap[:, k, 125:128], rhs=w.ap[:, k, :], start=(k == 0), stop=(k == 7)), [hTlast, w], [pm2], pe_acc=True)
                      P.op("act", lambda e, pm2=pm2, gi=gi: e.copy(out=cps.ap[:, (gi - 3) * 512:(gi - 2) * 512], in_=pm2.ap[:3, :]), [pm2], [cps])
              P.dma("sp", T(ncp, None), cps, store=True)
              P.dma("sp", T(nks, None), ps_s[:, 512:1024], store=True)
              P.dma("sp", T(nvs, None), ps_s[:, 1024:1536], store=True)
              for r in range(3):
                  P.dma("sp", T(ncs[:, r, :], None), T(ps_s.ap[r + 1:NS_TOK:4, 1536:3072], ps_s.res), store=True)
              P.dma("sp", T(ps_d, None), ps_s, store=True)
              for t in range(NT_OWN):
                  o = ob.next()
                  hTo = hTo_r.next()
                  P.dma("sp", hTo, T(hTown_d[t].rearrange("p (k n) -> p k n", k=8), None))
                  for half in range(2):
                      pm = pmm.next()
                      for k in range(8):
                          P.op("pe", lambda e, k=k, pm=pm, half=half, hTo=hTo: e.matmul(out=pm.ap, lhsT=hTo.ap[:, k, :], rhs=wkv.ap[:, k, half * 512:(half + 1) * 512], start=(k == 0), stop=(k == 7)), [hTo, wkv], [pm], pe_acc=True)
                      P.op(("act", "dve")[half], lambda e, pm=pm, half=half, o=o: (e.copy if half == 0 else e.tensor_copy)(out=o.ap[:, half * 512:(half + 1) * 512], in_=pm.ap), [pm], [o])
                  P.dma("sp", T(nk_own[t * 128:(t + 1) * 128, :], None), o[:, 0:512], store=True)
                  P.dma("sp", T(nv_own[t * 128:(t + 1) * 128, :], None), o[:, 512:1024], store=True)
              P.emit()

        mid = ExitStack()
        oBown = sbt(mid, "oBown", [128, 4, OWN], BF16)
        oAown = sbt(mid, "oAown", [128, 4, OWN], BF16)
        oBs = sbt(mid, "oBs", [128, 4, NS_TOK], BF16)
        oAs = sbt(mid, "oAs", [128, 4, NS_TOK], BF16)
        with ExitStack() as st:
            memset(oBown, 0.0)
            memset(oAown, 0.0)
            memset(oAs, 0.0)
            tl = {"wst": ring(st, sbt, "wst", [128, 512], F32, 4)}
            cst = sbt(st, "cst", [128, NCST, 128], F32)
            P.dma("sp", cst, T(cst_d, None))
            C_UINCL, C_BIGU, C_NBIGL, C_SU, C_BD, C_OFFM = (cst[:, i, :] for i in range(6))
            ones = sbt(st, "ones", [128, 128], F32)
            memset(ones, 1.0)
            cw = sbt(st, "cw", [128, 12, 4], F32)
            P.dma("sp", cw, T(cw_d, None))
            negA = sbt(st, "negA", [128, 4], F32)
            dtb = sbt(st, "dtb", [128, 4], F32)
            gnw = sbt(st, "gnw", [128, 128], F32)
            P.dma("sp", negA, T(alog_d, None))
            P.dma("sp", dtb, T(dtb_d, None))
            P.dma("sp", gnw, T(gnw_d, None))
            act(negA, negA, AF.Exp)
            ts(negA, negA, -1.0, ALU.mult)
            wgd = sbt(st, "wgd", [128, 8, 2056], BF16)
            for gi in range(4):
                load_w(tl, wgd[:, :, gi * 512:(gi + 1) * 512], w_in, 1536 + gi * 512, 512, scale=nmw_t)
            load_w(tl, wgd[:, :, 2048:2056], w_in, 3584, 8, scale=nmw_t)
            S = [[sbt(st, f"S{h}_{i}", [128, 128], F32) for i in range(2)] for h in range(4)]
            for h in range(4):
                memset(S[h][0], 0.0)
            ub_r = ring(st, sbt, "ub", [128, 3, 515], F32, 2)
            hist = [sbt(st, f"hist{h}", [128, 3, 3], F32) for h in range(4)]
            for h in range(4):
                memset(hist[h], 0.0)
            hTg_r = ring(st, sbt, "hTg", [128, 8, 512], BF16, 2)
            pbank = [pst(st, f"pb{i}", [128, 4, 128], F32) for i in range(3)]
            pu = pst(st, "pu", [128, 3, 512], F32)
            pss = pst(st, "pss", [128, 512], F32)
            pq = Ring([T(pbank[i % 3].ap[:, (i // 3) % 4, :], pbank[i % 3].res) for i in range(12)])
            sq_r = Ring([sbt(st, f"sq{i}", [128, 128], F32) for i in range(32)])
            col_r = Ring([sbt(st, f"col{i}", [128, 1], F32) for i in range(32)])
            big_r = Ring([sbt(st, f"big{i}", [128, 3, 512], F32) for i in range(3)])
            nrm_r = Ring([sbt(st, f"nrm{i}", [128, 512], F32) for i in range(4)])
            abg = sbt(st, "abg", [128, 4, 8], F32)
            gb_r = Ring([sbt(st, f"gb{i}", [128, 4, 4], F32) for i in range(8)])
            ogb_r = Ring([sbt(st, f"ogb{i}", [128, 128], BF16) for i in range(2)])
            ptrb = ring(st, pst, "ptrb", [128, 128], BF16, 1)
            l2e = sbt(st, "l2e", [128, 1], F32)
            memset(l2e, 1e-6)
            dtb4 = sbt(st, "dtb4", [128, 4, 4], F32)
            negA4 = sbt(st, "negA4", [128, 4, 4], F32)
            for c in range(4):
                cp(dtb4[:, c, :], dtb, eng="pool")
                cp(negA4[:, c, :], negA, eng="pool")

            STAGE = int(os.environ.get('GDN_STAGE', '99'))

            def gdn_chunk(h, qT, kT, vT, gcol, bcol, nbcol, hTc, own_dst, selcol, Sin, Sout):
                G = sq_r.next()
                ts(G, C_UINCL, gcol, ALU.mult)
                PA, PB, pm1 = pq.next(), pq.next(), pq.next()
                mm(PA, ones, G, True, False)
                mm(PA, identf, C_BIGU, False, True)
                mm(PB, ones, G, True, False)
                mm(PB, identf, C_NBIGL, False, True)
                mm(pm1[:, 0:1], ones, G[:, 127:128])
                mm(pm1[:, 1:2], C_UINCL, gcol)
                cg, ncg, ecg, egl, edl = (col_r.next() for _ in range(5))
                cp(cg, pm1[:, 1:2], eng="act")
                ts(ncg, pm1[:, 1:2], -1.0, ALU.mult)
                decay, decayT = sq_r.next(), sq_r.next()
                act(decay, PA, AF.Exp, bias=cg, scale=-1.0)
                act(decayT, PB, AF.Exp, bias=ncg, scale=1.0)
                act(ecg, cg, AF.Exp)
                act(egl, pm1[:, 0:1], AF.Exp)
                act(edl, pm1[:, 0:1], AF.Exp, bias=ncg, scale=1.0)
                if STAGE < 4:
                    return None
                KK, QKT = pq.next(), pq.next()
                mm(KK, kT, kT)
                mm(QKT, kT, qT)
                t0, Wm, AT = sq_r.next(), sq_r.next(), sq_r.next()
                tt(t0, KK, decayT, ALU.mult)
                stt(Wm, t0, bcol, C_SU, ALU.mult, ALU.mult)
                tt(AT, QKT, decayT, ALU.mult)
                WmTp = pq.next()
                tr(WmTp, Wm, identf)
                WmT = sq_r.next()
                cp(WmT, WmTp, eng="act")
                if STAGE < 5:
                    return None
                X, XT, OffT, Pk = sq_r.next(), sq_r.next(), sq_r.next(), sq_r.next()
                tt(X, Wm, C_BD, ALU.mult)
                tt(XT, WmT, C_BD, ALU.mult)
                tt(OffT, WmT, C_OFFM, ALU.mult)
                tt(Pk, identf, X, ALU.subtract)
                for lvl in range(1, 6):
                    p2 = pq.next()
                    mm(p2, X, XT)
                    X2T = sq_r.next()
                    if lvl < 5:
                        p1 = pq.next()
                        mm(p1, XT, X)
                        X2 = sq_r.next()
                        cp(X2, p1, eng="act")
                    cp(X2T, p2, eng="dve")
                    p3 = pq.next()
                    mm(p3, X2T, Pk)
                    Pn = sq_r.next()
                    tt(Pn, p3, Pk, ALU.add)
                    Pk = Pn
                    XT = X2T
                    if lvl < 5:
                        X = X2
                Pd = Pk
                PdTp = pq.next()
                tr(PdTp, Pd, identf)
                PdT = sq_r.next()
                cp(PdT, PdTp, eng="act")
                t1p = pq.next()
                mm(t1p, OffT, Pd)
                t1 = sq_r.next()
                cp(t1, t1p, eng="dve")
                w2p = pq.next()
                mm(w2p, PdT, t1)
                W = sq_r.next()
                tt(W, Pd, w2p, ALU.subtract)
                if STAGE < 6:
                    return None
                ktp, vtp = pq.next(), pq.next()
                tr(ktp, kT, identf)
                tr(vtp, vT, identf)
                ke, kd, vtm = sq_r.next(), sq_r.next(), sq_r.next()
                ts(ke, ktp, ecg, ALU.mult)
                ts(kd, ktp, edl, ALU.mult)
                cp(vtm, vtp, eng="act")
                u0p, w0Tp = pq.next(), pq.next()
                mm(u0p, W, vtm)
                mm(w0Tp, ke, W)
                u0b, w0T = sq_r.next(), sq_r.next()
                ts(u0b, u0p, bcol, ALU.mult)
                cp(w0T, w0Tp, eng="act")
                if STAGE < 7:
                    return None
                wSp = pq.next()
                mm(wSp, w0T, Sin)
                vn = sq_r.next()
                stt(vn, wSp, nbcol, u0b, ALU.mult, ALU.add)
                qSp, Avp = pq.next(), pq.next()
                mm(qSp, qT, Sin)
                mm(Avp, AT, vn)
                Av, o = sq_r.next(), sq_r.next()
                cp(Av, Avp, eng="act")
                stt(o, qSp, ecg, Av, ALU.mult, ALU.add)
                KVp = pq.next()
                mm(KVp, kd, vn)
                stt(Sout, Sin, egl, KVp, ALU.mult, ALU.add)
                if STAGE < 8:
                    return None
                zp = pq.next()
                for k in range(8):
                    mm(zp, hTc[:, k, :], wgd[:, k, 1536 + h * 128:1536 + (h + 1) * 128], k == 0, k == 7)
                osq, ms = sq_r.next(), col_r.next()
                tt(osq, o, o, ALU.mult, eng="pool")
                rsum(ms, osq)
                act(ms, ms, AF.Ln, bias=epsc, scale=1.0 / 128)
                act(ms, ms, AF.Exp, scale=-0.5)
                ez, sz, og = sq_r.next(), sq_r.next(), sq_r.next()
                act(ez, zp, AF.Exp, scale=-1.0)
                ts(ez, ez, 1.0, ALU.add, eng="pool")
                recip(ez, ez)
                tt(sz, zp, ez, ALU.mult)
                stt(og, o, ms, gnw, ALU.mult, ALU.mult)
                ogb = ogb_r.next()
                tt(ogb, og, sz, ALU.mult)
                return ogb

            for g in range(int(os.environ.get('GDN_G', '16'))):
                hTg = hTg_r.next()
                for t in range(4):
                    P.dma("sp", hTg[:, :, t * 128:(t + 1) * 128], T(hT_d[4 * g + t].rearrange("p (k n) -> p k n", k=8), None), nowaw=True)
                SUB = int(os.environ.get('GDN_SUB', '9'))
                for c in range(4 if SUB >= 1 else 0):
                    abp = pq.next()
                    for k in range(8):
                        mm(abp[:, 0:8], hTg[:, k, c * 128:(c + 1) * 128], wgd[:, k, 2048:2056], k == 0, k == 7)
                    if os.environ.get('NOCP') != '1':
                        cp(abg[:, c, :], abp[:, 0:8], eng=os.environ.get("CPENG", "act"))
                xa, gg, eb, beta, nbeta = (gb_r.next() for _ in range(5))
                if SUB >= 2:
                    tt(xa, abg[:, :, 0:4], dtb4, ALU.add)
                if SUB >= 3:
                    act(xa, xa, AF.Exp)
                    act(xa, xa, AF.Ln, bias=onec, scale=1.0)
                if SUB >= 4:
                    tt(gg, xa, negA4, ALU.mult)
                    act(eb, abg[:, :, 4:8], AF.Exp, scale=-1.0)
                    ts(eb, eb, 1.0, ALU.add)
                if SUB >= 5:
                    recip(beta, eb)
                    ts(nbeta, beta, -1.0, ALU.mult)
                for h in range(int(os.environ.get('GDN_H', '4')) if STAGE >= 2 else 0):
                    for part in range(3):
                        for k in range(8):
                            mm(pu[:, part, :], wgd[:, k, part * 512 + h * 128:part * 512 + (h + 1) * 128], hTg[:, k, :], k == 0, k == 7)
                    ub = ub_r.next()
                    cp(ub[:, :, 0:3], hist[h], eng="pool")
                    cp(ub[:, :, 3:515], pu, eng="act")
                    cp(hist[h], ub[:, :, 512:515], eng="pool")
                    y = big_r.next()
                    for part in range(3):
                        ci = part * 4 + h
                        ts(y[:, part, :], ub[:, part, 0:512], cw[:, ci, 0:1], ALU.mult)
                        for w in range(1, 4):
                            stt(y[:, part, :], ub[:, part, w:w + 512], cw[:, ci, w:w + 1], y[:, part, :], ALU.mult, ALU.add)
                    e = big_r.next()
                    act(e, y, AF.Exp, scale=-1.0)
                    ts(e, e, 1.0, ALU.add, eng="pool")
                    recip(e, e)
                    sl = big_r.next()
                    tt(sl, y, e, ALU.mult, eng="pool")
                    sqq = e
                    tt(sqq[:, 0:2, :], sl[:, 0:2, :], sl[:, 0:2, :], ALU.mult, eng="pool")
                    qn, kn = nrm_r.next(), nrm_r.next()
                    for part, dst in ((0, qn), (1, kn)):
                        mm(pss, ones, sqq[:, part, :])
                        rs = nrm_r.next()
                        act(rs, pss, AF.Ln, bias=l2e, scale=1.0)
                        act(rs, rs, AF.Exp, scale=-0.5)
                        if part == 0:
                            stt(dst, sl[:, 0, :], 128.0 ** -0.5, rs, ALU.mult, ALU.mult)
                        else:
                            tt(dst, sl[:, 1, :], rs, ALU.mult)
                    for c in range(4 if STAGE >= 3 else 0):
                        cs = slice(c * 128, (c + 1) * 128)
                        nchunk = 4 * g + c
                        Sin, Sout = S[h][nchunk % 2], S[h][(nchunk + 1) % 2]
                        ogb = gdn_chunk(h, qn[:, cs], kn[:, cs], sl[:, 2, cs], gg[:, c, h:h + 1], beta[:, c, h:h + 1], nbeta[:, c, h:h + 1],
                                        hTg[:, :, cs], None, None, Sin, Sout)
                        if ogb is None:
                            continue
                        oTp = ptrb.next()
                        tr(oTp, ogb, identb)
                        dst = oBown[:, h, (g // 4) * 512 + c * 128:(g // 4) * 512 + (c + 1) * 128]
                        stt(dst, oTp, sel_t[:, (g % 4):(g % 4) + 1], dst, ALU.mult, ALU.add)
            for h in range(4):
                P.dma("sp", T(nsp[h], None), S[h][(NT_FULL) % 2], store=True)

            def flat(t, rows, c0, c1):
                return T(t.ap.rearrange("p a b -> p (a b)")[:rows, c0:c1], t.res)

            def bview(t, pat, **kw):
                return T(t.ap.rearrange(pat, **kw), t.res)

            n = NS_TOK
            CS_UINCL, CS_BIGU, CS_NBIGL, CS_SU, CS_BD = (cst[:n, 7 + i, :n] for i in range(5))
            RM = cst[:n, 12, 0:16]
            idn = identf[:n, :n]
            usb, scb, zb = big_r.next(), big_r.next(), big_r.next()
            us = flat(usb, n, 0, 1536)
            scs = flat(scb, 48, 0, 1536)
            zs = flat(zb, n, 0, 512)
            abs_ = flat(zb, n, 512, 520)
            P.dma("sp", us, T(ps_d[:, 1536:3072], None))
            P.dma("sp", scs, T(sc_d, None))
            P.dma("sp", zs, T(ps_d[:, 3072:3584], None))
            P.dma("sp", abs_, T(ps_d[:, 3584:3592], None), nowaw=True)
            uext = sbt(st, "uext", [128, 12, 16, 7], F32)
            ys = sbt(st, "ys", [128, 12, 64], F32)
            es = sbt(st, "es", [128, 12, 64], F32)
            s_all = es
            for cc in range(12):
                ptq = pq.next()
                tr(ptq[:, 0:64], us[:, cc * 128:(cc + 1) * 128], idn)
                tr(ptq[:, 64:112], scs[:, cc * 128:(cc + 1) * 128], identf[:48, :48])
                cp(uext[:, cc, :, 3:7], T(ptq.ap[:, 0:64].rearrange("p (s t) -> p s t", t=4), ptq.res), eng="act")
                cp(uext[:, cc, :, 0:3], T(ptq.ap[:, 64:112].rearrange("p (s t) -> p s t", t=3), ptq.res), eng="dve")
                yv = T(ys.ap[:, cc, :].rearrange("p (s t) -> p s t", t=4), ys.res)
                ts(yv, uext[:, cc, :, 0:4], cw[:, cc, 0:1], ALU.mult)
                for w in range(1, 4):
                    stt(yv, uext[:, cc, :, w:w + 4], cw[:, cc, w:w + 1], yv, ALU.mult, ALU.add)
            act(es, ys, AF.Exp, scale=-1.0)
            ts(es, es, 1.0, ALU.add, eng="pool")
            recip(es, es)
            tt(s_all, ys, es, ALU.mult, eng="pool")
            xas, ggs, ebs, betas = (gb_r.next() for _ in range(4))
            xa2, gg2, eb2, be2 = (T(t.ap.rearrange("p a b -> p (a b)")[:n, 0:4], t.res) for t in (xas, ggs, ebs, betas))
            tt(xa2, abs_[:, 0:4], dtb[:n], ALU.add)
            act(xa2, xa2, AF.Exp)
            act(xa2, xa2, AF.Ln, bias=onec[:n], scale=1.0)
            tt(gg2, xa2, negA[:n], ALU.mult)
            act(eb2, abs_[:, 4:8], AF.Exp, scale=-1.0)
            ts(eb2, eb2, 1.0, ALU.add)
            recip(be2, eb2)
            W3 = sbt(st, "W3", [n, 16 * n], F32)
            I3 = sbt(st, "I3", [n, 16 * n], F32)
            memset(W3, 0.0)
            memset(I3, 0.0)

            def diagv(t):
                a = t.ap
                return T(bass.AP(tensor=a.tensor, offset=a.offset, ap=[list(a.ap[0]), [n + 4, 16], [1, 4]]), t.res)
            cp(diagv(I3), T(idn.ap.rearrange("p (s t) -> p s t", t=4), idn.res), eng="pool")
            w0Tm = sbt(st, "w0Tm", [128, 16 * n], F32)
            qem = sbt(st, "qem", [128, 16 * n], F32)
            Sall = sbt(st, "Sall", [128, 16, 128], F32)
            Snew = Sall
            EGL = sbt(st, "EGL", [128, 16], F32)
            for h in range(4):
                P.dma("sp", Sall, T(ssm_d[:, h].rearrange("s p d -> p s d"), None))
                qs_, ks_, vs_ = s_all[:, h, :], s_all[:, 4 + h, :], s_all[:, 8 + h, :]
                sqv = nrm_r.next()
                tt(sqv[:, 0:n], qs_, qs_, ALU.mult, eng="pool")
                tt(sqv[:, n:2 * n], ks_, ks_, ALU.mult, eng="pool")
                mm(pss[:, 0:2 * n], ones, sqv[:, 0:2 * n])
                rsv = nrm_r.next()
                act(rsv[:, 0:2 * n], pss[:, 0:2 * n], AF.Ln, bias=l2e, scale=1.0)
                act(rsv[:, 0:2 * n], rsv[:, 0:2 * n], AF.Exp, scale=-0.5)
                qkn = nrm_r.next()
                qn, kn = qkn[:, 0:n], qkn[:, n:2 * n]
                stt(qn, qs_, 128.0 ** -0.5, rsv[:, 0:n], ALU.mult, ALU.mult)
                tt(kn, ks_, rsv[:, n:2 * n], ALU.mult)
                gcol, bcol = gg2[:, h:h + 1], be2[:, h:h + 1]
                G = sq_r.next()[:n, :n]
                ts(G, CS_UINCL, gcol, ALU.mult)
                PA, PB, Pm, pm1 = pq.next(), pq.next(), pq.next(), pq.next()
                mm(PA[:n, :n], ones[:n, :n], G, True, False)
                mm(PA[:n, :n], idn, CS_BIGU, False, True)
                mm(PB[:n, :n], ones[:n, :n], G, True, False)
                mm(PB[:n, :n], idn, CS_NBIGL, False, True)
                mm(Pm[:, :n], ones[:n, :], G)
                mm(pm1[:n, 0:1], CS_BD, gcol)
                mm(pm1[:n, 1:2], CS_UINCL, gcol)
                cg, ncg, ecg, edl = (col_r.next()[:n] for _ in range(4))
                cp(cg, pm1[:n, 1:2], eng="act")
                ts(ncg, pm1[:n, 1:2], -1.0, ALU.mult)
                decay, decayT = sq_r.next()[:n, :n], sq_r.next()[:n, :n]
                act(decay, PA[:n, :n], AF.Exp, bias=cg, scale=-1.0)
                act(decayT, PB[:n, :n], AF.Exp, bias=ncg, scale=1.0)
                act(ecg, cg, AF.Exp)
                act(edl, pm1[:n, 0:1], AF.Exp, bias=ncg, scale=1.0)
                act(EGL, T(Pm.ap[:, 3:n:4], Pm.res), AF.Exp)
                KK, QKT = pq.next(), pq.next()
                mm(KK[:n, :n], kn, kn)
                mm(QKT[:n, :n], kn, qn)
                t0, Wm, AT = sq_r.next()[:n, :n], sq_r.next()[:n, :n], sq_r.next()[:n, :n]
                tt(t0, KK[:n, :n], decayT, ALU.mult)
                stt(Wm, t0, bcol, CS_SU, ALU.mult, ALU.mult)
                tt(AT, QKT[:n, :n], decayT, ALU.mult)
                WmTp = pq.next()
                tr(WmTp[:n, :n], Wm, idn)
                WmT = sq_r.next()[:n, :n]
                cp(WmT, WmTp[:n, :n], eng="act")
                Pk = sq_r.next()[:n, :n]
                tt(Pk, idn, Wm, ALU.subtract)
                p2 = pq.next()
                mm(p2[:n, :n], Wm, WmT)
                X2T = sq_r.next()[:n, :n]
                cp(X2T, p2[:n, :n], eng="dve")
                p3 = pq.next()
                mm(p3[:n, :n], X2T, Pk)
                W = sq_r.next()[:n, :n]
                tt(W, p3[:n, :n], Pk, ALU.add)
                dgb = sq_r.next()[:n, :n]
                ts(dgb, idn, bcol, ALU.mult)
                pbr = pq.next()
                mm(pbr[:n, :n], ones[:n, :n], dgb)
                Wb = sq_r.next()[:n, :n]
                tt(Wb, pbr[:n, :n], W, ALU.mult)
                cp(diagv(W3), T(Wb.ap.rearrange("p (s t) -> p s t", t=4), Wb.res), eng="pool")
                ktp, vtp, qtp = pq.next(), pq.next(), pq.next()
                tr(ktp[:n, :], kn, identf)
                tr(vtp[:n, :], vs_, identf)
                tr(qtp[:n, :], qn, identf)
                ke, kd, vtm, qetm = sq_r.next()[:n], sq_r.next()[:n], sq_r.next()[:n], sq_r.next()[:n]
                ts(ke, ktp[:n, :], ecg, ALU.mult)
                ts(kd, ktp[:n, :], edl, ALU.mult)
                cp(vtm, vtp[:n, :], eng="act")
                ts(qetm, qtp[:n, :], ecg, ALU.mult)
                u0p = pq.next()
                mm(u0p[:n, :], Wb, vtm)
                u0b = sq_r.next()[:n]
                cp(u0b, u0p[:n, :], eng="act")
                for half in range(2):
                    mm(pu[:, half, :], ke, W3[:, half * 512:(half + 1) * 512])
                cp(w0Tm, T(pu.ap[:, 0:2, :].rearrange("p a b -> p (a b)"), pu.res), eng="act")
                for half in range(2):
                    mm(pu[:, half, :], qetm, I3[:, half * 512:(half + 1) * 512])
                cp(qem, T(pu.ap[:, 0:2, :].rearrange("p a b -> p (a b)"), pu.res), eng="dve")
                pw = pq.next()
                for sq_i in range(16):
                    mm(pw[:n, :], w0Tm[:, sq_i * n:(sq_i + 1) * n], Sall[:, sq_i, :], sq_i == 0, sq_i == 15)
                vn = sq_r.next()[:n]
                tt(vn, u0b, pw[:n, :], ALU.subtract)
                po = pq.next()
                for sq_i in range(16):
                    mm(po[:n, :], qem[:, sq_i * n:(sq_i + 1) * n], Sall[:, sq_i, :], sq_i == 0, False)
                mm(po[:n, :], AT, vn, False, True)
                o = sq_r.next()[:n]
                cp(o, po[:n, :], eng="act")
                for sq_i in range(16):
                    kdm = sq_r.next()[:n]
                    ts(kdm, kd, RM[:, sq_i:sq_i + 1], ALU.mult, eng="pool")
                    pkv = pq.next()
                    mm(pkv, kdm, vn)
                    stt(Snew[:, sq_i, :], Sall[:, sq_i, :], EGL[:, sq_i:sq_i + 1], pkv, ALU.mult, ALU.add)
                P.dma("sp", T(nss[:, h].rearrange("s p d -> p s d"), None), Snew, store=True)
                osq, ms = sq_r.next()[:n], col_r.next()[:n]
                tt(osq, o, o, ALU.mult, eng="pool")
                rsum(ms, osq)
                act(ms, ms, AF.Ln, bias=epsc[:n], scale=1.0 / 128)
                act(ms, ms, AF.Exp, scale=-0.5)
                zh = zs[:, h * 128:(h + 1) * 128]
                ez, sz, og = sq_r.next()[:n], sq_r.next()[:n], sq_r.next()[:n]
                act(ez, zh, AF.Exp, scale=-1.0)
                ts(ez, ez, 1.0, ALU.add, eng="pool")
                recip(ez, ez)
                tt(sz, zh, ez, ALU.mult)
                stt(og, o, ms, gnw[:n], ALU.mult, ALU.mult)
                ogb = ogb_r.next()[:n]
                tt(ogb, og, sz, ALU.mult)
                oTp = ptrb.next()
                tr(oTp[:, :n], ogb, identb[:n, :n])
                cp(oBs[:, h, :], oTp[:, :n], eng="act")
            P.emit()

        with ExitStack() as st:
            tl = {"wst": ring(st, sbt, "wst", [128, 512], F32, 4)}
            NPAIR = int(os.environ.get('ATT_PAIRS', '4'))
            cst6 = sbt(st, "cst6", [128, 128], F32)
            P.dma("sp", cst6, T(cst_d[:, 6, :], None))
            trilb = sbt(st, "trilb", [128, 128], BF16)
            onesb = sbt(st, "onesb", [128, 128], BF16)
            cp(trilb, cst6)
            memset(onesb, 1.0)
            sbb = sbt(st, "sbb", [128, 8], F32)
            P.dma("sp", sbb, T(sbb_d, None))
            maskt = sbt(st, "maskt", [128, 16, 512], BF16)
            P.dma("sp", maskt, T(mask_d, None))
            KT = sbt(st, "KT", [128, SEQ], BF16)
            Vt = sbt(st, "Vt", [128, NT_FULL, 128], BF16)
            qT = sbt(st, "qT", [128, OWN], BF16)
            wqkv = sbt(st, "wqkv", [128, 8, 384], BF16)
            hTg_r = ring(st, sbt, "hTg3", [128, 8, 512], BF16, 2)
            pS = ring(st, pst, "pS", [128, 512], F32, 3)
            pproj = Ring([pS.tiles[0]])
            pR_r = ring(st, pst, "pR", [128, 512], F32, 2)
            pT_r = ring(st, pst, "pT", [128, 512], F32, 2)
            pO = pst(st, "pO", [128, 512], F32)
            e_r = ring(st, sbt, "e3", [128, 512], BF16, 4)
            sp_r = ring(st, sbt, "sp3", [128, 512], BF16, 3)
            Rt_r = ring(st, sbt, "Rt3", [128, 512], F32, 2)
            ex_r = ring(st, sbt, "ex3", [128, 512], BF16, 2)
            w_r = ring(st, sbt, "w3", [128, 512], BF16, 2)
            carry = [sbt(st, f"carry{i}", [128, 512], F32) for i in range(2)]
            for p in range(NPAIR):
                load_w(tl, wqkv[:, :, 0:128], w_in, p * 128, 128, scale=nmw_t)
                load_w(tl, wqkv[:, :, 128:256], w_in, 512 + p * 128, 128, scale=nmw_t)
                load_w(tl, wqkv[:, :, 256:384], w_in, 1024 + p * 128, 128, scale=nmw_t)
                for g in range(16):
                    hTg = hTg_r.next()
                    for t in range(4):
                        P.dma("sp", hTg[:, :, t * 128:(t + 1) * 128], T(hT_d[4 * g + t].rearrange("p (k n) -> p k n", k=8), None), nowaw=True)
                    pk = pproj.next()
                    for k in range(8):
                        mm(pk, wqkv[:, k, 128:256], hTg[:, k, :], k == 0, k == 7)
                    cp(KT[:, g * 512:(g + 1) * 512], pk, eng="act")
                    pv = pproj.next()
                    for t in range(4):
                        for k in range(8):
                            mm(pv[:, t * 128:(t + 1) * 128], hTg[:, k, t * 128:(t + 1) * 128], wqkv[:, k, 256:384], k == 0, k == 7)
                    cp(Vt[:, 4 * g:4 * g + 4, :], T(pv.ap.rearrange("p (t c) -> p t c", t=4), pv.res), eng="dve")
                for i in range(4):
                    hTg = hTg_r.next()
                    for t in range(4):
                        P.dma("sp", hTg[:, :, t * 128:(t + 1) * 128], T(hTown_d[4 * i + t].rearrange("p (k n) -> p k n", k=8), None), nowaw=True)
                    pq_ = pproj.next()
                    for k in range(8):
                        mm(pq_, wqkv[:, k, 0:128], hTg[:, k, :], k == 0, k == 7)
                    cp(qT[:, i * 512:(i + 1) * 512], pq_, eng="act")
                for hh in range(2):
                    h = 2 * p + hh
                    hs = slice(64 * hh, 64 * hh + 64)
                    for i in range(int(os.environ.get('ATT_SLOTS', '4'))):
                        nkb = 16 * i + 16
                        cur = 0
                        memset(carry[0], 0.0)
                        def front(KB):
                            Sp = pS.next()
                            masked = KB >= 16 * i
                            mm(Sp, KT[hs, KB * 128:(KB + 1) * 128], qT[hs, i * 512:(i + 1) * 512], True, not masked)
                            if masked:
                                mm(Sp, identb, maskt[:, KB - 16 * i, :], False, True)
                            e = e_r.next()
                            act(e, Sp, AF.Exp, bias=sbb[:, h:h + 1], scale=0.125)
                            sp = sp_r.next()
                            act(sp, e, AF.Ln, bias=onec, scale=1.0)
                            return (KB, e, sp)

                        def midstage(stg):
                            KB, e, sp = stg
                            pR, pT = pR_r.next(), pT_r.next()
                            mm(pR, trilb, sp)
                            mm(pT, onesb, sp)
                            return (KB, e, pR, pT)

                        def back(stg, idx, cur):
                            KB, e, pR, pT = stg
                            Rt = Rt_r.next()
                            tt(Rt, pR, carry[cur], ALU.add)
                            tt(carry[1 - cur], pT, carry[cur], ALU.add)
                            ex = ex_r.next()
                            act(ex, Rt, AF.Exp, scale=-1.0)
                            w = w_r.next()
                            tt(w, e, ex, ALU.mult)
                            mm(pO, Vt[:, KB, :], w, idx == 0, idx == nkb - 1)

                        order = list(range(nkb - 1, -1, -1))
                        fq, mq = [], []
                        done = 0
                        for KB in order:
                            fq.append(front(KB))
                            if len(fq) >= 2:
                                mq.append(midstage(fq.pop(0)))
                            if len(mq) >= 2:
                                back(mq.pop(0), done, cur)
                                cur = 1 - cur
                                done += 1
                        while fq:
                            mq.append(midstage(fq.pop(0)))
                            if len(mq) >= 2:
                                back(mq.pop(0), done, cur)
                                cur = 1 - cur
                                done += 1
                        while mq:
                            back(mq.pop(0), done, cur)
                            cur = 1 - cur
                            done += 1
                        cp(oAown[hs, p, i * 512:(i + 1) * 512], pO[hs, :], eng="act")
            P.emit()


        with ExitStack() as st:
            NSEQ = int(os.environ.get('SATT_SEQS', '16'))
            SST = int(os.environ.get('SATT_STAGE', '9'))
            n = NS_TOK
            cst6 = sbt(st, "cst6s", [128, 128], F32)
            P.dma("sp", cst6, T(cst_d[:, 6, :], None))
            trilb = sbt(st, "trilbs", [128, 128], BF16)
            onesb = sbt(st, "onesbs", [128, 128], BF16)
            cp(trilb, cst6)
            memset(onesb, 1.0)
            ptt = sbt(st, "ptt", [128, 256], I32)
            idx = sbt(st, "idx", [128, 256], I32)
            iot = sbt(st, "iot", [128, 256], I32)
            P.dma("sp", ptt, T(pt_d.partition_broadcast(128), None))
            P.dma("sp", iot, T(iota_d, None))
            P.op("dve", lambda e: e.tensor_scalar(out=idx.ap, in0=ptt.ap, scalar1=7, scalar2=None, op0=ALU.logical_shift_left), [ptt], [idx])
            P.op("dve", lambda e: e.tensor_tensor(out=idx.ap, in0=idx.ap, in1=iot.ap, op=ALU.bitwise_or), [idx, iot], [idx])
            sbrow = sbt(st, "sbrow", [128, 512], F32)
            P.dma("sp", sbrow, T(sbrow_d, None))
            mnew = sbt(st, "mnew", [128, 32], F32)
            P.dma("sp", mnew, T(mnew_d, None))
            selE = sbt(st, "selE", [64, 256], F32)
            P.dma("sp", selE, T(selE_d, None))
            dm = sbt(st, "dm", [32, 512], F32)
            P.dma("sp", dm, T(dm_d, None))
            qkv = sbt(st, "qkvs", [n, 1536], F32)
            P.dma("sp", qkv, T(ps_d[:, 0:1536], None))
            qsT = sbt(st, "qsT", [128, 4, n], BF16)
            ksT = sbt(st, "ksT", [128, 4, 256], BF16)
            memset(ksT, 0.0)
            ptk_r = ring(st, pst, "ptk", [128, 4, 128], F32, 2)
            pS_r = ring(st, pst, "pSs", [128, 512], F32, 2)
            pR = pst(st, "pRs", [128, 512], F32)
            pT = pst(st, "pTs", [128, 512], F32)
            pO = pst(st, "pOs", [128, 512], F32)
            psm = pst(st, "psm", [128, 512], F32)
            for p in range(4):
                ptk = ptk_r.next()
                tr(ptk[:, 0, :n], qkv[:, p * 128:(p + 1) * 128], identf[:n, :n])
                tr(ptk[:, 1, :n], qkv[:, 512 + p * 128:512 + (p + 1) * 128], identf[:n, :n])
                cp(qsT[:, p, :], ptk[:, 0, :n], eng="act")
                cp(ksT[:, p, 0:n], ptk[:, 1, :n], eng="dve")
            kpg_r = ring(st, sbt, "kpg", [128, 512], F32, 3)
            vpg_r = ring(st, sbt, "vpg", [128, 512], F32, 3)
            Vb_r = ring(st, sbt, "Vb", [128, 16, 512], BF16, 2)
            KTs_r = ring(st, sbt, "KTs", [128, 4, 128], BF16, 3)
            f_r = ring(st, sbt, "fs", [128, 512], F32, 5)
            b_r = ring(st, sbt, "bs", [128, 512], BF16, 3)
            C = sbt(st, "Cs", [128, 16, 32], F32)
            sm_r = ring(st, sbt, "sms", [128, 512], F32, 4)
            smb_r = ring(st, sbt, "smbs", [128, 512], BF16, 4)
            o2 = sbt(st, "o2s", [32, 2, 64], F32)
            for sq_i in range(NSEQ):
                Sp = pS_r.next()
                Vb = Vb_r.next()
                tsl = slice(4 * sq_i, 4 * sq_i + 4)
                for pg in range(16):
                    col = sq_i * 16 + pg
                    kpg, vpg = kpg_r.next(), vpg_r.next()
                    for dst, src in ((kpg, ck_d), (vpg, cv_d)):
                        def gfn(e, dst=dst, src=src, col=col):
                            return e.indirect_dma_start(out=dst.ap, out_offset=None, in_=src,
                                                        in_offset=bass.IndirectOffsetOnAxis(ap=idx.ap[:, col:col + 1], axis=0))
                        P._add("pool", gfn, [idx], [dst], dma=True, evres=dst.res)
                    cp(Vb[:, pg, :], vpg, eng=("dve", "act")[pg % 2])
                    if SST < 2:
                        tr(ptk_r.next()[:, 0, :], kpg[:, 0:128], identf)
                        continue
                    ptk = ptk_r.next()
                    for p in range(4):
                        tr(ptk[:, p, :], kpg[:, p * 128:(p + 1) * 128], identf)
                    KTs = KTs_r.next()
                    cp(KTs, ptk, eng=("act", "dve")[pg % 2])
                    if SST < 3:
                        continue
                    for p in range(4):
                        for hh in range(2):
                            hs = slice(64 * hh, 64 * hh + 64)
                            c0 = pg * 32 + (2 * p + hh) * 4
                            mm(Sp[:, c0:c0 + 4], KTs[hs, p, :], qsT[hs, p, tsl])
                if SST < 4:
                    continue
                for p in range(4):
                    for hh in range(2):
                        hs = slice(64 * hh, 64 * hh + 64)
                        c0 = (2 * p + hh) * 4
                        mm(psm[:, c0:c0 + 4], ksT[hs, p, 4 * sq_i:4 * sq_i + 128], qsT[hs, p, tsl])
                z, e = f_r.next(), f_r.next()
                stt(z, Sp, 0.125, sbrow, ALU.mult, ALU.add)
                act(e, z, AF.Exp)
                sp = b_r.next()
                act(sp, e, AF.Ln, bias=onec, scale=1.0)
                zn, en = sm_r.next(), sm_r.next()
                stt(zn[:, 0:32], psm[:, 0:32], 0.125, sbrow[:, 0:32], ALU.mult, ALU.add)
                act(en[:, 0:32], zn[:, 0:32], AF.Exp)
                tt(en[:, 0:32], en[:, 0:32], mnew, ALU.mult)
                spn = smb_r.next()
                act(spn[:, 0:32], en[:, 0:32], AF.Ln, bias=onec, scale=1.0)
                mm(pR, trilb, sp)
                mm(pT, onesb, sp)
                mm(psm[:, 64:96], onesb, spn[:, 0:32])
                mm(psm[:, 128:160], trilb, spn[:, 0:32])
                cp(C[:, 15, :], psm[:, 64:96], eng="dve")
                for pg in range(14, -1, -1):
                    tt(C[:, pg, :], pT[:, (pg + 1) * 32:(pg + 2) * 32], C[:, pg + 1, :], ALU.add)
                R, ex = f_r.next(), f_r.next()
                tt(R, pR, T(C.ap.rearrange("p a b -> p (a b)"), C.res), ALU.add)
                act(ex, R, AF.Exp, scale=-1.0)
                w = b_r.next()
                tt(w, e, ex, ALU.mult, eng="pool")
                exn = sm_r.next()
                act(exn[:, 0:32], psm[:, 128:160], AF.Exp, scale=-1.0)
                wn = smb_r.next()
                tt(wn[:, 0:32], en[:, 0:32], exn[:, 0:32], ALU.mult)
                if SST < 5:
                    continue
                mm(pO, selE[:, 4 * sq_i:4 * sq_i + 128], qkv[:, 1024:1536])
                v4 = smb_r.next()
                cp(v4, pO, eng="act")
                for pg in range(16):
                    mm(pR[0:32, :], w[:, pg * 32:(pg + 1) * 32], Vb[:, pg, :], pg == 0, False)
                mm(pR[0:32, :], wn[:, 0:32], v4, False, True)
                od = sm_r.next()[0:32]
                tt(od, pR[0:32, :], dm, ALU.mult)
                rsum(o2[:, 0, :], T(od.ap.rearrange("p (h d) -> p d h", h=8), od.res))
                cp(o2[:, 1, :], o2[:, 0, :], eng="pool")
                tr(psm[:, 256:288], T(o2.ap.rearrange("p a b -> p (a b)"), o2.res), identf[:32, :32])
                for hh in range(2):
                    hs = slice(64 * hh, 64 * hh + 64)
                    src = T(psm.ap[hs, 256:288].rearrange("p (a b c) -> p a b c", a=4, b=2)[:, :, hh, :], psm.res)
                    cp(oAs[hs, :, tsl], src, eng=("act", "dve")[hh])
            P.emit()

        with ExitStack() as st:
            tl = {"wst": ring(st, sbt, "wst", [128, 512], F32, 4),
                  "sq": ring(st, sbt, "sq4", [128, D], F32, 1),
                  "ss": ring(st, sbt, "ss4", [128, 1], F32, 4),
                  "hb": ring(st, sbt, "hb4", [128, D], BF16, 2),
                  "ptr": ring(st, pst, "ptr4", [128, 8, 128], BF16, 1)}
            w2 = sbt(st, "w2", [128, 8, 2048], BF16)
            for gi in range(4):
                load_w(tl, w2[:, :, gi * 512:(gi + 1) * 512], w_in, 3592 + gi * 512, 512, scale=nmw_t)
            wpa = sbt(st, "wpa", [128, 4, D], BF16)
            wpb = sbt(st, "wpb", [128, 4, D], BF16)
            wo = sbt(st, "wo", [128, 8, D], BF16)
            for half in range(2):
                load_w(tl, wpa[:, :, half * 512:(half + 1) * 512], w_pa_d, half * 512, 512, nk=4)
                load_w(tl, wpb[:, :, half * 512:(half + 1) * 512], w_pb_d, half * 512, 512, nk=4)
                load_w(tl, wo[:, :, half * 512:(half + 1) * 512], w_o_d, half * 512, 512, nk=8)
            hTg_r = ring(st, sbt, "hTg4", [128, 8, 512], BF16, 2)
            pg = ring(st, pst, "pg4", [128, 512], F32, 5)
            pmix = ring(st, pst, "pmix4", [128, 512], F32, 2)
            sg_r = ring(st, sbt, "sg4", [128, 512], F32, 4)
            mm_r = ring(st, sbt, "mm4", [128, 512], F32, 4)
            mT_r = ring(st, sbt, "mT4", [128, 8, 512], BF16, 2)
            x_r = ring(st, sbt, "x4", [128, D], F32, 3)
            hT_r = ring(st, sbt, "hTt4", [128, 8, 128], BF16, 2)
            groups = []
            for G in range(4):
                groups.append(dict(n=512, hT_src=[hTown_d[4 * G + t] for t in range(4)], hT_sb=None,
                                   oA=oAown[:, :, G * 512:(G + 1) * 512], oB=oBown[:, :, G * 512:(G + 1) * 512],
                                   x_rows=[xown[(4 * G + t) * 128:(4 * G + t + 1) * 128, :] for t in range(4)],
                                   row0=G * 512, tile0=4 * G))
            groups.append(dict(n=NS_TOK, hT_src=[], hT_sb=hsT, oA=oAs, oB=oBs, x_rows=[xs[:, :]], row0=OWN, tile0=NT_OWN))
            for gd in groups:
                n = gd["n"]
                if gd["hT_sb"] is not None:
                    hTg = gd["hT_sb"]
                else:
                    hTg = hTg_r.next()
                for t, src in enumerate(gd["hT_src"]):
                    P.dma("sp", hTg[:, :, t * 128:(t + 1) * 128], T(src.rearrange("p (k n) -> p k n", k=8), None), nowaw=True)
                mT = mT_r.next()
                for c in range(8):
                    cs = slice(c * 128, (c + 1) * 128)
                    pgA, pgB, pyA, pyB = pg.next(), pg.next(), pg.next(), pg.next()
                    for k in range(8):
                        mm(pgA[:, :n], w2[:, k, cs], hTg[:, k, :n], k == 0, k == 7)
                    for k in range(8):
                        mm(pgB[:, :n], w2[:, k, 1024 + c * 128:1024 + (c + 1) * 128], hTg[:, k, :n], k == 0, k == 7)
                    for k in range(4):
                        mm(pyA[:, :n], wpa[:, k, cs], gd["oA"][:, k, :], k == 0, k == 3)
                    for k in range(4):
                        mm(pyB[:, :n], wpb[:, k, cs], gd["oB"][:, k, :], k == 0, k == 3)
                    sgA, sgB = sg_r.next(), sg_r.next()
                    for sg_, pg_ in ((sgA, pgA), (sgB, pgB)):
                        act(sg_[:, :n], pg_[:, :n], AF.Exp, scale=-1.0)
                        ts(sg_[:, :n], sg_[:, :n], 1.0, ALU.add, eng="pool")
                        recip(sg_[:, :n], sg_[:, :n])
                    mA, mB = mm_r.next(), mm_r.next()
                    tt(mA[:, :n], pyA[:, :n], sgA[:, :n], ALU.mult)
                    tt(mB[:, :n], pyB[:, :n], sgB[:, :n], ALU.mult)
                    tt(mT[:, c, :n], mA[:, :n], mB[:, :n], ALU.add, eng="pool")
                for t, xr in enumerate(gd["x_rows"]):
                    nt = min(128, n - t * 128)
                    xt = x_r.next()
                    P.dma("sp", xt[:nt], T(xr, None))
                    for half in range(2):
                        pm = pmix.next()
                        for c in range(8):
                            mm(pm[:nt, :], mT[:, c, t * 128:t * 128 + nt], wo[:, c, half * 512:(half + 1) * 512], c == 0, c == 7)
                        tt(xt[:nt, half * 512:(half + 1) * 512], pm[:nt, :], xt[:nt, half * 512:(half + 1) * 512], ALU.add)
                    r0 = gd["row0"] + t * 128
                    P.dma("sp", T(x1_d[r0:r0 + nt, :], None), xt[:nt], store=True)
                    hTt = hT_r.next()
                    norm_core(tl, xt, hTt[:, :, :nt], nt)
                    P.dma("sp", T(hmT_d[gd["tile0"] + t].rearrange("p (k n) -> p k n", k=8)[:, :, :nt], None), hTt[:, :, :nt], store=True)
            P.emit()

        mid.close()
        with ExitStack() as st:
            tl = {"wst": ring(st, sbt, "wst", [128, 512], F32, 4)}
            nmlw = sbt(st, "nmlw", [128, 8], F32)
            P.dma("sp", nmlw, T(nmlw_d, None))
            nfw = sbt(st, "nfw", [128, D], F32)
            P.dma("sp", nfw, T(nfw_d, None))
            wup = sbt(st, "wup", [128, 8, 4 * D], BF16)
            wdn = sbt(st, "wdn", [128, 32, D], BF16)
            for gi in range(8):
                load_w(tl, wup[:, :, gi * 512:(gi + 1) * 512], w_up_d, gi * 512, 512, scale=nmlw)
            for half in range(2):
                load_w(tl, wdn[:, :, half * 512:(half + 1) * 512], w_down_d, half * 512, 512, nk=32)
            actT = sbt(st, "actT", [128, 32, 256], BF16)
            hm_r = ring(st, sbt, "hm5", [128, 8, 256], BF16, 2)
            pu5 = ring(st, pst, "pu5", [128, 512], F32, 3)
            pd5 = ring(st, pst, "pd5", [128, 512], F32, 2)
            tmp_r = ring(st, sbt, "tmp5", [128, 256], F32, 2)
            x_r = ring(st, sbt, "x5", [128, D], F32, 2)
            sq5 = sbt(st, "sq5", [128, D], F32)
            ss_r = ring(st, sbt, "ss5", [128, 1], F32, 4)
            y_r = ring(st, sbt, "y5", [128, D], F32, 2)
            subs = []
            for sg in range(8):
                subs.append(dict(n=256, tiles=[2 * sg, 2 * sg + 1], row0=sg * 256, out=y_own))
            subs.append(dict(n=NS_TOK, tiles=[NT_OWN], row0=OWN, out=y_s))
            for sd in subs:
                n = sd["n"]
                hm = hm_r.next()
                for t, tile in enumerate(sd["tiles"]):
                    nt = min(128, n - t * 128)
                    P.dma("sp", hm[:, :, t * 128:t * 128 + nt], T(hmT_d[tile].rearrange("p (k n) -> p k n", k=8)[:, :, :nt], None), nowaw=True)
                for f in range(32):
                    pu = pu5.next()
                    for k in range(8):
                        mm(pu[:, :n], wup[:, k, f * 128:(f + 1) * 128], hm[:, k, :n], k == 0, k == 7)
                    tmp = tmp_r.next()
                    ts(tmp[:, :n], pu[:, :n], 0.0, ALU.max)
                    tt(actT[:, f, :n], tmp[:, :n], tmp[:, :n], ALU.mult, eng="pool")
                for t in range(len(sd["tiles"])):
                    nt = min(128, n - t * 128)
                    r0 = sd["row0"] + t * 128
                    xt = x_r.next()
                    P.dma("sp", xt[:nt], T(x1_d[r0:r0 + nt, :], None))
                    for half in range(2):
                        pd = pd5.next()
                        for f in range(32):
                            mm(pd[:nt, :], actT[:, f, t * 128:t * 128 + nt], wdn[:, f, half * 512:(half + 1) * 512], f == 0, f == 31)
                        tt(xt[:nt, half * 512:(half + 1) * 512], pd[:nt, :], xt[:nt, half * 512:(half + 1) * 512], ALU.add)
                    tt(sq5[:nt], xt[:nt], xt[:nt], ALU.mult, eng="pool")
                    ss = ss_r.next()
                    rsum(ss[:nt], sq5[:nt])
                    act(ss[:nt], ss[:nt], AF.Ln, bias=epsc[:nt], scale=1.0 / D)
                    act(ss[:nt], ss[:nt], AF.Exp, scale=-0.5)
                    y = y_r.next()
                    stt(y[:nt], xt[:nt], ss[:nt, 0:1], nfw[:nt], ALU.mult, ALU.mult)
                    ro = sd["row0"] - (0 if sd["out"] is y_own else OWN) + t * 128
                    P.dma("sp", T(sd["out"][ro:ro + nt, :], None), y[:nt], store=True)
            P.emit()
    return nc


_NC = None


def kernel(x_prompt, x_sample, cache_k, cache_v, page_table, state_conv, state_ssm,
           norm_mix_w, w_in, sb_bias, conv_w, a_log, dt_bias, gdn_norm_w, w_pa, w_pb, w_o,
           norm_mlp_w, w_up, w_down, norm_final_w):
    global _NC
    f32 = np.float32
    x_prompt = np.asarray(x_prompt, f32)
    x_sample = np.asarray(x_sample, f32)
    nc = build()
    ident = np.eye(128, dtype=f32)
    nmw = np.ascontiguousarray(np.asarray(norm_mix_w, f32)[0].reshape(8, 128).T)
    w_in0 = np.ascontiguousarray(np.asarray(w_in, f32)[0])
    cst = np.zeros((128, NCST, 128), f32)
    ii = np.arange(128)
    r_, c_ = ii[:, None], ii[None, :]
    cst[:, 0] = (r_ <= c_)
    cst[:, 1] = BIG * (c_ > r_)
    cst[:, 2] = -BIG * (r_ > c_)
    cst[:, 3] = (r_ < c_)
    cst[:, 4] = ((r_ // 64) == (c_ // 64))
    cst[:, 5] = 1.0 - cst[:, 4]
    cst[:, 6] = (r_ >= c_)
    same = ((r_ // 4) == (c_ // 4))
    cst[:, 7] = (r_ <= c_) & same
    cst[:, 8] = BIG * ((c_ > r_) | ~same)
    cst[:, 9] = -BIG * ((r_ > c_) | ~same)
    cst[:, 10] = (r_ < c_) & same
    cst[:, 11] = same
    cst[:, 12] = ((r_ // 4) == c_)
    state_conv = np.asarray(state_conv, f32)
    ck = np.asarray(cache_k, f32).reshape(2560 * 128, 512)
    cv = np.asarray(cache_v, f32).reshape(2560 * 128, 512)
    page_table = np.asarray(page_table, np.int32)
    if os.environ.get('NPOOL_DBG'):
        npd = int(os.environ['NPOOL_DBG'])
        ck, cv, page_table = ck[:npd * 128], cv[:npd * 128], page_table % npd
    iota = np.ascontiguousarray(np.tile(np.arange(128, dtype=np.int32)[:, None], (1, 256)))
    sbrow = np.ascontiguousarray(np.tile(np.repeat(np.asarray(sb_bias, f32)[0], 4)[None, :], (128, 16)))
    mnew = np.zeros((128, 32), f32)
    selE = np.zeros((64, 256), f32)
    selE[np.arange(64), np.arange(64)] = 1.0
    for t1 in range(4):
        for hq in range(32):
            mnew[t1, hq] = 1.0 if t1 < (hq % 4) else 0.0
    dmm = np.zeros((32, 8, 64), f32)
    for hq in range(32):
        dmm[hq, hq // 4, :] = 1.0
    dmm = dmm.reshape(32, 512)
    state_ssm = np.asarray(state_ssm, f32)
    sbb = np.ascontiguousarray(np.tile(np.asarray(sb_bias, f32)[0][None, :], (128, 1)))
    masks = []
    for jj in range(4):
        m = np.zeros((128, 16, 512), f32)
        for r in range(4):
            for kb in range(4):
                kbrel = 4 * r + kb
                for qb in range(4):
                    qbrel = 4 * jj + qb
                    if kbrel < qbrel:
                        m[:, r * 4 + kb, qb * 128:(qb + 1) * 128] = 1.0
                    elif kbrel == qbrel:
                        m[:, r * 4 + kb, qb * 128:(qb + 1) * 128] = (r_ < c_)
        masks.append(((m - 1.0) * 30000.0).astype(ml_dtypes.bfloat16))
    cwl = np.ascontiguousarray(np.asarray(conv_w, f32)[0].reshape(4, 3, 4, 128).transpose(3, 1, 2, 0).reshape(128, 12, 4))
    alog_bc = np.ascontiguousarray(np.tile(np.asarray(a_log, f32)[0][None, :], (128, 1)))
    dtb_bc = np.ascontiguousarray(np.tile(np.asarray(dt_bias, f32)[0][None, :], (128, 1)))
    gnw_bc = np.ascontiguousarray(np.tile(np.asarray(gdn_norm_w, f32)[0][None, :], (128, 1)))
    nmlw = np.ascontiguousarray(np.asarray(norm_mlp_w, f32)[0].reshape(8, 128).T)
    nfw_bc = np.ascontiguousarray(np.tile(np.asarray(norm_final_w, f32)[None, :], (128, 1)))
    w_pa0 = np.ascontiguousarray(np.asarray(w_pa, f32)[0])
    w_pb0 = np.ascontiguousarray(np.asarray(w_pb, f32)[0])
    w_o0 = np.ascontiguousarray(np.asarray(w_o, f32)[0])
    w_up0 = np.ascontiguousarray(np.asarray(w_up, f32)[0])
    w_down0 = np.ascontiguousarray(np.asarray(w_down, f32)[0])
    in_maps = []
    own_idx = []
    for c in range(NCORE):
        b, j = c // 4, c % 4
        groups = [4 * i + j for i in range(4)]
        idx = np.concatenate([np.arange(512 * g, 512 * g + 512) for g in groups])
        own_idx.append(idx)
        in_maps.append({
            "xfull": np.ascontiguousarray(x_prompt[b]),
            "xown": np.ascontiguousarray(x_prompt[b][idx]),
            "xs": np.ascontiguousarray(x_sample[16 * c:16 * c + 16].reshape(64, D)),
            "w_in": w_in0, "nmw": nmw, "ident": ident, "cst": cst, "cw": cwl, "alog_bc": alog_bc, "dtb_bc": dtb_bc,
            "gnw_bc": gnw_bc, "w_pa": w_pa0, "w_pb": w_pb0, "w_o": w_o0, "w_up": w_up0, "w_down": w_down0,
            "nmlw": nmlw, "nfw_bc": nfw_bc, "sbb": sbb,
            "cache_k": ck, "cache_v": cv, "pt": np.ascontiguousarray(page_table[16 * c:16 * c + 16].reshape(1, 256)),
            "iota": iota, "sbrow": sbrow, "mnew": mnew, "dm": dmm, "selE": selE,
            "sc": np.ascontiguousarray(state_conv[0, 16 * c:16 * c + 16].reshape(48, 1536)),
            "ssm": np.ascontiguousarray(state_ssm[0, 16 * c:16 * c + 16]), "maskd": masks[j], "sel": np.ascontiguousarray(np.tile((np.arange(4) == j).astype(f32)[None, :], (128, 1))),
        })
    if os.environ.get('RETURN_MAPS') == '1':
        return nc, in_maps
    res = run_bass_kernel_spmd(nc, in_maps, core_ids=list(range(NCORE)))
    R = res.results
    y_prompt = np.zeros((2, SEQ, D), f32)
    y_sample = np.zeros((128, 4, D), f32)
    nkp = np.zeros((2, SEQ, 512), f32)
    nvp = np.zeros((2, SEQ, 512), f32)
    nksa = np.zeros((128, 4, 512), f32)
    nvsa = np.zeros((128, 4, 512), f32)
    ncpa = np.zeros((1, 2, 3, 1536), f32)
    ncsa = np.zeros((1, 128, 3, 1536), f32)
    nsp = np.zeros((1, 2, 4, 128, 128), f32)
    nss = np.zeros((1, 128, 4, 128, 128), f32)
    for c in range(NCORE):
        b, j = c // 4, c % 4
        r = R[c]
        y_prompt[b][own_idx[c]] = r["y_own"]
        y_sample[16 * c:16 * c + 16] = r["y_s"].reshape(16, 4, D)
        nkp[b][own_idx[c]] = r["nk_own"]
        nvp[b][own_idx[c]] = r["nv_own"]
        nksa[16 * c:16 * c + 16] = r["nks"].reshape(16, 4, 512)
        nvsa[16 * c:16 * c + 16] = r["nvs"].reshape(16, 4, 512)
        ncsa[0, 16 * c:16 * c + 16] = r["ncs"]
        nss[0, 16 * c:16 * c + 16] = r["nss"]
        if j == 0:
            ncpa[0, b] = r["ncp"]
            nsp[0, b] = r["nsp"]
    return (y_prompt, y_sample,
            nkp.reshape(1, 2, 64, 128, 8, 64), nvp.reshape(1, 2, 64, 128, 8, 64),
            nksa.reshape(1, 128, 4, 8, 64), nvsa.reshape(1, 128, 4, 8, 64),
            ncpa, ncsa, nsp, nss)
```

```python
import os
import numpy as np
import concourse.bass as bass
import concourse.mybir as mybir
from concourse.bass_utils import run_bass_kernel_spmd

F32 = mybir.dt.float32
BF16 = mybir.dt.bfloat16
I32 = mybir.dt.int32
AF = mybir.ActivationFunctionType
ALU = mybir.AluOpType
AX = mybir.AxisListType

EPOCH = 20000


class Res:
    _n = 0

    def __init__(self, name):
        Res._n += 1
        self.name = f"{name}#{Res._n}"
        self.writers = []
        self.readers = []
        self.dma_sem = None
        self.dma_cnt = 0
        self.pe_acc = False
        self.store_res = None


class T:
    def __init__(self, ap, res):
        self.ap = ap
        self.res = res

    def __getitem__(self, idx):
        return T(self.ap[idx], self.res)


class Op:
    __slots__ = ("eng", "idx", "fn", "waits", "needed", "dma_res", "clock", "semval", "is_dma")


class Prog:
    ENGS = ("pe", "act", "dve", "pool", "sp")

    def __init__(self, nc):
        self.nc = nc
        self.ops = {e: [] for e in self.ENGS}
        self.clock = {e: {} for e in self.ENGS}
        self.dma_res = []
        self.dma_clock = {}
        self._dom2res = {}
        self.all_res = []

    def res(self, name):
        r = Res(name)
        self.all_res.append(r)
        return r

    def _add(self, eng, fn, reads, writes, dma=False, pe_acc=False, evres=None, nowaw=False):
        reads = [r for r in reads if r is not None and r.res is not None]
        writes = [w for w in writes if w is not None and w.res is not None]
        op = Op()
        op.eng = eng
        op.idx = len(self.ops[eng]) + 1
        op.fn = fn
        op.needed = False
        op.is_dma = dma
        op.dma_res = None
        deps = []
        for r in reads:
            deps += r.res.writers
        for w in writes:
            if pe_acc and eng == "pe" and w.res.pe_acc and all(ev[0] == "pe" for ev in w.res.writers):
                deps += w.res.readers
            else:
                if not nowaw:
                    deps += w.res.writers
                deps += w.res.readers
        clk = self.clock[eng]
        waits = []
        best = {}
        for ev in deps:
            dom, val = ev[0], ev[1]
            if dom == eng and eng in ("pe", "sp"):
                continue
            if clk.get(dom, 0) >= val:
                continue
            if best.get(dom, 0) < val:
                best[dom] = val
        for dom, val in best.items():
            waits.append((dom, val))
            if isinstance(dom, str):
                src = self.ops[dom][val - 1]
                src.needed = True
                for d2, v2 in src.clock.items():
                    if clk.get(d2, 0) < v2:
                        clk[d2] = v2
            else:
                snap = self.dma_clock.get((dom, val), {})
                for d2, v2 in snap.items():
                    if clk.get(d2, 0) < v2:
                        clk[d2] = v2
            if clk.get(dom, 0) < val:
                clk[dom] = val
        op.waits = waits
        if dma:
            tgt = evres
            if tgt.dma_sem is None:
                tgt.dma_sem = True
                self.dma_res.append(tgt)
            tgt.dma_cnt += 1
            if eng == "pool":
                tgt.sw = True
            op.dma_res = tgt
            ev = (("dma", tgt.name), tgt.dma_cnt)
            self.dma_clock[ev] = dict(clk)
            self._dom2res[("dma", tgt.name)] = tgt
        else:
            ev = (eng, op.idx)
        op.clock = dict(clk)
        if not dma:
            op.clock[eng] = op.idx
        for r in reads:
            r.res.readers.append(ev)
        for w in writes:
            if pe_acc and eng == "pe" and w.res.pe_acc and all(e2[0] == "pe" for e2 in w.res.writers):
                w.res.writers = [ev]
            else:
                w.res.writers = [ev]
            w.res.readers = []
            w.res.pe_acc = bool(pe_acc and eng == "pe")
        self.ops[eng].append(op)
        return op

    def op(self, eng, fn, reads=(), writes=(), pe_acc=False):
        return self._add(eng, fn, list(reads), list(writes), pe_acc=pe_acc)

    def dma(self, eng, out, in_, store=False, nowaw=False, fn=None, **kw):
        if store:
            if in_.res.store_res is None:
                in_.res.store_res = Res("st_" + in_.res.name)
            evres = in_.res.store_res
        else:
            evres = out.res
        if fn is None:
            def fn(e, o=out.ap, i=in_.ap, kw=kw):
                return e.dma_start(out=o, in_=i, **kw)
        return self._add(eng, fn, [in_], [out], dma=True, evres=evres, nowaw=nowaw)

    def barrier(self):
        evs = []
        for e in self.ENGS:
            for op in reversed(self.ops[e]):
                if not op.is_dma and op.fn is not None:
                    evs.append((e, op.idx))
                    break
        for r in self.dma_res:
            if r.dma_cnt:
                evs.append((("dma", r.name), r.dma_cnt))
        bres = T(None, Res("barrier"))
        bres.res.writers = evs
        for e in self.ENGS:
            self._add(e, None, [bres], [])

    def setup_sems(self, stack, n_dma=70):
        nc = self.nc
        self.dma_pool_sw = [[stack.enter_context(nc.semaphore(f"dsw_{k}")), 0] for k in range(8)]
        self.sems = {e: [stack.enter_context(nc.semaphore(f"s_{e}_{k}")) for k in range(3)] for e in self.ENGS}
        self.count = {e: 0 for e in self.ENGS}
        self.dma_pool = [[stack.enter_context(nc.semaphore(f"d_{k}")), 0] for k in range(n_dma)]
        self.all_res = []

    def emit(self):
        nc = self.nc
        self.barrier()
        nsem = {}
        for e in self.ENGS:
            c = self.count[e]
            for op in self.ops[e]:
                if op.needed and not op.is_dma:
                    c += 1
                    op.semval = c
                else:
                    op.semval = None
            nsem[e] = c - self.count[e]
            self.count[e] = c
        ihw = isw = 0
        for r in self.dma_res:
            if getattr(r, "sw", False):
                slot = self.dma_pool_sw[isw]
                isw += 1
            else:
                slot = self.dma_pool[ihw]
                ihw += 1
            r.dma_sem = slot
            r.dma_base = slot[1]
            slot[1] += 16 * r.dma_cnt
        print("pass ops:", {e: len(self.ops[e]) for e in self.ENGS}, "flagged:", nsem, "dma sems:", len(self.dma_res), flush=True)
        if os.environ.get("DUMP_WAITS"):
            for e in self.ENGS:
                for op in self.ops[e][:60]:
                    ws = []
                    for dom, val in op.waits:
                        if isinstance(dom, str):
                            ws.append((dom, val, self.ops[dom][val - 1].semval))
                        else:
                            r = self._dom2res[dom]
                            ws.append((dom[1], val, r.dma_base + 16 * val))
                    print("  ", e, op.idx, "dma" if op.is_dma else "", "sem=%s" % op.semval, ws)
        sems = self.sems
        engobj = {"pe": "tensor", "act": "scalar", "dve": "vector", "pool": "gpsimd", "sp": "sync"}
        with nc.Block() as block:
            def make(e):
                def body(eng):
                    for op in self.ops[e]:
                        for dom, val in op.waits:
                            if isinstance(dom, str):
                                sv = self.ops[dom][val - 1].semval
                                k = (sv - 1) // EPOCH
                                eng.wait_ge(sems[dom][k], sv - k * EPOCH)
                            else:
                                r = self._dom2res[dom]
                                eng.wait_ge(r.dma_sem[0], r.dma_base + 16 * val)
                        if op.fn is None:
                            continue
                        ins = op.fn(eng)
                        if op.is_dma:
                            ins.then_inc(op.dma_res.dma_sem[0], 16)
                        elif op.semval is not None:
                            k = (op.semval - 1) // EPOCH
                            ins.then_inc(sems[e][k], 1)
                return body
            for e in self.ENGS:
                getattr(block, engobj[e])(make(e))
        self.ops = {e: [] for e in self.ENGS}
        self.clock = {e: {} for e in self.ENGS}
        for r in self.all_res:
            r.writers = []
            r.readers = []
            r.dma_sem = None
            r.dma_cnt = 0
            r.pe_acc = False
            r.store_res = None
            r.sw = False
        self.dma_res = []
        self.dma_clock = {}
        self._dom2res = {}


from contextlib import ExitStack
import os
import ml_dtypes

NCORE = 8
D = 1024
IN_DIM = 5640
SEQ = 8192
NT_FULL = SEQ // 128
OWN = 2048
NT_OWN = OWN // 128
NS_TOK = 64
NCST = 13
BIG = 300.0
EPS = 1e-6


class Ring:
    def __init__(self, tiles):
        self.tiles = tiles
        self.i = 0

    def next(self):
        t = self.tiles[self.i % len(self.tiles)]
        self.i += 1
        return t


def build():
    nc = bass.Bass("TRN2", target_bir_lowering=False)

    def din(name, shape, dt=F32):
        return nc.dram_tensor(name, list(shape), dt, kind="ExternalInput").ap()

    def dout(name, shape, dt=F32):
        return nc.dram_tensor(name, list(shape), dt, kind="ExternalOutput").ap()

    xfull = din("xfull", [SEQ, D])
    xown = din("xown", [OWN, D])
    xs = din("xs", [NS_TOK, D])
    w_in = din("w_in", [D, IN_DIM])
    nmw = din("nmw", [128, 8])
    ident_d = din("ident", [128, 128])
    cst_d = din("cst", [128, NCST, 128])
    cw_d = din("cw", [128, 12, 4])
    alog_d = din("alog_bc", [128, 4])
    dtb_d = din("dtb_bc", [128, 4])
    gnw_d = din("gnw_bc", [128, 128])
    sel_d = din("sel", [128, 4])
    sbb_d = din("sbb", [128, 8])
    mask_d = din("maskd", [128, 16, 512], BF16)
    sc_d = din("sc", [48, 1536])
    NPOOL = int(os.environ.get('NPOOL_DBG', '2560'))
    ck_d = din("cache_k", [NPOOL * 128, 512])
    cv_d = din("cache_v", [NPOOL * 128, 512])
    pt_d = din("pt", [1, 256], I32)
    iota_d = din("iota", [128, 256], I32)
    sbrow_d = din("sbrow", [128, 512])
    mnew_d = din("mnew", [128, 32])
    selE_d = din("selE", [64, 256])
    dm_d = din("dm", [32, 512])
    ssm_d = din("ssm", [16, 4, 128, 128])
    nss = dout("nss", [16, 4, 128, 128])
    nsp = dout("nsp", [4, 128, 128])
    hT_d = nc.dram_tensor("hT_d", [NT_FULL, 128, 1024], BF16, kind="Internal").ap()
    hTown_d = nc.dram_tensor("hTown_d", [NT_OWN, 128, 1024], BF16, kind="Internal").ap()
    ps_d = nc.dram_tensor("ps_d", [NS_TOK, IN_DIM], F32, kind="Internal").ap()
    x1_d = nc.dram_tensor("x1_d", [OWN + NS_TOK, D], F32, kind="Internal").ap()
    hmT_d = nc.dram_tensor("hmT_d", [NT_OWN + 1, 128, 1024], BF16, kind="Internal").ap()
    w_pa_d = din("w_pa", [512, D])
    w_pb_d = din("w_pb", [512, D])
    w_o_d = din("w_o", [D, D])
    w_up_d = din("w_up", [D, 4 * D])
    w_down_d = din("w_down", [4 * D, D])
    nmlw_d = din("nmlw", [128, 8])
    nfw_d = din("nfw_bc", [128, D])
    y_own = dout("y_own", [OWN, D])
    y_s = dout("y_s", [NS_TOK, D])

    nk_own = dout("nk_own", [OWN, 512])
    nv_own = dout("nv_own", [OWN, 512])
    nks = dout("nks", [NS_TOK, 512])
    nvs = dout("nvs", [NS_TOK, 512])
    ncp = dout("ncp", [3, 1536])
    ncs = dout("ncs", [16, 3, 1536])

    P = Prog(nc)
    with ExitStack() as gst:
        P.setup_sems(gst)

        uid = [0]

        def sbt(st, name, shape, dt):
            uid[0] += 1
            name = f"{name}_{uid[0]}"
            return T(st.enter_context(nc.sbuf_tensor(name, list(shape), dt))[:], P.res(name))

        def pst(st, name, shape, dt):
            uid[0] += 1
            name = f"{name}_{uid[0]}"
            return T(st.enter_context(nc.psum_tensor(name, list(shape), dt))[:], P.res(name))

        def ring(st, fn, name, shape, dt, n):
            return Ring([fn(st, f"{name}{i}", shape, dt) for i in range(n)])

        identf = sbt(gst, "identf", [128, 128], F32)
        identb = sbt(gst, "identb", [128, 128], BF16)
        epsc = sbt(gst, "epsc", [128, 1], F32)
        nmw_t = sbt(gst, "nmw_t", [128, 8], F32)
        hsT = sbt(gst, "hsT", [128, 8, NS_TOK], BF16)
        hTlast = sbt(gst, "hTlast", [128, 8, 128], BF16)
        onec = sbt(gst, "onec", [128, 1], F32)
        sel_t = sbt(gst, "sel_t", [128, 4], F32)


        def apx(v):
            return v.ap if isinstance(v, T) else v

        def mm(out, lhsT, rhs, start=True, stop=True):
            P.op("pe", lambda e: e.matmul(out=out.ap, lhsT=lhsT.ap, rhs=rhs.ap, start=start, stop=stop), [lhsT, rhs], [out], pe_acc=True)

        def tr(out, in_, idn):
            P.op("pe", lambda e: e.transpose(out=out.ap, in_=in_.ap, identity=idn.ap), [in_, idn], [out], pe_acc=True)

        def act(out, in_, func, bias=None, scale=None):
            kw = {}
            rd = [in_]
            if bias is not None:
                kw["bias"] = apx(bias)
                if isinstance(bias, T):
                    rd.append(bias)
            if scale is not None:
                kw["scale"] = apx(scale)
                if isinstance(scale, T):
                    rd.append(scale)
            P.op("act", lambda e: e.activation(out=out.ap, in_=in_.ap, func=func, **kw), rd, [out])

        def ts(out, in0, s1, op0, s2=None, op1=None, eng="dve"):
            rd = [in0] + [v for v in (s1, s2) if isinstance(v, T)]
            if op1 is None:
                P.op(eng, lambda e: e.tensor_scalar(out=out.ap, in0=in0.ap, scalar1=apx(s1), scalar2=None, op0=op0), rd, [out])
            else:
                P.op(eng, lambda e: e.tensor_scalar(out=out.ap, in0=in0.ap, scalar1=apx(s1), scalar2=apx(s2), op0=op0, op1=op1), rd, [out])

        def tt(out, in0, in1, op, eng="dve"):
            P.op(eng, lambda e: e.tensor_tensor(out=out.ap, in0=in0.ap, in1=in1.ap, op=op), [in0, in1], [out])

        def stt(out, in0, scalar, in1, op0, op1):
            rd = [in0, in1] + ([scalar] if isinstance(scalar, T) else [])
            P.op("dve", lambda e: e.scalar_tensor_tensor(out=out.ap, in0=in0.ap, scalar=apx(scalar), in1=in1.ap, op0=op0, op1=op1), rd, [out])

        def cp(out, in_, eng="dve"):
            if eng == "act":
                P.op("act", lambda e: e.copy(out=out.ap, in_=in_.ap), [in_], [out])
            else:
                P.op(eng, lambda e: e.tensor_copy(out=out.ap, in_=in_.ap), [in_], [out])

        def recip(out, in_):
            P.op("dve", lambda e: e.reciprocal(out=out.ap, in_=in_.ap), [in_], [out])

        def rsum(out, in_):
            P.op("dve", lambda e: e.reduce_sum(out=out.ap, in_=in_.ap, axis=AX.X), [in_], [out])

        def memset(t, v, eng="pool"):
            P.op(eng, lambda e: e.memset(t.ap, v), [], [t])

        def sub(t, idx, name):
            return T(t.ap[idx], P.res(name))

        def norm_tile(tl, x_rows_ap, hT_dst, nt=128):
            xt = tl["x"].next()
            P.dma("sp", xt[:nt], T(x_rows_ap, None))
            norm_core(tl, xt, hT_dst, nt)

        def norm_core(tl, xt, hT_dst, nt=128):
            sq = tl["sq"].next()
            P.op("act", lambda e: e.activation(out=sq.ap[:nt], in_=xt.ap[:nt], func=AF.Square), [xt], [sq])
            ss = tl["ss"].next()
            P.op("dve", lambda e: e.reduce_sum(out=ss.ap[:nt], in_=sq.ap[:nt], axis=AX.X), [sq], [ss])
            P.op("act", lambda e: e.activation(out=ss.ap[:nt], in_=ss.ap[:nt], func=AF.Ln, bias=epsc.ap[:nt], scale=1.0 / D), [ss, epsc], [ss])
            P.op("act", lambda e: e.activation(out=ss.ap[:nt], in_=ss.ap[:nt], func=AF.Exp, scale=-0.5), [ss], [ss])
            hb = tl["hb"].next()
            P.op("dve", lambda e: e.tensor_scalar(out=hb.ap[:nt], in0=xt.ap[:nt], scalar1=ss.ap[:nt, 0:1], scalar2=None, op0=ALU.mult), [xt, ss], [hb])
            pt = tl["ptr"].next()
            for k in range(8):
                P.op("pe", lambda e, k=k: e.transpose(out=pt.ap[:, k, :nt], in_=hb.ap[:nt, k * 128:(k + 1) * 128], identity=identb.ap[:nt, :nt]), [hb, identb], [pt], pe_acc=True)
            P.op("act", lambda e: e.copy(out=hT_dst.ap, in_=pt.ap[:, :, :nt]), [pt], [hT_dst])

        def load_w(tl, dst, w_dram, c0, n, scale=None, nk=8):
            for k in range(nk):
                stg = tl["wst"].next()
                P.dma("sp", stg[:, :n], T(w_dram[k * 128:(k + 1) * 128, c0:c0 + n], None))
                eng = ("pool", "dve")[k % 2]
                if scale is not None:
                    P.op(eng, lambda e, k=k, stg=stg: e.tensor_scalar(out=dst.ap[:, k, :n], in0=stg.ap[:, :n], scalar1=scale.ap[:, k:k + 1], scalar2=None, op0=ALU.mult), [stg, scale], [dst])
                else:
                    P.op(eng, lambda e, k=k, stg=stg: e.tensor_copy(out=dst.ap[:, k, :n], in_=stg.ap[:, :n]), [stg], [dst])

        with ExitStack() as st:
            tl = {
                "x": ring(st, sbt, "x", [128, D], F32, 3),
                "sq": ring(st, sbt, "sq", [128, D], F32, 2),
                "ss": ring(st, sbt, "ss", [128, 1], F32, 4),
                "hb": ring(st, sbt, "hb", [128, D], BF16, 2),
                "ptr": ring(st, pst, "ptr", [128, 8, 128], BF16, 2),
            }
            hTt = ring(st, sbt, "hTt", [128, 8, 128], BF16, 3)
            P.dma("sp", identf, T(ident_d, None))
            P.dma("sp", nmw_t, T(nmw, None))
            P.op("pool", lambda e: e.memset(epsc.ap, EPS), [], [epsc])
            P.op("pool", lambda e: e.memset(onec.ap, 1.0), [], [onec])
            P.dma("sp", sel_t, T(sel_d, None))
            P.op("dve", lambda e: e.tensor_copy(out=identb.ap, in_=identf.ap), [identf], [identb])
            for t in sorted(set(list(range(int(os.environ.get('NT0', NT_FULL)))) + [NT_FULL - 1])):
                dst = hTt.next() if t < NT_FULL - 1 else hTlast
                norm_tile(tl, xfull[t * 128:(t + 1) * 128, :], dst)
                P.dma("sp", T(hT_d[t].rearrange("p (k n) -> p k n", k=8), None), dst, store=True)
            for t in range(min(NT_OWN, int(os.environ.get('NT0', NT_OWN)))):
                dst = hTt.next()
                norm_tile(tl, xown[t * 128:(t + 1) * 128, :], dst)
                P.dma("sp", T(hTown_d[t].rearrange("p (k n) -> p k n", k=8), None), dst, store=True)
            norm_tile(tl, xs[:, :], hsT, nt=NS_TOK)
            P.emit()

        for _skip in ([] if os.environ.get('SKIP1') == '1' else [0]):
          with ExitStack() as st:
              tl = {"wst": ring(st, sbt, "wst", [128, 512], F32, 4)}
              wg = ring(st, sbt, "wg", [128, 8, 512], BF16, 2)
              wkv = sbt(st, "wkv", [128, 8, 1024], BF16)
              pmm = ring(st, pst, "pmm", [128, 512], F32, 4)
              ob = ring(st, sbt, "ob", [128, 1024], F32, 3)
              cps = sbt(st, "cps", [3, 1536], F32)
              ps_s = sbt(st, "ps_s", [NS_TOK, IN_DIM], F32)
              hTo_r = ring(st, sbt, "hTo", [128, 8, 128], BF16, 3)
              ngrp = (IN_DIM + 511) // 512
              for gi in range(ngrp):
                  c0 = gi * 512
                  n = min(512, IN_DIM - c0)
                  w = wg.next()
                  load_w(tl, w, w_in, c0, n, scale=nmw_t)
                  pm = pmm.next()
                  for k in range(8):
                      P.op("pe", lambda e, k=k, w=w, pm=pm, n=n: e.matmul(out=pm.ap[:NS_TOK, :n], lhsT=hsT.ap[:, k, :], rhs=w.ap[:, k, :n], start=(k == 0), stop=(k == 7)), [hsT, w], [pm], pe_acc=True)
                  P.op("act", lambda e, pm=pm, c0=c0, n=n: e.copy(out=ps_s.ap[:, c0:c0 + n], in_=pm.ap[:NS_TOK, :n]), [pm], [ps_s])
                  if gi in (1, 2):
                      P.op("pool", lambda e, w=w, gi=gi: e.tensor_copy(out=wkv.ap[:, :, (gi - 1) * 512:gi * 512], in_=w.ap), [w], [wkv])
                  if gi in (3, 4, 5):
                      pm2 = pmm.next()
                      for k in range(8):
                          P.op("pe", lambda e, k=k, w=w, pm2=pm2: e.matmul(out=pm2.ap[:3, :], lhsT=hTlast.ap[:, k, 125:128], rhs=w.ap[:, k, :], start=(k == 0), stop=(k == 7)), [hTlast, w], [pm2], pe_acc=True)
                      P.op("act", lambda e, pm2=pm2, gi=gi: e.copy(out=cps.ap[:, (gi - 3) * 512:(gi - 2) * 512], in_=pm2.ap[:3, :]), [pm2], [cps])
              P.dma("sp", T(ncp, None), cps, store=True)
              P.dma("sp", T(nks, None), ps_s[:, 512:1024], store=True)
              P.dma("sp", T(nvs, None), ps_s[:, 1024:1536], store=True)
              for r in range(3):
                  P.dma("sp", T(ncs[:, r, :], None), T(ps_s.ap[r + 1:NS_TOK:4, 1536:3072], ps_s.res), store=True)
              P.dma("sp", T(ps_d, None), ps_s, store=True)
              for t in range(NT_OWN):
                  o = ob.next()
                  hTo = hTo_r.next()
                  P.dma("sp", hTo, T(hTown_d[t].rearrange("p (k n) -> p k n", k=8), None))
                  for half in range(2):
                      pm = pmm.next()
                      for k in range(8):
                          P.op("pe", lambda e, k=k, pm=pm, half=half, hTo=hTo: e.matmul(out=pm.ap, lhsT=hTo.ap[:, k, :], rhs=wkv.ap[:, k, half * 512:(half + 1) * 512], start=(k == 0), stop=(k == 7)), [hTo, wkv], [pm], pe_acc=True)
                      P.op(("act", "dve")[half], lambda e, pm=pm, half=half, o=o: (e.copy if half == 0 else e.tensor_copy)(out=o.ap[:, half * 512:(half + 1) * 512], in_=pm.ap), [pm], [o])
                  P.dma("sp", T(nk_own[t * 128:(t + 1) * 128, :], None), o[:, 0:512], store=True)
                  P.dma("sp", T(nv_own[t * 128:(t + 1) * 128, :], None), o[:, 512:1024], store=True)
              P.emit()

        mid = ExitStack()
        oBown = sbt(mid, "oBown", [128, 4, OWN], BF16)
        oAown = sbt(mid, "oAown", [128, 4, OWN], BF16)
        oBs = sbt(mid, "oBs", [128, 4, NS_TOK], BF16)
        oAs = sbt(mid, "oAs", [128, 4, NS_TOK], BF16)
        with ExitStack() as st:
            memset(oBown, 0.0)
            memset(oAown, 0.0)
            memset(oAs, 0.0)
            tl = {"wst": ring(st, sbt, "wst", [128, 512], F32, 4)}
            cst = sbt(st, "cst", [128, NCST, 128], F32)
            P.dma("sp", cst, T(cst_d, None))
            C_UINCL, C_BIGU, C_NBIGL, C_SU, C_BD, C_OFFM = (cst[:, i, :] for i in range(6))
            ones = sbt(st, "ones", [128, 128], F32)
            memset(ones, 1.0)
            cw = sbt(st, "cw", [128, 12, 4], F32)
            P.dma("sp", cw, T(cw_d, None))
            negA = sbt(st, "negA", [128, 4], F32)
            dtb = sbt(st, "dtb", [128, 4], F32)
            gnw = sbt(st, "gnw", [128, 128], F32)
            P.dma("sp", negA, T(alog_d, None))
            P.dma("sp", dtb, T(dtb_d, None))
            P.dma("sp", gnw, T(gnw_d, None))
            act(negA, negA, AF.Exp)
            ts(negA, negA, -1.0, ALU.mult)
            wgd = sbt(st, "wgd", [128, 8, 2056], BF16)
            for gi in range(4):
                load_w(tl, wgd[:, :, gi * 512:(gi + 1) * 512], w_in, 1536 + gi * 512, 512, scale=nmw_t)
            load_w(tl, wgd[:, :, 2048:2056], w_in, 3584, 8, scale=nmw_t)
            S = [[sbt(st, f"S{h}_{i}", [128, 128], F32) for i in range(2)] for h in range(4)]
            for h in range(4):
                memset(S[h][0], 0.0)
            ub_r = ring(st, sbt, "ub", [128, 3, 515], F32, 2)
            hist = [sbt(st, f"hist{h}", [128, 3, 3], F32) for h in range(4)]
            for h in range(4):
                memset(hist[h], 0.0)
            hTg_r = ring(st, sbt, "hTg", [128, 8, 512], BF16, 2)
            pbank = [pst(st, f"pb{i}", [128, 4, 128], F32) for i in range(3)]
            pu = pst(st, "pu", [128, 3, 512], F32)
            pss = pst(st, "pss", [128, 512], F32)
            pq = Ring([T(pbank[i % 3].ap[:, (i // 3) % 4, :], pbank[i % 3].res) for i in range(12)])
            sq_r = Ring([sbt(st, f"sq{i}", [128, 128], F32) for i in range(32)])
            col_r = Ring([sbt(st, f"col{i}", [128, 1], F32) for i in range(32)])
            big_r = Ring([sbt(st, f"big{i}", [128, 3, 512], F32) for i in range(3)])
            nrm_r = Ring([sbt(st, f"nrm{i}", [128, 512], F32) for i in range(4)])
            abg = sbt(st, "abg", [128, 4, 8], F32)
            gb_r = Ring([sbt(st, f"gb{i}", [128, 4, 4], F32) for i in range(8)])
            ogb_r = Ring([sbt(st, f"ogb{i}", [128, 128], BF16) for i in range(2)])
            ptrb = ring(st, pst, "ptrb", [128, 128], BF16, 1)
            l2e = sbt(st, "l2e", [128, 1], F32)
            memset(l2e, 1e-6)
            dtb4 = sbt(st, "dtb4", [128, 4, 4], F32)
            negA4 = sbt(st, "negA4", [128, 4, 4], F32)
            for c in range(4):
                cp(dtb4[:, c, :], dtb, eng="pool")
                cp(negA4[:, c, :], negA, eng="pool")

            STAGE = int(os.environ.get('GDN_STAGE', '99'))

            def gdn_chunk(h, qT, kT, vT, gcol, bcol, nbcol, hTc, own_dst, selcol, Sin, Sout):
                G = sq_r.next()
                ts(G, C_UINCL, gcol, ALU.mult)
                PA, PB, pm1 = pq.next(), pq.next(), pq.next()
                mm(PA, ones, G, True, False)
                mm(PA, identf, C_BIGU, False, True)
                mm(PB, ones, G, True, False)
                mm(PB, identf, C_NBIGL, False, True)
                mm(pm1[:, 0:1], ones, G[:, 127:128])
                mm(pm1[:, 1:2], C_UINCL, gcol)
                cg, ncg, ecg, egl, edl = (col_r.next() for _ in range(5))
                cp(cg, pm1[:, 1:2], eng="act")
                ts(ncg, pm1[:, 1:2], -1.0, ALU.mult)
                decay, decayT = sq_r.next(), sq_r.next()
                act(decay, PA, AF.Exp, bias=cg, scale=-1.0)
                act(decayT, PB, AF.Exp, bias=ncg, scale=1.0)
                act(ecg, cg, AF.Exp)
                act(egl, pm1[:, 0:1], AF.Exp)
                act(edl, pm1[:, 0:1], AF.Exp, bias=ncg, scale=1.0)
                if STAGE < 4:
                    return None
                KK, QKT = pq.next(), pq.next()
                mm(KK, kT, kT)
                mm(QKT, kT, qT)
                t0, Wm, AT = sq_r.next(), sq_r.next(), sq_r.next()
                tt(t0, KK, decayT, ALU.mult)
                stt(Wm, t0, bcol, C_SU, ALU.mult, ALU.mult)
                tt(AT, QKT, decayT, ALU.mult)
                WmTp = pq.next()
                tr(WmTp, Wm, identf)
                WmT = sq_r.next()
                cp(WmT, WmTp, eng="act")
                if STAGE < 5:
                    return None
                X, XT, OffT, Pk = sq_r.next(), sq_r.next(), sq_r.next(), sq_r.next()
                tt(X, Wm, C_BD, ALU.mult)
                tt(XT, WmT, C_BD, ALU.mult)
                tt(OffT, WmT, C_OFFM, ALU.mult)
                tt(Pk, identf, X, ALU.subtract)
                for lvl in range(1, 6):
                    p2 = pq.next()
                    mm(p2, X, XT)
                    X2T = sq_r.next()
                    if lvl < 5:
                        p1 = pq.next()
                        mm(p1, XT, X)
                        X2 = sq_r.next()
                        cp(X2, p1, eng="act")
                    cp(X2T, p2, eng="dve")
                    p3 = pq.next()
                    mm(p3, X2T, Pk)
                    Pn = sq_r.next()
                    tt(Pn, p3, Pk, ALU.add)
                    Pk = Pn
                    XT = X2T
                    if lvl < 5:
                        X = X2
                Pd = Pk
                PdTp = pq.next()
                tr(PdTp, Pd, identf)
                PdT = sq_r.next()
                cp(PdT, PdTp, eng="act")
                t1p = pq.next()
                mm(t1p, OffT, Pd)
                t1 = sq_r.next()
                cp(t1, t1p, eng="dve")
                w2p = pq.next()
                mm(w2p, PdT, t1)
                W = sq_r.next()
                tt(W, Pd, w2p, ALU.subtract)
                if STAGE < 6:
                    return None
                ktp, vtp = pq.next(), pq.next()
                tr(ktp, kT, identf)
                tr(vtp, vT, identf)
                ke, kd, vtm = sq_r.next(), sq_r.next(), sq_r.next()
                ts(ke, ktp, ecg, ALU.mult)
                ts(kd, ktp, edl, ALU.mult)
                cp(vtm, vtp, eng="act")
                u0p, w0Tp = pq.next(), pq.next()
                mm(u0p, W, vtm)
                mm(w0Tp, ke, W)
                u0b, w0T = sq_r.next(), sq_r.next()
                ts(u0b, u0p, bcol, ALU.mult)
                cp(w0T, w0Tp, eng="act")
                if STAGE < 7:
                    return None
                wSp = pq.next()
                mm(wSp, w0T, Sin)
                vn = sq_r.next()
                stt(vn, wSp, nbcol, u0b, ALU.mult, ALU.add)
                qSp, Avp = pq.next(), pq.next()
                mm(qSp, qT, Sin)
                mm(Avp, AT, vn)
                Av, o = sq_r.next(), sq_r.next()
                cp(Av, Avp, eng="act")
                stt(o, qSp, ecg, Av, ALU.mult, ALU.add)
                KVp = pq.next()
                mm(KVp, kd, vn)
                stt(Sout, Sin, egl, KVp, ALU.mult, ALU.add)
                if STAGE < 8:
                    return None
                zp = pq.next()
                for k in range(8):
                    mm(zp, hTc[:, k, :], wgd[:, k, 1536 + h * 128:1536 + (h + 1) * 128], k == 0, k == 7)
                osq, ms = sq_r.next(), col_r.next()
                tt(osq, o, o, ALU.mult, eng="pool")
                rsum(ms, osq)
                act(ms, ms, AF.Ln, bias=epsc, scale=1.0 / 128)
                act(ms, ms, AF.Exp, scale=-0.5)
                ez, sz, og = sq_r.next(), sq_r.next(), sq_r.next()
                act(ez, zp, AF.Exp, scale=-1.0)
                ts(ez, ez, 1.0, ALU.add, eng="pool")
                recip(ez, ez)
                tt(sz, zp, ez, ALU.mult)
                stt(og, o, ms, gnw, ALU.mult, ALU.mult)
                ogb = ogb_r.next()
                tt(ogb, og, sz, ALU.mult)
                return ogb

            for g in range(int(os.environ.get('GDN_G', '16'))):
                hTg = hTg_r.next()
                for t in range(4):
                    P.dma("sp", hTg[:, :, t * 128:(t + 1) * 128], T(hT_d[4 * g + t].rearrange("p (k n) -> p k n", k=8), None), nowaw=True)
                SUB = int(os.environ.get('GDN_SUB', '9'))
                for c in range(4 if SUB >= 1 else 0):
                    abp = pq.next()
                    for k in range(8):
                        mm(abp[:, 0:8], hTg[:, k, c * 128:(c + 1) * 128], wgd[:, k, 2048:2056], k == 0, k == 7)
                    if os.environ.get('NOCP') != '1':
                        cp(abg[:, c, :], abp[:, 0:8], eng=os.environ.get("CPENG", "act"))
                xa, gg, eb, beta, nbeta = (gb_r.next() for _ in range(5))
                if SUB >= 2:
                    tt(xa, abg[:, :, 0:4], dtb4, ALU.add)
                if SUB >= 3:
                    act(xa, xa, AF.Exp)
                    act(xa, xa, AF.Ln, bias=onec, scale=1.0)
                if SUB >= 4:
                    tt(gg, xa, negA4, ALU.mult)
                    act(eb, abg[:, :, 4:8], AF.Exp, scale=-1.0)
                    ts(eb, eb, 1.0, ALU.add)
                if SUB >= 5:
                    recip(beta, eb)
                    ts(nbeta, beta, -1.0, ALU.mult)
                for h in range(int(os.environ.get('GDN_H', '4')) if STAGE >= 2 else 0):
                    for part in range(3):
                        for k in range(8):
                            mm(pu[:, part, :], wgd[:, k, part * 512 + h * 128:part * 512 + (h + 1) * 128], hTg[:, k, :], k == 0, k == 7)
                    ub = ub_r.next()
                    cp(ub[:, :, 0:3], hist[h], eng="pool")
                    cp(ub[:, :, 3:515], pu, eng="act")
                    cp(hist[h], ub[:, :, 512:515], eng="pool")
                    y = big_r.next()
                    for part in range(3):
                        ci = part * 4 + h
                        ts(y[:, part, :], ub[:, part, 0:512], cw[:, ci, 0:1], ALU.mult)
                        for w in range(1, 4):
                            stt(y[:, part, :], ub[:, part, w:w + 512], cw[:, ci, w:w + 1], y[:, part, :], ALU.mult, ALU.add)
                    e = big_r.next()
                    act(e, y, AF.Exp, scale=-1.0)
                    ts(e, e, 1.0, ALU.add, eng="pool")
                    recip(e, e)
                    sl = big_r.next()
                    tt(sl, y, e, ALU.mult, eng="pool")
                    sqq = e
                    tt(sqq[:, 0:2, :], sl[:, 0:2, :], sl[:, 0:2, :], ALU.mult, eng="pool")
                    qn, kn = nrm_r.next(), nrm_r.next()
                    for part, dst in ((0, qn), (1, kn)):
                        mm(pss, ones, sqq[:, part, :])
                        rs = nrm_r.next()
                        act(rs, pss, AF.Ln, bias=l2e, scale=1.0)
                        act(rs, rs, AF.Exp, scale=-0.5)
                        if part == 0:
                            stt(dst, sl[:, 0, :], 128.0 ** -0.5, rs, ALU.mult, ALU.mult)
                        else:
                            tt(dst, sl[:, 1, :], rs, ALU.mult)
                    for c in range(4 if STAGE >= 3 else 0):
                        cs = slice(c * 128, (c + 1) * 128)
                        nchunk = 4 * g + c
                        Sin, Sout = S[h][nchunk % 2], S[h][(nchunk + 1) % 2]
                        ogb = gdn_chunk(h, qn[:, cs], kn[:, cs], sl[:, 2, cs], gg[:, c, h:h + 1], beta[:, c, h:h + 1], nbeta[:, c, h:h + 1],
                                        hTg[:, :, cs], None, None, Sin, Sout)
                        if ogb is None:
                            continue
                        oTp = ptrb.next()
                        tr(oTp, ogb, identb)
                        dst = oBown[:, h, (g // 4) * 512 + c * 128:(g // 4) * 512 + (c + 1) * 128]
                        stt(dst, oTp, sel_t[:, (g % 4):(g % 4) + 1], dst, ALU.mult, ALU.add)
            for h in range(4):
                P.dma("sp", T(nsp[h], None), S[h][(NT_FULL) % 2], store=True)

            def flat(t, rows, c0, c1):
                return T(t.ap.rearrange("p a b -> p (a b)")[:rows, c0:c1], t.res)

            def bview(t, pat, **kw):
                return T(t.ap.rearrange(pat, **kw), t.res)

            n = NS_TOK
            CS_UINCL, CS_BIGU, CS_NBIGL, CS_SU, CS_BD = (cst[:n, 7 + i, :n] for i in range(5))
            RM = cst[:n, 12, 0:16]
            idn = identf[:n, :n]
            usb, scb, zb = big_r.next(), big_r.next(), big_r.next()
            us = flat(usb, n, 0, 1536)
            scs = flat(scb, 48, 0, 1536)
            zs = flat(zb, n, 0, 512)
            abs_ = flat(zb, n, 512, 520)
            P.dma("sp", us, T(ps_d[:, 1536:3072], None))
            P.dma("sp", scs, T(sc_d, None))
            P.dma("sp", zs, T(ps_d[:, 3072:3584], None))
            P.dma("sp", abs_, T(ps_d[:, 3584:3592], None), nowaw=True)
            uext = sbt(st, "uext", [128, 12, 16, 7], F32)
            ys = sbt(st, "ys", [128, 12, 64], F32)
            es = sbt(st, "es", [128, 12, 64], F32)
            s_all = es
            for cc in range(12):
                ptq = pq.next()
                tr(ptq[:, 0:64], us[:, cc * 128:(cc + 1) * 128], idn)
                tr(ptq[:, 64:112], scs[:, cc * 128:(cc + 1) * 128], identf[:48, :48])
                cp(uext[:, cc, :, 3:7], T(ptq.ap[:, 0:64].rearrange("p (s t) -> p s t", t=4), ptq.res), eng="act")
                cp(uext[:, cc, :, 0:3], T(ptq.ap[:, 64:112].rearrange("p (s t) -> p s t", t=3), ptq.res), eng="dve")
                yv = T(ys.ap[:, cc, :].rearrange("p (s t) -> p s t", t=4), ys.res)
                ts(yv, uext[:, cc, :, 0:4], cw[:, cc, 0:1], ALU.mult)
                for w in range(1, 4):
                    stt(yv, uext[:, cc, :, w:w + 4], cw[:, cc, w:w + 1], yv, ALU.mult, ALU.add)
            act(es, ys, AF.Exp, scale=-1.0)
            ts(es, es, 1.0, ALU.add, eng="pool")
            recip(es, es)
            tt(s_all, ys, es, ALU.mult, eng="pool")
            xas, ggs, ebs, betas = (gb_r.next() for _ in range(4))
            xa2, gg2, eb2, be2 = (T(t.ap.rearrange("p a b -> p (a b)")[:n, 0:4], t.res) for t in (xas, ggs, ebs, betas))
            tt(xa2, abs_[:, 0:4], dtb[:n], ALU.add)
            act(xa2, xa2, AF.Exp)
            act(xa2, xa2, AF.Ln, bias=onec[:n], scale=1.0)
            tt(gg2, xa2, negA[:n], ALU.mult)
            act(eb2, abs_[:, 4:8], AF.Exp, scale=-1.0)
            ts(eb2, eb2, 1.0, ALU.add)
            recip(be2, eb2)
            W3 = sbt(st, "W3", [n, 16 * n], F32)
            I3 = sbt(st, "I3", [n, 16 * n], F32)
            memset(W3, 0.0)
            memset(I3, 0.0)

            def diagv(t):
                a = t.ap
                return T(bass.AP(tensor=a.tensor, offset=a.offset, ap=[list(a.ap[0]), [n + 4, 16], [1, 4]]), t.res)
            cp(diagv(I3), T(idn.ap.rearrange("p (s t) -> p s t", t=4), idn.res), eng="pool")
            w0Tm = sbt(st, "w0Tm", [128, 16 * n], F32)
            qem = sbt(st, "qem", [128, 16 * n], F32)
            Sall = sbt(st, "Sall", [128, 16, 128], F32)
            Snew = Sall
            EGL = sbt(st, "EGL", [128, 16], F32)
            for h in range(4):
                P.dma("sp", Sall, T(ssm_d[:, h].rearrange("s p d -> p s d"), None))
                qs_, ks_, vs_ = s_all[:, h, :], s_all[:, 4 + h, :], s_all[:, 8 + h, :]
                sqv = nrm_r.next()
                tt(sqv[:, 0:n], qs_, qs_, ALU.mult, eng="pool")
                tt(sqv[:, n:2 * n], ks_, ks_, ALU.mult, eng="pool")
                mm(pss[:, 0:2 * n], ones, sqv[:, 0:2 * n])
                rsv = nrm_r.next()
                act(rsv[:, 0:2 * n], pss[:, 0:2 * n], AF.Ln, bias=l2e, scale=1.0)
                act(rsv[:, 0:2 * n], rsv[:, 0:2 * n], AF.Exp, scale=-0.5)
                qkn = nrm_r.next()
                qn, kn = qkn[:, 0:n], qkn[:, n:2 * n]
                stt(qn, qs_, 128.0 ** -0.5, rsv[:, 0:n], ALU.mult, ALU.mult)
                tt(kn, ks_, rsv[:, n:2 * n], ALU.mult)
                gcol, bcol = gg2[:, h:h + 1], be2[:, h:h + 1]
                G = sq_r.next()[:n, :n]
                ts(G, CS_UINCL, gcol, ALU.mult)
                PA, PB, Pm, pm1 = pq.next(), pq.next(), pq.next(), pq.next()
                mm(PA[:n, :n], ones[:n, :n], G, True, False)
                mm(PA[:n, :n], idn, CS_BIGU, False, True)
                mm(PB[:n, :n], ones[:n, :n], G, True, False)
                mm(PB[:n, :n], idn, CS_NBIGL, False, True)
                mm(Pm[:, :n], ones[:n, :], G)
                mm(pm1[:n, 0:1], CS_BD, gcol)
                mm(pm1[:n, 1:2], CS_UINCL, gcol)
                cg, ncg, ecg, edl = (col_r.next()[:n] for _ in range(4))
                cp(cg, pm1[:n, 1:2], eng="act")
                ts(ncg, pm1[:n, 1:2], -1.0, ALU.mult)
                decay, decayT = sq_r.next()[:n, :n], sq_r.next()[:n, :n]
                act(decay, PA[:n, :n], AF.Exp, bias=cg, scale=-1.0)
                act(decayT, PB[:n, :n], AF.Exp, bias=ncg, scale=1.0)
                act(ecg, cg, AF.Exp)
                act(edl, pm1[:n, 0:1], AF.Exp, bias=ncg, scale=1.0)
                act(EGL, T(Pm.ap[:, 3:n:4], Pm.res), AF.Exp)
                KK, QKT = pq.next(), pq.next()
                mm(KK[:n, :n], kn, kn)
                mm(QKT[:n, :n], kn, qn)
                t0, Wm, AT = sq_r.next()[:n, :n], sq_r.next()[:n, :n], sq_r.next()[:n, :n]
                tt(t0, KK[:n, :n], decayT, ALU.mult)
                stt(Wm, t0, bcol, CS_SU, ALU.mult, ALU.mult)
                tt(AT, QKT[:n, :n], decayT, ALU.mult)
                WmTp = pq.next()
                tr(WmTp[:n, :n], Wm, idn)
                WmT = sq_r.next()[:n, :n]
                cp(WmT, WmTp[:n, :n], eng="act")
                Pk = sq_r.next()[:n, :n]
                tt(Pk, idn, Wm, ALU.subtract)
                p2 = pq.next()
                mm(p2[:n, :n], Wm, WmT)
                X2T = sq_r.next()[:n, :n]
                cp(X2T, p2[:n, :n], eng="dve")
                p3 = pq.next()
                mm(p3[:n, :n], X2T, Pk)
                W = sq_r.next()[:n, :n]
                tt(W, p3[:n, :n], Pk, ALU.add)
                dgb = sq_r.next()[:n, :n]
                ts(dgb, idn, bcol, ALU.mult)
                pbr = pq.next()
                mm(pbr[:n, :n], ones[:n, :n], dgb)
                Wb = sq_r.next()[:n, :n]
                tt(Wb, pbr[:n, :n], W, ALU.mult)
                cp(diagv(W3), T(Wb.ap.rearrange("p (s t) -> p s t", t=4), Wb.res), eng="pool")
                ktp, vtp, qtp = pq.next(), pq.next(), pq.next()
                tr(ktp[:n, :], kn, identf)
                tr(vtp[:n, :], vs_, identf)
                tr(qtp[:n, :], qn, identf)
                ke, kd, vtm, qetm = sq_r.next()[:n], sq_r.next()[:n], sq_r.next()[:n], sq_r.next()[:n]
                ts(ke, ktp[:n, :], ecg, ALU.mult)
                ts(kd, ktp[:n, :], edl, ALU.mult)
                cp(vtm, vtp[:n, :], eng="act")
                ts(qetm, qtp[:n, :], ecg, ALU.mult)
                u0p = pq.next()
                mm(u0p[:n, :], Wb, vtm)
                u0b = sq_r.next()[:n]
                cp(u0b, u0p[:n, :], eng="act")
                for half in range(2):
                    mm(pu[:, half, :], ke, W3[:, half * 512:(half + 1) * 512])
                cp(w0Tm, T(pu.ap[:, 0:2, :].rearrange("p a b -> p (a b)"), pu.res), eng="act")
                for half in range(2):
                    mm(pu[:, half, :], qetm, I3[:, half * 512:(half + 1) * 512])
                cp(qem, T(pu.ap[:, 0:2, :].rearrange("p a b -> p (a b)"), pu.res), eng="dve")
                pw = pq.next()
                for sq_i in range(16):
                    mm(pw[:n, :], w0Tm[:, sq_i * n:(sq_i + 1) * n], Sall[:, sq_i, :], sq_i == 0, sq_i == 15)
                vn = sq_r.next()[:n]
                tt(vn, u0b, pw[:n, :], ALU.subtract)
                po = pq.next()
                for sq_i in range(16):
                    mm(po[:n, :], qem[:, sq_i * n:(sq_i + 1) * n], Sall[:, sq_i, :], sq_i == 0, False)
                mm(po[:n, :], AT, vn, False, True)
                o = sq_r.next()[:n]
                cp(o, po[:n, :], eng="act")
                for sq_i in range(16):
                    kdm = sq_r.next()[:n]
                    ts(kdm, kd, RM[:, sq_i:sq_i + 1], ALU.mult, eng="pool")
                    pkv = pq.next()
                    mm(pkv, kdm, vn)
                    stt(Snew[:, sq_i, :], Sall[:, sq_i, :], EGL[:, sq_i:sq_i + 1], pkv, ALU.mult, ALU.add)
                P.dma("sp", T(nss[:, h].rearrange("s p d -> p s d"), None), Snew, store=True)
                osq, ms = sq_r.next()[:n], col_r.next()[:n]
                tt(osq, o, o, ALU.mult, eng="pool")
                rsum(ms, osq)
                act(ms, ms, AF.Ln, bias=epsc[:n], scale=1.0 / 128)
                act(ms, ms, AF.Exp, scale=-0.5)
                zh = zs[:, h * 128:(h + 1) * 128]
                ez, sz, og = sq_r.next()[:n], sq_r.next()[:n], sq_r.next()[:n]
                act(ez, zh, AF.Exp, scale=-1.0)
                ts(ez, ez, 1.0, ALU.add, eng="pool")
                recip(ez, ez)
                tt(sz, zh, ez, ALU.mult)
                stt(og, o, ms, gnw[:n], ALU.mult, ALU.mult)
                ogb = ogb_r.next()[:n]
                tt(ogb, og, sz, ALU.mult)
                oTp = ptrb.next()
                tr(oTp[:, :n], ogb, identb[:n, :n])
                cp(oBs[:, h, :], oTp[:, :n], eng="act")
            P.emit()

        with ExitStack() as st:
            tl = {"wst": ring(st, sbt, "wst", [128, 512], F32, 4)}
            NPAIR = int(os.environ.get('ATT_PAIRS', '4'))
            cst6 = sbt(st, "cst6", [128, 128], F32)
            P.dma("sp", cst6, T(cst_d[:, 6, :], None))
            trilb = sbt(st, "trilb", [128, 128], BF16)
            onesb = sbt(st, "onesb", [128, 128], BF16)
            cp(trilb, cst6)
            memset(onesb, 1.0)
            sbb = sbt(st, "sbb", [128, 8], F32)
            P.dma("sp", sbb, T(sbb_d, None))
            maskt = sbt(st, "maskt", [128, 16, 512], BF16)
            P.dma("sp", maskt, T(mask_d, None))
            KT = sbt(st, "KT", [128, SEQ], BF16)
            Vt = sbt(st, "Vt", [128, NT_FULL, 128], BF16)
            qT = sbt(st, "qT", [128, OWN], BF16)
            wqkv = sbt(st, "wqkv", [128, 8, 384], BF16)
            hTg_r = ring(st, sbt, "hTg3", [128, 8, 512], BF16, 2)
            pS = ring(st, pst, "pS", [128, 512], F32, 3)
            pproj = Ring([pS.tiles[0]])
            pR_r = ring(st, pst, "pR", [128, 512], F32, 2)
            pT_r = ring(st, pst, "pT", [128, 512], F32, 2)
            pO = pst(st, "pO", [128, 512], F32)
            e_r = ring(st, sbt, "e3", [128, 512], BF16, 4)
            sp_r = ring(st, sbt, "sp3", [128, 512], BF16, 3)
            Rt_r = ring(st, sbt, "Rt3", [128, 512], F32, 2)
            ex_r = ring(st, sbt, "ex3", [128, 512], BF16, 2)
            w_r = ring(st, sbt, "w3", [128, 512], BF16, 2)
            carry = [sbt(st, f"carry{i}", [128, 512], F32) for i in range(2)]
            for p in range(NPAIR):
                load_w(tl, wqkv[:, :, 0:128], w_in, p * 128, 128, scale=nmw_t)
                load_w(tl, wqkv[:, :, 128:256], w_in, 512 + p * 128, 128, scale=nmw_t)
                load_w(tl, wqkv[:, :, 256:384], w_in, 1024 + p * 128, 128, scale=nmw_t)
                for g in range(16):
                    hTg = hTg_r.next()
                    for t in range(4):
                        P.dma("sp", hTg[:, :, t * 128:(t + 1) * 128], T(hT_d[4 * g + t].rearrange("p (k n) -> p k n", k=8), None), nowaw=True)
                    pk = pproj.next()
                    for k in range(8):
                        mm(pk, wqkv[:, k, 128:256], hTg[:, k, :], k == 0, k == 7)
                    cp(KT[:, g * 512:(g + 1) * 512], pk, eng="act")
                    pv = pproj.next()
                    for t in range(4):
                        for k in range(8):
                            mm(pv[:, t * 128:(t + 1) * 128], hTg[:, k, t * 128:(t + 1) * 128], wqkv[:, k, 256:384], k == 0, k == 7)
                    cp(Vt[:, 4 * g:4 * g + 4, :], T(pv.ap.rearrange("p (t c) -> p t c", t=4), pv.res), eng="dve")
                for i in range(4):
                    hTg = hTg_r.next()
                    for t in range(4):
                        P.dma("sp", hTg[:, :, t * 128:(t + 1) * 128], T(hTown_d[4 * i + t].rearrange("p (k n) -> p k n", k=8), None), nowaw=True)
                    pq_ = pproj.next()
                    for k in range(8):
                        mm(pq_, wqkv[:, k, 0:128], hTg[:, k, :], k == 0, k == 7)
                    cp(qT[:, i * 512:(i + 1) * 512], pq_, eng="act")
                for hh in range(2):
                    h = 2 * p + hh
                    hs = slice(64 * hh, 64 * hh + 64)
                    for i in range(int(os.environ.get('ATT_SLOTS', '4'))):
                        nkb = 16 * i + 16
                        cur = 0
                        memset(carry[0], 0.0)
                        def front(KB):
                            Sp = pS.next()
                            masked = KB >= 16 * i
                            mm(Sp, KT[hs, KB * 128:(KB + 1) * 128], qT[hs, i * 512:(i + 1) * 512], True, not masked)
                            if masked:
                                mm(Sp, identb, maskt[:, KB - 16 * i, :], False, True)
                            e = e_r.next()
                            act(e, Sp, AF.Exp, bias=sbb[:, h:h + 1], scale=0.125)
                            sp = sp_r.next()
                            act(sp, e, AF.Ln, bias=onec, scale=1.0)
                            return (KB, e, sp)

                        def midstage(stg):
                            KB, e, sp = stg
                            pR, pT = pR_r.next(), pT_r.next()
                            mm(pR, trilb, sp)
                            mm(pT, onesb, sp)
                            return (KB, e, pR, pT)

                        def back(stg, idx, cur):
                            KB, e, pR, pT = stg
                            Rt = Rt_r.next()
                            tt(Rt, pR, carry[cur], ALU.add)
                            tt(carry[1 - cur], pT, carry[cur], ALU.add)
                            ex = ex_r.next()
                            act(ex, Rt, AF.Exp, scale=-1.0)
                            w = w_r.next()
                            tt(w, e, ex, ALU.mult)
                            mm(pO, Vt[:, KB, :], w, idx == 0, idx == nkb - 1)

                        order = list(range(nkb - 1, -1, -1))
                        fq, mq = [], []
                        done = 0
                        for KB in order:
                            fq.append(front(KB))
                            if len(fq) >= 2:
                                mq.append(midstage(fq.pop(0)))
                            if len(mq) >= 2:
                                back(mq.pop(0), done, cur)
                                cur = 1 - cur
                                done += 1
                        while fq:
                            mq.append(midstage(fq.pop(0)))
                            if len(mq) >= 2:
                                back(mq.pop(0), done, cur)
                                cur = 1 - cur
                                done += 1
                        while mq:
                            back(mq.pop(0), done, cur)
                            cur = 1 - cur
                            done += 1
                        cp(oAown[hs, p, i * 512:(i + 1) * 512], pO[hs, :], eng="act")
            P.emit()


        with ExitStack() as st:
            NSEQ = int(os.environ.get('SATT_SEQS', '16'))
            SST = int(os.environ.get('SATT_STAGE', '9'))
            n = NS_TOK
            cst6 = sbt(st, "cst6s", [128, 128], F32)
            P.dma("sp", cst6, T(cst_d[:, 6, :], None))
            trilb = sbt(st, "trilbs", [128, 128], BF16)
            onesb = sbt(st, "onesbs", [128, 128], BF16)
            cp(trilb, cst6)
            memset(onesb, 1.0)
            ptt = sbt(st, "ptt", [128, 256], I32)
            idx = sbt(st, "idx", [128, 256], I32)
            iot = sbt(st, "iot", [128, 256], I32)
            P.dma("sp", ptt, T(pt_d.partition_broadcast(128), None))
            P.dma("sp", iot, T(iota_d, None))
            P.op("dve", lambda e: e.tensor_scalar(out=idx.ap, in0=ptt.ap, scalar1=7, scalar2=None, op0=ALU.logical_shift_left), [ptt], [idx])
            P.op("dve", lambda e: e.tensor_tensor(out=idx.ap, in0=idx.ap, in1=iot.ap, op=ALU.bitwise_or), [idx, iot], [idx])
            sbrow = sbt(st, "sbrow", [128, 512], F32)
            P.dma("sp", sbrow, T(sbrow_d, None))
            mnew = sbt(st, "mnew", [128, 32], F32)
            P.dma("sp", mnew, T(mnew_d, None))
            selE = sbt(st, "selE", [64, 256], F32)
            P.dma("sp", selE, T(selE_d, None))
            dm = sbt(st, "dm", [32, 512], F32)
            P.dma("sp", dm, T(dm_d, None))
            qkv = sbt(st, "qkvs", [n, 1536], F32)
            P.dma("sp", qkv, T(ps_d[:, 0:1536], None))
            qsT = sbt(st, "qsT", [128, 4, n], BF16)
            ksT = sbt(st, "ksT", [128, 4, 256], BF16)
            memset(ksT, 0.0)
            ptk_r = ring(st, pst, "ptk", [128, 4, 128], F32, 2)
            pS_r = ring(st, pst, "pSs", [128, 512], F32, 2)
            pR = pst(st, "pRs", [128, 512], F32)
            pT = pst(st, "pTs", [128, 512], F32)
            pO = pst(st, "pOs", [128, 512], F32)
            psm = pst(st, "psm", [128, 512], F32)
            for p in range(4):
                ptk = ptk_r.next()
                tr(ptk[:, 0, :n], qkv[:, p * 128:(p + 1) * 128], identf[:n, :n])
                tr(ptk[:, 1, :n], qkv[:, 512 + p * 128:512 + (p + 1) * 128], identf[:n, :n])
                cp(qsT[:, p, :], ptk[:, 0, :n], eng="act")
                cp(ksT[:, p, 0:n], ptk[:, 1, :n], eng="dve")
            kpg_r = ring(st, sbt, "kpg", [128, 512], F32, 3)
            vpg_r = ring(st, sbt, "vpg", [128, 512], F32, 3)
            Vb_r = ring(st, sbt, "Vb", [128, 16, 512], BF16, 2)
            KTs_r = ring(st, sbt, "KTs", [128, 4, 128], BF16, 3)
            f_r = ring(st, sbt, "fs", [128, 512], F32, 5)
            b_r = ring(st, sbt, "bs", [128, 512], BF16, 3)
            C = sbt(st, "Cs", [128, 16, 32], F32)
            sm_r = ring(st, sbt, "sms", [128, 512], F32, 4)
            smb_r = ring(st, sbt, "smbs", [128, 512], BF16, 4)
            o2 = sbt(st, "o2s", [32, 2, 64], F32)
            for sq_i in range(NSEQ):
                Sp = pS_r.next()
                Vb = Vb_r.next()
                tsl = slice(4 * sq_i, 4 * sq_i + 4)
                for pg in range(16):
                    col = sq_i * 16 + pg
                    kpg, vpg = kpg_r.next(), vpg_r.next()
                    for dst, src in ((kpg, ck_d), (vpg, cv_d)):
                        def gfn(e, dst=dst, src=src, col=col):
                            return e.indirect_dma_start(out=dst.ap, out_offset=None, in_=src,
                                                        in_offset=bass.IndirectOffsetOnAxis(ap=idx.ap[:, col:col + 1], axis=0))
                        P._add("pool", gfn, [idx], [dst], dma=True, evres=dst.res)
                    cp(Vb[:, pg, :], vpg, eng=("dve", "act")[pg % 2])
                    if SST < 2:
                        tr(ptk_r.next()[:, 0, :], kpg[:, 0:128], identf)
                        continue
                    ptk = ptk_r.next()
                    for p in range(4):
                        tr(ptk[:, p, :], kpg[:, p * 128:(p + 1) * 128], identf)
                    KTs = KTs_r.next()
                    cp(KTs, ptk, eng=("act", "dve")[pg % 2])
                    if SST < 3:
                        continue
                    for p in range(4):
                        for hh in range(2):
                            hs = slice(64 * hh, 64 * hh + 64)
                            c0 = pg * 32 + (2 * p + hh) * 4
                            mm(Sp[:, c0:c0 + 4], KTs[hs, p, :], qsT[hs, p, tsl])
                if SST < 4:
                    continue
                for p in range(4):
                    for hh in range(2):
                        hs = slice(64 * hh, 64 * hh + 64)
                        c0 = (2 * p + hh) * 4
                        mm(psm[:, c0:c0 + 4], ksT[hs, p, 4 * sq_i:4 * sq_i + 128], qsT[hs, p, tsl])
                z, e = f_r.next(), f_r.next()
                stt(z, Sp, 0.125, sbrow, ALU.mult, ALU.add)
                act(e, z, AF.Exp)
                sp = b_r.next()
                act(sp, e, AF.Ln, bias=onec, scale=1.0)
                zn, en = sm_r.next(), sm_r.next()
                stt(zn[:, 0:32], psm[:, 0:32], 0.125, sbrow[:, 0:32], ALU.mult, ALU.add)
                act(en[:, 0:32], zn[:, 0:32], AF.Exp)
                tt(en[:, 0:32], en[:, 0:32], mnew, ALU.mult)
                spn = smb_r.next()
                act(spn[:, 0:32], en[:, 0:32], AF.Ln, bias=onec, scale=1.0)
                mm(pR, trilb, sp)
                mm(pT, onesb, sp)
                mm(psm[:, 64:96], onesb, spn[:, 0:32])
                mm(psm[:, 128:160], trilb, spn[:, 0:32])
                cp(C[:, 15, :], psm[:, 64:96], eng="dve")
                for pg in range(14, -1, -1):
                    tt(C[:, pg, :], pT[:, (pg + 1) * 32:(pg + 2) * 32], C[:, pg + 1, :], ALU.add)
                R, ex = f_r.next(), f_r.next()
                tt(R, pR, T(C.ap.rearrange("p a b -> p (a b)"), C.res), ALU.add)
                act(ex, R, AF.Exp, scale=-1.0)
                w = b_r.next()
                tt(w, e, ex, ALU.mult, eng="pool")
                exn = sm_r.next()
                act(exn[:, 0:32], psm[:, 128:160], AF.Exp, scale=-1.0)
                wn = smb_r.next()
                tt(wn[:, 0:32], en[:, 0:32], exn[:, 0:32], ALU.mult)
                if SST < 5:
                    continue
                mm(pO, selE[:, 4 * sq_i:4 * sq_i + 128], qkv[:, 1024:1536])
                v4 = smb_r.next()
                cp(v4, pO, eng="act")
                for pg in range(16):
                    mm(pR[0:32, :], w[:, pg * 32:(pg + 1) * 32], Vb[:, pg, :], pg == 0, False)
                mm(pR[0:32, :], wn[:, 0:32], v4, False, True)
                od = sm_r.next()[0:32]
                tt(od, pR[0:32, :], dm, ALU.mult)
                rsum(o2[:, 0, :], T(od.ap.rearrange("p (h d) -> p d h", h=8), od.res))
                cp(o2[:, 1, :], o2[:, 0, :], eng="pool")
                tr(psm[:, 256:288], T(o2.ap.rearrange("p a b -> p (a b)"), o2.res), identf[:32, :32])
                for hh in range(2):
                    hs = slice(64 * hh, 64 * hh + 64)
                    src = T(psm.ap[hs, 256:288].rearrange("p (a b c) -> p a b c", a=4, b=2)[:, :, hh, :], psm.res)
                    cp(oAs[hs, :, tsl], src, eng=("act", "dve")[hh])
            P.emit()

        with ExitStack() as st:
            tl = {"wst": ring(st, sbt, "wst", [128, 512], F32, 4),
                  "sq": ring(st, sbt, "sq4", [128, D], F32, 1),
                  "ss": ring(st, sbt, "ss4", [128, 1], F32, 4),
                  "hb": ring(st, sbt, "hb4", [128, D], BF16, 2),
                  "ptr": ring(st, pst, "ptr4", [128, 8, 128], BF16, 1)}
            w2 = sbt(st, "w2", [128, 8, 2048], BF16)
            for gi in range(4):
                load_w(tl, w2[:, :, gi * 512:(gi + 1) * 512], w_in, 3592 + gi * 512, 512, scale=nmw_t)
            wpa = sbt(st, "wpa", [128, 4, D], BF16)
            wpb = sbt(st, "wpb", [128, 4, D], BF16)
            wo = sbt(st, "wo", [128, 8, D], BF16)
            for half in range(2):
                load_w(tl, wpa[:, :, half * 512:(half + 1) * 512], w_pa_d, half * 512, 512, nk=4)
                load_w(tl, wpb[:, :, half * 512:(half + 1) * 512], w_pb_d, half * 512, 512, nk=4)
                load_w(tl, wo[:, :, half * 512:(half + 1) * 512], w_o_d, half * 512, 512, nk=8)
            hTg_r = ring(st, sbt, "hTg4", [128, 8, 512], BF16, 2)
            pg = ring(st, pst, "pg4", [128, 512], F32, 5)
            pmix = ring(st, pst, "pmix4", [128, 512], F32, 2)
            sg_r = ring(st, sbt, "sg4", [128, 512], F32, 4)
            mm_r = ring(st, sbt, "mm4", [128, 512], F32, 4)
            mT_r = ring(st, sbt, "mT4", [128, 8, 512], BF16, 2)
            x_r = ring(st, sbt, "x4", [128, D], F32, 3)
            hT_r = ring(st, sbt, "hTt4", [128, 8, 128], BF16, 2)
            groups = []
            for G in range(4):
                groups.append(dict(n=512, hT_src=[hTown_d[4 * G + t] for t in range(4)], hT_sb=None,
                                   oA=oAown[:, :, G * 512:(G + 1) * 512], oB=oBown[:, :, G * 512:(G + 1) * 512],
                                   x_rows=[xown[(4 * G + t) * 128:(4 * G + t + 1) * 128, :] for t in range(4)],
                                   row0=G * 512, tile0=4 * G))
            groups.append(dict(n=NS_TOK, hT_src=[], hT_sb=hsT, oA=oAs, oB=oBs, x_rows=[xs[:, :]], row0=OWN, tile0=NT_OWN))
            for gd in groups:
                n = gd["n"]
                if gd["hT_sb"] is not None:
                    hTg = gd["hT_sb"]
                else:
                    hTg = hTg_r.next()
                for t, src in enumerate(gd["hT_src"]):
                    P.dma("sp", hTg[:, :, t * 128:(t + 1) * 128], T(src.rearrange("p (k n) -> p k n", k=8), None), nowaw=True)
                mT = mT_r.next()
                for c in range(8):
                    cs = slice(c * 128, (c + 1) * 128)
                    pgA, pgB, pyA, pyB = pg.next(), pg.next(), pg.next(), pg.next()
                    for k in range(8):
                        mm(pgA[:, :n], w2[:, k, cs], hTg[:, k, :n], k == 0, k == 7)
                    for k in range(8):
                        mm(pgB[:, :n], w2[:, k, 1024 + c * 128:1024 + (c + 1) * 128], hTg[:, k, :n], k == 0, k == 7)
                    for k in range(4):
                        mm(pyA[:, :n], wpa[:, k, cs], gd["oA"][:, k, :], k == 0, k == 3)
                    for k in range(4):
                        mm(pyB[:, :n], wpb[:, k, cs], gd["oB"][:, k, :], k == 0, k == 3)
                    sgA, sgB = sg_r.next(), sg_r.next()
                    for sg_, pg_ in ((sgA, pgA), (sgB, pgB)):
                        act(sg_[:, :n], pg_[:, :n], AF.Exp, scale=-1.0)
                        ts(sg_[:, :n], sg_[:, :n], 1.0, ALU.add, eng="pool")
                        recip(sg_[:, :n], sg_[:, :n])
                    mA, mB = mm_r.next(), mm_r.next()
                    tt(mA[:, :n], pyA[:, :n], sgA[:, :n], ALU.mult)
                    tt(mB[:, :n], pyB[:, :n], sgB[:, :n], ALU.mult)
                    tt(mT[:, c, :n], mA[:, :n], mB[:, :n], ALU.add)
                for t, xr in enumerate(gd["x_rows"]):
                    nt = min(128, n - t * 128)
                    xt = x_r.next()
                    P.dma("sp", xt[:nt], T(xr, None))
                    for half in range(2):
                        pm = pmix.next()
                        for c in range(8):
                            mm(pm[:nt, :], mT[:, c, t * 128:t * 128 + nt], wo[:, c, half * 512:(half + 1) * 512], c == 0, c == 7)
                        tt(xt[:nt, half * 512:(half + 1) * 512], pm[:nt, :], xt[:nt, half * 512:(half + 1) * 512], ALU.add)
                    r0 = gd["row0"] + t * 128
                    P.dma("sp", T(x1_d[r0:r0 + nt, :], None), xt[:nt], store=True)
                    hTt = hT_r.next()
                    norm_core(tl, xt, hTt[:, :, :nt], nt)
                    P.dma("sp", T(hmT_d[gd["tile0"] + t].rearrange("p (k n) -> p k n", k=8)[:, :, :nt], None), hTt[:, :, :nt], store=True)
            P.emit()

        mid.close()
        with ExitStack() as st:
            tl = {"wst": ring(st, sbt, "wst", [128, 512], F32, 4)}
            nmlw = sbt(st, "nmlw", [128, 8], F32)
            P.dma("sp", nmlw, T(nmlw_d, None))
            nfw = sbt(st, "nfw", [128, D], F32)
            P.dma("sp", nfw, T(nfw_d, None))
            wup = sbt(st, "wup", [128, 8, 4 * D], BF16)
            wdn = sbt(st, "wdn", [128, 32, D], BF16)
            for gi in range(8):
                load_w(tl, wup[:, :, gi * 512:(gi + 1) * 512], w_up_d, gi * 512, 512, scale=nmlw)
            for half in range(2):
                load_w(tl, wdn[:, :, half * 512:(half + 1) * 512], w_down_d, half * 512, 512, nk=32)
            actT = sbt(st, "actT", [128, 32, 256], BF16)
            hm_r = ring(st, sbt, "hm5", [128, 8, 256], BF16, 2)
            pu5 = ring(st, pst, "pu5", [128, 512], F32, 3)
            pd5 = ring(st, pst, "pd5", [128, 512], F32, 2)
            tmp_r = ring(st, sbt, "tmp5", [128, 256], F32, 2)
            x_r = ring(st, sbt, "x5", [128, D], F32, 2)
            sq5 = sbt(st, "sq5", [128, D], F32)
            ss_r = ring(st, sbt, "ss5", [128, 1], F32, 4)
            y_r = ring(st, sbt, "y5", [128, D], F32, 2)
            subs = []
            for sg in range(8):
                subs.append(dict(n=256, tiles=[2 * sg, 2 * sg + 1], row0=sg * 256, out=y_own))
            subs.append(dict(n=NS_TOK, tiles=[NT_OWN], row0=OWN, out=y_s))
            for sd in subs:
                n = sd["n"]
                hm = hm_r.next()
                for t, tile in enumerate(sd["tiles"]):
                    nt = min(128, n - t * 128)
                    P.dma("sp", hm[:, :, t * 128:t * 128 + nt], T(hmT_d[tile].rearrange("p (k n) -> p k n", k=8)[:, :, :nt], None), nowaw=True)
                for f in range(32):
                    pu = pu5.next()
                    for k in range(8):
                        mm(pu[:, :n], wup[:, k, f * 128:(f + 1) * 128], hm[:, k, :n], k == 0, k == 7)
                    tmp = tmp_r.next()
                    ts(tmp[:, :n], pu[:, :n], 0.0, ALU.max)
                    act(actT[:, f, :n], tmp[:, :n], AF.Square)
                for t in range(len(sd["tiles"])):
                    nt = min(128, n - t * 128)
                    r0 = sd["row0"] + t * 128
                    xt = x_r.next()
                    P.dma("sp", xt[:nt], T(x1_d[r0:r0 + nt, :], None))
                    for half in range(2):
                        pd = pd5.next()
                        for f in range(32):
                            mm(pd[:nt, :], actT[:, f, t * 128:t * 128 + nt], wdn[:, f, half * 512:(half + 1) * 512], f == 0, f == 31)
                        tt(xt[:nt, half * 512:(half + 1) * 512], pd[:nt, :], xt[:nt, half * 512:(half + 1) * 512], ALU.add)
                    tt(sq5[:nt], xt[:nt], xt[:nt], ALU.mult, eng="pool")
                    ss = ss_r.next()
                    rsum(ss[:nt], sq5[:nt])
                    act(ss[:nt], ss[:nt], AF.Ln, bias=epsc[:nt], scale=1.0 / D)
                    act(ss[:nt], ss[:nt], AF.Exp, scale=-0.5)
                    y = y_r.next()
                    stt(y[:nt], xt[:nt], ss[:nt, 0:1], nfw[:nt], ALU.mult, ALU.mult)
                    ro = sd["row0"] - (0 if sd["out"] is y_own else OWN) + t * 128
                    P.dma("sp", T(sd["out"][ro:ro + nt, :], None), y[:nt], store=True)
            P.emit()
    return nc


_NC = None


def kernel(x_prompt, x_sample, cache_k, cache_v, page_table, state_conv, state_ssm,
           norm_mix_w, w_in, sb_bias, conv_w, a_log, dt_bias, gdn_norm_w, w_pa, w_pb, w_o,
           norm_mlp_w, w_up, w_down, norm_final_w):
    global _NC
    f32 = np.float32
    x_prompt = np.asarray(x_prompt, f32)
    x_sample = np.asarray(x_sample, f32)
    nc = build()
    ident = np.eye(128, dtype=f32)
    nmw = np.ascontiguousarray(np.asarray(norm_mix_w, f32)[0].reshape(8, 128).T)
    w_in0 = np.ascontiguousarray(np.asarray(w_in, f32)[0])
    cst = np.zeros((128, NCST, 128), f32)
    ii = np.arange(128)
    r_, c_ = ii[:, None], ii[None, :]
    cst[:, 0] = (r_ <= c_)
    cst[:, 1] = BIG * (c_ > r_)
    cst[:, 2] = -BIG * (r_ > c_)
    cst[:, 3] = (r_ < c_)
    cst[:, 4] = ((r_ // 64) == (c_ // 64))
    cst[:, 5] = 1.0 - cst[:, 4]
    cst[:, 6] = (r_ >= c_)
    same = ((r_ // 4) == (c_ // 4))
    cst[:, 7] = (r_ <= c_) & same
    cst[:, 8] = BIG * ((c_ > r_) | ~same)
    cst[:, 9] = -BIG * ((r_ > c_) | ~same)
    cst[:, 10] = (r_ < c_) & same
    cst[:, 11] = same
    cst[:, 12] = ((r_ // 4) == c_)
    state_conv = np.asarray(state_conv, f32)
    ck = np.asarray(cache_k, f32).reshape(2560 * 128, 512)
    cv = np.asarray(cache_v, f32).reshape(2560 * 128, 512)
    page_table = np.asarray(page_table, np.int32)
    if os.environ.get('NPOOL_DBG'):
        npd = int(os.environ['NPOOL_DBG'])
        ck, cv, page_table = ck[:npd * 128], cv[:npd * 128], page_table % npd
    iota = np.ascontiguousarray(np.tile(np.arange(128, dtype=np.int32)[:, None], (1, 256)))
    sbrow = np.ascontiguousarray(np.tile(np.repeat(np.asarray(sb_bias, f32)[0], 4)[None, :], (128, 16)))
    mnew = np.zeros((128, 32), f32)
    selE = np.zeros((64, 256), f32)
    selE[np.arange(64), np.arange(64)] = 1.0
    for t1 in range(4):
        for hq in range(32):
            mnew[t1, hq] = 1.0 if t1 < (hq % 4) else 0.0
    dmm = np.zeros((32, 8, 64), f32)
    for hq in range(32):
        dmm[hq, hq // 4, :] = 1.0
    dmm = dmm.reshape(32, 512)
    state_ssm = np.asarray(state_ssm, f32)
    sbb = np.ascontiguousarray(np.tile(np.asarray(sb_bias, f32)[0][None, :], (128, 1)))
    masks = []
    for jj in range(4):
        m = np.zeros((128, 16, 512), f32)
        for r in range(4):
            for kb in range(4):
                kbrel = 4 * r + kb
                for qb in range(4):
                    qbrel = 4 * jj + qb
                    if kbrel < qbrel:
                        m[:, r * 4 + kb, qb * 128:(qb + 1) * 128] = 1.0
                    elif kbrel == qbrel:
                        m[:, r * 4 + kb, qb * 128:(qb + 1) * 128] = (r_ < c_)
        masks.append(((m - 1.0) * 30000.0).astype(ml_dtypes.bfloat16))
    cwl = np.ascontiguousarray(np.asarray(conv_w, f32)[0].reshape(4, 3, 4, 128).transpose(3, 1, 2, 0).reshape(128, 12, 4))
    alog_bc = np.ascontiguousarray(np.tile(np.asarray(a_log, f32)[0][None, :], (128, 1)))
    dtb_bc = np.ascontiguousarray(np.tile(np.asarray(dt_bias, f32)[0][None, :], (128, 1)))
    gnw_bc = np.ascontiguousarray(np.tile(np.asarray(gdn_norm_w, f32)[0][None, :], (128, 1)))
    nmlw = np.ascontiguousarray(np.asarray(norm_mlp_w, f32)[0].reshape(8, 128).T)
    nfw_bc = np.ascontiguousarray(np.tile(np.asarray(norm_final_w, f32)[None, :], (128, 1)))
    w_pa0 = np.ascontiguousarray(np.asarray(w_pa, f32)[0])
    w_pb0 = np.ascontiguousarray(np.asarray(w_pb, f32)[0])
    w_o0 = np.ascontiguousarray(np.asarray(w_o, f32)[0])
    w_up0 = np.ascontiguousarray(np.asarray(w_up, f32)[0])
    w_down0 = np.ascontiguousarray(np.asarray(w_down, f32)[0])
    in_maps = []
    own_idx = []
    for c in range(NCORE):
        b, j = c // 4, c % 4
        groups = [4 * i + j for i in range(4)]
        idx = np.concatenate([np.arange(512 * g, 512 * g + 512) for g in groups])
        own_idx.append(idx)
        in_maps.append({
            "xfull": np.ascontiguousarray(x_prompt[b]),
            "xown": np.ascontiguousarray(x_prompt[b][idx]),
            "xs": np.ascontiguousarray(x_sample[16 * c:16 * c + 16].reshape(64, D)),
            "w_in": w_in0, "nmw": nmw, "ident": ident, "cst": cst, "cw": cwl, "alog_bc": alog_bc, "dtb_bc": dtb_bc,
            "gnw_bc": gnw_bc, "w_pa": w_pa0, "w_pb": w_pb0, "w_o": w_o0, "w_up": w_up0, "w_down": w_down0,
            "nmlw": nmlw, "nfw_bc": nfw_bc, "sbb": sbb,
            "cache_k": ck, "cache_v": cv, "pt": np.ascontiguousarray(page_table[16 * c:16 * c + 16].reshape(1, 256)),
            "iota": iota, "sbrow": sbrow, "mnew": mnew, "dm": dmm, "selE": selE,
            "sc": np.ascontiguousarray(state_conv[0, 16 * c:16 * c + 16].reshape(48, 1536)),
            "ssm": np.ascontiguousarray(state_ssm[0, 16 * c:16 * c + 16]), "maskd": masks[j], "sel": np.ascontiguousarray(np.tile((np.arange(4) == j).astype(f32)[None, :], (128, 1))),
        })
    if os.environ.get('RETURN_MAPS') == '1':
        return nc, in_maps
    res = run_bass_kernel_spmd(nc, in_maps, core_ids=list(range(NCORE)))
    R = res.results
    y_prompt = np.zeros((2, SEQ, D), f32)
    y_sample = np.zeros((128, 4, D), f32)
    nkp = np.zeros((2, SEQ, 512), f32)
    nvp = np.zeros((2, SEQ, 512), f32)
    nksa = np.zeros((128, 4, 512), f32)
    nvsa = np.zeros((128, 4, 512), f32)
    ncpa = np.zeros((1, 2, 3, 1536), f32)
    ncsa = np.zeros((1, 128, 3, 1536), f32)
    nsp = np.zeros((1, 2, 4, 128, 128), f32)
    nss = np.zeros((1, 128, 4, 128, 128), f32)
    for c in range(NCORE):
        b, j = c // 4, c % 4
        r = R[c]
        y_prompt[b][own_idx[c]] = r["y_own"]
        y_sample[16 * c:16 * c + 16] = r["y_s"].reshape(16, 4, D)
        nkp[b][own_idx[c]] = r["nk_own"]
        nvp[b][own_idx[c]] = r["nv_own"]
        nksa[16 * c:16 * c + 16] = r["nks"].reshape(16, 4, 512)
        nvsa[16 * c:16 * c + 16] = r["nvs"].reshape(16, 4, 512)
        ncsa[0, 16 * c:16 * c + 16] = r["ncs"]
        nss[0, 16 * c:16 * c + 16] = r["nss"]
        if j == 0:
            ncpa[0, b] = r["ncp"]
            nsp[0, b] = r["nsp"]
    return (y_prompt, y_sample,
            nkp.reshape(1, 2, 64, 128, 8, 64), nvp.reshape(1, 2, 64, 128, 8, 64),
            nksa.reshape(1, 128, 4, 8, 64), nvsa.reshape(1, 128, 4, 8, 64),
            ncpa, ncsa, nsp, nss)
```

```python
import os
import numpy as np
import concourse.bass as bass
import concourse.mybir as mybir
from concourse.bass_utils import run_bass_kernel_spmd

F32 = mybir.dt.float32
BF16 = mybir.dt.bfloat16
I32 = mybir.dt.int32
AF = mybir.ActivationFunctionType
ALU = mybir.AluOpType
AX = mybir.AxisListType

EPOCH = 20000


class Res:
    _n = 0

    def __init__(self, name):
        Res._n += 1
        self.name = f"{name}#{Res._n}"
        self.writers = []
        self.readers = []
        self.dma_sem = None
        self.dma_cnt = 0
        self.pe_acc = False
        self.store_res = None


class T:
    def __init__(self, ap, res):
        self.ap = ap
        self.res = res

    def __getitem__(self, idx):
        return T(self.ap[idx], self.res)


class Op:
    __slots__ = ("eng", "idx", "fn", "waits", "needed", "dma_res", "clock", "semval", "is_dma")


class Prog:
    ENGS = ("pe", "act", "dve", "pool", "sp")

    def __init__(self, nc):
        self.nc = nc
        self.ops = {e: [] for e in self.ENGS}
        self.clock = {e: {} for e in self.ENGS}
        self.dma_res = []
        self.dma_clock = {}
        self._dom2res = {}
        self.all_res = []

    def res(self, name):
        r = Res(name)
        self.all_res.append(r)
        return r

    def _add(self, eng, fn, reads, writes, dma=False, pe_acc=False, evres=None, nowaw=False):
        reads = [r for r in reads if r is not None and r.res is not None]
        writes = [w for w in writes if w is not None and w.res is not None]
        op = Op()
        op.eng = eng
        op.idx = len(self.ops[eng]) + 1
        op.fn = fn
        op.needed = False
        op.is_dma = dma
        op.dma_res = None
        deps = []
        for r in reads:
            deps += r.res.writers
        for w in writes:
            if pe_acc and eng == "pe" and w.res.pe_acc and all(ev[0] == "pe" for ev in w.res.writers):
                deps += w.res.readers
            else:
                if not nowaw:
                    deps += w.res.writers
                deps += w.res.readers
        clk = self.clock[eng]
        waits = []
        best = {}
        for ev in deps:
            dom, val = ev[0], ev[1]
            if dom == eng and eng in ("pe", "sp"):
                continue
            if clk.get(dom, 0) >= val:
                continue
            if best.get(dom, 0) < val:
                best[dom] = val
        for dom, val in best.items():
            waits.append((dom, val))
            if isinstance(dom, str):
                src = self.ops[dom][val - 1]
                src.needed = True
                for d2, v2 in src.clock.items():
                    if clk.get(d2, 0) < v2:
                        clk[d2] = v2
            else:
                snap = self.dma_clock.get((dom, val), {})
                for d2, v2 in snap.items():
                    if clk.get(d2, 0) < v2:
                        clk[d2] = v2
            if clk.get(dom, 0) < val:
                clk[dom] = val
        op.waits = waits
        if dma:
            tgt = evres
            if tgt.dma_sem is None:
                tgt.dma_sem = True
                self.dma_res.append(tgt)
            tgt.dma_cnt += 1
            if eng == "pool":
                tgt.sw = True
            op.dma_res = tgt
            ev = (("dma", tgt.name), tgt.dma_cnt)
            self.dma_clock[ev] = dict(clk)
            self._dom2res[("dma", tgt.name)] = tgt
        else:
            ev = (eng, op.idx)
        op.clock = dict(clk)
        if not dma:
            op.clock[eng] = op.idx
        for r in reads:
            r.res.readers.append(ev)
        for w in writes:
            if pe_acc and eng == "pe" and w.res.pe_acc and all(e2[0] == "pe" for e2 in w.res.writers):
                w.res.writers = [ev]
            else:
                w.res.writers = [ev]
            w.res.readers = []
            w.res.pe_acc = bool(pe_acc and eng == "pe")
        self.ops[eng].append(op)
        return op

    def op(self, eng, fn, reads=(), writes=(), pe_acc=False):
        return self._add(eng, fn, list(reads), list(writes), pe_acc=pe_acc)

    def dma(self, eng, out, in_, store=False, nowaw=False, fn=None, **kw):
        if store:
            if in_.res.store_res is None:
                in_.res.store_res = Res("st_" + in_.res.name)
            evres = in_.res.store_res
        else:
            evres = out.res
        if fn is None:
            def fn(e, o=out.ap, i=in_.ap, kw=kw):
                return e.dma_start(out=o, in_=i, **kw)
        return self._add(eng, fn, [in_], [out], dma=True, evres=evres, nowaw=nowaw)

    def barrier(self):
        evs = []
        for e in self.ENGS:
            for op in reversed(self.ops[e]):
                if not op.is_dma and op.fn is not None:
                    evs.append((e, op.idx))
                    break
        for r in self.dma_res:
            if r.dma_cnt:
                evs.append((("dma", r.name), r.dma_cnt))
        bres = T(None, Res("barrier"))
        bres.res.writers = evs
        for e in self.ENGS:
            self._add(e, None, [bres], [])

    def setup_sems(self, stack, n_dma=70):
        nc = self.nc
        self.dma_pool_sw = [[stack.enter_context(nc.semaphore(f"dsw_{k}")), 0] for k in range(8)]
        self.sems = {e: [stack.enter_context(nc.semaphore(f"s_{e}_{k}")) for k in range(3)] for e in self.ENGS}
        self.count = {e: 0 for e in self.ENGS}
        self.dma_pool = [[stack.enter_context(nc.semaphore(f"d_{k}")), 0] for k in range(n_dma)]
        self.all_res = []

    def emit(self):
        nc = self.nc
        self.barrier()
        nsem = {}
        for e in self.ENGS:
            c = self.count[e]
            for op in self.ops[e]:
                if op.needed and not op.is_dma:
                    c += 1
                    op.semval = c
                else:
                    op.semval = None
            nsem[e] = c - self.count[e]
            self.count[e] = c
        ihw = isw = 0
        for r in self.dma_res:
            if getattr(r, "sw", False):
                slot = self.dma_pool_sw[isw]
                isw += 1
            else:
                slot = self.dma_pool[ihw]
                ihw += 1
            r.dma_sem = slot
            r.dma_base = slot[1]
            slot[1] += 16 * r.dma_cnt
        print("pass ops:", {e: len(self.ops[e]) for e in self.ENGS}, "flagged:", nsem, "dma sems:", len(self.dma_res), flush=True)
        if os.environ.get("DUMP_WAITS"):
            for e in self.ENGS:
                for op in self.ops[e][:60]:
                    ws = []
                    for dom, val in op.waits:
                        if isinstance(dom, str):
                            ws.append((dom, val, self.ops[dom][val - 1].semval))
                        else:
                            r = self._dom2res[dom]
                            ws.append((dom[1], val, r.dma_base + 16 * val))
                    print("  ", e, op.idx, "dma" if op.is_dma else "", "sem=%s" % op.semval, ws)
        sems = self.sems
        engobj = {"pe": "tensor", "act": "scalar", "dve": "vector", "pool": "gpsimd", "sp": "sync"}
        with nc.Block() as block:
            def make(e):
                def body(eng):
                    for op in self.ops[e]:
                        for dom, val in op.waits:
                            if isinstance(dom, str):
                                sv = self.ops[dom][val - 1].semval
                                k = (sv - 1) // EPOCH
                                eng.wait_ge(sems[dom][k], sv - k * EPOCH)
                            else:
                                r = self._dom2res[dom]
                                eng.wait_ge(r.dma_sem[0], r.dma_base + 16 * val)
                        if op.fn is None:
                            continue
                        ins = op.fn(eng)
                        if op.is_dma:
                            ins.then_inc(op.dma_res.dma_sem[0], 16)
                        elif op.semval is not None:
                            k = (op.semval - 1) // EPOCH
                            ins.then_inc(sems[e][k], 1)
                return body
            for e in self.ENGS:
                getattr(block, engobj[e])(make(e))
        self.ops = {e: [] for e in self.ENGS}
        self.clock = {e: {} for e in self.ENGS}
        for r in self.all_res:
            r.writers = []
            r.readers = []
            r.dma_sem = None
            r.dma_cnt = 0
            r.pe_acc = False
            r.store_res = None
            r.sw = False
        self.dma_res = []
        self.dma_clock = {}
        self._dom2res = {}


from contextlib import ExitStack
import os
import ml_dtypes

NCORE = 8
D = 1024
IN_DIM = 5640
SEQ = 8192
NT_FULL = SEQ // 128
OWN = 2048
NT_OWN = OWN // 128
NS_TOK = 64
NCST = 13
BIG = 300.0
EPS = 1e-6


class Ring:
    def __init__(self, tiles):
        self.tiles = tiles
        self.i = 0

    def next(self):
        t = self.tiles[self.i % len(self.tiles)]
        self.i += 1
        return t


def build():
    nc = bass.Bass("TRN2", target_bir_lowering=False)

    def din(name, shape, dt=F32):
        return nc.dram_tensor(name, list(shape), dt, kind="ExternalInput").ap()

    def dout(name, shape, dt=F32):
        return nc.dram_tensor(name, list(shape), dt, kind="ExternalOutput").ap()

    xfull = din("xfull", [SEQ, D])
    xown = din("xown", [OWN, D])
    xs = din("xs", [NS_TOK, D])
    w_in = din("w_in", [D, IN_DIM])
    nmw = din("nmw", [128, 8])
    ident_d = din("ident", [128, 128])
    cst_d = din("cst", [128, NCST, 128])
    cw_d = din("cw", [128, 12, 4])
    alog_d = din("alog_bc", [128, 4])
    dtb_d = din("dtb_bc", [128, 4])
    gnw_d = din("gnw_bc", [128, 128])
    sel_d = din("sel", [128, 4])
    sbb_d = din("sbb", [128, 8])
    mask_d = din("maskd", [128, 16, 512], BF16)
    sc_d = din("sc", [48, 1536])
    NPOOL = int(os.environ.get('NPOOL_DBG', '2560'))
    ck_d = din("cache_k", [NPOOL * 128, 512])
    cv_d = din("cache_v", [NPOOL * 128, 512])
    pt_d = din("pt", [1, 256], I32)
    iota_d = din("iota", [128, 256], I32)
    sbrow_d = din("sbrow", [128, 512])
    mnew_d = din("mnew", [128, 32])
    selE_d = din("selE", [64, 256])
    dm_d = din("dm", [32, 512])
    ssm_d = din("ssm", [16, 4, 128, 128])
    nss = dout("nss", [16, 4, 128, 128])
    nsp = dout("nsp", [4, 128, 128])
    hT_d = nc.dram_tensor("hT_d", [NT_FULL, 128, 1024], BF16, kind="Internal").ap()
    hTown_d = nc.dram_tensor("hTown_d", [NT_OWN, 128, 1024], BF16, kind="Internal").ap()
    ps_d = nc.dram_tensor("ps_d", [NS_TOK, IN_DIM], F32, kind="Internal").ap()
    x1_d = nc.dram_tensor("x1_d", [OWN + NS_TOK, D], F32, kind="Internal").ap()
    hmT_d = nc.dram_tensor("hmT_d", [NT_OWN + 1, 128, 1024], BF16, kind="Internal").ap()
    w_pa_d = din("w_pa", [512, D])
    w_pb_d = din("w_pb", [512, D])
    w_o_d = din("w_o", [D, D])
    w_up_d = din("w_up", [D, 4 * D])
    w_down_d = din("w_down", [4 * D, D])
    nmlw_d = din("nmlw", [128, 8])
    nfw_d = din("nfw_bc", [128, D])
    y_own = dout("y_own", [OWN, D])
    y_s = dout("y_s", [NS_TOK, D])

    nk_own = dout("nk_own", [OWN, 512])
    nv_own = dout("nv_own", [OWN, 512])
    nks = dout("nks", [NS_TOK, 512])
    nvs = dout("nvs", [NS_TOK, 512])
    ncp = dout("ncp", [3, 1536])
    ncs = dout("ncs", [16, 3, 1536])

    P = Prog(nc)
    with ExitStack() as gst:
        P.setup_sems(gst)

        uid = [0]

        def sbt(st, name, shape, dt):
            uid[0] += 1
            name = f"{name}_{uid[0]}"
            return T(st.enter_context(nc.sbuf_tensor(name, list(shape), dt))[:], P.res(name))

        def pst(st, name, shape, dt):
            uid[0] += 1
            name = f"{name}_{uid[0]}"
            return T(st.enter_context(nc.psum_tensor(name, list(shape), dt))[:], P.res(name))

        def ring(st, fn, name, shape, dt, n):
            return Ring([fn(st, f"{name}{i}", shape, dt) for i in range(n)])

        identf = sbt(gst, "identf", [128, 128], F32)
        identb = sbt(gst, "identb", [128, 128], BF16)
        epsc = sbt(gst, "epsc", [128, 1], F32)
        nmw_t = sbt(gst, "nmw_t", [128, 8], F32)
        hsT = sbt(gst, "hsT", [128, 8, NS_TOK], BF16)
        hTlast = sbt(gst, "hTlast", [128, 8, 128], BF16)
        onec = sbt(gst, "onec", [128, 1], F32)
        sel_t = sbt(gst, "sel_t", [128, 4], F32)


        def apx(v):
            return v.ap if isinstance(v, T) else v

        def mm(out, lhsT, rhs, start=True, stop=True):
            P.op("pe", lambda e: e.matmul(out=out.ap, lhsT=lhsT.ap, rhs=rhs.ap, start=start, stop=stop), [lhsT, rhs], [out], pe_acc=True)

        def tr(out, in_, idn):
            P.op("pe", lambda e: e.transpose(out=out.ap, in_=in_.ap, identity=idn.ap), [in_, idn], [out], pe_acc=True)

        def act(out, in_, func, bias=None, scale=None):
            kw = {}
            rd = [in_]
            if bias is not None:
                kw["bias"] = apx(bias)
                if isinstance(bias, T):
                    rd.append(bias)
            if scale is not None:
                kw["scale"] = apx(scale)
                if isinstance(scale, T):
                    rd.append(scale)
            P.op("act", lambda e: e.activation(out=out.ap, in_=in_.ap, func=func, **kw), rd, [out])

        def ts(out, in0, s1, op0, s2=None, op1=None, eng="dve"):
            rd = [in0] + [v for v in (s1, s2) if isinstance(v, T)]
            if op1 is None:
                P.op(eng, lambda e: e.tensor_scalar(out=out.ap, in0=in0.ap, scalar1=apx(s1), scalar2=None, op0=op0), rd, [out])
            else:
                P.op(eng, lambda e: e.tensor_scalar(out=out.ap, in0=in0.ap, scalar1=apx(s1), scalar2=apx(s2), op0=op0, op1=op1), rd, [out])

        def tt(out, in0, in1, op, eng="dve"):
            P.op(eng, lambda e: e.tensor_tensor(out=out.ap, in0=in0.ap, in1=in1.ap, op=op), [in0, in1], [out])

        def stt(out, in0, scalar, in1, op0, op1):
            rd = [in0, in1] + ([scalar] if isinstance(scalar, T) else [])
            P.op("dve", lambda e: e.scalar_tensor_tensor(out=out.ap, in0=in0.ap, scalar=apx(scalar), in1=in1.ap, op0=op0, op1=op1), rd, [out])

        def cp(out, in_, eng="dve"):
            if eng == "act":
                P.op("act", lambda e: e.copy(out=out.ap, in_=in_.ap), [in_], [out])
            else:
                P.op(eng, lambda e: e.tensor_copy(out=out.ap, in_=in_.ap), [in_], [out])

        def recip(out, in_):
            P.op("dve", lambda e: e.reciprocal(out=out.ap, in_=in_.ap), [in_], [out])

        def rsum(out, in_):
            P.op("dve", lambda e: e.reduce_sum(out=out.ap, in_=in_.ap, axis=AX.X), [in_], [out])

        def memset(t, v, eng="pool"):
            P.op(eng, lambda e: e.memset(t.ap, v), [], [t])

        def sub(t, idx, name):
            return T(t.ap[idx], P.res(name))

        def norm_tile(tl, x_rows_ap, hT_dst, nt=128):
            xt = tl["x"].next()
            P.dma("sp", xt[:nt], T(x_rows_ap, None))
            norm_core(tl, xt, hT_dst, nt)

        def norm_core(tl, xt, hT_dst, nt=128):
            sq = tl["sq"].next()
            P.op("act", lambda e: e.activation(out=sq.ap[:nt], in_=xt.ap[:nt], func=AF.Square), [xt], [sq])
            ss = tl["ss"].next()
            P.op("dve", lambda e: e.reduce_sum(out=ss.ap[:nt], in_=sq.ap[:nt], axis=AX.X), [sq], [ss])
            P.op("act", lambda e: e.activation(out=ss.ap[:nt], in_=ss.ap[:nt], func=AF.Ln, bias=epsc.ap[:nt], scale=1.0 / D), [ss, epsc], [ss])
            P.op("act", lambda e: e.activation(out=ss.ap[:nt], in_=ss.ap[:nt], func=AF.Exp, scale=-0.5), [ss], [ss])
            hb = tl["hb"].next()
            P.op("dve", lambda e: e.tensor_scalar(out=hb.ap[:nt], in0=xt.ap[:nt], scalar1=ss.ap[:nt, 0:1], scalar2=None, op0=ALU.mult), [xt, ss], [hb])
            pt = tl["ptr"].next()
            for k in range(8):
                P.op("pe", lambda e, k=k: e.transpose(out=pt.ap[:, k, :nt], in_=hb.ap[:nt, k * 128:(k + 1) * 128], identity=identb.ap[:nt, :nt]), [hb, identb], [pt], pe_acc=True)
            P.op("act", lambda e: e.copy(out=hT_dst.ap, in_=pt.ap[:, :, :nt]), [pt], [hT_dst])

        def load_w(tl, dst, w_dram, c0, n, scale=None, nk=8):
            for k in range(nk):
                stg = tl["wst"].next()
                P.dma("sp", stg[:, :n], T(w_dram[k * 128:(k + 1) * 128, c0:c0 + n], None))
                eng = ("pool", "dve")[k % 2]
                if scale is not None:
                    P.op(eng, lambda e, k=k, stg=stg: e.tensor_scalar(out=dst.ap[:, k, :n], in0=stg.ap[:, :n], scalar1=scale.ap[:, k:k + 1], scalar2=None, op0=ALU.mult), [stg, scale], [dst])
                else:
                    P.op(eng, lambda e, k=k, stg=stg: e.tensor_copy(out=dst.ap[:, k, :n], in_=stg.ap[:, :n]), [stg], [dst])

        with ExitStack() as st:
            tl = {
                "x": ring(st, sbt, "x", [128, D], F32, 3),
                "sq": ring(st, sbt, "sq", [128, D], F32, 2),
                "ss": ring(st, sbt, "ss", [128, 1], F32, 4),
                "hb": ring(st, sbt, "hb", [128, D], BF16, 2),
                "ptr": ring(st, pst, "ptr", [128, 8, 128], BF16, 2),
            }
            hTt = ring(st, sbt, "hTt", [128, 8, 128], BF16, 3)
            P.dma("sp", identf, T(ident_d, None))
            P.dma("sp", nmw_t, T(nmw, None))
            P.op("pool", lambda e: e.memset(epsc.ap, EPS), [], [epsc])
            P.op("pool", lambda e: e.memset(onec.ap, 1.0), [], [onec])
            P.dma("sp", sel_t, T(sel_d, None))
            P.op("dve", lambda e: e.tensor_copy(out=identb.ap, in_=identf.ap), [identf], [identb])
            for t in sorted(set(list(range(int(os.environ.get('NT0', NT_FULL)))) + [NT_FULL - 1])):
                dst = hTt.next() if t < NT_FULL - 1 else hTlast
                norm_tile(tl, xfull[t * 128:(t + 1) * 128, :], dst)
                P.dma("sp", T(hT_d[t].rearrange("p (k n) -> p k n", k=8), None), dst, store=True)
            for t in range(min(NT_OWN, int(os.environ.get('NT0', NT_OWN)))):
                dst = hTt.next()
                norm_tile(tl, xown[t * 128:(t + 1) * 128, :], dst)
                P.dma("sp", T(hTown_d[t].rearrange("p (k n) -> p k n", k=8), None), dst, store=True)
            norm_tile(tl, xs[:, :], hsT, nt=NS_TOK)
            P.emit()

        for _skip in ([] if os.environ.get('SKIP1') == '1' else [0]):
          with ExitStack() as st:
              tl = {"wst": ring(st, sbt, "wst", [128, 512], F32, 4)}
              wg = ring(st, sbt, "wg", [128, 8, 512], BF16, 2)
              wkv = sbt(st, "wkv", [128, 8, 1024], BF16)
              pmm = ring(st, pst, "pmm", [128, 512], F32, 4)
              ob = ring(st, sbt, "ob", [128, 1024], F32, 3)
              cps = sbt(st, "cps", [3, 1536], F32)
              ps_s = sbt(st, "ps_s", [NS_TOK, IN_DIM], F32)
              hTo_r = ring(st, sbt, "hTo", [128, 8, 128], BF16, 3)
              ngrp = (IN_DIM + 511) // 512
              for gi in range(ngrp):
                  c0 = gi * 512
                  n = min(512, IN_DIM - c0)
                  w = wg.next()
                  load_w(tl, w, w_in, c0, n, scale=nmw_t)
                  pm = pmm.next()
                  for k in range(8):
                      P.op("pe", lambda e, k=k, w=w, pm=pm, n=n: e.matmul(out=pm.ap[:NS_TOK, :n], lhsT=hsT.ap[:, k, :], rhs=w.ap[:, k, :n], start=(k == 0), stop=(k == 7)), [hsT, w], [pm], pe_acc=True)
                  P.op("act", lambda e, pm=pm, c0=c0, n=n: e.copy(out=ps_s.ap[:, c0:c0 + n], in_=pm.ap[:NS_TOK, :n]), [pm], [ps_s])
                  if gi in (1, 2):
                      P.op("pool", lambda e, w=w, gi=gi: e.tensor_copy(out=wkv.ap[:, :, (gi - 1) * 512:gi * 512], in_=w.ap), [w], [wkv])
                  if gi in (3, 4, 5):
                      pm2 = pmm.next()
                      for k in range(8):
                          P.op("pe", lambda e, k=k, w=w, pm2=pm2: e.matmul(out=pm2.ap[:3, :], lhsT=hTlast.ap[:, k, 125:128], rhs=w.ap[:, k, :], start=(k == 0), stop=(k == 7)), [hTlast, w], [pm2], pe_acc=True)
                      P.op("act", lambda e, pm2=pm2, gi=gi: e.copy(out=cps.ap[:, (gi - 3) * 512:(gi - 2) * 512], in_=pm2.ap[:3, :]), [pm2], [cps])
              P.dma("sp", T(ncp, None), cps, store=True)
              P.dma("sp", T(nks, None), ps_s[:, 512:1024], store=True)
              P.dma("sp", T(nvs, None), ps_s[:, 1024:1536], store=True)
              for r in range(3):
                  P.dma("sp", T(ncs[:, r, :], None), T(ps_s.ap[r + 1:NS_TOK:4, 1536:3072], ps_s.res), store=True)
              P.dma("sp", T(ps_d, None), ps_s, store=True)
              for t in range(NT_OWN):
                  o = ob.next()
                  hTo = hTo_r.next()
                  P.dma("sp", hTo, T(hTown_d[t].rearrange("p (k n) -> p k n", k=8), None))
                  for half in range(2):
                      pm = pmm.next()
                      for k in range(8):
                          P.op("pe", lambda e, k=k, pm=pm, half=half, hTo=hTo: e.matmul(out=pm.ap, lhsT=hTo.ap[:, k, :], rhs=wkv.ap[:, k, half * 512:(half + 1) * 512], start=(k == 0), stop=(k == 7)), [hTo, wkv], [pm], pe_acc=True)
                      P.op(("act", "dve")[half], lambda e, pm=pm, half=half, o=o: (e.copy if half == 0 else e.tensor_copy)(out=o.ap[:, half * 512:(half + 1) * 512], in_=pm.ap), [pm], [o])
                  P.dma("sp", T(nk_own[t * 128:(t + 1) * 128, :], None), o[:, 0:512], store=True)
                  P.dma("sp", T(nv_own[t * 128:(t + 1) * 128, :], None), o[:, 512:1024], store=True)
              P.emit()

        mid = ExitStack()
        oBown = sbt(mid, "oBown", [128, 4, OWN], BF16)
        oAown = sbt(mid, "oAown", [128, 4, OWN], BF16)
        oBs = sbt(mid, "oBs", [128, 4, NS_TOK], BF16)
        oAs = sbt(mid, "oAs", [128, 4, NS_TOK], BF16)
        with ExitStack() as st:
            memset(oBown, 0.0)
            memset(oAown, 0.0)
            memset(oAs, 0.0)
            tl = {"wst": ring(st, sbt, "wst", [128, 512], F32, 4)}
            cst = sbt(st, "cst", [128, NCST, 128], F32)
            P.dma("sp", cst, T(cst_d, None))
            C_UINCL, C_BIGU, C_NBIGL, C_SU, C_BD, C_OFFM = (cst[:, i, :] for i in range(6))
            ones = sbt(st, "ones", [128, 128], F32)
            memset(ones, 1.0)
            cw = sbt(st, "cw", [128, 12, 4], F32)
            P.dma("sp", cw, T(cw_d, None))
            negA = sbt(st, "negA", [128, 4], F32)
            dtb = sbt(st, "dtb", [128, 4], F32)
            gnw = sbt(st, "gnw", [128, 128], F32)
            P.dma("sp", negA, T(alog_d, None))
            P.dma("sp", dtb, T(dtb_d, None))
            P.dma("sp", gnw, T(gnw_d, None))
            act(negA, negA, AF.Exp)
            ts(negA, negA, -1.0, ALU.mult)
            wgd = sbt(st, "wgd", [128, 8, 2056], BF16)
            for gi in range(4):
                load_w(tl, wgd[:, :, gi * 512:(gi + 1) * 512], w_in, 1536 + gi * 512, 512, scale=nmw_t)
            load_w(tl, wgd[:, :, 2048:2056], w_in, 3584, 8, scale=nmw_t)
            S = [[sbt(st, f"S{h}_{i}", [128, 128], F32) for i in range(2)] for h in range(4)]
            for h in range(4):
                memset(S[h][0], 0.0)
            ub_r = ring(st, sbt, "ub", [128, 3, 515], F32, 2)
            hist = [sbt(st, f"hist{h}", [128, 3, 3], F32) for h in range(4)]
            for h in range(4):
                memset(hist[h], 0.0)
            hTg_r = ring(st, sbt, "hTg", [128, 8, 512], BF16, 2)
            pbank = [pst(st, f"pb{i}", [128, 4, 128], F32) for i in range(3)]
            pu = pst(st, "pu", [128, 3, 512], F32)
            pss = pst(st, "pss", [128, 512], F32)
            pq = Ring([T(pbank[i % 3].ap[:, (i // 3) % 4, :], pbank[i % 3].res) for i in range(12)])
            sq_r = Ring([sbt(st, f"sq{i}", [128, 128], F32) for i in range(32)])
            col_r = Ring([sbt(st, f"col{i}", [128, 1], F32) for i in range(32)])
            big_r = Ring([sbt(st, f"big{i}", [128, 3, 512], F32) for i in range(3)])
            nrm_r = Ring([sbt(st, f"nrm{i}", [128, 512], F32) for i in range(4)])
            abg = sbt(st, "abg", [128, 4, 8], F32)
            gb_r = Ring([sbt(st, f"gb{i}", [128, 4, 4], F32) for i in range(8)])
            ogb_r = Ring([sbt(st, f"ogb{i}", [128, 128], BF16) for i in range(2)])
            ptrb = ring(st, pst, "ptrb", [128, 128], BF16, 1)
            l2e = sbt(st, "l2e", [128, 1], F32)
            memset(l2e, 1e-6)
            dtb4 = sbt(st, "dtb4", [128, 4, 4], F32)
            negA4 = sbt(st, "negA4", [128, 4, 4], F32)
            for c in range(4):
                cp(dtb4[:, c, :], dtb, eng="pool")
                cp(negA4[:, c, :], negA, eng="pool")

            STAGE = int(os.environ.get('GDN_STAGE', '99'))

            def gdn_chunk(h, qT, kT, vT, gcol, bcol, nbcol, hTc, own_dst, selcol, Sin, Sout):
                G = sq_r.next()
                ts(G, C_UINCL, gcol, ALU.mult)
                PA, PB, pm1 = pq.next(), pq.next(), pq.next()
                mm(PA, ones, G, True, False)
                mm(PA, identf, C_BIGU, False, True)
                mm(PB, ones, G, True, False)
                mm(PB, identf, C_NBIGL, False, True)
                mm(pm1[:, 0:1], ones, G[:, 127:128])
                mm(pm1[:, 1:2], C_UINCL, gcol)
                cg, ncg, ecg, egl, edl = (col_r.next() for _ in range(5))
                cp(cg, pm1[:, 1:2], eng="act")
                ts(ncg, pm1[:, 1:2], -1.0, ALU.mult)
                decay, decayT = sq_r.next(), sq_r.next()
                act(decay, PA, AF.Exp, bias=cg, scale=-1.0)
                act(decayT, PB, AF.Exp, bias=ncg, scale=1.0)
                act(ecg, cg, AF.Exp)
                act(egl, pm1[:, 0:1], AF.Exp)
                act(edl, pm1[:, 0:1], AF.Exp, bias=ncg, scale=1.0)
                if STAGE < 4:
                    return None
                KK, QKT = pq.next(), pq.next()
                mm(KK, kT, kT)
                mm(QKT, kT, qT)
                t0, Wm, AT = sq_r.next(), sq_r.next(), sq_r.next()
                tt(t0, KK, decayT, ALU.mult)
                stt(Wm, t0, bcol, C_SU, ALU.mult, ALU.mult)
                tt(AT, QKT, decayT, ALU.mult)
                WmTp = pq.next()
                tr(WmTp, Wm, identf)
                WmT = sq_r.next()
                cp(WmT, WmTp, eng="act")
                if STAGE < 5:
                    return None
                X, XT, OffT, Pk = sq_r.next(), sq_r.next(), sq_r.next(), sq_r.next()
                tt(X, Wm, C_BD, ALU.mult)
                tt(XT, WmT, C_BD, ALU.mult)
                tt(OffT, WmT, C_OFFM, ALU.mult)
                tt(Pk, identf, X, ALU.subtract)
                for lvl in range(1, 6):
                    p2 = pq.next()
                    mm(p2, X, XT)
                    X2T = sq_r.next()
                    if lvl < 5:
                        p1 = pq.next()
                        mm(p1, XT, X)
                        X2 = sq_r.next()
                        cp(X2, p1, eng="act")
                    cp(X2T, p2, eng="dve")
                    p3 = pq.next()
                    mm(p3, X2T, Pk)
                    Pn = sq_r.next()
                    tt(Pn, p3, Pk, ALU.add)
                    Pk = Pn
                    XT = X2T
                    if lvl < 5:
                        X = X2
                Pd = Pk
                PdTp = pq.next()
                tr(PdTp, Pd, identf)
                PdT = sq_r.next()
                cp(PdT, PdTp, eng="act")
                t1p = pq.next()
                mm(t1p, OffT, Pd)
                t1 = sq_r.next()
                cp(t1, t1p, eng="dve")
                w2p = pq.next()
                mm(w2p, PdT, t1)
                W = sq_r.next()
                tt(W, Pd, w2p, ALU.subtract)
                if STAGE < 6:
                    return None
                ktp, vtp = pq.next(), pq.next()
                tr(ktp, kT, identf)
                tr(vtp, vT, identf)
                ke, kd, vtm = sq_r.next(), sq_r.next(), sq_r.next()
                ts(ke, ktp, ecg, ALU.mult)
                ts(kd, ktp, edl, ALU.mult)
                cp(vtm, vtp, eng="act")
                u0p, w0Tp = pq.next(), pq.next()
                mm(u0p, W, vtm)
                mm(w0Tp, ke, W)
                u0b, w0T = sq_r.next(), sq_r.next()
                ts(u0b, u0p, bcol, ALU.mult)
                cp(w0T, w0Tp, eng="act")
                if STAGE < 7:
                    return None
                wSp = pq.next()
                mm(wSp, w0T, Sin)
                vn = sq_r.next()
                stt(vn, wSp, nbcol, u0b, ALU.mult, ALU.add)
                qSp, Avp = pq.next(), pq.next()
                mm(qSp, qT, Sin)
                mm(Avp, AT, vn)
                Av, o = sq_r.next(), sq_r.next()
                cp(Av, Avp, eng="act")
                stt(o, qSp, ecg, Av, ALU.mult, ALU.add)
                KVp = pq.next()
                mm(KVp, kd, vn)
                stt(Sout, Sin, egl, KVp, ALU.mult, ALU.add)
                if STAGE < 8:
                    return None
                zp = pq.next()
                for k in range(8):
                    mm(zp, hTc[:, k, :], wgd[:, k, 1536 + h * 128:1536 + (h + 1) * 128], k == 0, k == 7)
                osq, ms = sq_r.next(), col_r.next()
                tt(osq, o, o, ALU.mult)
                rsum(ms, osq)
                act(ms, ms, AF.Ln, bias=epsc, scale=1.0 / 128)
                act(ms, ms, AF.Exp, scale=-0.5)
                ez, sz, og = sq_r.next(), sq_r.next(), sq_r.next()
                act(ez, zp, AF.Exp, scale=-1.0)
                ts(ez, ez, 1.0, ALU.add)
                recip(ez, ez)
                tt(sz, zp, ez, ALU.mult)
                stt(og, o, ms, gnw, ALU.mult, ALU.mult)
                ogb = ogb_r.next()
                tt(ogb, og, sz, ALU.mult)
                return ogb

            for g in range(int(os.environ.get('GDN_G', '16'))):
                hTg = hTg_r.next()
                for t in range(4):
                    P.dma("sp", hTg[:, :, t * 128:(t + 1) * 128], T(hT_d[4 * g + t].rearrange("p (k n) -> p k n", k=8), None), nowaw=True)
                SUB = int(os.environ.get('GDN_SUB', '9'))
                for c in range(4 if SUB >= 1 else 0):
                    abp = pq.next()
                    for k in range(8):
                        mm(abp[:, 0:8], hTg[:, k, c * 128:(c + 1) * 128], wgd[:, k, 2048:2056], k == 0, k == 7)
                    if os.environ.get('NOCP') != '1':
                        cp(abg[:, c, :], abp[:, 0:8], eng=os.environ.get("CPENG", "act"))
                xa, gg, eb, beta, nbeta = (gb_r.next() for _ in range(5))
                if SUB >= 2:
                    tt(xa, abg[:, :, 0:4], dtb4, ALU.add)
                if SUB >= 3:
                    act(xa, xa, AF.Exp)
                    act(xa, xa, AF.Ln, bias=onec, scale=1.0)
                if SUB >= 4:
                    tt(gg, xa, negA4, ALU.mult)
                    act(eb, abg[:, :, 4:8], AF.Exp, scale=-1.0)
                    ts(eb, eb, 1.0, ALU.add)
                if SUB >= 5:
                    recip(beta, eb)
                    ts(nbeta, beta, -1.0, ALU.mult)
                for h in range(int(os.environ.get('GDN_H', '4')) if STAGE >= 2 else 0):
                    for part in range(3):
                        for k in range(8):
                            mm(pu[:, part, :], wgd[:, k, part * 512 + h * 128:part * 512 + (h + 1) * 128], hTg[:, k, :], k == 0, k == 7)
                    ub = ub_r.next()
                    cp(ub[:, :, 0:3], hist[h], eng="pool")
                    cp(ub[:, :, 3:515], pu, eng="act")
                    cp(hist[h], ub[:, :, 512:515], eng="pool")
                    y = big_r.next()
                    for part in range(3):
                        ci = part * 4 + h
                        ts(y[:, part, :], ub[:, part, 0:512], cw[:, ci, 0:1], ALU.mult)
                        for w in range(1, 4):
                            stt(y[:, part, :], ub[:, part, w:w + 512], cw[:, ci, w:w + 1], y[:, part, :], ALU.mult, ALU.add)
                    e = big_r.next()
                    act(e, y, AF.Exp, scale=-1.0)
                    ts(e, e, 1.0, ALU.add)
                    recip(e, e)
                    sl = big_r.next()
                    tt(sl, y, e, ALU.mult)
                    sqq = e
                    tt(sqq[:, 0:2, :], sl[:, 0:2, :], sl[:, 0:2, :], ALU.mult, eng="pool")
                    qn, kn = nrm_r.next(), nrm_r.next()
                    for part, dst in ((0, qn), (1, kn)):
                        mm(pss, ones, sqq[:, part, :])
                        rs = nrm_r.next()
                        act(rs, pss, AF.Ln, bias=l2e, scale=1.0)
                        act(rs, rs, AF.Exp, scale=-0.5)
                        if part == 0:
                            stt(dst, sl[:, 0, :], 128.0 ** -0.5, rs, ALU.mult, ALU.mult)
                        else:
                            tt(dst, sl[:, 1, :], rs, ALU.mult)
                    for c in range(4 if STAGE >= 3 else 0):
                        cs = slice(c * 128, (c + 1) * 128)
                        nchunk = 4 * g + c
                        Sin, Sout = S[h][nchunk % 2], S[h][(nchunk + 1) % 2]
                        ogb = gdn_chunk(h, qn[:, cs], kn[:, cs], sl[:, 2, cs], gg[:, c, h:h + 1], beta[:, c, h:h + 1], nbeta[:, c, h:h + 1],
                                        hTg[:, :, cs], None, None, Sin, Sout)
                        if ogb is None:
                            continue
                        oTp = ptrb.next()
                        tr(oTp, ogb, identb)
                        dst = oBown[:, h, (g // 4) * 512 + c * 128:(g // 4) * 512 + (c + 1) * 128]
                        stt(dst, oTp, sel_t[:, (g % 4):(g % 4) + 1], dst, ALU.mult, ALU.add)
            for h in range(4):
                P.dma("sp", T(nsp[h], None), S[h][(NT_FULL) % 2], store=True)

            def flat(t, rows, c0, c1):
                return T(t.ap.rearrange("p a b -> p (a b)")[:rows, c0:c1], t.res)

            def bview(t, pat, **kw):
                return T(t.ap.rearrange(pat, **kw), t.res)

            n = NS_TOK
            CS_UINCL, CS_BIGU, CS_NBIGL, CS_SU, CS_BD = (cst[:n, 7 + i, :n] for i in range(5))
            RM = cst[:n, 12, 0:16]
            idn = identf[:n, :n]
            usb, scb, zb = big_r.next(), big_r.next(), big_r.next()
            us = flat(usb, n, 0, 1536)
            scs = flat(scb, 48, 0, 1536)
            zs = flat(zb, n, 0, 512)
            abs_ = flat(zb, n, 512, 520)
            P.dma("sp", us, T(ps_d[:, 1536:3072], None))
            P.dma("sp", scs, T(sc_d, None))
            P.dma("sp", zs, T(ps_d[:, 3072:3584], None))
            P.dma("sp", abs_, T(ps_d[:, 3584:3592], None), nowaw=True)
            uext = sbt(st, "uext", [128, 12, 16, 7], F32)
            ys = sbt(st, "ys", [128, 12, 64], F32)
            es = sbt(st, "es", [128, 12, 64], F32)
            s_all = es
            for cc in range(12):
                ptq = pq.next()
                tr(ptq[:, 0:64], us[:, cc * 128:(cc + 1) * 128], idn)
                tr(ptq[:, 64:112], scs[:, cc * 128:(cc + 1) * 128], identf[:48, :48])
                cp(uext[:, cc, :, 3:7], T(ptq.ap[:, 0:64].rearrange("p (s t) -> p s t", t=4), ptq.res), eng="act")
                cp(uext[:, cc, :, 0:3], T(ptq.ap[:, 64:112].rearrange("p (s t) -> p s t", t=3), ptq.res), eng="dve")
                yv = T(ys.ap[:, cc, :].rearrange("p (s t) -> p s t", t=4), ys.res)
                ts(yv, uext[:, cc, :, 0:4], cw[:, cc, 0:1], ALU.mult)
                for w in range(1, 4):
                    stt(yv, uext[:, cc, :, w:w + 4], cw[:, cc, w:w + 1], yv, ALU.mult, ALU.add)
            act(es, ys, AF.Exp, scale=-1.0)
            ts(es, es, 1.0, ALU.add, eng="pool")
            recip(es, es)
            tt(s_all, ys, es, ALU.mult, eng="pool")
            xas, ggs, ebs, betas = (gb_r.next() for _ in range(4))
            xa2, gg2, eb2, be2 = (T(t.ap.rearrange("p a b -> p (a b)")[:n, 0:4], t.res) for t in (xas, ggs, ebs, betas))
            tt(xa2, abs_[:, 0:4], dtb[:n], ALU.add)
            act(xa2, xa2, AF.Exp)
            act(xa2, xa2, AF.Ln, bias=onec[:n], scale=1.0)
            tt(gg2, xa2, negA[:n], ALU.mult)
            act(eb2, abs_[:, 4:8], AF.Exp, scale=-1.0)
            ts(eb2, eb2, 1.0, ALU.add)
            recip(be2, eb2)
            W3 = sbt(st, "W3", [n, 16 * n], F32)
            I3 = sbt(st, "I3", [n, 16 * n], F32)
            memset(W3, 0.0)
            memset(I3, 0.0)

            def diagv(t):
                a = t.ap
                return T(bass.AP(tensor=a.tensor, offset=a.offset, ap=[list(a.ap[0]), [n + 4, 16], [1, 4]]), t.res)
            cp(diagv(I3), T(idn.ap.rearrange("p (s t) -> p s t", t=4), idn.res), eng="pool")
            w0Tm = sbt(st, "w0Tm", [128, 16 * n], F32)
            qem = sbt(st, "qem", [128, 16 * n], F32)
            Sall = sbt(st, "Sall", [128, 16, 128], F32)
            Snew = Sall
            EGL = sbt(st, "EGL", [128, 16], F32)
            for h in range(4):
                P.dma("sp", Sall, T(ssm_d[:, h].rearrange("s p d -> p s d"), None))
                qs_, ks_, vs_ = s_all[:, h, :], s_all[:, 4 + h, :], s_all[:, 8 + h, :]
                sqv = nrm_r.next()
                tt(sqv[:, 0:n], qs_, qs_, ALU.mult, eng="pool")
                tt(sqv[:, n:2 * n], ks_, ks_, ALU.mult, eng="pool")
                mm(pss[:, 0:2 * n], ones, sqv[:, 0:2 * n])
                rsv = nrm_r.next()
                act(rsv[:, 0:2 * n], pss[:, 0:2 * n], AF.Ln, bias=l2e, scale=1.0)
                act(rsv[:, 0:2 * n], rsv[:, 0:2 * n], AF.Exp, scale=-0.5)
                qkn = nrm_r.next()
                qn, kn = qkn[:, 0:n], qkn[:, n:2 * n]
                stt(qn, qs_, 128.0 ** -0.5, rsv[:, 0:n], ALU.mult, ALU.mult)
                tt(kn, ks_, rsv[:, n:2 * n], ALU.mult)
                gcol, bcol = gg2[:, h:h + 1], be2[:, h:h + 1]
                G = sq_r.next()[:n, :n]
                ts(G, CS_UINCL, gcol, ALU.mult)
                PA, PB, Pm, pm1 = pq.next(), pq.next(), pq.next(), pq.next()
                mm(PA[:n, :n], ones[:n, :n], G, True, False)
                mm(PA[:n, :n], idn, CS_BIGU, False, True)
                mm(PB[:n, :n], ones[:n, :n], G, True, False)
                mm(PB[:n, :n], idn, CS_NBIGL, False, True)
                mm(Pm[:, :n], ones[:n, :], G)
                mm(pm1[:n, 0:1], CS_BD, gcol)
                mm(pm1[:n, 1:2], CS_UINCL, gcol)
                cg, ncg, ecg, edl = (col_r.next()[:n] for _ in range(4))
                cp(cg, pm1[:n, 1:2], eng="act")
                ts(ncg, pm1[:n, 1:2], -1.0, ALU.mult)
                decay, decayT = sq_r.next()[:n, :n], sq_r.next()[:n, :n]
                act(decay, PA[:n, :n], AF.Exp, bias=cg, scale=-1.0)
                act(decayT, PB[:n, :n], AF.Exp, bias=ncg, scale=1.0)
                act(ecg, cg, AF.Exp)
                act(edl, pm1[:n, 0:1], AF.Exp, bias=ncg, scale=1.0)
                act(EGL, T(Pm.ap[:, 3:n:4], Pm.res), AF.Exp)
                KK, QKT = pq.next(), pq.next()
                mm(KK[:n, :n], kn, kn)
                mm(QKT[:n, :n], kn, qn)
                t0, Wm, AT = sq_r.next()[:n, :n], sq_r.next()[:n, :n], sq_r.next()[:n, :n]
                tt(t0, KK[:n, :n], decayT, ALU.mult)
                stt(Wm, t0, bcol, CS_SU, ALU.mult, ALU.mult)
                tt(AT, QKT[:n, :n], decayT, ALU.mult)
                WmTp = pq.next()
                tr(WmTp[:n, :n], Wm, idn)
                WmT = sq_r.next()[:n, :n]
                cp(WmT, WmTp[:n, :n], eng="act")
                Pk = sq_r.next()[:n, :n]
                tt(Pk, idn, Wm, ALU.subtract)
                p2 = pq.next()
                mm(p2[:n, :n], Wm, WmT)
                X2T = sq_r.next()[:n, :n]
                cp(X2T, p2[:n, :n], eng="dve")
                p3 = pq.next()
                mm(p3[:n, :n], X2T, Pk)
                W = sq_r.next()[:n, :n]
                tt(W, p3[:n, :n], Pk, ALU.add)
                dgb = sq_r.next()[:n, :n]
                ts(dgb, idn, bcol, ALU.mult)
                pbr = pq.next()
                mm(pbr[:n, :n], ones[:n, :n], dgb)
                Wb = sq_r.next()[:n, :n]
                tt(Wb, pbr[:n, :n], W, ALU.mult)
                cp(diagv(W3), T(Wb.ap.rearrange("p (s t) -> p s t", t=4), Wb.res), eng="pool")
                ktp, vtp, qtp = pq.next(), pq.next(), pq.next()
                tr(ktp[:n, :], kn, identf)
                tr(vtp[:n, :], vs_, identf)
                tr(qtp[:n, :], qn, identf)
                ke, kd, vtm, qetm = sq_r.next()[:n], sq_r.next()[:n], sq_r.next()[:n], sq_r.next()[:n]
                ts(ke, ktp[:n, :], ecg, ALU.mult)
                ts(kd, ktp[:n, :], edl, ALU.mult)
                cp(vtm, vtp[:n, :], eng="act")
                ts(qetm, qtp[:n, :], ecg, ALU.mult)
                u0p = pq.next()
                mm(u0p[:n, :], Wb, vtm)
                u0b = sq_r.next()[:n]
                cp(u0b, u0p[:n, :], eng="act")
                for half in range(2):
                    mm(pu[:, half, :], ke, W3[:, half * 512:(half + 1) * 512])
                cp(w0Tm, T(pu.ap[:, 0:2, :].rearrange("p a b -> p (a b)"), pu.res), eng="act")
                for half in range(2):
                    mm(pu[:, half, :], qetm, I3[:, half * 512:(half + 1) * 512])
                cp(qem, T(pu.ap[:, 0:2, :].rearrange("p a b -> p (a b)"), pu.res), eng="dve")
                pw = pq.next()
                for sq_i in range(16):
                    mm(pw[:n, :], w0Tm[:, sq_i * n:(sq_i + 1) * n], Sall[:, sq_i, :], sq_i == 0, sq_i == 15)
                vn = sq_r.next()[:n]
                tt(vn, u0b, pw[:n, :], ALU.subtract)
                po = pq.next()
                for sq_i in range(16):
                    mm(po[:n, :], qem[:, sq_i * n:(sq_i + 1) * n], Sall[:, sq_i, :], sq_i == 0, False)
                mm(po[:n, :], AT, vn, False, True)
                o = sq_r.next()[:n]
                cp(o, po[:n, :], eng="act")
                for sq_i in range(16):
                    kdm = sq_r.next()[:n]
                    ts(kdm, kd, RM[:, sq_i:sq_i + 1], ALU.mult, eng="pool")
                    pkv = pq.next()
                    mm(pkv, kdm, vn)
                    stt(Snew[:, sq_i, :], Sall[:, sq_i, :], EGL[:, sq_i:sq_i + 1], pkv, ALU.mult, ALU.add)
                P.dma("sp", T(nss[:, h].rearrange("s p d -> p s d"), None), Snew, store=True)
                osq, ms = sq_r.next()[:n], col_r.next()[:n]
                tt(osq, o, o, ALU.mult, eng="pool")
                rsum(ms, osq)
                act(ms, ms, AF.Ln, bias=epsc[:n], scale=1.0 / 128)
                act(ms, ms, AF.Exp, scale=-0.5)
                zh = zs[:, h * 128:(h + 1) * 128]
                ez, sz, og = sq_r.next()[:n], sq_r.next()[:n], sq_r.next()[:n]
                act(ez, zh, AF.Exp, scale=-1.0)
                ts(ez, ez, 1.0, ALU.add, eng="pool")
                recip(ez, ez)
                tt(sz, zh, ez, ALU.mult)
                stt(og, o, ms, gnw[:n], ALU.mult, ALU.mult)
                ogb = ogb_r.next()[:n]
                tt(ogb, og, sz, ALU.mult)
                oTp = ptrb.next()
                tr(oTp[:, :n], ogb, identb[:n, :n])
                cp(oBs[:, h, :], oTp[:, :n], eng="act")
            P.emit()

        with ExitStack() as st:
            tl = {"wst": ring(st, sbt, "wst", [128, 512], F32, 4)}
            NPAIR = int(os.environ.get('ATT_PAIRS', '4'))
            cst6 = sbt(st, "cst6", [128, 128], F32)
            P.dma("sp", cst6, T(cst_d[:, 6, :], None))
            trilb = sbt(st, "trilb", [128, 128], BF16)
            onesb = sbt(st, "onesb", [128, 128], BF16)
            cp(trilb, cst6)
            memset(onesb, 1.0)
            sbb = sbt(st, "sbb", [128, 8], F32)
            P.dma("sp", sbb, T(sbb_d, None))
            maskt = sbt(st, "maskt", [128, 16, 512], BF16)
            P.dma("sp", maskt, T(mask_d, None))
            KT = sbt(st, "KT", [128, SEQ], BF16)
            Vt = sbt(st, "Vt", [128, NT_FULL, 128], BF16)
            qT = sbt(st, "qT", [128, OWN], BF16)
            wqkv = sbt(st, "wqkv", [128, 8, 384], BF16)
            hTg_r = ring(st, sbt, "hTg3", [128, 8, 512], BF16, 2)
            pS = ring(st, pst, "pS", [128, 512], F32, 3)
            pproj = Ring([pS.tiles[0]])
            pR_r = ring(st, pst, "pR", [128, 512], F32, 2)
            pT_r = ring(st, pst, "pT", [128, 512], F32, 2)
            pO = pst(st, "pO", [128, 512], F32)
            e_r = ring(st, sbt, "e3", [128, 512], BF16, 4)
            sp_r = ring(st, sbt, "sp3", [128, 512], BF16, 3)
            Rt_r = ring(st, sbt, "Rt3", [128, 512], F32, 2)
            ex_r = ring(st, sbt, "ex3", [128, 512], BF16, 2)
            w_r = ring(st, sbt, "w3", [128, 512], BF16, 2)
            carry = [sbt(st, f"carry{i}", [128, 512], F32) for i in range(2)]
            for p in range(NPAIR):
                load_w(tl, wqkv[:, :, 0:128], w_in, p * 128, 128, scale=nmw_t)
                load_w(tl, wqkv[:, :, 128:256], w_in, 512 + p * 128, 128, scale=nmw_t)
                load_w(tl, wqkv[:, :, 256:384], w_in, 1024 + p * 128, 128, scale=nmw_t)
                for g in range(16):
                    hTg = hTg_r.next()
                    for t in range(4):
                        P.dma("sp", hTg[:, :, t * 128:(t + 1) * 128], T(hT_d[4 * g + t].rearrange("p (k n) -> p k n", k=8), None), nowaw=True)
                    pk = pproj.next()
                    for k in range(8):
                        mm(pk, wqkv[:, k, 128:256], hTg[:, k, :], k == 0, k == 7)
                    cp(KT[:, g * 512:(g + 1) * 512], pk, eng="act")
                    pv = pproj.next()
                    for t in range(4):
                        for k in range(8):
                            mm(pv[:, t * 128:(t + 1) * 128], hTg[:, k, t * 128:(t + 1) * 128], wqkv[:, k, 256:384], k == 0, k == 7)
                    cp(Vt[:, 4 * g:4 * g + 4, :], T(pv.ap.rearrange("p (t c) -> p t c", t=4), pv.res), eng="dve")
                for i in range(4):
                    hTg = hTg_r.next()
                    for t in range(4):
                        P.dma("sp", hTg[:, :, t * 128:(t + 1) * 128], T(hTown_d[4 * i + t].rearrange("p (k n) -> p k n", k=8), None), nowaw=True)
                    pq_ = pproj.next()
                    for k in range(8):
                        mm(pq_, wqkv[:, k, 0:128], hTg[:, k, :], k == 0, k == 7)
                    cp(qT[:, i * 512:(i + 1) * 512], pq_, eng="act")
                for hh in range(2):
                    h = 2 * p + hh
                    hs = slice(64 * hh, 64 * hh + 64)
                    for i in range(int(os.environ.get('ATT_SLOTS', '4'))):
                        nkb = 16 * i + 16
                        cur = 0
                        memset(carry[0], 0.0)
                        def front(KB):
                            Sp = pS.next()
                            masked = KB >= 16 * i
                            mm(Sp, KT[hs, KB * 128:(KB + 1) * 128], qT[hs, i * 512:(i + 1) * 512], True, not masked)
                            if masked:
                                mm(Sp, identb, maskt[:, KB - 16 * i, :], False, True)
                            e = e_r.next()
                            act(e, Sp, AF.Exp, bias=sbb[:, h:h + 1], scale=0.125)
                            sp = sp_r.next()
                            act(sp, e, AF.Ln, bias=onec, scale=1.0)
                            return (KB, e, sp)

                        def midstage(stg):
                            KB, e, sp = stg
                            pR, pT = pR_r.next(), pT_r.next()
                            mm(pR, trilb, sp)
                            mm(pT, onesb, sp)
                            return (KB, e, pR, pT)

                        def back(stg, idx, cur):
                            KB, e, pR, pT = stg
                            Rt = Rt_r.next()
                            tt(Rt, pR, carry[cur], ALU.add)
                            tt(carry[1 - cur], pT, carry[cur], ALU.add)
                            ex = ex_r.next()
                            act(ex, Rt, AF.Exp, scale=-1.0)
                            w = w_r.next()
                            tt(w, e, ex, ALU.mult)
                            mm(pO, Vt[:, KB, :], w, idx == 0, idx == nkb - 1)

                        order = list(range(nkb - 1, -1, -1))
                        fq, mq = [], []
                        done = 0
                        for KB in order:
                            fq.append(front(KB))
                            if len(fq) >= 2:
                                mq.append(midstage(fq.pop(0)))
                            if len(mq) >= 2:
                                back(mq.pop(0), done, cur)
                                cur = 1 - cur
                                done += 1
                        while fq:
                            mq.append(midstage(fq.pop(0)))
                            if len(mq) >= 2:
                                back(mq.pop(0), done, cur)
                                cur = 1 - cur
                                done += 1
                        while mq:
                            back(mq.pop(0), done, cur)
                            cur = 1 - cur
                            done += 1
                        cp(oAown[hs, p, i * 512:(i + 1) * 512], pO[hs, :], eng="act")
            P.emit()


        with ExitStack() as st:
            NSEQ = int(os.environ.get('SATT_SEQS', '16'))
            SST = int(os.environ.get('SATT_STAGE', '9'))
            n = NS_TOK
            cst6 = sbt(st, "cst6s", [128, 128], F32)
            P.dma("sp", cst6, T(cst_d[:, 6, :], None))
            trilb = sbt(st, "trilbs", [128, 128], BF16)
            onesb = sbt(st, "onesbs", [128, 128], BF16)
            cp(trilb, cst6)
            memset(onesb, 1.0)
            ptt = sbt(st, "ptt", [128, 256], I32)
            idx = sbt(st, "idx", [128, 256], I32)
            iot = sbt(st, "iot", [128, 256], I32)
            P.dma("sp", ptt, T(pt_d.partition_broadcast(128), None))
            P.dma("sp", iot, T(iota_d, None))
            P.op("dve", lambda e: e.tensor_scalar(out=idx.ap, in0=ptt.ap, scalar1=7, scalar2=None, op0=ALU.logical_shift_left), [ptt], [idx])
            P.op("dve", lambda e: e.tensor_tensor(out=idx.ap, in0=idx.ap, in1=iot.ap, op=ALU.bitwise_or), [idx, iot], [idx])
            sbrow = sbt(st, "sbrow", [128, 512], F32)
            P.dma("sp", sbrow, T(sbrow_d, None))
            mnew = sbt(st, "mnew", [128, 32], F32)
            P.dma("sp", mnew, T(mnew_d, None))
            selE = sbt(st, "selE", [64, 256], F32)
            P.dma("sp", selE, T(selE_d, None))
            dm = sbt(st, "dm", [32, 512], F32)
            P.dma("sp", dm, T(dm_d, None))
            qkv = sbt(st, "qkvs", [n, 1536], F32)
            P.dma("sp", qkv, T(ps_d[:, 0:1536], None))
            qsT = sbt(st, "qsT", [128, 4, n], BF16)
            ksT = sbt(st, "ksT", [128, 4, 256], BF16)
            memset(ksT, 0.0)
            ptk_r = ring(st, pst, "ptk", [128, 4, 128], F32, 2)
            pS_r = ring(st, pst, "pSs", [128, 512], F32, 2)
            pR = pst(st, "pRs", [128, 512], F32)
            pT = pst(st, "pTs", [128, 512], F32)
            pO = pst(st, "pOs", [128, 512], F32)
            psm = pst(st, "psm", [128, 512], F32)
            for p in range(4):
                ptk = ptk_r.next()
                tr(ptk[:, 0, :n], qkv[:, p * 128:(p + 1) * 128], identf[:n, :n])
                tr(ptk[:, 1, :n], qkv[:, 512 + p * 128:512 + (p + 1) * 128], identf[:n, :n])
                cp(qsT[:, p, :], ptk[:, 0, :n], eng="act")
                cp(ksT[:, p, 0:n], ptk[:, 1, :n], eng="dve")
            kpg_r = ring(st, sbt, "kpg", [128, 512], F32, 3)
            vpg_r = ring(st, sbt, "vpg", [128, 512], F32, 3)
            Vb_r = ring(st, sbt, "Vb", [128, 16, 512], BF16, 2)
            KTs_r = ring(st, sbt, "KTs", [128, 4, 128], BF16, 3)
            f_r = ring(st, sbt, "fs", [128, 512], F32, 5)
            b_r = ring(st, sbt, "bs", [128, 512], BF16, 3)
            C = sbt(st, "Cs", [128, 16, 32], F32)
            sm_r = ring(st, sbt, "sms", [128, 512], F32, 4)
            smb_r = ring(st, sbt, "smbs", [128, 512], BF16, 4)
            o2 = sbt(st, "o2s", [32, 2, 64], F32)
            for sq_i in range(NSEQ):
                Sp = pS_r.next()
                Vb = Vb_r.next()
                tsl = slice(4 * sq_i, 4 * sq_i + 4)
                for pg in range(16):
                    col = sq_i * 16 + pg
                    kpg, vpg = kpg_r.next(), vpg_r.next()
                    for dst, src in ((kpg, ck_d), (vpg, cv_d)):
                        def gfn(e, dst=dst, src=src, col=col):
                            return e.indirect_dma_start(out=dst.ap, out_offset=None, in_=src,
                                                        in_offset=bass.IndirectOffsetOnAxis(ap=idx.ap[:, col:col + 1], axis=0))
                        P._add("pool", gfn, [idx], [dst], dma=True, evres=dst.res)
                    cp(Vb[:, pg, :], vpg, eng=("dve", "act")[pg % 2])
                    if SST < 2:
                        tr(ptk_r.next()[:, 0, :], kpg[:, 0:128], identf)
                        continue
                    ptk = ptk_r.next()
                    for p in range(4):
                        tr(ptk[:, p, :], kpg[:, p * 128:(p + 1) * 128], identf)
                    KTs = KTs_r.next()
                    cp(KTs, ptk, eng=("act", "dve")[pg % 2])
                    if SST < 3:
                        continue
                    for p in range(4):
                        for hh in range(2):
                            hs = slice(64 * hh, 64 * hh + 64)
                            c0 = pg * 32 + (2 * p + hh) * 4
                            mm(Sp[:, c0:c0 + 4], KTs[hs, p, :], qsT[hs, p, tsl])
                if SST < 4:
                    continue
                for p in range(4):
                    for hh in range(2):
                        hs = slice(64 * hh, 64 * hh + 64)
                        c0 = (2 * p + hh) * 4
                        mm(psm[:, c0:c0 + 4], ksT[hs, p, 4 * sq_i:4 * sq_i + 128], qsT[hs, p, tsl])
                z, e = f_r.next(), f_r.next()
                stt(z, Sp, 0.125, sbrow, ALU.mult, ALU.add)
                act(e, z, AF.Exp)
                sp = b_r.next()
                act(sp, e, AF.Ln, bias=onec, scale=1.0)
                zn, en = sm_r.next(), sm_r.next()
                stt(zn[:, 0:32], psm[:, 0:32], 0.125, sbrow[:, 0:32], ALU.mult, ALU.add)
                act(en[:, 0:32], zn[:, 0:32], AF.Exp)
                tt(en[:, 0:32], en[:, 0:32], mnew, ALU.mult)
                spn = smb_r.next()
                act(spn[:, 0:32], en[:, 0:32], AF.Ln, bias=onec, scale=1.0)
                mm(pR, trilb, sp)
                mm(pT, onesb, sp)
                mm(psm[:, 64:96], onesb, spn[:, 0:32])
                mm(psm[:, 128:160], trilb, spn[:, 0:32])
                cp(C[:, 15, :], psm[:, 64:96], eng="dve")
                for pg in range(14, -1, -1):
                    tt(C[:, pg, :], pT[:, (pg + 1) * 32:(pg + 2) * 32], C[:, pg + 1, :], ALU.add)
                R, ex = f_r.next(), f_r.next()
                tt(R, pR, T(C.ap.rearrange("p a b -> p (a b)"), C.res), ALU.add)
                act(ex, R, AF.Exp, scale=-1.0)
                w = b_r.next()
                tt(w, e, ex, ALU.mult, eng="pool")
                exn = sm_r.next()
                act(exn[:, 0:32], psm[:, 128:160], AF.Exp, scale=-1.0)
                wn = smb_r.next()
                tt(wn[:, 0:32], en[:, 0:32], exn[:, 0:32], ALU.mult)
                if SST < 5:
                    continue
                mm(pO, selE[:, 4 * sq_i:4 * sq_i + 128], qkv[:, 1024:1536])
                v4 = smb_r.next()
                cp(v4, pO, eng="act")
                for pg in range(16):
                    mm(pR[0:32, :], w[:, pg * 32:(pg + 1) * 32], Vb[:, pg, :], pg == 0, False)
                mm(pR[0:32, :], wn[:, 0:32], v4, False, True)
                od = sm_r.next()[0:32]
                tt(od, pR[0:32, :], dm, ALU.mult)
                rsum(o2[:, 0, :], T(od.ap.rearrange("p (h d) -> p d h", h=8), od.res))
                cp(o2[:, 1, :], o2[:, 0, :], eng="pool")
                tr(psm[:, 256:288], T(o2.ap.rearrange("p a b -> p (a b)"), o2.res), identf[:32, :32])
                for hh in range(2):
                    hs = slice(64 * hh, 64 * hh + 64)
                    src = T(psm.ap[hs, 256:288].rearrange("p (a b c) -> p a b c", a=4, b=2)[:, :, hh, :], psm.res)
                    cp(oAs[hs, :, tsl], src, eng=("act", "dve")[hh])
            P.emit()

        with ExitStack() as st:
            tl = {"wst": ring(st, sbt, "wst", [128, 512], F32, 4),
                  "sq": ring(st, sbt, "sq4", [128, D], F32, 1),
                  "ss": ring(st, sbt, "ss4", [128, 1], F32, 4),
                  "hb": ring(st, sbt, "hb4", [128, D], BF16, 2),
                  "ptr": ring(st, pst, "ptr4", [128, 8, 128], BF16, 1)}
            w2 = sbt(st, "w2", [128, 8, 2048], BF16)
            for gi in range(4):
                load_w(tl, w2[:, :, gi * 512:(gi + 1) * 512], w_in, 3592 + gi * 512, 512, scale=nmw_t)
            wpa = sbt(st, "wpa", [128, 4, D], BF16)
            wpb = sbt(st, "wpb", [128, 4, D], BF16)
            wo = sbt(st, "wo", [128, 8, D], BF16)
            for half in range(2):
                load_w(tl, wpa[:, :, half * 512:(half + 1) * 512], w_pa_d, half * 512, 512, nk=4)
                load_w(tl, wpb[:, :, half * 512:(half + 1) * 512], w_pb_d, half * 512, 512, nk=4)
                load_w(tl, wo[:, :, half * 512:(half + 1) * 512], w_o_d, half * 512, 512, nk=8)
            hTg_r = ring(st, sbt, "hTg4", [128, 8, 512], BF16, 2)
            pg = ring(st, pst, "pg4", [128, 512], F32, 5)
            pmix = ring(st, pst, "pmix4", [128, 512], F32, 2)
            sg_r = ring(st, sbt, "sg4", [128, 512], F32, 4)
            mm_r = ring(st, sbt, "mm4", [128, 512], F32, 4)
            mT_r = ring(st, sbt, "mT4", [128, 8, 512], BF16, 2)
            x_r = ring(st, sbt, "x4", [128, D], F32, 3)
            hT_r = ring(st, sbt, "hTt4", [128, 8, 128], BF16, 2)
            groups = []
            for G in range(4):
                groups.append(dict(n=512, hT_src=[hTown_d[4 * G + t] for t in range(4)], hT_sb=None,
                                   oA=oAown[:, :, G * 512:(G + 1) * 512], oB=oBown[:, :, G * 512:(G + 1) * 512],
                                   x_rows=[xown[(4 * G + t) * 128:(4 * G + t + 1) * 128, :] for t in range(4)],
                                   row0=G * 512, tile0=4 * G))
            groups.append(dict(n=NS_TOK, hT_src=[], hT_sb=hsT, oA=oAs, oB=oBs, x_rows=[xs[:, :]], row0=OWN, tile0=NT_OWN))
            for gd in groups:
                n = gd["n"]
                if gd["hT_sb"] is not None:
                    hTg = gd["hT_sb"]
                else:
                    hTg = hTg_r.next()
                for t, src in enumerate(gd["hT_src"]):
                    P.dma("sp", hTg[:, :, t * 128:(t + 1) * 128], T(src.rearrange("p (k n) -> p k n", k=8), None), nowaw=True)
                mT = mT_r.next()
                for c in range(8):
                    cs = slice(c * 128, (c + 1) * 128)
                    pgA, pgB, pyA, pyB = pg.next(), pg.next(), pg.next(), pg.next()
                    for k in range(8):
                        mm(pgA[:, :n], w2[:, k, cs], hTg[:, k, :n], k == 0, k == 7)
                    for k in range(8):
                        mm(pgB[:, :n], w2[:, k, 1024 + c * 128:1024 + (c + 1) * 128], hTg[:, k, :n], k == 0, k == 7)
                    for k in range(4):
                        mm(pyA[:, :n], wpa[:, k, cs], gd["oA"][:, k, :], k == 0, k == 3)
                    for k in range(4):
                        mm(pyB[:, :n], wpb[:, k, cs], gd["oB"][:, k, :], k == 0, k == 3)
                    sgA, sgB = sg_r.next(), sg_r.next()
                    for sg_, pg_ in ((sgA, pgA), (sgB, pgB)):
                        act(sg_[:, :n], pg_[:, :n], AF.Exp, scale=-1.0)
                        ts(sg_[:, :n], sg_[:, :n], 1.0, ALU.add, eng="pool")
                        recip(sg_[:, :n], sg_[:, :n])
                    mA, mB = mm_r.next(), mm_r.next()
                    tt(mA[:, :n], pyA[:, :n], sgA[:, :n], ALU.mult)
                    tt(mB[:, :n], pyB[:, :n], sgB[:, :n], ALU.mult)
                    tt(mT[:, c, :n], mA[:, :n], mB[:, :n], ALU.add)
                for t, xr in enumerate(gd["x_rows"]):
                    nt = min(128, n - t * 128)
                    xt = x_r.next()
                    P.dma("sp", xt[:nt], T(xr, None))
                    for half in range(2):
                        pm = pmix.next()
                        for c in range(8):
                            mm(pm[:nt, :], mT[:, c, t * 128:t * 128 + nt], wo[:, c, half * 512:(half + 1) * 512], c == 0, c == 7)
                        tt(xt[:nt, half * 512:(half + 1) * 512], pm[:nt, :], xt[:nt, half * 512:(half + 1) * 512], ALU.add)
                    r0 = gd["row0"] + t * 128
                    P.dma("sp", T(x1_d[r0:r0 + nt, :], None), xt[:nt], store=True)
                    hTt = hT_r.next()
                    norm_core(tl, xt, hTt[:, :, :nt], nt)
                    P.dma("sp", T(hmT_d[gd["tile0"] + t].rearrange("p (k n) -> p k n", k=8)[:, :, :nt], None), hTt[:, :, :nt], store=True)
            P.emit()

        mid.close()
        with ExitStack() as st:
            tl = {"wst": ring(st, sbt, "wst", [128, 512], F32, 4)}
            nmlw = sbt(st, "nmlw", [128, 8], F32)
            P.dma("sp", nmlw, T(nmlw_d, None))
            nfw = sbt(st, "nfw", [128, D], F32)
            P.dma("sp", nfw, T(nfw_d, None))
            wup = sbt(st, "wup", [128, 8, 4 * D], BF16)
            wdn = sbt(st, "wdn", [128, 32, D], BF16)
            for gi in range(8):
                load_w(tl, wup[:, :, gi * 512:(gi + 1) * 512], w_up_d, gi * 512, 512, scale=nmlw)
            for half in range(2):
                load_w(tl, wdn[:, :, half * 512:(half + 1) * 512], w_down_d, half * 512, 512, nk=32)
            actT = sbt(st, "actT", [128, 32, 256], BF16)
            hm_r = ring(st, sbt, "hm5", [128, 8, 256], BF16, 2)
            pu5 = ring(st, pst, "pu5", [128, 512], F32, 3)
            pd5 = ring(st, pst, "pd5", [128, 512], F32, 2)
            tmp_r = ring(st, sbt, "tmp5", [128, 256], F32, 2)
            x_r = ring(st, sbt, "x5", [128, D], F32, 2)
            sq5 = sbt(st, "sq5", [128, D], F32)
            ss_r = ring(st, sbt, "ss5", [128, 1], F32, 4)
            y_r = ring(st, sbt, "y5", [128, D], F32, 2)
            subs = []
            for sg in range(8):
                subs.append(dict(n=256, tiles=[2 * sg, 2 * sg + 1], row0=sg * 256, out=y_own))
            subs.append(dict(n=NS_TOK, tiles=[NT_OWN], row0=OWN, out=y_s))
            for sd in subs:
                n = sd["n"]
                hm = hm_r.next()
                for t, tile in enumerate(sd["tiles"]):
                    nt = min(128, n - t * 128)
                    P.dma("sp", hm[:, :, t * 128:t * 128 + nt], T(hmT_d[tile].rearrange("p (k n) -> p k n", k=8)[:, :, :nt], None), nowaw=True)
                for f in range(32):
                    pu = pu5.next()
                    for k in range(8):
                        mm(pu[:, :n], wup[:, k, f * 128:(f + 1) * 128], hm[:, k, :n], k == 0, k == 7)
                    tmp = tmp_r.next()
                    ts(tmp[:, :n], pu[:, :n], 0.0, ALU.max)
                    act(actT[:, f, :n], tmp[:, :n], AF.Square)
                for t in range(len(sd["tiles"])):
                    nt = min(128, n - t * 128)
                    r0 = sd["row0"] + t * 128
                    xt = x_r.next()
                    P.dma("sp", xt[:nt], T(x1_d[r0:r0 + nt, :], None))
                    for half in range(2):
                        pd = pd5.next()
                        for f in range(32):
                            mm(pd[:nt, :], actT[:, f, t * 128:t * 128 + nt], wdn[:, f, half * 512:(half + 1) * 512], f == 0, f == 31)
                        tt(xt[:nt, half * 512:(half + 1) * 512], pd[:nt, :], xt[:nt, half * 512:(half + 1) * 512], ALU.add)
                    tt(sq5[:nt], xt[:nt], xt[:nt], ALU.mult, eng="pool")
                    ss = ss_r.next()
                    rsum(ss[:nt], sq5[:nt])
                    act(ss[:nt], ss[:nt], AF.Ln, bias=epsc[:nt], scale=1.0 / D)
                    act(ss[:nt], ss[:nt], AF.Exp, scale=-0.5)
                    y = y_r.next()
                    stt(y[:nt], xt[:nt], ss[:nt, 0:1], nfw[:nt], ALU.mult, ALU.mult)
                    ro = sd["row0"] - (0 if sd["out"] is y_own else OWN) + t * 128
                    P.dma("sp", T(sd["out"][ro:ro + nt, :], None), y[:nt], store=True)
            P.emit()
    return nc


_NC = None


def kernel(x_prompt, x_sample, cache_k, cache_v, page_table, state_conv, state_ssm,
           norm_mix_w, w_in, sb_bias, conv_w, a_log, dt_bias, gdn_norm_w, w_pa, w_pb, w_o,
           norm_mlp_w, w_up, w_down, norm_final_w):
    global _NC
    f32 = np.float32
    x_prompt = np.asarray(x_prompt, f32)
    x_sample = np.asarray(x_sample, f32)
    nc = build()
    ident = np.eye(128, dtype=f32)
    nmw = np.ascontiguousarray(np.asarray(norm_mix_w, f32)[0].reshape(8, 128).T)
    w_in0 = np.ascontiguousarray(np.asarray(w_in, f32)[0])
    cst = np.zeros((128, NCST, 128), f32)
    ii = np.arange(128)
    r_, c_ = ii[:, None], ii[None, :]
    cst[:, 0] = (r_ <= c_)
    cst[:, 1] = BIG * (c_ > r_)
    cst[:, 2] = -BIG * (r_ > c_)
    cst[:, 3] = (r_ < c_)
    cst[:, 4] = ((r_ // 64) == (c_ // 64))
    cst[:, 5] = 1.0 - cst[:, 4]
    cst[:, 6] = (r_ >= c_)
    same = ((r_ // 4) == (c_ // 4))
    cst[:, 7] = (r_ <= c_) & same
    cst[:, 8] = BIG * ((c_ > r_) | ~same)
    cst[:, 9] = -BIG * ((r_ > c_) | ~same)
    cst[:, 10] = (r_ < c_) & same
    cst[:, 11] = same
    cst[:, 12] = ((r_ // 4) == c_)
    state_conv = np.asarray(state_conv, f32)
    ck = np.asarray(cache_k, f32).reshape(2560 * 128, 512)
    cv = np.asarray(cache_v, f32).reshape(2560 * 128, 512)
    page_table = np.asarray(page_table, np.int32)
    if os.environ.get('NPOOL_DBG'):
        npd = int(os.environ['NPOOL_DBG'])
        ck, cv, page_table = ck[:npd * 128], cv[:npd * 128], page_table % npd
    iota = np.ascontiguousarray(np.tile(np.arange(128, dtype=np.int32)[:, None], (1, 256)))
    sbrow = np.ascontiguousarray(np.tile(np.repeat(np.asarray(sb_bias, f32)[0], 4)[None, :], (128, 16)))
    mnew = np.zeros((128, 32), f32)
    selE = np.zeros((64, 256), f32)
    selE[np.arange(64), np.arange(64)] = 1.0
    for t1 in range(4):
        for hq in range(32):
            mnew[t1, hq] = 1.0 if t1 < (hq % 4) else 0.0
    dmm = np.zeros((32, 8, 64), f32)
    for hq in range(32):
        dmm[hq, hq // 4, :] = 1.0
    dmm = dmm.reshape(32, 512)
    state_ssm = np.asarray(state_ssm, f32)
    sbb = np.ascontiguousarray(np.tile(np.asarray(sb_bias, f32)[0][None, :], (128, 1)))
    masks = []
    for jj in range(4):
        m = np.zeros((128, 16, 512), f32)
        for r in range(4):
            for kb in range(4):
                kbrel = 4 * r + kb
                for qb in range(4):
                    qbrel = 4 * jj + qb
                    if kbrel < qbrel:
                        m[:, r * 4 + kb, qb * 128:(qb + 1) * 128] = 1.0
                    elif kbrel == qbrel:
                        m[:, r * 4 + kb, qb * 128:(qb + 1) * 128] = (r_ < c_)
        masks.append(((m - 1.0) * 30000.0).astype(ml_dtypes.bfloat16))
    cwl = np.ascontiguousarray(np.asarray(conv_w, f32)[0].reshape(4, 3, 4, 128).transpose(3, 1, 2, 0).reshape(128, 12, 4))
    alog_bc = np.ascontiguousarray(np.tile(np.asarray(a_log, f32)[0][None, :], (128, 1)))
    dtb_bc = np.ascontiguousarray(np.tile(np.asarray(dt_bias, f32)[0][None, :], (128, 1)))
    gnw_bc = np.ascontiguousarray(np.tile(np.asarray(gdn_norm_w, f32)[0][None, :], (128, 1)))
    nmlw = np.ascontiguousarray(np.asarray(norm_mlp_w, f32)[0].reshape(8, 128).T)
    nfw_bc = np.ascontiguousarray(np.tile(np.asarray(norm_final_w, f32)[None, :], (128, 1)))
    w_pa0 = np.ascontiguousarray(np.asarray(w_pa, f32)[0])
    w_pb0 = np.ascontiguousarray(np.asarray(w_pb, f32)[0])
    w_o0 = np.ascontiguousarray(np.asarray(w_o, f32)[0])
    w_up0 = np.ascontiguousarray(np.asarray(w_up, f32)[0])
    w_down0 = np.ascontiguousarray(np.asarray(w_down, f32)[0])
    in_maps = []
    own_idx = []
    for c in range(NCORE):
        b, j = c // 4, c % 4
        groups = [4 * i + j for i in range(4)]
        idx = np.concatenate([np.arange(512 * g, 512 * g + 512) for g in groups])
        own_idx.append(idx)
        in_maps.append({
            "xfull": np.ascontiguousarray(x_prompt[b]),
            "xown": np.ascontiguousarray(x_prompt[b][idx]),
            "xs": np.ascontiguousarray(x_sample[16 * c:16 * c + 16].reshape(64, D)),
            "w_in": w_in0, "nmw": nmw, "ident": ident, "cst": cst, "cw": cwl, "alog_bc": alog_bc, "dtb_bc": dtb_bc,
            "gnw_bc": gnw_bc, "w_pa": w_pa0, "w_pb": w_pb0, "w_o": w_o0, "w_up": w_up0, "w_down": w_down0,
            "nmlw": nmlw, "nfw_bc": nfw_bc, "sbb": sbb,
            "cache_k": ck, "cache_v": cv, "pt": np.ascontiguousarray(page_table[16 * c:16 * c + 16].reshape(1, 256)),
            "iota": iota, "sbrow": sbrow, "mnew": mnew, "dm": dmm, "selE": selE,
            "sc": np.ascontiguousarray(state_conv[0, 16 * c:16 * c + 16].reshape(48, 1536)),
            "ssm": np.ascontiguousarray(state_ssm[0, 16 * c:16 * c + 16]), "maskd": masks[j], "sel": np.ascontiguousarray(np.tile((np.arange(4) == j).astype(f32)[None, :], (128, 1))),
        })
    if os.environ.get('RETURN_MAPS') == '1':
        return nc, in_maps
    res = run_bass_kernel_spmd(nc, in_maps, core_ids=list(range(NCORE)))
    R = res.results
    y_prompt = np.zeros((2, SEQ, D), f32)
    y_sample = np.zeros((128, 4, D), f32)
    nkp = np.zeros((2, SEQ, 512), f32)
    nvp = np.zeros((2, SEQ, 512), f32)
    nksa = np.zeros((128, 4, 512), f32)
    nvsa = np.zeros((128, 4, 512), f32)
    ncpa = np.zeros((1, 2, 3, 1536), f32)
    ncsa = np.zeros((1, 128, 3, 1536), f32)
    nsp = np.zeros((1, 2, 4, 128, 128), f32)
    nss = np.zeros((1, 128, 4, 128, 128), f32)
    for c in range(NCORE):
        b, j = c // 4, c % 4
        r = R[c]
        y_prompt[b][own_idx[c]] = r["y_own"]
        y_sample[16 * c:16 * c + 16] = r["y_s"].reshape(16, 4, D)
        nkp[b][own_idx[c]] = r["nk_own"]
        nvp[b][own_idx[c]] = r["nv_own"]
        nksa[16 * c:16 * c + 16] = r["nks"].reshape(16, 4, 512)
        nvsa[16 * c:16 * c + 16] = r["nvs"].reshape(16, 4, 512)
        ncsa[0, 16 * c:16 * c + 16] = r["ncs"]
        nss[0, 16 * c:16 * c + 16] = r["nss"]
        if j == 0:
            ncpa[0, b] = r["ncp"]
            nsp[0, b] = r["nsp"]
    return (y_prompt, y_sample,
            nkp.reshape(1, 2, 64, 128, 8, 64), nvp.reshape(1, 2, 64, 128, 8, 64),
            nksa.reshape(1, 128, 4, 8, 64), nvsa.reshape(1, 128, 4, 8, 64),
            ncpa, ncsa, nsp, nss)
```
